# Optimizing a Trainium2 kernel written in Bass

```python
import math
import jax, jax.numpy as jnp
from jax import lax
import numpy as np

D_MODEL = 1024
BATCH = 2
SEQ = 8192
DEPTH = 1
DEC_BATCH = 128
DEC_SEQ = 1
PAST_LEN = 8192
PAGE_SIZE = 128

D_MIX = D_MODEL
D_RNN = D_MIX // 2
N_RNN_BLOCKS = 8
RNN_BLOCK = D_RNN // N_RNN_BLOCKS
CONV_W = 4
LRU_C = 8.0
D_ATT = D_MIX - D_RNN
HEAD_DIM = 64
N_HEADS = D_ATT // HEAD_DIM
N_KV_HEADS = 2
GROUP = N_HEADS // N_KV_HEADS
KV_DIM = N_KV_HEADS * HEAD_DIM
WINDOW = 128
N_BUCKETS = 32
MAX_DISTANCE = 128
EPS = 1e-6
NEG_INF = -1e30
SPLITS = (D_RNN, 2 * D_RNN, 2 * D_RNN + D_ATT, 2 * D_RNN + D_ATT + KV_DIM, 2 * D_RNN + D_ATT + 2 * KV_DIM)
D_IN = 2 * D_RNN + 2 * D_ATT + 2 * KV_DIM

kernel_name = 'hymba_rglru_swa_sink_decode_step'


def rms_norm(x, g):
    xf = x.astype(jnp.float32)
    y = xf * lax.rsqrt(jnp.mean(xf * xf, axis=-1, keepdims=True) + EPS)
    return (y * g.astype(jnp.float32)).astype(x.dtype)


def t5_bucket(dist):
    dist = jnp.maximum(dist, 0)
    max_exact = N_BUCKETS // 2
    d = jnp.maximum(dist, 1).astype(jnp.float32)
    large = max_exact + (jnp.log(d / max_exact) / math.log(MAX_DISTANCE / max_exact)
                         * (N_BUCKETS - max_exact)).astype(jnp.int32)
    large = jnp.minimum(large, N_BUCKETS - 1)
    return jnp.where(dist < max_exact, dist, large)


def rel_bias_from_dist(dist, rel_bias):
    b = rel_bias.astype(jnp.float32)[t5_bucket(dist)]
    return jnp.moveaxis(b, -1, 0).reshape(N_KV_HEADS, GROUP, *dist.shape)


def softmax_with_sink(s, sinks):
    sk = jnp.broadcast_to(sinks.astype(jnp.float32).reshape(N_KV_HEADS, GROUP, 1, 1), s.shape[:-1] + (1,))
    p = jax.nn.softmax(jnp.concatenate([s, sk], axis=-1), axis=-1)
    return p[..., :-1]


def window_attention_prompt(q, k, v, sinks, rel_bias):
    B, S = q.shape[:2]
    nb = S // WINDOW
    qb = q.astype(jnp.float32).reshape(B, nb, WINDOW, N_KV_HEADS, GROUP, HEAD_DIM) * (HEAD_DIM ** -0.5)
    pad = jnp.zeros((B, WINDOW, N_KV_HEADS, HEAD_DIM), jnp.float32)
    kp = jnp.concatenate([pad, k.astype(jnp.float32)], axis=1).reshape(B, nb + 1, WINDOW, N_KV_HEADS, HEAD_DIM)
    vp = jnp.concatenate([pad, v.astype(jnp.float32)], axis=1).reshape(B, nb + 1, WINDOW, N_KV_HEADS, HEAD_DIM)
    kb = jnp.concatenate([kp[:, :-1], kp[:, 1:]], axis=2)
    vb = jnp.concatenate([vp[:, :-1], vp[:, 1:]], axis=2)
    qi = jnp.arange(WINDOW)[:, None]
    kj = jnp.arange(2 * WINDOW)[None, :]
    dist = qi + WINDOW - kj
    band = (dist >= 0) & (dist < WINDOW)
    has_prev = (jnp.arange(nb)[:, None, None] > 0) | (kj >= WINDOW)[None]
    mask = band[None] & has_prev
    bias = rel_bias_from_dist(dist, rel_bias)
    s = jnp.einsum('bnqhgd,bnkhd->bnhgqk', qb, kb) + bias[None, None]
    s = jnp.where(mask[None, :, None, None], s, NEG_INF)
    p = softmax_with_sink(s, sinks)
    o = jnp.einsum('bnhgqk,bnkhd->bnqhgd', p, vb)
    return o.reshape(B, S, D_ATT)


def window_attention_sample(q, k, v, k_buf, v_buf, sinks, rel_bias):
    B, T = q.shape[:2]
    wb = k_buf.shape[1]
    kc = jnp.concatenate([k_buf.astype(k.dtype), k], axis=1)
    vc = jnp.concatenate([v_buf.astype(v.dtype), v], axis=1)
    qf = q.astype(jnp.float32).reshape(B, T, N_KV_HEADS, GROUP, HEAD_DIM) * (HEAD_DIM ** -0.5)
    qi = jnp.arange(T)[:, None]
    kj = jnp.arange(wb + T)[None, :]
    dist = qi + wb - kj
    mask = (dist >= 0) & (dist < WINDOW)
    bias = rel_bias_from_dist(dist, rel_bias)
    s = jnp.einsum('bqhgd,bkhd->bhgqk', qf, kc.astype(jnp.float32)) + bias[None]
    s = jnp.where(mask[None, None, None], s, NEG_INF)
    p = softmax_with_sink(s, sinks)
    o = jnp.einsum('bhgqk,bkhd->bqhgd', p, vc.astype(jnp.float32))
    return o.reshape(B, T, D_ATT), kc[:, T:], vc[:, T:]


def causal_conv(x, conv_state, w, b):
    T = x.shape[1]
    xp = jnp.concatenate([conv_state.astype(x.dtype), x], axis=1)
    y = b.astype(x.dtype) + w[0].astype(x.dtype) * xp[:, 0:T]
    for tap in range(1, CONV_W):
        y = y + w[tap].astype(x.dtype) * xp[:, tap:tap + T]
    return y, xp[:, T:]


def rg_lru(x, h0, w_a, b_a, w_x, b_x, lam):
    B, T, _ = x.shape
    xf = x.astype(jnp.float32)
    xb = xf.reshape(B, T, N_RNN_BLOCKS, RNN_BLOCK)
    r = jax.nn.sigmoid(jnp.einsum('btni,nij->btnj', xb, w_a.astype(jnp.float32)).reshape(B, T, D_RNN)
                       + b_a.astype(jnp.float32))
    i = jax.nn.sigmoid(jnp.einsum('btni,nij->btnj', xb, w_x.astype(jnp.float32)).reshape(B, T, D_RNN)
                       + b_x.astype(jnp.float32))
    log_a = -LRU_C * r * jax.nn.softplus(-lam.astype(jnp.float32))
    a = jnp.exp(log_a)
    bx = jnp.sqrt(-jnp.expm1(2.0 * log_a)) * (i * xf)

    def combine(left, right):
        a1, b1 = left
        a2, b2 = right
        return a1 * a2, a2 * b1 + b2

    a_cum, b_cum = lax.associative_scan(combine, (a, bx), axis=1)
    h = a_cum * h0.astype(jnp.float32)[:, None] + b_cum
    return h, h[:, -1]


def decoder_layer(x, conv_state, h0, k_buf, v_buf, win_buf, rel_bias,
                  norm_pre, norm_post, w_in, conv_w, conv_b, w_gate_a, b_gate_a,
                  w_gate_x, b_gate_x, lru_lambda, attn_sinks, w_out):
    B, T, _ = x.shape
    xn = rms_norm(x, norm_pre)
    proj = jnp.einsum('btd,de->bte', xn, w_in.astype(xn.dtype))
    x_rnn, g_rnn, q, k, v, g_att = jnp.split(proj, SPLITS, axis=-1)
    xc, new_conv = causal_conv(x_rnn, conv_state, conv_w, conv_b)
    h, h_last = rg_lru(xc, h0, w_gate_a, b_gate_a, w_gate_x, b_gate_x, lru_lambda)
    rnn_out = h * jax.nn.silu(g_rnn.astype(jnp.float32))
    q = q.reshape(B, T, N_HEADS, HEAD_DIM)
    k = k.reshape(B, T, N_KV_HEADS, HEAD_DIM)
    v = v.reshape(B, T, N_KV_HEADS, HEAD_DIM)
    if k_buf is None:
        att = window_attention_prompt(q, k, v, attn_sinks, rel_bias)
        new_k, new_v = k[:, T - win_buf:], v[:, T - win_buf:]
    else:
        att, new_k, new_v = window_attention_sample(q, k, v, k_buf, v_buf, attn_sinks, rel_bias)
    att_out = att * jax.nn.silu(g_att.astype(jnp.float32))
    mix = jnp.concatenate([rnn_out, att_out], axis=-1).astype(x.dtype)
    out = jnp.einsum('bte,ed->btd', mix, w_out.astype(x.dtype))
    y = x + rms_norm(out, norm_post)
    return y, new_conv, h_last, new_k, new_v


def setup_inputs(seed: int = 0) -> dict:
    key = jax.random.key(seed)
    ks = jax.random.split(key, 20)
    win_buf = min(WINDOW, PAST_LEN)
    nrm = jax.random.normal
    x_prompt = nrm(ks[0], (BATCH, SEQ, D_MODEL), jnp.float32)
    x_sample = nrm(ks[1], (DEC_BATCH, DEC_SEQ, D_MODEL), jnp.float32)
    state_conv = nrm(ks[2], (DEPTH, DEC_BATCH, CONV_W - 1, D_RNN), jnp.float32)
    state_rnn = 0.5 * nrm(ks[3], (DEPTH, DEC_BATCH, D_RNN), jnp.float32)
    cache_k_win = nrm(ks[4], (DEPTH, DEC_BATCH, win_buf, N_KV_HEADS, HEAD_DIM), jnp.float32)
    cache_v_win = nrm(ks[5], (DEPTH, DEC_BATCH, win_buf, N_KV_HEADS, HEAD_DIM), jnp.float32)
    norm_pre = 1.0 + 0.05 * nrm(ks[6], (DEPTH, D_MODEL), jnp.float32)
    norm_post = 1.0 + 0.05 * nrm(ks[7], (DEPTH, D_MODEL), jnp.float32)
    w_in = nrm(ks[8], (DEPTH, D_MODEL, D_IN), jnp.float32) * D_MODEL ** -0.5
    conv_w = nrm(ks[9], (DEPTH, CONV_W, D_RNN), jnp.float32) * CONV_W ** -0.5
    conv_b = 0.01 * nrm(ks[10], (DEPTH, D_RNN), jnp.float32)
    w_gate_a = nrm(ks[11], (DEPTH, N_RNN_BLOCKS, RNN_BLOCK, RNN_BLOCK), jnp.float32) * RNN_BLOCK ** -0.5
    b_gate_a = 0.01 * nrm(ks[12], (DEPTH, D_RNN), jnp.float32)
    w_gate_x = nrm(ks[13], (DEPTH, N_RNN_BLOCKS, RNN_BLOCK, RNN_BLOCK), jnp.float32) * RNN_BLOCK ** -0.5
    b_gate_x = 0.01 * nrm(ks[14], (DEPTH, D_RNN), jnp.float32)
    a0 = jax.random.uniform(ks[15], (DEPTH, D_RNN), jnp.float32, minval=0.9, maxval=0.999)
    s = a0 ** (1.0 / LRU_C)
    lru_lambda = jnp.log(s) - jnp.log1p(-s)
    attn_sinks = 0.5 * nrm(ks[16], (DEPTH, N_HEADS), jnp.float32)
    rel_bias = 0.1 * nrm(ks[17], (N_BUCKETS, N_HEADS), jnp.float32)
    w_out = nrm(ks[18], (DEPTH, D_MIX, D_MODEL), jnp.float32) * D_MIX ** -0.5
    return {'x_prompt': x_prompt, 'x_sample': x_sample, 'state_conv': state_conv, 'state_rnn': state_rnn,
            'cache_k_win': cache_k_win, 'cache_v_win': cache_v_win,
            'norm_pre': norm_pre, 'norm_post': norm_post, 'w_in': w_in, 'conv_w': conv_w, 'conv_b': conv_b,
            'w_gate_a': w_gate_a, 'b_gate_a': b_gate_a, 'w_gate_x': w_gate_x, 'b_gate_x': b_gate_x,
            'lru_lambda': lru_lambda, 'attn_sinks': attn_sinks, 'rel_bias': rel_bias, 'w_out': w_out}


def reference(x_prompt, x_sample, state_conv, state_rnn, cache_k_win, cache_v_win,
              norm_pre, norm_post, w_in, conv_w, conv_b, w_gate_a, b_gate_a, w_gate_x, b_gate_x,
              lru_lambda, attn_sinks, rel_bias, w_out):
    win_buf = cache_k_win.shape[2]
    y_prompt, y_sample = x_prompt, x_sample
    conv_p, rnn_p, kw_p, vw_p = [], [], [], []
    conv_s, rnn_s, kw_s, vw_s = [], [], [], []
    for l in range(DEPTH):
        lw = (norm_pre[l], norm_post[l], w_in[l], conv_w[l], conv_b[l], w_gate_a[l], b_gate_a[l],
              w_gate_x[l], b_gate_x[l], lru_lambda[l], attn_sinks[l], w_out[l])
        zc = jnp.zeros((y_prompt.shape[0], CONV_W - 1, D_RNN), y_prompt.dtype)
        zh = jnp.zeros((y_prompt.shape[0], D_RNN), jnp.float32)
        y_prompt, c, h, kw, vw = decoder_layer(y_prompt, zc, zh, None, None, win_buf, rel_bias, *lw)
        conv_p.append(c)
        rnn_p.append(h)
        kw_p.append(kw)
        vw_p.append(vw)
        y_sample, c, h, kw, vw = decoder_layer(y_sample, state_conv[l], state_rnn[l], cache_k_win[l],
                                               cache_v_win[l], win_buf, rel_bias, *lw)
        conv_s.append(c)
        rnn_s.append(h)
        kw_s.append(kw)
        vw_s.append(vw)
    new_conv_prompt = jnp.stack(conv_p)
    new_rnn_prompt = jnp.stack(rnn_p)
    new_k_win_prompt = jnp.stack(kw_p)
    new_v_win_prompt = jnp.stack(vw_p)
    new_conv_sample = jnp.stack(conv_s)
    new_rnn_sample = jnp.stack(rnn_s)
    new_k_win_sample = jnp.stack(kw_s)
    new_v_win_sample = jnp.stack(vw_s)
    return (y_prompt, y_sample, new_conv_prompt, new_rnn_prompt, new_k_win_prompt, new_v_win_prompt,
            new_conv_sample, new_rnn_sample, new_k_win_sample, new_v_win_sample)
```

```python
import contextlib
import numpy as np
import concourse.bass as bass
import concourse.mybir as mybir
from concourse.bass_utils import run_bass_kernel_spmd

F32 = mybir.dt.float32
BF16 = mybir.dt.bfloat16
ALU = mybir.AluOpType
AF = mybir.ActivationFunctionType

NCORES = 8
D = 1024
D_IN = 2304
SEQ = 8192
CH = 2048
NT = 16
EPS = 1e-6
NEG = -1e30
LPOS = [0, 4, 1, 5, 2, 6, 3, 7]
C_XR, C_GR, C_Q, C_K, C_V, C_GA = 0, 512, 1024, 1536, 1664, 1792
SB = 16
PRE = 3 * CH
NPT = PRE // 128

ENGS = ("sync", "scalar", "vector", "gpsimd", "tensor")
CC_INC = 1
GROUPS = {"setup": 11, "W1": 9, "W2": 8, "W3": 8, "samp": 18, "samp2": 3, "outs": 14}
GROUPS_SEEN = {}
STAGE = 99
SUB = 99
NBLK = 99
NOOUT = 0
NOLAST = 0
SS = 99
DM = 255
S1 = 99


class Prog:
    def __init__(self, nc, es):
        self.nc = nc
        self.es = es
        self.q = {e: [] for e in ENGS}
        self.sig = {e: 0 for e in ENGS}
        self.pending = {e: False for e in ENGS}
        self.waited = {}
        self.bufs = {}
        self.dma_cnt = {}
        self.grp_seen = {}
        self.sems = {}

    def sem(self, key):
        if key not in self.sems:
            name = "s_" + "_".join(str(k) for k in key)
            self.sems[key] = self.es.enter_context(self.nc.semaphore(name))
        return self.sems[key]

    def _deps(self, eng, reads, writes, extra, skip_key=None):
        deps = set(extra)
        for r in reads:
            b = self.bufs.get(r)
            if b and b["w"] is not None:
                deps.add(b["w"])
            if b and r.startswith("pb"):
                deps.update(h for h in b["r"] if h[1] != eng)
        for w in writes:
            b = self.bufs.get(w)
            if b:
                if b["w"] is not None:
                    deps.add(b["w"])
                deps.update(b["r"])
        best = {}
        for d in deps:
            if d is None:
                continue
            key = d[:2]
            if eng == "tensor" and key == ("eng", "tensor"):
                continue
            if key == skip_key:
                continue
            if d[2] > best.get(key, 0):
                best[key] = d[2]
        waits = []
        for key, val in best.items():
            if self.waited.get((eng, key), 0) >= val:
                continue
            self.waited[(eng, key)] = val
            waits.append((key, val))
        return waits

    def _track(self, h, reads, writes):
        for r in reads:
            b = self.bufs.setdefault(r, {"w": None, "r": []})
            b["r"].append(h)
        for w in writes:
            self.bufs[w] = {"w": h, "r": []}

    def op(self, eng, fn, reads=(), writes=(), signal=True, deps=()):
        waits = self._deps(eng, reads, writes, deps)
        if signal:
            self.sig[eng] += 1
            h = ("eng", eng, self.sig[eng])
            self.pending[eng] = False
        else:
            h = ("eng", eng, self.sig[eng] + 1)
            self.pending[eng] = True
        self._track(h, reads, writes)
        semw = [(self.sem(k), v) for k, v in waits]
        mysem = self.sem(("eng", eng)) if signal else None

        def emit(e):
            for s, v in semw:
                e.wait_ge(s, v)
            ins = fn(e)
            if mysem is not None:
                ins.then_inc(mysem, 1)

        self.q[eng].append(emit)
        return h

    def dma(self, eng, out, in_, slot, reads=(), writes=(), deps=(), group=None, **kw):
        waits = self._deps(eng, reads, writes, deps, skip_key=(("dma", group) if group is not None else None))
        if group is not None:
            slot = group
            self.grp_seen[group] = self.grp_seen.get(group, 0) + 1
            cnt = 16 * GROUPS[group]
        else:
            cnt = self.dma_cnt.get(slot, 0) + 16
        self.dma_cnt[slot] = cnt
        h = ("dma", slot, cnt)
        self._track(h, reads, writes)
        semw = [(self.sem(k), v) for k, v in waits]
        mysem = self.sem(("dma", slot))

        def emit(e):
            for s, v in semw:
                e.wait_ge(s, v)
            e.dma_start(out=out, in_=in_, **kw).then_inc(mysem, 16)

        self.q[eng].append(emit)
        return h

    def cc(self, eng, fn, reads=(), writes=(), inc=None):
        inc = CC_INC if inc is None else inc
        waits = self._deps(eng, reads, writes, ())
        cnt = self.dma_cnt.get("cc", 0) + inc
        self.dma_cnt["cc"] = cnt
        h = ("dma", "cc", cnt)
        self._track(h, reads, writes)
        semw = [(self.sem(k), v) for k, v in waits]
        mysem = self.sem(("dma", "cc"))

        def emit(e):
            for s, v in semw:
                e.wait_ge(s, v)
            fn(e).then_inc(mysem, 1)

        self.q[eng].append(emit)
        return h

    def wait_all(self, eng, handles):
        waits = self._deps(eng, (), (), handles)
        semw = [(self.sem(k), v) for k, v in waits]

        def emit(e):
            for s, v in semw:
                e.wait_ge(s, v)

        self.q[eng].append(emit)

    def emit(self):
        assert not self.pending["tensor"], "PE has unsignalled trailing ops"
        GROUPS_SEEN.clear()
        GROUPS_SEEN.update(self.grp_seen)
        with self.nc.Block() as block:
            @block.sync
            def _(e):
                for f in self.q["sync"]:
                    f(e)

            @block.scalar
            def _(e):
                for f in self.q["scalar"]:
                    f(e)

            @block.vector
            def _(e):
                for f in self.q["vector"]:
                    f(e)

            @block.gpsimd
            def _(e):
                for f in self.q["gpsimd"]:
                    f(e)

            @block.tensor
            def _(e):
                for f in self.q["tensor"]:
                    f(e)


def build_program():
    nc = _build_program()
    if any(GROUPS.get(g) != n for g, n in GROUPS_SEEN.items()):
        GROUPS.update(GROUPS_SEEN)
        nc = _build_program()
        assert all(GROUPS.get(g) == n for g, n in GROUPS_SEEN.items())
    return nc


def _build_program():
    nc = bass.Bass("TRN2", target_bir_lowering=False)

    def din(name, shape):
        return nc.dram_tensor(name, list(shape), F32, kind="ExternalInput").ap()

    def dout(name, shape):
        return nc.dram_tensor(name, list(shape), F32, kind="ExternalOutput").ap()

    xc_d = din("xc", [PRE + CH + 128, D])
    flag_d = din("flag", [128, 3])
    w_in_d = din("w_in", [D, D_IN])
    w_out_d = din("w_out", [D, D])
    gpre_d = din("gpre", [128, 8])
    gpost_d = din("gpost", [128, D])
    convw_d = din("convw", [128, 16])
    vec4_d = din("vec4", [128, 16])
    bda_d = din("bda", [128, 512])
    bdx_d = din("bdx", [128, 512])
    sinks_d = din("sinks", [128, 8])
    biasg_d = din("biasg", [128, 2048])
    maskc_d = din("maskc", [128, 256])
    hmask_d = din("hmask", [128, 1])
    sel_d = din("sel", [128, 8])
    ident_d = din("ident", [128, 128])

    xs_d = din("xs", [128, D])
    sconv_d = din("sconv", [SB, 1536])
    srnn_d = din("srnn", [SB, 512])
    ck_d = din("ck", [SB * 128 + 128, 128])
    cv_d = din("cv", [SB * 128 + 128, 128])
    rowp_d = din("rowp", [SB, 4096])
    biass_d = din("biass", [128, 8])

    y_d = dout("y", [CH, D])
    ys_d = dout("ys", [SB, D])
    nconvs_d = dout("nconvs", [SB, 1536])
    nrnns_d = dout("nrnns", [SB, 512])
    nks_d = dout("nks", [SB, 128, 128])
    nvs_d = dout("nvs", [SB, 128, 128])
    nconv_d = dout("nconv", [3, 512])
    nrnn_d = dout("nrnn", [512])
    nk_d = dout("nk", [128, 128])
    nv_d = dout("nv", [128, 128])

    bounce = nc.dram_tensor("bounce", [128, 8], F32)
    gath = nc.dram_tensor("gath", [NCORES * 128, 8], F32)
    kvb = nc.dram_tensor("kvb", [SB, 256], F32)

    out_handles = []

    with contextlib.ExitStack() as es:
        P = Prog(nc, es)
        def sb(name, shape, dt=F32):
            return es.enter_context(nc.sbuf_tensor("sb_" + name, list(shape), dt))

        Win = sb("Win", [128, 8, D_IN], BF16)
        Wout = sb("Wout", [128, 8, D], BF16)
        P1 = sb("P1", [128, 4, CH], BF16)
        P2 = sb("P2", [128, 4, CH], BF16)
        attT = sb("attT", [128, 4, CH], BF16)
        KT = sb("KT", [128, 128 + 512], BF16)
        Vg = sb("Vg", [128, 5, 132], BF16)
        xt = [sb(f"xt{i}", [128, D]) for i in range(2)]
        xn = [sb(f"xn{i}", [128, D], BF16) for i in range(2)]
        xnT = sb("xnT", [128, 8, 512], BF16)
        xr = sb("xr", [128, 4, 515], BF16)
        xrt = sb("xrt", [128, 4, 4])
        xcv = sb("xcv", [128, 2, 512])
        u_b = sb("u_b", [128, 2, 512])
        a_b = sb("a_b", [128, 2, 512])
        a2_b = sb("a2_b", [128, 2, 512])
        t1_b = sb("t1_b", [128, 2, 512])
        th = [sb(f"th{i}", [128, 512]) for i in range(2)]
        hl = sb("hl", [128, 512])
        Ac = sb("Ac", [128, 512])
        QT = [sb(f"QT{g}", [128, 4, 512], BF16) for g in range(2)]
        BiasT = sb("BiasT", [128, 2, 2, 512], BF16)
        BiasF = sb("BiasF", [128, 2, 512], BF16)
        PT = [sb(f"PT{i}", [128, 2, 2, 512], BF16) for i in range(2)]
        ua = sb("ua", [128, 512])
        sgr = sb("sgr", [128, 512])
        att_o = sb("att_o", [128, 512], BF16)
        mixT = [sb(f"mixT{i}", [128, 4, 128], BF16) for i in range(2)]
        tt = sb("tt", [128, D])
        gpost = sb("gpost", [128, D])
        junk2 = sb("junk2", [128, 512], BF16)
        bda = sb("bda", [128, 4, 128])
        bdx = sb("bdx", [128, 4, 128])
        diagw = sb("diagw", [128, 4, 4, 128], BF16)
        ident = sb("ident", [128, 128], BF16)
        maskc = sb("maskc", [128, 2, 128])
        gpre = sb("gpre", [128, 8])
        convw = sb("convw", [128, 4, 4])
        vec4 = sb("vec4", [128, 4, 4])
        hb = sb("hb", [128, 2, 4])
        chalf = sb("chalf", [128, 4])
        sinks = sb("sinks", [128, 2, 4])
        sinkexp2 = sb("sinkexp2", [128, 2, 4])
        hmask = sb("hmask", [128, 1])
        sel = sb("sel", [128, 8])
        cst = sb("cst", [128, 4])
        ss = sb("ss", [128, 17 + NPT])
        ms = sb("ms", [128, 17 + NPT])
        rstd = sb("rstd", [128, 17 + NPT])
        flag = sb("flag", [128, 3])
        ss2 = sb("ss2", [128, NT, 2])
        ms2 = sb("ms2", [128, NT])
        rstd2 = sb("rstd2", [128, NT])
        car = sb("car", [128, 8])
        dsum = sb("dsum", [128, 4, 2])
        rden = sb("rden", [128, 4, 2])
        G_sb = sb("G_sb", [128, 8, 8])
        Ap = sb("Ap", [128, 8, 4])
        Bp = sb("Bp", [128, 8, 4])
        hscan = sb("hscan", [128, 4, 8])
        h0 = sb("h0", [128, 4])
        hfin = sb("hfin", [128, 4])
        kout = sb("kout", [128, 128])
        vout = sb("vout", [128, 128])
        sth = sb("sth", [128, 8])
        ident32 = sb("ident32", [128, 128])
        ones32 = sb("ones32", [128, 128])
        biass = sb("biass", [128, 8])
        uaT = sb("uaT", [128, 4, SB])
        sm = sb("sm", [128, 16])

        pb = [es.enter_context(nc.psum_tensor(f"pb{i}", [128, 512], F32)) for i in range(8)]
        pbT = [p[:].bitcast(BF16).rearrange("p (k c) -> p k c", c=128) for p in pb]
        bank_ctr = [0]

        def bank():
            i = bank_ctr[0]
            bank_ctr[0] = (i + 1) % 8
            return i

        def mk(eng):
            def f(name, *args, r=(), w=(), sig=True, **kw):
                return P.op(eng, lambda e: getattr(e, name)(*args, **kw), reads=r, writes=w, signal=sig)
            return f
        V, A, G = mk("vector"), mk("scalar"), mk("gpsimd")
        _pe = mk("tensor")

        def PE(name, *args, r=(), w=(), sig=False, **kw):
            return _pe(name, *args, r=r, w=w, sig=sig, **kw)

        def ld(dst, src, name, group="setup"):
            P.dma("sync", dst, src, name, writes=[name], group=group)

        ld(gpre[:], gpre_d, "gpre")
        ld(convw[:].rearrange("p g t -> p (g t)"), convw_d, "convw")
        ld(vec4[:].rearrange("p a g -> p (a g)"), vec4_d, "vec4")
        ld(hmask[:], hmask_d, "hmask")
        ld(sel[:], sel_d, "sel")
        ld(sinks[:].rearrange("p g c -> p (g c)"), sinks_d, "sinks")
        ld(maskc[:].rearrange("p b q -> p (b q)"), maskc_d, "maskc")
        ld(bda[:].rearrange("p g m -> p (g m)"), bda_d, "bda")
        ld(bdx[:].rearrange("p g m -> p (g m)"), bdx_d, "bdx")
        ld(gpost[:], gpost_d, "gpost")
        ld(xt[0][:], biasg_d[:, 0:1024], "xt0", group=None)
        ld(xt[1][:], biasg_d[:, 1024:2048], "xt1", group=None)
        P.dma("gpsimd", ident[:], ident_d, "ident", writes=["ident"], group="W1")
        for k in range(8):
            for h in range(2):
                P.dma("gpsimd", Win[:, k, h * 1152:(h + 1) * 1152], w_in_d[k * 128:(k + 1) * 128, h * 1152:(h + 1) * 1152],
                      f"Win{k}_{h}", writes=[f"Win{k}_{h}"], group=("W1" if h == 0 else "W2"))
        for k in range(8):
            P.dma("gpsimd", Wout[:, k, :], w_out_d[k * 128:(k + 1) * 128, :], f"Wout{k}", writes=[f"Wout{k}"], group="W3")

        G("memset", cst[:, 0:1], -0.5, w=["cst"])
        G("memset", cst[:, 1:2], 0.0, r=["cst"], w=["cst"])
        G("memset", cst[:, 2:3], 1.0 / 16.0, r=["cst"], w=["cst"])
        G("memset", Vg[:], 1.0, w=["Vg"])
        G("memset", QT[0][64:128], 0.0, w=["QT0"])
        G("memset", QT[1][0:64], 0.0, w=["QT1"])
        G("memset", ones32[:], 1.0, w=["ones32"])
        A("activation", out=sth[:, 0:4], in_=vec4[:, 3, :], func=AF.Exp, scale=-1.0, r=["vec4"], w=["sth"])
        A("activation", out=sth[:, 4:8], in_=sth[:, 0:4], func=AF.Ln, bias=1.0, r=["sth"], w=["sth"])
        V("tensor_scalar", out=chalf[:], in0=sth[:, 4:8], scalar1=-4.0, scalar2=None, op0=ALU.mult, r=["sth"], w=["chalf"])
        V("tensor_scalar", out=hb[:], in0=vec4[:, 1:3, :], scalar1=0.5, scalar2=None, op0=ALU.mult, r=["vec4"], w=["hb"])
        A("activation", out=sinkexp2[:], in_=sinks[:], func=AF.Exp, r=["sinks"], w=["sinkexp2"])
        V("tensor_scalar", out=sinkexp2[:], in0=sinkexp2[:], scalar1=2.0, scalar2=None, op0=ALU.mult, r=["sinkexp2"], w=["sinkexp2"])
        for blk in range(2):
            V("tensor_tensor", out=BiasT[:, blk].rearrange("p g (c q) -> p (g c) q", q=128),
                                                 in0=xt[blk][:].rearrange("p (h q) -> p h q", q=128),
                                                 in1=maskc[:, blk, :].unsqueeze(1).broadcast_to([128, 8, 128]), op=ALU.add,
              r=[f"xt{blk}", "maskc"], w=["BiasT"])
        V("tensor_tensor", out=xt[0][:].rearrange("p (h q) -> p h q", q=128), in0=xt[0][:].rearrange("p (h q) -> p h q", q=128),
                                    in1=maskc[:, 0, :].unsqueeze(1).broadcast_to([128, 8, 128]), op=ALU.add,
          r=["xt0", "maskc"], w=["xt0"])
        V("tensor_scalar", out=BiasF[:].rearrange("p g c -> p (g c)"), in0=xt[0][:], scalar1=hmask[:, 0:1], scalar2=None, op0=ALU.add,
          r=["xt0", "hmask"], w=["BiasF"])
        for g in range(4):
            for tap in range(4):
                V("tensor_scalar", out=diagw[:, g, tap, :], in0=ident[:], scalar1=convw[:, g, tap:tap + 1], scalar2=None, op0=ALU.mult,
                  r=["ident", "convw"], w=["diagw"])

        def win_names(c0, c1):
            hs = sorted({c0 // 1152, (c1 - 1) // 1152})
            return hs

        def tile_front(t, j):
            s = t % 2
            row = PRE + t * 128 if t < 17 else (t - 17) * 128
            P.dma("sync", xt[s][:], xc_d[row:row + 128, :], f"xt{s}", writes=[f"xt{s}"])
            A("activation", out=xn[s][:], in_=xt[s][:], func=AF.Square, accum_out=ss[:, t:t + 1],
              r=[f"xt{s}"], w=[f"xn{s}", f"ss{t}"])
            G("tensor_scalar", out=ms[:, t:t + 1], in0=ss[:, t:t + 1], scalar1=1.0 / D, scalar2=EPS, op0=ALU.mult, op1=ALU.add,
              r=[f"ss{t}"], w=[f"ms{t}"])
            G("tensor_tensor", out=rstd[:, t:t + 1], in0=ms[:, t:t + 1], in1=cst[:, 0:1], op=ALU.pow,
              r=[f"ms{t}", "cst"], w=[f"rstd{t}"])
            G("tensor_scalar", out=xn[s][:], in0=xt[s][:], scalar1=rstd[:, t:t + 1], scalar2=0.0, op0=ALU.mult, op1=ALU.add,
              r=[f"xt{s}", f"rstd{t}"], w=[f"xn{s}"])
            b = bank()
            for k in range(8):
                PE("transpose", out=pbT[b][:, k, :], in_=xn[s][:, k * 128:(k + 1) * 128], identity=ident[:],
                   r=[f"xn{s}", "ident"], w=[f"pb{b}"], sig=(k == 7))
            V("tensor_tensor", out=xnT[:, :, j * 128:(j + 1) * 128], in0=pbT[b][:, :, :],
                                        in1=gpre[:].unsqueeze(2).broadcast_to([128, 8, 128]), op=ALU.mult,
              r=[f"pb{b}", "gpre"], w=["xnT"])

        def fm_chunk(c0, N):
            b = bank()
            h = c0 // 1152
            for k in range(8):
                PE("matmul", pb[b][:, 0:N], lhsT=Win[:, k, c0:c0 + 128], rhs=xnT[:, k, 0:N], start=(k == 0), stop=(k == 7),
                   r=[f"Win{k}_{h}", "xnT"], w=[f"pb{b}"], sig=(k == 7))
            return b

        thc = [0]

        def th_slot():
            thc[0] ^= 1
            return thc[0]

        def rnn_chain(N, t0, pre=False):
            for gp in range(2):
                gs = (2 * gp, 2 * gp + 1)
                for gi, g in enumerate(gs):
                    b = bank()
                    for tap in range(4):
                        PE("matmul", pb[b][:, 0:N], lhsT=diagw[:, g, tap, :], rhs=xr[:, g, tap:tap + N],
                                                                 start=(tap == 0), stop=(tap == 3),
                           r=["diagw", "xr"], w=[f"pb{b}"], sig=(tap == 3))
                    A("activation", out=xcv[:, gi, 0:N], in_=pb[b][:, 0:N], func=AF.Identity, bias=vec4[:, 0, g:g + 1],
                      r=[f"pb{b}", "vec4"], w=[f"xcv{gi}"])
                    if pre:
                        continue
                    b = fm_chunk(C_GR + g * 128, N)
                    s = th_slot()
                    A("activation", out=th[s][:, 0:N], in_=pb[b][:, 0:N], func=AF.Tanh, scale=0.5, r=[f"pb{b}"], w=[f"th{s}"])
                    V("scalar_tensor_tensor", out=u_b[:, gi, 0:N], in0=th[s][:, 0:N], scalar=1.0, in1=pb[b][:, 0:N],
                                                                        op0=ALU.add, op1=ALU.mult,
                      r=[f"pb{b}", f"th{s}"], w=[f"u{gi}"])
                for gi, g in enumerate(gs):
                    bA = bank()
                    PE("matmul", pb[bA][:, 0:N], lhsT=bda[:, g, :], rhs=xcv[:, gi, 0:N], start=True, stop=True,
                       r=["bda", f"xcv{gi}"], w=[f"pb{bA}"], sig=True)
                    bX = bank()
                    PE("matmul", pb[bX][:, 0:N], lhsT=bdx[:, g, :], rhs=xcv[:, gi, 0:N], start=True, stop=True,
                       r=["bdx", f"xcv{gi}"], w=[f"pb{bX}"], sig=True)
                    s = th_slot()
                    A("activation", out=th[s][:, 0:N], in_=pb[bA][:, 0:N], func=AF.Tanh, scale=0.5, bias=hb[:, 0, g:g + 1],
                      r=[f"pb{bA}", "hb"], w=[f"th{s}"])
                    A("activation", out=a_b[:, gi, 0:N], in_=th[s][:, 0:N], func=AF.Exp, scale=chalf[:, g:g + 1], bias=chalf[:, g:g + 1],
                      r=[f"th{s}", "chalf"], w=[f"a{gi}"])
                    G("tensor_tensor", out=a2_b[:, gi, 0:N], in0=a_b[:, gi, 0:N], in1=a_b[:, gi, 0:N], op=ALU.mult,
                      r=[f"a{gi}"], w=[f"a2{gi}"])
                    s2 = th_slot()
                    A("activation", out=th[s2][:, 0:N], in_=pb[bX][:, 0:N], func=AF.Tanh, scale=0.5, bias=hb[:, 1, g:g + 1],
                      r=[f"pb{bX}", "hb"], w=[f"th{s2}"])
                    V("scalar_tensor_tensor", out=t1_b[:, gi, 0:N], in0=th[s2][:, 0:N], scalar=1.0, in1=xcv[:, gi, 0:N],
                                                                     op0=ALU.add, op1=ALU.mult,
                      r=[f"th{s2}", f"xcv{gi}"], w=[f"t1{gi}"])
                rnn_tail.append((gs, N, t0, pre))
                if SUB >= 4:
                    flush_rnn_tail()
                else:
                    rnn_tail.clear()

        def block(bi, t0, nt, pre=False):
            N = nt * 128
            halo = (bi == 0)
            last = (bi == 4) and not NOLAST
            for j in range(nt):
                tile_front(t0 + j, j)
            if not halo and Nprev[0] > 0:
                npv = Nprev[0]
                V("tensor_copy", out=xr[:, :, 0:3], in_=xr[:, :, npv:npv + 3], r=["xr"], w=["xr"])
            for g in range(4):
                b = fm_chunk(C_XR + g * 128, N)
                A("activation", out=xr[:, g, 3:3 + N], in_=pb[b][:, 0:N], func=AF.Copy, r=[f"pb{b}"], w=["xr"])
                if last:
                    A("activation", out=xrt[:, g, :], in_=pb[b][:, N - 4:N], func=AF.Copy, r=[f"pb{b}"], w=["xrt"])
            Nprev[0] = N
            if pre:
                rnn_chain(N, t0, pre=True)
                return
            b = fm_chunk(C_K, N)
            if not halo:
                kpv, vpv = KTprev[0], Vprev[0]
                if kpv > 0:
                    V("tensor_copy", out=KT[:, 0:128], in_=KT[:, kpv:kpv + 128], r=["KT"], w=["KT"])
                    V("tensor_copy", out=Vg[:, 0, :], in_=Vg[:, vpv, :], r=["Vg"], w=["Vg"])
                A("activation", out=KT[:, 128:128 + N], in_=pb[b][:, 0:N], func=AF.Copy, r=[f"pb{b}"], w=["KT"])
                KTprev[0] = N
                Vprev[0] = nt
            else:
                A("activation", out=KT[:, 0:128], in_=pb[b][:, 0:N], func=AF.Copy, r=[f"pb{b}"], w=["KT"])
                KTprev[0] = 0
                Vprev[0] = 0
            for j in range(nt):
                t = t0 + j
                vs = 0 if halo else j + 1
                bV = bank()
                for k in range(8):
                    PE("matmul", pb[bV][:, 0:128], lhsT=xnT[:, k, j * 128:(j + 1) * 128], rhs=Win[:, k, C_V:C_V + 128],
                                                    start=(k == 0), stop=(k == 7),
                       r=[f"Win{k}_1", "xnT"], w=[f"pb{bV}"], sig=(k == 7))
                V("tensor_copy", out=Vg[:, vs, :].rearrange("p (g e) -> p g e", e=66)[:, :, 0:64],
                                                 in_=pb[bV][:, 0:128].rearrange("p (g d) -> p g d", d=64),
                  r=[f"pb{bV}"], w=["Vg"])
                if t == NT and not NOLAST:
                    A("activation", out=vout[:], in_=pb[bV][:, 0:128], func=AF.Copy, r=[f"pb{bV}"], w=["vout"])
                    bK = bank()
                    for k in range(8):
                        PE("matmul", pb[bK][:, 0:128], lhsT=xnT[:, k, j * 128:(j + 1) * 128], rhs=Win[:, k, C_K:C_K + 128],
                                                        start=(k == 0), stop=(k == 7),
                           r=[f"Win{k}_1", "xnT"], w=[f"pb{bK}"], sig=(k == 7))
                    A("activation", out=kout[:], in_=pb[bK][:, 0:128], func=AF.Copy, r=[f"pb{bK}"], w=["kout"])
            if halo or SUB < 2:
                return
            for cc in range(4):
                b = fm_chunk(C_Q + cc * 128, N)
                V("tensor_scalar", out=QT[0][0:64, cc, 0:N], in0=pb[b][0:64, 0:N], scalar1=0.125, scalar2=None, op0=ALU.mult,
                  r=[f"pb{b}"], w=["QT0"])
                V("tensor_scalar", out=QT[1][64:128, cc, 0:N], in0=pb[b][64:128, 0:N], scalar1=0.125, scalar2=None, op0=ALU.mult,
                  r=[f"pb{b}"], w=["QT1"])
            if SUB < 3:
                return
            rnn_chain(N, t0)
            if SUB < 5:
                return
            for j in range(nt):
                t = t0 + j
                bG = bank()
                for k in range(8):
                    PE("matmul", pb[bG][:, 0:512], lhsT=xnT[:, k, j * 128:(j + 1) * 128], rhs=Win[:, k, C_GA:C_GA + 512],
                                                    start=(k == 0), stop=(k == 7),
                       r=[f"Win{k}_1", "xnT"], w=[f"pb{bG}"], sig=(k == 7))
                s = th_slot()
                A("activation", out=th[s][:], in_=pb[bG][:], func=AF.Tanh, scale=0.5, r=[f"pb{bG}"], w=[f"th{s}"])
                V("scalar_tensor_tensor", out=ua[:], in0=th[s][:], scalar=1.0, in1=pb[bG][:], op0=ALU.add, op1=ALU.mult,
                  r=[f"pb{bG}", f"th{s}"], w=["ua"])
                if SUB >= 6:
                    attention(t, j)

        def attention(t, j):
            ps = t % 2
            for blk in range(2):
                kc = (j + blk) * 128
                for g in range(2):
                    b = bank()
                    bias_ap = BiasF[:, g, :] if (t == 1 and blk == 0) else BiasT[:, blk, g, :]
                    bname = "BiasF" if (t == 1 and blk == 0) else "BiasT"
                    PE("matmul", pb[b][:].rearrange("p (c q) -> p c q", q=128), lhsT=KT[:, kc:kc + 128],
                                                           rhs=QT[g][:, :, j * 128:(j + 1) * 128], start=True, stop=False,
                       r=["KT", f"QT{g}"], w=[f"pb{b}"])
                    PE("matmul", pb[b][:], lhsT=ident[:], rhs=bias_ap, start=False, stop=True,
                       r=["ident", bname], w=[f"pb{b}"], sig=True)
                    A("activation", out=PT[ps][:, blk, g, :], in_=pb[b][:], func=AF.Exp, r=[f"pb{b}"], w=[f"PT{ps}"])
            if SUB < 7:
                return
            bO = []
            for g in range(2):
                b = bank()
                bO.append(b)
                for cc in range(4):
                    for blk in range(2):
                        PE("matmul", pb[b][:, cc * 66:cc * 66 + 66], lhsT=PT[ps][:, blk, g, cc * 128:(cc + 1) * 128],
                                                                        rhs=Vg[:, j + blk, g * 66:(g + 1) * 66], start=(blk == 0), stop=(blk == 1),
                           r=[f"PT{ps}", "Vg"], w=[f"pb{b}"], sig=(cc == 3 and blk == 1))
                V("scalar_tensor_tensor", out=dsum[:, :, g], in0=pb[b][:, 0:264].rearrange("p (c e) -> p c e", e=66)[:, :, 64],
                                                             scalar=2.0, in1=sinkexp2[:, g, :], op0=ALU.mult, op1=ALU.add,
                  r=[f"pb{b}", "sinkexp2"], w=["dsum"])
            if SUB < 8:
                return
            V("reciprocal", out=rden[:], in_=dsum[:], r=["dsum"], w=["rden"])
            G("tensor_tensor", out=sgr[:].rearrange("p (c g d) -> p c g d", g=2, d=64), in0=ua[:].rearrange("p (c g d) -> p c g d", g=2, d=64),
                                        in1=rden[:].unsqueeze(3).broadcast_to([128, 4, 2, 64]), op=ALU.mult,
              r=["ua", "rden"], w=["sgr"])
            for g in range(2):
                b = bO[g]
                V("tensor_tensor", out=att_o[:].rearrange("p (c g d) -> p c g d", g=2, d=64)[:, :, g, :],
                                                      in0=pb[b][:, 0:264].rearrange("p (c e) -> p c e", e=66)[:, :, 0:64],
                                                      in1=sgr[:].rearrange("p (c g d) -> p c g d", g=2, d=64)[:, :, g, :], op=ALU.mult,
                  r=[f"pb{b}", "sgr"], w=["att_o"])
            b = bank()
            for cc in range(4):
                PE("transpose", out=pbT[b][:, cc, :], in_=att_o[:, cc * 128:(cc + 1) * 128], identity=ident[:],
                   r=["att_o", "ident"], w=[f"pb{b}"], sig=(cc == 3))
            A("activation", out=attT[:, :, (t - 1) * 128:t * 128], in_=pbT[b][:, 0:4, :], func=AF.Copy, r=[f"pb{b}"], w=[f"attT{t}"])

        rnn_tail = []
        first_scan = [True]
        first_A = [True]

        def flush_rnn_tail():
            for gs, N, t0, pre in rnn_tail:
                for gi, g in enumerate(gs):
                    A("activation", out=a2_b[:, gi, 0:N], in_=a2_b[:, gi, 0:N], func=AF.Sqrt, scale=-1.0 / 16.0, bias=1.0 / 16.0,
                      r=[f"a2{gi}"], w=[f"a2{gi}"])
            for gs, N, t0, pre in rnn_tail:
                c0 = (t0 - 1) * 128
                for gi, g in enumerate(gs):
                    G("tensor_tensor", out=t1_b[:, gi, 0:N], in0=t1_b[:, gi, 0:N], in1=a2_b[:, gi, 0:N], op=ALU.mult,
                      r=[f"t1{gi}", f"a2{gi}"], w=[f"t1{gi}"])
                    fs = first_scan[0]
                    fsA = first_A[0]
                    V("tensor_tensor_scan", out=hl[:, 0:N], data0=a_b[:, gi, 0:N], data1=t1_b[:, gi, 0:N],
                                                                              initial=(0.0 if fs else car[:, 4 + g:5 + g]), op0=ALU.mult, op1=ALU.add,
                      r=[f"a{gi}", f"t1{gi}", "car"], w=["hl"])
                    V("tensor_copy", out=car[:, 4 + g:5 + g], in_=hl[:, N - 1:N], r=["hl", "car"], w=["car"])
                    if pre:
                        continue
                    V("tensor_tensor_scan", out=Ac[:, 0:N], data0=a_b[:, gi, 0:N], data1=cst[:, 1:2].broadcast_to([128, N]),
                                                                              initial=(1.0 if fsA else car[:, g:g + 1]), op0=ALU.mult, op1=ALU.add,
                      r=[f"a{gi}", "cst", "car"], w=["Ac"])
                    V("tensor_copy", out=car[:, g:g + 1], in_=Ac[:, N - 1:N], r=["Ac", "car"], w=["car"])
                    G("tensor_tensor", out=P1[:, g, c0:c0 + N], in0=hl[:, 0:N], in1=u_b[:, gi, 0:N], op=ALU.mult,
                      r=["hl", f"u{gi}"], w=[f"P1_{g}_{c0}"])
                    G("tensor_tensor", out=P2[:, g, c0:c0 + N], in0=Ac[:, 0:N], in1=u_b[:, gi, 0:N], op=ALU.mult,
                      r=["Ac", f"u{gi}"], w=[f"P2_{g}_{c0}"])
                if gs[1] == 3:
                    first_scan[0] = False
                    if not pre:
                        first_A[0] = False
            rnn_tail.clear()

        Nprev = [0]
        KTprev = [0]
        Vprev = [0]
        G("memset", xr[:], 0.0, w=["xr"])
        ld(flag[:], flag_d, "flag")
        for pb_i in range(NPT // 4):
            block(100 + pb_i, 17 + 4 * pb_i, 4, pre=True)
            if pb_i % 4 == 3:
                kf = pb_i // 4
                V("tensor_scalar", out=car[:, 4:8], in0=car[:, 4:8], scalar1=flag[:, kf:kf + 1], scalar2=None, op0=ALU.mult,
                  r=["car", "flag"], w=["car"])
        Nprev[0] = 0
        blocks = [(0, 0, 1), (1, 1, 4), (2, 5, 4), (3, 9, 4), (4, 13, 4)]
        for bi, t0, nt in blocks:
            if STAGE >= (1 if bi == 0 else 2 if bi == 1 else 3) and bi <= NBLK:
                block(bi, t0, nt)

        if STAGE >= 3 and not NOOUT:
            for g in range(4):
                out_handles.append(P.dma("sync", nconv_d[:, g * 128:(g + 1) * 128].rearrange("t p -> p t"), xrt[:, g, 1:4], f"o_nconv{g}", reads=["xrt"],
                                         allow_slow_non_contiguous=True, group="outs"))
            out_handles.append(P.dma("sync", nk_d, kout[:], "o_nk", reads=["kout"], group="outs"))
            out_handles.append(P.dma("sync", nv_d, vout[:], "o_nv", reads=["vout"], group="outs"))

        if STAGE >= 4:
            G("memset", h0[:], 0.0, w=["h0"])
            V("tensor_scalar", out=hfin[:], in0=car[:, 4:8], scalar1=2.0, scalar2=None, op0=ALU.mult, r=["car"], w=["hfin"])
            out_handles.append(P.dma("sync", nrnn_d.rearrange("(g p) -> p g", p=128), hfin[:], "o_nrnn", reads=["hfin"],
                                     allow_slow_non_contiguous=True, group="outs"))

        def sample_path():
            F = lambda ap: ap.bitcast(F32)
            xnTs = QT[0][:].rearrange("p c n -> p (c n)")[:, 0:1024].rearrange("p (k t) -> p k t", t=128)
            Kn = F(xnT[:].rearrange("p k n -> p (k n)")).rearrange("p (s c) -> p s c", c=128)
            Vn = [F(PT[h][:].rearrange("p a b n -> p (a b n)")).rearrange("p (s c) -> p s c", c=128) for h in range(2)]
            KnT = [a_b[:].rearrange("p a n -> p (a n)").rearrange("p (s c) -> p s c", c=128),
                   a2_b[:].rearrange("p a n -> p (a n)").rearrange("p (s c) -> p s c", c=128)]
            xcT = u_b[:].rearrange("p a n -> p (a n)")[:, 0:512].rearrange("p (g t) -> p g t", t=128)
            Qm = th[0][:, 0:128].rearrange("p (s q) -> p s q", q=8)
            Pt = hl[:, 0:128]
            sc = Ac[:, 0:128]
            rds = Ac[:, 128:256]
            attn = hl[:, 128:256]
            xcs = t1_b[:, 0, :]
            tA = t1_b[:, 1, :]
            tB = xcv[:, 0, :]
            tC = xcv[:, 1, :]
            tD = ua[:]
            tE = sgr[:]
            mixtm = xn[0][:, 0:512]
            mixTs = xn[1][:].rearrange("p (k t) -> p k t", t=128)
            R16 = slice(0, SB)
            flat32 = lambda t, pat: F(t[:].rearrange(pat))
            hosts = [(xt[1][:], "xt1"), (flat32(QT[1], "p c n -> p (c n)"), "QT1"), (flat32(BiasT, "p a b n -> p (a b n)"), "BiasT"),
                     (flat32(diagw, "p g t m -> p (g t m)"), "diagw")]
            rp, rpn = [], []
            for hap, hname in hosts:
                for i2 in range(2):
                    rp.append(hap[R16, i2 * 512:(i2 + 1) * 512])
                    rpn.append(hname)
            for i8 in range(8):
                P.dma("sync", rp[i8], rowp_d[:, i8 * 512:(i8 + 1) * 512], f"rowp{i8}", writes=[rpn[i8]], group="samp")
            sst_t = [tt[R16, 0:512], tt[R16, 512:1024], F(BiasF[:].rearrange("p g n -> p (g n)"))[R16, 0:512]]
            sst_n = ["tt", "tt", "BiasF"]
            for t3 in range(3):
                P.dma("sync", sst_t[t3], sconv_d[:, t3 * 512:(t3 + 1) * 512], f"sst{t3}", writes=[sst_n[t3]], group="samp")
            hprev = th[1][R16, :]
            P.dma("sync", hprev, srnn_d, "hprev", writes=["th1"], group="samp")
            kv_s = F(att_o[:])[R16, :]
            ld(ident32[:], ident_d, "ident32", group="samp")
            ld(biass[:], biass_d, "biass", group="samp")
            def win(src, h):
                return src[1 + 8 * h * 128:1 + (8 * h + 8) * 128, :].rearrange("(s k) c -> k s c", k=128)
            big = [None]
            for h in range(2):
                big[0] = P.dma("sync", Kn[:, 8 * h:8 * h + 8, :], win(ck_d, h), f"Kn_a{h}", writes=["xnT", f"KnH{h}"], deps=[big[0]])
                big[0] = P.dma("sync", Vn[h][:], win(cv_d, h), f"Vn_a{h}", writes=[f"PT{h}"], deps=[big[0]])
            if SS < 1:
                return
            if S1 < 1:
                return
            pass
            if S1 < 2:
                return
            P.dma("sync", xt[0][:], xs_d, "xt0", writes=["xt0"])
            if S1 < 3:
                return
            pass
            if S1 < 4:
                return
            A("activation", out=xn[0][:], in_=xt[0][:], func=AF.Square, accum_out=sm[:, 0:1], r=["xt0", "sm"], w=["xn0", "sm"])
            if S1 < 5:
                return
            V("tensor_scalar", out=sm[:, 1:2], in0=sm[:, 0:1], scalar1=1.0 / D, scalar2=EPS, op0=ALU.mult, op1=ALU.add, r=["sm"], w=["sm"])
            if S1 < 6:
                return
            A("activation", out=sm[:, 1:2], in_=sm[:, 1:2], func=AF.Sqrt, r=["sm"], w=["sm"])
            if S1 < 7:
                return
            V("reciprocal", out=sm[:, 2:3], in_=sm[:, 1:2], r=["sm"], w=["sm"])
            if S1 < 8:
                return
            A("activation", out=xn[0][:], in_=xt[0][:], func=AF.Copy, scale=sm[:, 2:3], r=["xt0", "sm"], w=["xn0"])
            if S1 < 9:
                return
            b = bank()
            for k in range(8):
                PE("transpose", out=pbT[b][:, k, :], in_=xn[0][:, k * 128:(k + 1) * 128], identity=ident[:], r=["xn0", "ident"], w=[f"pb{b}"], sig=(k == 7))
            if S1 < 10:
                return
            V("tensor_tensor", out=xnTs, in0=pbT[b][:, :, :], in1=gpre[:].unsqueeze(2).broadcast_to([128, 8, 128]), op=ALU.mult,
              r=[f"pb{b}", "gpre"], w=["QT0"])

            def tm_cols(c0, w):
                bb = bank()
                for k in range(8):
                    PE("matmul", pb[bb][:, 0:w], lhsT=xnTs[:, k, :], rhs=Win[:, k, c0:c0 + w], start=(k == 0), stop=(k == 7),
                       r=["QT0"] + [f"Win{k}_{hh}" for hh in sorted({c0 // 1152, (c0 + w - 1) // 1152})], w=[f"pb{bb}"], sig=(k == 7))
                return bb

            def fm_cols(c0):
                bb = bank()
                hh = c0 // 1152
                for k in range(8):
                    PE("matmul", pb[bb][:, 0:128], lhsT=Win[:, k, c0:c0 + 128], rhs=xnTs[:, k, :], start=(k == 0), stop=(k == 7),
                       r=["QT0", f"Win{k}_{hh}"], w=[f"pb{bb}"], sig=(k == 7))
                return bb

            if SS < 2:
                return
            bKV = tm_cols(C_K, 256)
            A("activation", out=kv_s, in_=pb[bKV][R16, 0:256], func=AF.Copy, r=[f"pb{bKV}"], w=["att_o"])
            P.dma("sync", kvb.ap(), kv_s, "kvb", reads=["att_o"], writes=["kvb"])
            P.dma("sync", Kn[127:128], kvb.ap()[:, 0:128].unsqueeze(0), "Kn_b", reads=["kvb", "xnT", "KnH0", "KnH1"], writes=["xnT"], group="samp2")
            for h in range(2):
                P.dma("sync", Vn[h][127:128], kvb.ap()[8 * h:8 * h + 8, 128:256].unsqueeze(0), f"Vn_b{h}", reads=["kvb", f"PT{h}"], writes=[f"PT{h}"], group="samp2")
            for h in range(2):
                big[0] = P.dma("sync", nks_d[8 * h:8 * h + 8].rearrange("s k c -> k s c"), Kn[:, 8 * h:8 * h + 8, :], f"o_nks{h}", reads=["xnT", f"KnH{h}"], deps=[big[0]])
                out_handles.append(big[0])
                big[0] = P.dma("sync", nvs_d[8 * h:8 * h + 8].rearrange("s k c -> k s c"), Vn[h][:], f"o_nvs{h}", reads=[f"PT{h}"], deps=[big[0]])
                out_handles.append(big[0])
            if SS < 3:
                return
            for cc in range(4):
                bq = fm_cols(C_Q + cc * 128)
                V("tensor_scalar", out=Qm[0:64, :, 2 * cc], in0=pb[bq][0:64, 0:SB], scalar1=0.125, scalar2=None, op0=ALU.mult, r=[f"pb{bq}"], w=["th0"])
                V("tensor_scalar", out=Qm[64:128, :, 2 * cc + 1], in0=pb[bq][64:128, 0:SB], scalar1=0.125, scalar2=None, op0=ALU.mult, r=[f"pb{bq}"], w=["th0"])
            for cc in range(4):
                bg = fm_cols(C_GA + cc * 128)
                A("activation", out=tD[:, 0:SB], in_=pb[bg][:, 0:SB], func=AF.Tanh, scale=0.5, r=[f"pb{bg}"], w=["ua"])
                V("scalar_tensor_tensor", out=uaT[:, cc, :], in0=tD[:, 0:SB], scalar=1.0, in1=pb[bg][:, 0:SB], op0=ALU.add, op1=ALU.mult,
                  r=[f"pb{bg}", "ua"], w=["uaT"])
            if SS < 4:
                return
            bXR = tm_cols(C_XR, 512)
            bGR = tm_cols(C_GR, 512)
            cw = lambda tap: rp[tap]
            cb_r, bga_r, bgx_r, lam_r = rp[4], rp[5], rp[6], rp[7]
            A("activation", out=tB[R16], in_=pb[bXR][R16, :], func=AF.Copy, r=[f"pb{bXR}"], w=["xcv0"])
            out_handles.append(P.dma("sync", nconvs_d[:, 0:512], sst_t[1], "o_ncs_a", reads=["tt"]))
            out_handles.append(P.dma("sync", nconvs_d[:, 512:1024], sst_t[2], "o_ncs_c", reads=["BiasF"], group="outs"))
            out_handles.append(P.dma("sync", nconvs_d[:, 1024:1536], tB[R16], "o_ncs_b", reads=["xcv0"], group="outs"))
            V("tensor_tensor", out=xcs[R16], in0=tB[R16], in1=cw(3), op=ALU.mult, r=["xcv0", rpn[3]], w=["t10"])
            for tap in range(3):
                V("tensor_tensor", out=tA[R16], in0=sst_t[tap], in1=cw(tap), op=ALU.mult, r=[sst_n[tap], rpn[tap]], w=["t11"])
                V("tensor_tensor", out=xcs[R16], in0=xcs[R16], in1=tA[R16], op=ALU.add, r=["t10", "t11"], w=["t10"])
            V("tensor_tensor", out=xcs[R16], in0=xcs[R16], in1=cb_r, op=ALU.add, r=["t10", rpn[4]], w=["t10"])
            b = bank()
            for g in range(4):
                PE("transpose", out=pb[b][:, g * 128:(g + 1) * 128], in_=xcs[:, g * 128:(g + 1) * 128], identity=ident32[:],
                   r=["t10", "ident32"], w=[f"pb{b}"], sig=(g == 3))
            V("tensor_copy", out=xcT, in_=pb[b][:].rearrange("p (g t) -> p g t", t=128), r=[f"pb{b}"], w=["u0"])
            bA, bX = bank(), bank()
            for g in range(4):
                PE("matmul", pb[bA][:, g * 128:(g + 1) * 128], lhsT=xcT[:, g, :], rhs=bda[:, g, :], start=True, stop=True, r=["u0", "bda"], w=[f"pb{bA}"], sig=(g == 3))
            for g in range(4):
                PE("matmul", pb[bX][:, g * 128:(g + 1) * 128], lhsT=xcT[:, g, :], rhs=bdx[:, g, :], start=True, stop=True, r=["u0", "bdx"], w=[f"pb{bX}"], sig=(g == 3))
            A("activation", out=tC[R16], in_=lam_r, func=AF.Exp, scale=-1.0, r=[rpn[7]], w=["xcv1"])
            A("activation", out=tC[R16], in_=tC[R16], func=AF.Ln, bias=1.0, r=["xcv1"], w=["xcv1"])
            V("tensor_scalar", out=tC[R16], in0=tC[R16], scalar1=-4.0, scalar2=None, op0=ALU.mult, r=["xcv1"], w=["xcv1"])
            V("tensor_tensor", out=tA[R16], in0=pb[bA][R16, :], in1=bga_r, op=ALU.add, r=[f"pb{bA}", rpn[5]], w=["t11"])
            A("activation", out=tA[R16], in_=tA[R16], func=AF.Tanh, scale=0.5, r=["t11"], w=["t11"])
            V("scalar_tensor_tensor", out=tA[R16], in0=tA[R16], scalar=1.0, in1=tC[R16], op0=ALU.add, op1=ALU.mult, r=["t11", "xcv1"], w=["t11"])
            A("activation", out=tA[R16], in_=tA[R16], func=AF.Exp, r=["t11"], w=["t11"])
            V("tensor_tensor", out=tC[R16], in0=pb[bX][R16, :], in1=bgx_r, op=ALU.add, r=[f"pb{bX}", rpn[6]], w=["xcv1"])
            A("activation", out=tC[R16], in_=tC[R16], func=AF.Tanh, scale=0.5, r=["xcv1"], w=["xcv1"])
            V("scalar_tensor_tensor", out=tC[R16], in0=tC[R16], scalar=1.0, in1=xcs[R16], op0=ALU.add, op1=ALU.mult, r=["xcv1", "t10"], w=["xcv1"])
            A("activation", out=tE[R16], in_=pb[bGR][R16, :], func=AF.Tanh, scale=0.5, r=[f"pb{bGR}"], w=["sgr"])
            V("scalar_tensor_tensor", out=tE[R16], in0=tE[R16], scalar=1.0, in1=pb[bGR][R16, :], op0=ALU.add, op1=ALU.mult, r=["sgr", f"pb{bGR}"], w=["sgr"])
            V("tensor_tensor", out=tD[R16], in0=tA[R16], in1=tA[R16], op=ALU.mult, r=["t11"], w=["ua"])
            A("activation", out=tD[R16], in_=tD[R16], func=AF.Sqrt, scale=-1.0 / 16.0, bias=1.0 / 16.0, r=["ua"], w=["ua"])
            V("tensor_tensor", out=tC[R16], in0=tC[R16], in1=tD[R16], op=ALU.mult, r=["xcv1", "ua"], w=["xcv1"])
            V("tensor_tensor", out=tA[R16], in0=tA[R16], in1=hprev, op=ALU.mult, r=["t11", "th1"], w=["t11"])
            V("scalar_tensor_tensor", out=tC[R16], in0=tC[R16], scalar=2.0, in1=tA[R16], op0=ALU.mult, op1=ALU.add, r=["xcv1", "t11"], w=["xcv1"])
            out_handles.append(P.dma("sync", nrnns_d, tC[R16], "o_nrs", reads=["xcv1"], group="outs"))
            V("scalar_tensor_tensor", out=mixtm[R16], in0=tC[R16], scalar=0.5, in1=tE[R16], op0=ALU.mult, op1=ALU.mult, r=["xcv1", "sgr"], w=["xn0"])
            b = bank()
            for g in range(4):
                PE("transpose", out=pbT[b][:, g, :], in_=mixtm[:, g * 128:(g + 1) * 128], identity=ident[:], r=["xn0", "ident"], w=[f"pb{b}"], sig=(g == 3))
            V("tensor_copy", out=mixTs[:, 0:4, :], in_=pbT[b][:, 0:4, :], r=[f"pb{b}"], w=["xn1"])
            if SS < 5:
                return
            for q4 in range(4):
                b = bank()
                for i4 in range(4):
                    sq = q4 * 4 + i4
                    PE("transpose", out=pb[b][:, i4 * 128:(i4 + 1) * 128], in_=Kn[:, sq, :], identity=ident32[:], r=["xnT", "ident32"], w=[f"pb{b}"], sig=(i4 == 3))
                hh, s0 = q4 // 2, (q4 % 2) * 4
                V("tensor_copy", out=KnT[hh][:, s0:s0 + 4, :], in_=pb[b][:].rearrange("p (s k) -> p s k", k=128), r=[f"pb{b}"], w=[f"a{hh}" if hh == 0 else "a20"])
            bS = bank()
            for sq in range(SB):
                hh, s0 = sq // 8, sq % 8
                PE("matmul", pb[bS][:, sq * 8:(sq + 1) * 8], lhsT=KnT[hh][:, s0, :], rhs=Qm[:, sq, :], start=True, stop=True,
                   r=["a0" if hh == 0 else "a20", "th0"], w=[f"pb{bS}"], sig=(sq == SB - 1))
            V("tensor_tensor", out=sc.rearrange("p (s q) -> p s q", q=8), in0=pb[bS][:, 0:128].rearrange("p (s q) -> p s q", q=8),
              in1=biass[:].unsqueeze(1).broadcast_to([128, SB, 8]), op=ALU.add, r=[f"pb{bS}", "biass"], w=["Ac"])
            A("activation", out=Pt, in_=sc, func=AF.Exp, r=["Ac"], w=["hl"])
            bO = bank()
            for sq in range(SB):
                hh, s0 = sq // 8, sq % 8
                PE("matmul", pb[bO][:, sq * 8:(sq + 1) * 8], lhsT=Vn[hh][:, s0, :], rhs=Pt[:, sq * 8:(sq + 1) * 8], start=True, stop=True,
                   r=[f"PT{hh}", "hl"], w=[f"pb{bO}"], sig=(sq == SB - 1))
            bD = bank()
            PE("matmul", pb[bD][:, 0:128], lhsT=ones32[:], rhs=Pt, start=True, stop=True, r=["ones32", "hl"], w=[f"pb{bD}"], sig=True)
            V("tensor_copy", out=sm[:, 8:16].rearrange("p (c g) -> p c g", g=2), in_=sinkexp2[:].rearrange("p g c -> p c g"), r=["sinkexp2", "sm"], w=["sm"])
            V("scalar_tensor_tensor", out=rds.rearrange("p (s q) -> p s q", q=8), in0=pb[bD][:, 0:128].rearrange("p (s q) -> p s q", q=8),
              scalar=2.0, in1=sm[:, 8:16].unsqueeze(1).broadcast_to([128, SB, 8]), op0=ALU.mult, op1=ALU.add,
              r=[f"pb{bD}", "sm"], w=["Ac"])
            V("reciprocal", out=rds, in_=rds, r=["Ac"], w=["Ac"])
            V("tensor_tensor", out=attn, in0=pb[bO][:, 0:128], in1=rds, op=ALU.mult, r=[f"pb{bO}", "Ac"], w=["hl"])
            av = attn.rearrange("p (s c g) -> p c s g", c=4, g=2)
            V("tensor_tensor", out=mixTs[0:64, 4:8, 0:SB], in0=av[0:64, :, :, 0], in1=uaT[0:64, :, :], op=ALU.mult, r=["hl", "uaT"], w=["xn1"])
            V("tensor_tensor", out=mixTs[64:128, 4:8, 0:SB], in0=av[64:128, :, :, 1], in1=uaT[64:128, :, :], op=ALU.mult, r=["hl", "uaT"], w=["xn1"])
            if SS < 6:
                return
            bY = [bank(), bank()]
            for kk in range(8):
                for hf in range(2):
                    PE("matmul", pb[bY[hf]][:], lhsT=mixTs[:, kk, :], rhs=Wout[:, kk, hf * 512:(hf + 1) * 512], start=(kk == 0), stop=(kk == 7),
                       r=["xn1", f"Wout{kk}"], w=[f"pb{bY[hf]}"], sig=(kk == 7))
            for hf in range(2):
                A("activation", out=junk2[:], in_=pb[bY[hf]][:], func=AF.Square, accum_out=sm[:, 3 + hf:4 + hf], r=[f"pb{bY[hf]}", "sm"], w=["junk2", "sm"])
            V("tensor_tensor", out=sm[:, 5:6], in0=sm[:, 3:4], in1=sm[:, 4:5], op=ALU.add, r=["sm"], w=["sm"])
            V("tensor_scalar", out=sm[:, 5:6], in0=sm[:, 5:6], scalar1=1.0 / D, scalar2=EPS, op0=ALU.mult, op1=ALU.add, r=["sm"], w=["sm"])
            A("activation", out=sm[:, 5:6], in_=sm[:, 5:6], func=AF.Sqrt, r=["sm"], w=["sm"])
            V("reciprocal", out=sm[:, 6:7], in_=sm[:, 5:6], r=["sm"], w=["sm"])
            for hf in range(2):
                V("scalar_tensor_tensor", out=tt[:, hf * 512:(hf + 1) * 512], in0=pb[bY[hf]][:], scalar=sm[:, 6:7], in1=gpost[:, hf * 512:(hf + 1) * 512],
                  op0=ALU.mult, op1=ALU.mult, r=[f"pb{bY[hf]}", "sm", "gpost"], w=["tt"])
            V("tensor_tensor", out=xt[0][R16, :], in0=tt[R16, :], in1=xt[0][R16, :], op=ALU.add, r=["tt", "xt0"], w=["xt0"])
            out_handles.append(P.dma("sync", ys_d, xt[0][R16, :], "o_ys", reads=["xt0"]))


        if STAGE >= 5:
            for i in range(NT):
                s = i % 2
                c0 = i * 128
                blk0 = (i // 4) * 512
                P.dma("sync", xt[s][:], xc_d[PRE + (i + 1) * 128:PRE + (i + 2) * 128, :], f"xt{s}", writes=[f"xt{s}"])
                for g in range(4):
                    V("scalar_tensor_tensor", out=mixT[s][:, g, :], in0=P2[:, g, c0:c0 + 128], scalar=h0[:, g:g + 1], in1=P1[:, g, c0:c0 + 128],
                                                            op0=ALU.mult, op1=ALU.add,
                      r=[f"P1_{g}_{blk0}", f"P2_{g}_{blk0}", "h0"], w=[f"mixT{s}"])
                bY = [bank(), bank()]
                for kk in range(8):
                    lhs = mixT[s][:, kk, :] if kk < 4 else attT[:, kk - 4, c0:c0 + 128]
                    rn = [f"mixT{s}"] if kk < 4 else [f"attT{i + 1}"]
                    for hf in range(2):
                        PE("matmul", pb[bY[hf]][:], lhsT=lhs, rhs=Wout[:, kk, hf * 512:(hf + 1) * 512],
                                                                     start=(kk == 0), stop=(kk == 7),
                           r=rn + [f"Wout{kk}"], w=[f"pb{bY[hf]}"], sig=(kk == 7))
                for hf in range(2):
                    A("activation", out=junk2[:], in_=pb[bY[hf]][:], func=AF.Square, accum_out=ss2[:, i, hf:hf + 1],
                      r=[f"pb{bY[hf]}"], w=["junk2", f"ss2_{i}_{hf}"])
                G("tensor_tensor", out=ms2[:, i:i + 1], in0=ss2[:, i, 0:1], in1=ss2[:, i, 1:2], op=ALU.add,
                  r=[f"ss2_{i}_0", f"ss2_{i}_1"], w=[f"ms2_{i}"])
                G("tensor_scalar", out=ms2[:, i:i + 1], in0=ms2[:, i:i + 1], scalar1=1.0 / D, scalar2=EPS, op0=ALU.mult, op1=ALU.add,
                  r=[f"ms2_{i}"], w=[f"ms2_{i}"])
                G("tensor_tensor", out=rstd2[:, i:i + 1], in0=ms2[:, i:i + 1], in1=cst[:, 0:1], op=ALU.pow,
                  r=[f"ms2_{i}", "cst"], w=[f"rstd2_{i}"])
                for hf in range(2):
                    V("scalar_tensor_tensor", out=tt[:, hf * 512:(hf + 1) * 512], in0=pb[bY[hf]][:], scalar=rstd2[:, i:i + 1],
                                                              in1=gpost[:, hf * 512:(hf + 1) * 512], op0=ALU.mult, op1=ALU.mult,
                      r=[f"pb{bY[hf]}", f"rstd2_{i}", "gpost"], w=["tt"])
                G("tensor_tensor", out=xt[s][:], in0=tt[:], in1=xt[s][:], op=ALU.add, r=["tt", f"xt{s}"], w=[f"xt{s}"])
                out_handles.append(P.dma("sync", y_d[c0:c0 + 128, :], xt[s][:], f"o_y{s}", reads=[f"xt{s}"]))

        if STAGE >= 6:
            G("memset", th[0][:, 0:128], 0.0, w=["th0"])
            G("memset", t1_b[:], 0.0, w=["t10", "t11"])
            G("memset", xn[1][:], 0.0, w=["xn1"])
            sample_path()

        P.wait_all("sync", out_handles)
        P.emit()
    return nc


def _t5_bucket(dist):
    dist = np.maximum(dist, 0)
    max_exact = 16
    d = np.maximum(dist, 1).astype(np.float32)
    large = max_exact + (np.log(d / np.float32(max_exact)) / np.float32(np.log(128 / max_exact)) * np.float32(32 - max_exact)).astype(np.int32)
    large = np.minimum(large, 31)
    return np.where(dist < max_exact, dist, large)


_NC_CACHE = {}


def kernel(x_prompt, x_sample, state_conv, state_rnn, cache_k_win, cache_v_win,
           norm_pre, norm_post, w_in, conv_w, conv_b, w_gate_a, b_gate_a, w_gate_x, b_gate_x,
           lru_lambda, attn_sinks, rel_bias, w_out):
    f32 = np.float32
    x_prompt = np.asarray(x_prompt, f32)
    w_in0 = np.asarray(w_in, f32)[0]
    w_out0 = np.asarray(w_out, f32)[0]
    qperm = np.concatenate([np.arange(h * 64, h * 64 + 64) for h in LPOS])
    cols = np.concatenate([np.arange(0, 1024), 1024 + qperm, np.arange(1536, 1792), 1792 + qperm])
    w_in_p = np.ascontiguousarray(w_in0[:, cols])
    rows = np.concatenate([np.arange(0, 512), 512 + qperm])
    w_out_p = np.ascontiguousarray(w_out0[rows, :])

    def pg(v):
        return np.ascontiguousarray(np.asarray(v, f32).reshape(4, 128).T)

    gpre = np.ascontiguousarray(np.asarray(norm_pre, f32)[0].reshape(8, 128).T)
    gpost = np.ascontiguousarray(np.broadcast_to(np.asarray(norm_post, f32)[0][None, :], (128, D)))
    cw = np.asarray(conv_w, f32)[0]
    convw = np.ascontiguousarray(cw.reshape(4, 4, 128).transpose(2, 1, 0).reshape(128, 16))
    vec4 = np.ascontiguousarray(np.stack([pg(np.asarray(conv_b)[0]), pg(np.asarray(b_gate_a)[0]), pg(np.asarray(b_gate_x)[0]),
                                          pg(np.asarray(lru_lambda)[0])], axis=1).reshape(128, 16))

    def blockdiag(w):
        w = np.asarray(w, f32)[0]
        o = np.zeros((128, 4, 128), f32)
        for g in range(4):
            for h in range(2):
                o[h * 64:(h + 1) * 64, g, h * 64:(h + 1) * 64] = w[2 * g + h]
        return o.reshape(128, 512)

    bda = blockdiag(w_gate_a)
    bdx = blockdiag(w_gate_x)
    hd = np.array([[g * 4 + cc for cc in range(4)] for g in range(2)])
    sinks = np.ascontiguousarray(np.broadcast_to(np.asarray(attn_sinks, f32)[0][hd].reshape(1, 8), (128, 8)))
    rb = np.asarray(rel_bias, f32)
    kk = np.arange(128)[:, None]
    qq = np.arange(128)[None, :]
    biasg = np.zeros((128, 2, 2, 4, 128), f32)
    maskc = np.zeros((128, 2, 128), f32)
    for blk in range(2):
        dist = qq + (128 if blk == 0 else 0) - kk
        valid = (dist >= 0) & (dist < 128)
        bkt = _t5_bucket(np.clip(dist, 0, 127))
        for g in range(2):
            for cc in range(4):
                biasg[:, blk, g, cc, :] = rb[bkt, hd[g, cc]]
        maskc[:, blk, :] = np.where(valid, 0.0, NEG)
    biasg = biasg.reshape(128, 2048)
    maskc = maskc.reshape(128, 256)
    ident = np.eye(128, dtype=f32)

    xs_all = np.asarray(x_sample, f32)[:, 0, :]
    sconv_all = np.asarray(state_conv, f32)[0].reshape(128, 1536)
    srnn_all = np.asarray(state_rnn, f32)[0]
    ck_all = np.asarray(cache_k_win, f32)[0].reshape(128, 128, 128)
    cv_all = np.asarray(cache_v_win, f32)[0].reshape(128, 128, 128)
    rowv = np.concatenate([cw.reshape(-1), np.asarray(conv_b, f32)[0], np.asarray(b_gate_a, f32)[0], np.asarray(b_gate_x, f32)[0],
                           np.asarray(lru_lambda, f32)[0]])
    rowp = np.ascontiguousarray(np.broadcast_to(rowv[None, :], (SB, 4096)))
    posh = np.array([g * 4 + cc for cc in range(4) for g in range(2)])
    biass = np.ascontiguousarray(rb[_t5_bucket(127 - np.arange(128))][:, posh])

    def padrows(a):
        return np.concatenate([a.reshape(SB * 128, 128), np.zeros((128, 128), f32)], axis=0)

    in_maps = []
    for c in range(NCORES):
        b, j = c // 4, c % 4
        xc = np.zeros((PRE + CH + 128, D), f32)
        flag = np.zeros((128, 3), f32)
        for kf in range(3):
            cj = j - 3 + kf
            if cj >= 0:
                xc[kf * CH:(kf + 1) * CH] = x_prompt[b, cj * CH:(cj + 1) * CH]
                flag[:, kf] = 1.0
        xc[PRE + 128:] = x_prompt[b, j * CH:(j + 1) * CH]
        xc[PRE:PRE + 128] = xc[PRE - 128:PRE]
        hmask = np.full((128, 1), 0.0 if j > 0 else NEG, f32)
        sel = np.zeros((128, 8), f32)
        for r in range(NCORES):
            if r // 4 == b and r < c:
                sel[:, r] = 1.0
        in_maps.append({"xc": xc, "w_in": w_in_p, "w_out": w_out_p, "gpre": gpre, "gpost": gpost, "convw": convw, "vec4": vec4,
                        "bda": bda, "bdx": bdx, "sinks": sinks, "biasg": biasg, "maskc": maskc, "hmask": hmask, "sel": sel, "flag": flag,
                        "ident": ident, "xs": np.concatenate([xs_all[c * SB:(c + 1) * SB], np.zeros((128 - SB, D), f32)], axis=0),
                        "sconv": np.ascontiguousarray(sconv_all[c * SB:(c + 1) * SB]), "srnn": np.ascontiguousarray(srnn_all[c * SB:(c + 1) * SB]),
                        "ck": padrows(ck_all[c * SB:(c + 1) * SB]), "cv": padrows(cv_all[c * SB:(c + 1) * SB]),
                        "rowp": rowp, "biass": biass})

    if "nc" not in _NC_CACHE:
        _NC_CACHE["nc"] = build_program()
    nc = _NC_CACHE["nc"]
    res = run_bass_kernel_spmd(nc, in_maps, core_ids=list(range(NCORES)))
    R = res.results

    y_prompt = np.stack([np.concatenate([R[b * 4 + j]["y"] for j in range(4)], axis=0) for b in range(2)], axis=0)
    new_conv_p = np.stack([R[3]["nconv"], R[7]["nconv"]])[None]
    new_rnn_p = np.stack([R[3]["nrnn"], R[7]["nrnn"]])[None]
    new_k_p = np.stack([R[3]["nk"].reshape(128, 2, 64), R[7]["nk"].reshape(128, 2, 64)])[None]
    new_v_p = np.stack([R[3]["nv"].reshape(128, 2, 64), R[7]["nv"].reshape(128, 2, 64)])[None]
    cat = lambda k: np.concatenate([R[c][k] for c in range(NCORES)], axis=0)
    y_sample = cat("ys").reshape(128, 1, D).astype(f32)
    new_conv_s = cat("nconvs").reshape(1, 128, 3, 512).astype(f32)
    new_rnn_s = cat("nrnns").reshape(1, 128, 512).astype(f32)
    new_k_s = cat("nks").reshape(1, 128, 128, 2, 64).astype(f32)
    new_v_s = cat("nvs").reshape(1, 128, 128, 2, 64).astype(f32)
    return (y_prompt.astype(f32), y_sample, new_conv_p.astype(f32), new_rnn_p.astype(f32), new_k_p.astype(f32), new_v_p.astype(f32),
            new_conv_s, new_rnn_s, new_k_s, new_v_s)
```

```python
import contextlib
import numpy as np
import concourse.bass as bass
import concourse.mybir as mybir
from concourse.bass_utils import run_bass_kernel_spmd

F32 = mybir.dt.float32
BF16 = mybir.dt.bfloat16
ALU = mybir.AluOpType
AF = mybir.ActivationFunctionType

NCORES = 8
D = 1024
D_IN = 2304
SEQ = 8192
CH = 2048
NT = 16
EPS = 1e-6
NEG = -1e30
LPOS = [0, 4, 1, 5, 2, 6, 3, 7]
C_XR, C_GR, C_Q, C_K, C_V, C_GA = 0, 512, 1024, 1536, 1664, 1792
SB = 16
PRE = 3 * CH
NPT = PRE // 128

ENGS = ("sync", "scalar", "vector", "gpsimd", "tensor")
CC_INC = 1
GROUPS = {"setup": 11, "W1": 9, "W2": 8, "W3": 8, "samp": 18, "samp2": 3, "outs": 14}
GROUPS_SEEN = {}
STAGE = 99
SUB = 99
NBLK = 99
NOOUT = 0
NOLAST = 0
SS = 99
DM = 255
S1 = 99


class Prog:
    def __init__(self, nc, es):
        self.nc = nc
        self.es = es
        self.q = {e: [] for e in ENGS}
        self.sig = {e: 0 for e in ENGS}
        self.pending = {e: False for e in ENGS}
        self.waited = {}
        self.bufs = {}
        self.dma_cnt = {}
        self.grp_seen = {}
        self.sems = {}

    def sem(self, key):
        if key not in self.sems:
            name = "s_" + "_".join(str(k) for k in key)
            self.sems[key] = self.es.enter_context(self.nc.semaphore(name))
        return self.sems[key]

    def _deps(self, eng, reads, writes, extra, skip_key=None):
        deps = set(extra)
        for r in reads:
            b = self.bufs.get(r)
            if b and b["w"] is not None:
                deps.add(b["w"])
            if b and r.startswith("pb"):
                deps.update(h for h in b["r"] if h[1] != eng)
        for w in writes:
            b = self.bufs.get(w)
            if b:
                if b["w"] is not None:
                    deps.add(b["w"])
                deps.update(b["r"])
        best = {}
        for d in deps:
            if d is None:
                continue
            key = d[:2]
            if eng == "tensor" and key == ("eng", "tensor"):
                continue
            if key == skip_key:
                continue
            if d[2] > best.get(key, 0):
                best[key] = d[2]
        waits = []
        for key, val in best.items():
            if self.waited.get((eng, key), 0) >= val:
                continue
            self.waited[(eng, key)] = val
            waits.append((key, val))
        return waits

    def _track(self, h, reads, writes):
        for r in reads:
            b = self.bufs.setdefault(r, {"w": None, "r": []})
            b["r"].append(h)
        for w in writes:
            self.bufs[w] = {"w": h, "r": []}

    def op(self, eng, fn, reads=(), writes=(), signal=True, deps=()):
        waits = self._deps(eng, reads, writes, deps)
        if signal:
            self.sig[eng] += 1
            h = ("eng", eng, self.sig[eng])
            self.pending[eng] = False
        else:
            h = ("eng", eng, self.sig[eng] + 1)
            self.pending[eng] = True
        self._track(h, reads, writes)
        semw = [(self.sem(k), v) for k, v in waits]
        mysem = self.sem(("eng", eng)) if signal else None

        def emit(e):
            for s, v in semw:
                e.wait_ge(s, v)
            ins = fn(e)
            if mysem is not None:
                ins.then_inc(mysem, 1)

        self.q[eng].append(emit)
        return h

    def dma(self, eng, out, in_, slot, reads=(), writes=(), deps=(), group=None, **kw):
        waits = self._deps(eng, reads, writes, deps, skip_key=(("dma", group) if group is not None else None))
        if group is not None:
            slot = group
            self.grp_seen[group] = self.grp_seen.get(group, 0) + 1
            cnt = 16 * GROUPS[group]
        else:
            cnt = self.dma_cnt.get(slot, 0) + 16
        self.dma_cnt[slot] = cnt
        h = ("dma", slot, cnt)
        self._track(h, reads, writes)
        semw = [(self.sem(k), v) for k, v in waits]
        mysem = self.sem(("dma", slot))

        def emit(e):
            for s, v in semw:
                e.wait_ge(s, v)
            e.dma_start(out=out, in_=in_, **kw).then_inc(mysem, 16)

        self.q[eng].append(emit)
        return h

    def cc(self, eng, fn, reads=(), writes=(), inc=None):
        inc = CC_INC if inc is None else inc
        waits = self._deps(eng, reads, writes, ())
        cnt = self.dma_cnt.get("cc", 0) + inc
        self.dma_cnt["cc"] = cnt
        h = ("dma", "cc", cnt)
        self._track(h, reads, writes)
        semw = [(self.sem(k), v) for k, v in waits]
        mysem = self.sem(("dma", "cc"))

        def emit(e):
            for s, v in semw:
                e.wait_ge(s, v)
            fn(e).then_inc(mysem, 1)

        self.q[eng].append(emit)
        return h

    def wait_all(self, eng, handles):
        waits = self._deps(eng, (), (), handles)
        semw = [(self.sem(k), v) for k, v in waits]

        def emit(e):
            for s, v in semw:
                e.wait_ge(s, v)

        self.q[eng].append(emit)

    def emit(self):
        assert not self.pending["tensor"], "PE has unsignalled trailing ops"
        GROUPS_SEEN.clear()
        GROUPS_SEEN.update(self.grp_seen)
        with self.nc.Block() as block:
            @block.sync
            def _(e):
                for f in self.q["sync"]:
                    f(e)

            @block.scalar
            def _(e):
                for f in self.q["scalar"]:
                    f(e)

            @block.vector
            def _(e):
                for f in self.q["vector"]:
                    f(e)

            @block.gpsimd
            def _(e):
                for f in self.q["gpsimd"]:
                    f(e)

            @block.tensor
            def _(e):
                for f in self.q["tensor"]:
                    f(e)


def build_program():
    nc = _build_program()
    if any(GROUPS.get(g) != n for g, n in GROUPS_SEEN.items()):
        GROUPS.update(GROUPS_SEEN)
        nc = _build_program()
        assert all(GROUPS.get(g) == n for g, n in GROUPS_SEEN.items())
    return nc


def _build_program():
    nc = bass.Bass("TRN2", target_bir_lowering=False)

    def din(name, shape):
        return nc.dram_tensor(name, list(shape), F32, kind="ExternalInput").ap()

    def dout(name, shape):
        return nc.dram_tensor(name, list(shape), F32, kind="ExternalOutput").ap()

    xc_d = din("xc", [PRE + CH + 128, D])
    flag_d = din("flag", [128, 3])
    w_in_d = din("w_in", [D, D_IN])
    w_out_d = din("w_out", [D, D])
    gpre_d = din("gpre", [128, 8])
    gpost_d = din("gpost", [128, D])
    convw_d = din("convw", [128, 16])
    vec4_d = din("vec4", [128, 16])
    bda_d = din("bda", [128, 512])
    bdx_d = din("bdx", [128, 512])
    sinks_d = din("sinks", [128, 8])
    biasg_d = din("biasg", [128, 2048])
    maskc_d = din("maskc", [128, 256])
    hmask_d = din("hmask", [128, 1])
    sel_d = din("sel", [128, 8])
    ident_d = din("ident", [128, 128])

    xs_d = din("xs", [128, D])
    sconv_d = din("sconv", [SB, 1536])
    srnn_d = din("srnn", [SB, 512])
    ck_d = din("ck", [SB * 128 + 128, 128])
    cv_d = din("cv", [SB * 128 + 128, 128])
    rowp_d = din("rowp", [SB, 4096])
    biass_d = din("biass", [128, 8])

    y_d = dout("y", [CH, D])
    ys_d = dout("ys", [SB, D])
    nconvs_d = dout("nconvs", [SB, 1536])
    nrnns_d = dout("nrnns", [SB, 512])
    nks_d = dout("nks", [SB, 128, 128])
    nvs_d = dout("nvs", [SB, 128, 128])
    nconv_d = dout("nconv", [3, 512])
    nrnn_d = dout("nrnn", [512])
    nk_d = dout("nk", [128, 128])
    nv_d = dout("nv", [128, 128])

    bounce = nc.dram_tensor("bounce", [128, 8], F32)
    gath = nc.dram_tensor("gath", [NCORES * 128, 8], F32)
    kvb = nc.dram_tensor("kvb", [SB, 256], F32)

    out_handles = []

    with contextlib.ExitStack() as es:
        P = Prog(nc, es)
        def sb(name, shape, dt=F32):
            return es.enter_context(nc.sbuf_tensor("sb_" + name, list(shape), dt))

        Win = sb("Win", [128, 8, D_IN], BF16)
        Wout = sb("Wout", [128, 8, D], BF16)
        P1 = sb("P1", [128, 4, CH], BF16)
        P2 = sb("P2", [128, 4, CH], BF16)
        attT = sb("attT", [128, 4, CH], BF16)
        KT = sb("KT", [128, 128 + 512], BF16)
        Vg = sb("Vg", [128, 5, 132], BF16)
        xt = [sb(f"xt{i}", [128, D]) for i in range(2)]
        xn = [sb(f"xn{i}", [128, D], BF16) for i in range(2)]
        xnT = sb("xnT", [128, 8, 512], BF16)
        xr = sb("xr", [128, 4, 515], BF16)
        xrt = sb("xrt", [128, 4, 4])
        xcv = sb("xcv", [128, 2, 512])
        u_b = sb("u_b", [128, 2, 512])
        a_b = sb("a_b", [128, 2, 512])
        a2_b = sb("a2_b", [128, 2, 512])
        t1_b = sb("t1_b", [128, 2, 512])
        th = [sb(f"th{i}", [128, 512]) for i in range(2)]
        hl = sb("hl", [128, 512])
        Ac = sb("Ac", [128, 512])
        QT = [sb(f"QT{g}", [128, 4, 512], BF16) for g in range(2)]
        BiasT = sb("BiasT", [128, 2, 2, 512], BF16)
        BiasF = sb("BiasF", [128, 2, 512], BF16)
        PT = [sb(f"PT{i}", [128, 2, 2, 512], BF16) for i in range(2)]
        ua = sb("ua", [128, 512])
        sgr = sb("sgr", [128, 512])
        att_o = sb("att_o", [128, 512], BF16)
        mixT = [sb(f"mixT{i}", [128, 4, 128], BF16) for i in range(2)]
        tt = sb("tt", [128, D])
        gpost = sb("gpost", [128, D])
        junk2 = sb("junk2", [128, 512], BF16)
        bda = sb("bda", [128, 4, 128])
        bdx = sb("bdx", [128, 4, 128])
        diagw = sb("diagw", [128, 4, 4, 128], BF16)
        ident = sb("ident", [128, 128], BF16)
        maskc = sb("maskc", [128, 2, 128])
        gpre = sb("gpre", [128, 8])
        convw = sb("convw", [128, 4, 4])
        vec4 = sb("vec4", [128, 4, 4])
        hb = sb("hb", [128, 2, 4])
        chalf = sb("chalf", [128, 4])
        sinks = sb("sinks", [128, 2, 4])
        sinkexp2 = sb("sinkexp2", [128, 2, 4])
        hmask = sb("hmask", [128, 1])
        sel = sb("sel", [128, 8])
        cst = sb("cst", [128, 4])
        ss = sb("ss", [128, 17 + NPT])
        ms = sb("ms", [128, 17 + NPT])
        rstd = sb("rstd", [128, 17 + NPT])
        flag = sb("flag", [128, 3])
        ss2 = sb("ss2", [128, NT, 2])
        ms2 = sb("ms2", [128, NT])
        rstd2 = sb("rstd2", [128, NT])
        car = sb("car", [128, 8])
        dsum = sb("dsum", [128, 4, 2])
        rden = sb("rden", [128, 4, 2])
        G_sb = sb("G_sb", [128, 8, 8])
        Ap = sb("Ap", [128, 8, 4])
        Bp = sb("Bp", [128, 8, 4])
        hscan = sb("hscan", [128, 4, 8])
        h0 = sb("h0", [128, 4])
        hfin = sb("hfin", [128, 4])
        kout = sb("kout", [128, 128])
        vout = sb("vout", [128, 128])
        sth = sb("sth", [128, 8])
        ident32 = sb("ident32", [128, 128])
        ones32 = sb("ones32", [128, 128])
        biass = sb("biass", [128, 8])
        uaT = sb("uaT", [128, 4, SB])
        sm = sb("sm", [128, 16])

        pb = [es.enter_context(nc.psum_tensor(f"pb{i}", [128, 512], F32)) for i in range(8)]
        pbT = [p[:].bitcast(BF16).rearrange("p (k c) -> p k c", c=128) for p in pb]
        bank_ctr = [0]

        def bank():
            i = bank_ctr[0]
            bank_ctr[0] = (i + 1) % 8
            return i

        def mk(eng):
            def f(name, *args, r=(), w=(), sig=True, **kw):
                return P.op(eng, lambda e: getattr(e, name)(*args, **kw), reads=r, writes=w, signal=sig)
            return f
        V, A, G = mk("vector"), mk("scalar"), mk("gpsimd")
        _pe = mk("tensor")

        def PE(name, *args, r=(), w=(), sig=False, **kw):
            return _pe(name, *args, r=r, w=w, sig=sig, **kw)

        def ld(dst, src, name, group="setup"):
            P.dma("sync", dst, src, name, writes=[name], group=group)

        ld(gpre[:], gpre_d, "gpre")
        ld(convw[:].rearrange("p g t -> p (g t)"), convw_d, "convw")
        ld(vec4[:].rearrange("p a g -> p (a g)"), vec4_d, "vec4")
        ld(hmask[:], hmask_d, "hmask")
        ld(sel[:], sel_d, "sel")
        ld(sinks[:].rearrange("p g c -> p (g c)"), sinks_d, "sinks")
        ld(maskc[:].rearrange("p b q -> p (b q)"), maskc_d, "maskc")
        ld(bda[:].rearrange("p g m -> p (g m)"), bda_d, "bda")
        ld(bdx[:].rearrange("p g m -> p (g m)"), bdx_d, "bdx")
        ld(gpost[:], gpost_d, "gpost")
        ld(xt[0][:], biasg_d[:, 0:1024], "xt0", group=None)
        ld(xt[1][:], biasg_d[:, 1024:2048], "xt1", group=None)
        P.dma("gpsimd", ident[:], ident_d, "ident", writes=["ident"], group="W1")
        for k in range(8):
            for h in range(2):
                P.dma("gpsimd", Win[:, k, h * 1152:(h + 1) * 1152], w_in_d[k * 128:(k + 1) * 128, h * 1152:(h + 1) * 1152],
                      f"Win{k}_{h}", writes=[f"Win{k}_{h}"], group=("W1" if h == 0 else "W2"))
        for k in range(8):
            P.dma("gpsimd", Wout[:, k, :], w_out_d[k * 128:(k + 1) * 128, :], f"Wout{k}", writes=[f"Wout{k}"], group="W3")

        G("memset", cst[:, 0:1], -0.5, w=["cst"])
        G("memset", cst[:, 1:2], 0.0, r=["cst"], w=["cst"])
        G("memset", cst[:, 2:3], 1.0 / 16.0, r=["cst"], w=["cst"])
        G("memset", Vg[:], 1.0, w=["Vg"])
        G("memset", QT[0][64:128], 0.0, w=["QT0"])
        G("memset", QT[1][0:64], 0.0, w=["QT1"])
        G("memset", ones32[:], 1.0, w=["ones32"])
        A("activation", out=sth[:, 0:4], in_=vec4[:, 3, :], func=AF.Exp, scale=-1.0, r=["vec4"], w=["sth"])
        A("activation", out=sth[:, 4:8], in_=sth[:, 0:4], func=AF.Ln, bias=1.0, r=["sth"], w=["sth"])
        V("tensor_scalar", out=chalf[:], in0=sth[:, 4:8], scalar1=-4.0, scalar2=None, op0=ALU.mult, r=["sth"], w=["chalf"])
        V("tensor_scalar", out=hb[:], in0=vec4[:, 1:3, :], scalar1=0.5, scalar2=None, op0=ALU.mult, r=["vec4"], w=["hb"])
        A("activation", out=sinkexp2[:], in_=sinks[:], func=AF.Exp, r=["sinks"], w=["sinkexp2"])
        V("tensor_scalar", out=sinkexp2[:], in0=sinkexp2[:], scalar1=2.0, scalar2=None, op0=ALU.mult, r=["sinkexp2"], w=["sinkexp2"])
        for blk in range(2):
            V("tensor_tensor", out=BiasT[:, blk].rearrange("p g (c q) -> p (g c) q", q=128),
                                                 in0=xt[blk][:].rearrange("p (h q) -> p h q", q=128),
                                                 in1=maskc[:, blk, :].unsqueeze(1).broadcast_to([128, 8, 128]), op=ALU.add,
              r=[f"xt{blk}", "maskc"], w=["BiasT"])
        V("tensor_tensor", out=xt[0][:].rearrange("p (h q) -> p h q", q=128), in0=xt[0][:].rearrange("p (h q) -> p h q", q=128),
                                    in1=maskc[:, 0, :].unsqueeze(1).broadcast_to([128, 8, 128]), op=ALU.add,
          r=["xt0", "maskc"], w=["xt0"])
        V("tensor_scalar", out=BiasF[:].rearrange("p g c -> p (g c)"), in0=xt[0][:], scalar1=hmask[:, 0:1], scalar2=None, op0=ALU.add,
          r=["xt0", "hmask"], w=["BiasF"])
        for g in range(4):
            for tap in range(4):
                V("tensor_scalar", out=diagw[:, g, tap, :], in0=ident[:], scalar1=convw[:, g, tap:tap + 1], scalar2=None, op0=ALU.mult,
                  r=["ident", "convw"], w=["diagw"])

        def win_names(c0, c1):
            hs = sorted({c0 // 1152, (c1 - 1) // 1152})
            return hs

        def tile_front(t, j):
            s = t % 2
            row = PRE + t * 128 if t < 17 else (t - 17) * 128
            P.dma("sync", xt[s][:], xc_d[row:row + 128, :], f"xt{s}", writes=[f"xt{s}"])
            A("activation", out=xn[s][:], in_=xt[s][:], func=AF.Square, accum_out=ss[:, t:t + 1],
              r=[f"xt{s}"], w=[f"xn{s}", f"ss{t}"])
            G("tensor_scalar", out=ms[:, t:t + 1], in0=ss[:, t:t + 1], scalar1=1.0 / D, scalar2=EPS, op0=ALU.mult, op1=ALU.add,
              r=[f"ss{t}"], w=[f"ms{t}"])
            G("tensor_tensor", out=rstd[:, t:t + 1], in0=ms[:, t:t + 1], in1=cst[:, 0:1], op=ALU.pow,
              r=[f"ms{t}", "cst"], w=[f"rstd{t}"])
            G("tensor_scalar", out=xn[s][:], in0=xt[s][:], scalar1=rstd[:, t:t + 1], scalar2=0.0, op0=ALU.mult, op1=ALU.add,
              r=[f"xt{s}", f"rstd{t}"], w=[f"xn{s}"])
            b = bank()
            for k in range(8):
                PE("transpose", out=pbT[b][:, k, :], in_=xn[s][:, k * 128:(k + 1) * 128], identity=ident[:],
                   r=[f"xn{s}", "ident"], w=[f"pb{b}"], sig=(k == 7))
            V("tensor_tensor", out=xnT[:, :, j * 128:(j + 1) * 128], in0=pbT[b][:, :, :],
                                        in1=gpre[:].unsqueeze(2).broadcast_to([128, 8, 128]), op=ALU.mult,
              r=[f"pb{b}", "gpre"], w=["xnT"])

        def fm_chunk(c0, N):
            b = bank()
            h = c0 // 1152
            for k in range(8):
                PE("matmul", pb[b][:, 0:N], lhsT=Win[:, k, c0:c0 + 128], rhs=xnT[:, k, 0:N], start=(k == 0), stop=(k == 7),
                   r=[f"Win{k}_{h}", "xnT"], w=[f"pb{b}"], sig=(k == 7))
            return b

        thc = [0]

        def th_slot():
            thc[0] ^= 1
            return thc[0]

        def rnn_chain(N, t0, pre=False):
            for gp in range(2):
                gs = (2 * gp, 2 * gp + 1)
                for gi, g in enumerate(gs):
                    b = bank()
                    for tap in range(4):
                        PE("matmul", pb[b][:, 0:N], lhsT=diagw[:, g, tap, :], rhs=xr[:, g, tap:tap + N],
                                                                 start=(tap == 0), stop=(tap == 3),
                           r=["diagw", "xr"], w=[f"pb{b}"], sig=(tap == 3))
                    A("activation", out=xcv[:, gi, 0:N], in_=pb[b][:, 0:N], func=AF.Identity, bias=vec4[:, 0, g:g + 1],
                      r=[f"pb{b}", "vec4"], w=[f"xcv{gi}"])
                    if pre:
                        continue
                    b = fm_chunk(C_GR + g * 128, N)
                    s = th_slot()
                    A("activation", out=th[s][:, 0:N], in_=pb[b][:, 0:N], func=AF.Tanh, scale=0.5, r=[f"pb{b}"], w=[f"th{s}"])
                    V("scalar_tensor_tensor", out=u_b[:, gi, 0:N], in0=th[s][:, 0:N], scalar=1.0, in1=pb[b][:, 0:N],
                                                                        op0=ALU.add, op1=ALU.mult,
                      r=[f"pb{b}", f"th{s}"], w=[f"u{gi}"])
                for gi, g in enumerate(gs):
                    bA = bank()
                    PE("matmul", pb[bA][:, 0:N], lhsT=bda[:, g, :], rhs=xcv[:, gi, 0:N], start=True, stop=True,
                       r=["bda", f"xcv{gi}"], w=[f"pb{bA}"], sig=True)
                    bX = bank()
                    PE("matmul", pb[bX][:, 0:N], lhsT=bdx[:, g, :], rhs=xcv[:, gi, 0:N], start=True, stop=True,
                       r=["bdx", f"xcv{gi}"], w=[f"pb{bX}"], sig=True)
                    s = th_slot()
                    A("activation", out=th[s][:, 0:N], in_=pb[bA][:, 0:N], func=AF.Tanh, scale=0.5, bias=hb[:, 0, g:g + 1],
                      r=[f"pb{bA}", "hb"], w=[f"th{s}"])
                    A("activation", out=a_b[:, gi, 0:N], in_=th[s][:, 0:N], func=AF.Exp, scale=chalf[:, g:g + 1], bias=chalf[:, g:g + 1],
                      r=[f"th{s}", "chalf"], w=[f"a{gi}"])
                    G("tensor_tensor", out=a2_b[:, gi, 0:N], in0=a_b[:, gi, 0:N], in1=a_b[:, gi, 0:N], op=ALU.mult,
                      r=[f"a{gi}"], w=[f"a2{gi}"])
                    s2 = th_slot()
                    A("activation", out=th[s2][:, 0:N], in_=pb[bX][:, 0:N], func=AF.Tanh, scale=0.5, bias=hb[:, 1, g:g + 1],
                      r=[f"pb{bX}", "hb"], w=[f"th{s2}"])
                    V("scalar_tensor_tensor", out=t1_b[:, gi, 0:N], in0=th[s2][:, 0:N], scalar=1.0, in1=xcv[:, gi, 0:N],
                                                                     op0=ALU.add, op1=ALU.mult,
                      r=[f"th{s2}", f"xcv{gi}"], w=[f"t1{gi}"])
                rnn_tail.append((gs, N, t0, pre))
                if SUB >= 4:
                    flush_rnn_tail()
                else:
                    rnn_tail.clear()

        front_done = set()

        def front(t0, nt):
            if (t0, nt) in front_done:
                return
            front_done.add((t0, nt))
            for j in range(nt):
                tile_front(t0 + j, j)

        def block(bi, t0, nt, pre=False, nxt=None):
            N = nt * 128
            halo = (bi == 0)
            last = (bi == 4) and not NOLAST
            front(t0, nt)
            if not halo and Nprev[0] > 0:
                npv = Nprev[0]
                V("tensor_copy", out=xr[:, :, 0:3], in_=xr[:, :, npv:npv + 3], r=["xr"], w=["xr"])
            for g in range(4):
                b = fm_chunk(C_XR + g * 128, N)
                A("activation", out=xr[:, g, 3:3 + N], in_=pb[b][:, 0:N], func=AF.Copy, r=[f"pb{b}"], w=["xr"])
                if last:
                    A("activation", out=xrt[:, g, :], in_=pb[b][:, N - 4:N], func=AF.Copy, r=[f"pb{b}"], w=["xrt"])
            Nprev[0] = N
            if pre:
                if nxt is not None:
                    front(*nxt)
                rnn_chain(N, t0, pre=True)
                return
            b = fm_chunk(C_K, N)
            if not halo:
                kpv, vpv = KTprev[0], Vprev[0]
                if kpv > 0:
                    V("tensor_copy", out=KT[:, 0:128], in_=KT[:, kpv:kpv + 128], r=["KT"], w=["KT"])
                    V("tensor_copy", out=Vg[:, 0, :], in_=Vg[:, vpv, :], r=["Vg"], w=["Vg"])
                A("activation", out=KT[:, 128:128 + N], in_=pb[b][:, 0:N], func=AF.Copy, r=[f"pb{b}"], w=["KT"])
                KTprev[0] = N
                Vprev[0] = nt
            else:
                A("activation", out=KT[:, 0:128], in_=pb[b][:, 0:N], func=AF.Copy, r=[f"pb{b}"], w=["KT"])
                KTprev[0] = 0
                Vprev[0] = 0
            for j in range(nt):
                t = t0 + j
                vs = 0 if halo else j + 1
                bV = bank()
                for k in range(8):
                    PE("matmul", pb[bV][:, 0:128], lhsT=xnT[:, k, j * 128:(j + 1) * 128], rhs=Win[:, k, C_V:C_V + 128],
                                                    start=(k == 0), stop=(k == 7),
                       r=[f"Win{k}_1", "xnT"], w=[f"pb{bV}"], sig=(k == 7))
                V("tensor_copy", out=Vg[:, vs, :].rearrange("p (g e) -> p g e", e=66)[:, :, 0:64],
                                                 in_=pb[bV][:, 0:128].rearrange("p (g d) -> p g d", d=64),
                  r=[f"pb{bV}"], w=["Vg"])
                if t == NT and not NOLAST:
                    A("activation", out=vout[:], in_=pb[bV][:, 0:128], func=AF.Copy, r=[f"pb{bV}"], w=["vout"])
                    bK = bank()
                    for k in range(8):
                        PE("matmul", pb[bK][:, 0:128], lhsT=xnT[:, k, j * 128:(j + 1) * 128], rhs=Win[:, k, C_K:C_K + 128],
                                                        start=(k == 0), stop=(k == 7),
                           r=[f"Win{k}_1", "xnT"], w=[f"pb{bK}"], sig=(k == 7))
                    A("activation", out=kout[:], in_=pb[bK][:, 0:128], func=AF.Copy, r=[f"pb{bK}"], w=["kout"])
            if halo or SUB < 2:
                return
            for cc in range(4):
                b = fm_chunk(C_Q + cc * 128, N)
                V("tensor_scalar", out=QT[0][0:64, cc, 0:N], in0=pb[b][0:64, 0:N], scalar1=0.125, scalar2=None, op0=ALU.mult,
                  r=[f"pb{b}"], w=["QT0"])
                V("tensor_scalar", out=QT[1][64:128, cc, 0:N], in0=pb[b][64:128, 0:N], scalar1=0.125, scalar2=None, op0=ALU.mult,
                  r=[f"pb{b}"], w=["QT1"])
            if SUB < 3:
                return
            rnn_chain(N, t0)
            if SUB < 5:
                return
            for j in range(nt):
                t = t0 + j
                bG = bank()
                for k in range(8):
                    PE("matmul", pb[bG][:, 0:512], lhsT=xnT[:, k, j * 128:(j + 1) * 128], rhs=Win[:, k, C_GA:C_GA + 512],
                                                    start=(k == 0), stop=(k == 7),
                       r=[f"Win{k}_1", "xnT"], w=[f"pb{bG}"], sig=(k == 7))
                s = th_slot()
                A("activation", out=th[s][:], in_=pb[bG][:], func=AF.Tanh, scale=0.5, r=[f"pb{bG}"], w=[f"th{s}"])
                V("scalar_tensor_tensor", out=ua[:], in0=th[s][:], scalar=1.0, in1=pb[bG][:], op0=ALU.add, op1=ALU.mult,
                  r=[f"pb{bG}", f"th{s}"], w=["ua"])
                if SUB >= 6:
                    attention(t, j)

        def attention(t, j):
            ps = t % 2
            for blk in range(2):
                kc = (j + blk) * 128
                for g in range(2):
                    b = bank()
                    bias_ap = BiasF[:, g, :] if (t == 1 and blk == 0) else BiasT[:, blk, g, :]
                    bname = "BiasF" if (t == 1 and blk == 0) else "BiasT"
                    PE("matmul", pb[b][:].rearrange("p (c q) -> p c q", q=128), lhsT=KT[:, kc:kc + 128],
                                                           rhs=QT[g][:, :, j * 128:(j + 1) * 128], start=True, stop=False,
                       r=["KT", f"QT{g}"], w=[f"pb{b}"])
                    PE("matmul", pb[b][:], lhsT=ident[:], rhs=bias_ap, start=False, stop=True,
                       r=["ident", bname], w=[f"pb{b}"], sig=True)
                    A("activation", out=PT[ps][:, blk, g, :], in_=pb[b][:], func=AF.Exp, r=[f"pb{b}"], w=[f"PT{ps}"])
            if SUB < 7:
                return
            bO = []
            for g in range(2):
                b = bank()
                bO.append(b)
                for cc in range(4):
                    for blk in range(2):
                        PE("matmul", pb[b][:, cc * 66:cc * 66 + 66], lhsT=PT[ps][:, blk, g, cc * 128:(cc + 1) * 128],
                                                                        rhs=Vg[:, j + blk, g * 66:(g + 1) * 66], start=(blk == 0), stop=(blk == 1),
                           r=[f"PT{ps}", "Vg"], w=[f"pb{b}"], sig=(cc == 3 and blk == 1))
                V("scalar_tensor_tensor", out=dsum[:, :, g], in0=pb[b][:, 0:264].rearrange("p (c e) -> p c e", e=66)[:, :, 64],
                                                             scalar=2.0, in1=sinkexp2[:, g, :], op0=ALU.mult, op1=ALU.add,
                  r=[f"pb{b}", "sinkexp2"], w=["dsum"])
            if SUB < 8:
                return
            V("reciprocal", out=rden[:], in_=dsum[:], r=["dsum"], w=["rden"])
            G("tensor_tensor", out=sgr[:].rearrange("p (c g d) -> p c g d", g=2, d=64), in0=ua[:].rearrange("p (c g d) -> p c g d", g=2, d=64),
                                        in1=rden[:].unsqueeze(3).broadcast_to([128, 4, 2, 64]), op=ALU.mult,
              r=["ua", "rden"], w=["sgr"])
            for g in range(2):
                b = bO[g]
                V("tensor_tensor", out=att_o[:].rearrange("p (c g d) -> p c g d", g=2, d=64)[:, :, g, :],
                                                      in0=pb[b][:, 0:264].rearrange("p (c e) -> p c e", e=66)[:, :, 0:64],
                                                      in1=sgr[:].rearrange("p (c g d) -> p c g d", g=2, d=64)[:, :, g, :], op=ALU.mult,
                  r=[f"pb{b}", "sgr"], w=["att_o"])
            b = bank()
            for cc in range(4):
                PE("transpose", out=pbT[b][:, cc, :], in_=att_o[:, cc * 128:(cc + 1) * 128], identity=ident[:],
                   r=["att_o", "ident"], w=[f"pb{b}"], sig=(cc == 3))
            A("activation", out=attT[:, :, (t - 1) * 128:t * 128], in_=pbT[b][:, 0:4, :], func=AF.Copy, r=[f"pb{b}"], w=[f"attT{t}"])

        rnn_tail = []
        first_scan = [True]
        first_A = [True]

        def flush_rnn_tail():
            for gs, N, t0, pre in rnn_tail:
                for gi, g in enumerate(gs):
                    A("activation", out=a2_b[:, gi, 0:N], in_=a2_b[:, gi, 0:N], func=AF.Sqrt, scale=-1.0 / 16.0, bias=1.0 / 16.0,
                      r=[f"a2{gi}"], w=[f"a2{gi}"])
            for gs, N, t0, pre in rnn_tail:
                c0 = (t0 - 1) * 128
                for gi, g in enumerate(gs):
                    G("tensor_tensor", out=t1_b[:, gi, 0:N], in0=t1_b[:, gi, 0:N], in1=a2_b[:, gi, 0:N], op=ALU.mult,
                      r=[f"t1{gi}", f"a2{gi}"], w=[f"t1{gi}"])
                    fs = first_scan[0]
                    fsA = first_A[0]
                    V("tensor_tensor_scan", out=hl[:, 0:N], data0=a_b[:, gi, 0:N], data1=t1_b[:, gi, 0:N],
                                                                              initial=(0.0 if fs else car[:, 4 + g:5 + g]), op0=ALU.mult, op1=ALU.add,
                      r=[f"a{gi}", f"t1{gi}", "car"], w=["hl"])
                    V("tensor_copy", out=car[:, 4 + g:5 + g], in_=hl[:, N - 1:N], r=["hl", "car"], w=["car"])
                    if pre:
                        continue
                    V("tensor_tensor_scan", out=Ac[:, 0:N], data0=a_b[:, gi, 0:N], data1=cst[:, 1:2].broadcast_to([128, N]),
                                                                              initial=(1.0 if fsA else car[:, g:g + 1]), op0=ALU.mult, op1=ALU.add,
                      r=[f"a{gi}", "cst", "car"], w=["Ac"])
                    V("tensor_copy", out=car[:, g:g + 1], in_=Ac[:, N - 1:N], r=["Ac", "car"], w=["car"])
                    G("tensor_tensor", out=P1[:, g, c0:c0 + N], in0=hl[:, 0:N], in1=u_b[:, gi, 0:N], op=ALU.mult,
                      r=["hl", f"u{gi}"], w=[f"P1_{g}_{c0}"])
                    G("tensor_tensor", out=P2[:, g, c0:c0 + N], in0=Ac[:, 0:N], in1=u_b[:, gi, 0:N], op=ALU.mult,
                      r=["Ac", f"u{gi}"], w=[f"P2_{g}_{c0}"])
                if gs[1] == 3:
                    first_scan[0] = False
                    if not pre:
                        first_A[0] = False
            rnn_tail.clear()

        Nprev = [0]
        KTprev = [0]
        Vprev = [0]
        G("memset", xr[:], 0.0, w=["xr"])
        ld(flag[:], flag_d, "flag")
        for pb_i in range(NPT // 4):
            nxt = (17 + 4 * (pb_i + 1), 4) if pb_i + 1 < NPT // 4 else (0, 1)
            block(100 + pb_i, 17 + 4 * pb_i, 4, pre=True, nxt=nxt)
            if pb_i % 4 == 3:
                kf = pb_i // 4
                V("tensor_scalar", out=car[:, 4:8], in0=car[:, 4:8], scalar1=flag[:, kf:kf + 1], scalar2=None, op0=ALU.mult,
                  r=["car", "flag"], w=["car"])
        Nprev[0] = 0
        blocks = [(0, 0, 1), (1, 1, 4), (2, 5, 4), (3, 9, 4), (4, 13, 4)]
        for bi, t0, nt in blocks:
            if STAGE >= (1 if bi == 0 else 2 if bi == 1 else 3) and bi <= NBLK:
                block(bi, t0, nt)

        if STAGE >= 3 and not NOOUT:
            for g in range(4):
                out_handles.append(P.dma("sync", nconv_d[:, g * 128:(g + 1) * 128].rearrange("t p -> p t"), xrt[:, g, 1:4], f"o_nconv{g}", reads=["xrt"],
                                         allow_slow_non_contiguous=True, group="outs"))
            out_handles.append(P.dma("sync", nk_d, kout[:], "o_nk", reads=["kout"], group="outs"))
            out_handles.append(P.dma("sync", nv_d, vout[:], "o_nv", reads=["vout"], group="outs"))

        if STAGE >= 4:
            G("memset", h0[:], 0.0, w=["h0"])
            V("tensor_scalar", out=hfin[:], in0=car[:, 4:8], scalar1=2.0, scalar2=None, op0=ALU.mult, r=["car"], w=["hfin"])
            out_handles.append(P.dma("sync", nrnn_d.rearrange("(g p) -> p g", p=128), hfin[:], "o_nrnn", reads=["hfin"],
                                     allow_slow_non_contiguous=True, group="outs"))

        def sample_path():
            F = lambda ap: ap.bitcast(F32)
            xnTs = QT[0][:].rearrange("p c n -> p (c n)")[:, 0:1024].rearrange("p (k t) -> p k t", t=128)
            Kn = F(xnT[:].rearrange("p k n -> p (k n)")).rearrange("p (s c) -> p s c", c=128)
            Vn = [F(PT[h][:].rearrange("p a b n -> p (a b n)")).rearrange("p (s c) -> p s c", c=128) for h in range(2)]
            KnT = [a_b[:].rearrange("p a n -> p (a n)").rearrange("p (s c) -> p s c", c=128),
                   a2_b[:].rearrange("p a n -> p (a n)").rearrange("p (s c) -> p s c", c=128)]
            xcT = u_b[:].rearrange("p a n -> p (a n)")[:, 0:512].rearrange("p (g t) -> p g t", t=128)
            Qm = th[0][:, 0:128].rearrange("p (s q) -> p s q", q=8)
            Pt = hl[:, 0:128]
            sc = Ac[:, 0:128]
            rds = Ac[:, 128:256]
            attn = hl[:, 128:256]
            xcs = t1_b[:, 0, :]
            tA = t1_b[:, 1, :]
            tB = xcv[:, 0, :]
            tC = xcv[:, 1, :]
            tD = ua[:]
            tE = sgr[:]
            mixtm = xn[0][:, 0:512]
            mixTs = xn[1][:].rearrange("p (k t) -> p k t", t=128)
            R16 = slice(0, SB)
            flat32 = lambda t, pat: F(t[:].rearrange(pat))
            hosts = [(xt[1][:], "xt1"), (flat32(QT[1], "p c n -> p (c n)"), "QT1"), (flat32(BiasT, "p a b n -> p (a b n)"), "BiasT"),
                     (flat32(diagw, "p g t m -> p (g t m)"), "diagw")]
            rp, rpn = [], []
            for hap, hname in hosts:
                for i2 in range(2):
                    rp.append(hap[R16, i2 * 512:(i2 + 1) * 512])
                    rpn.append(hname)
            for i8 in range(8):
                P.dma("sync", rp[i8], rowp_d[:, i8 * 512:(i8 + 1) * 512], f"rowp{i8}", writes=[rpn[i8]], group="samp")
            sst_t = [tt[R16, 0:512], tt[R16, 512:1024], F(BiasF[:].rearrange("p g n -> p (g n)"))[R16, 0:512]]
            sst_n = ["tt", "tt", "BiasF"]
            for t3 in range(3):
                P.dma("sync", sst_t[t3], sconv_d[:, t3 * 512:(t3 + 1) * 512], f"sst{t3}", writes=[sst_n[t3]], group="samp")
            hprev = th[1][R16, :]
            P.dma("sync", hprev, srnn_d, "hprev", writes=["th1"], group="samp")
            kv_s = F(att_o[:])[R16, :]
            ld(ident32[:], ident_d, "ident32", group="samp")
            ld(biass[:], biass_d, "biass", group="samp")
            def win(src, h):
                return src[1 + 8 * h * 128:1 + (8 * h + 8) * 128, :].rearrange("(s k) c -> k s c", k=128)
            big = [None]
            for h in range(2):
                big[0] = P.dma("sync", Kn[:, 8 * h:8 * h + 8, :], win(ck_d, h), f"Kn_a{h}", writes=["xnT", f"KnH{h}"], deps=[big[0]])
                big[0] = P.dma("sync", Vn[h][:], win(cv_d, h), f"Vn_a{h}", writes=[f"PT{h}"], deps=[big[0]])
            if SS < 1:
                return
            if S1 < 1:
                return
            pass
            if S1 < 2:
                return
            P.dma("sync", xt[0][:], xs_d, "xt0", writes=["xt0"])
            if S1 < 3:
                return
            pass
            if S1 < 4:
                return
            A("activation", out=xn[0][:], in_=xt[0][:], func=AF.Square, accum_out=sm[:, 0:1], r=["xt0", "sm"], w=["xn0", "sm"])
            if S1 < 5:
                return
            V("tensor_scalar", out=sm[:, 1:2], in0=sm[:, 0:1], scalar1=1.0 / D, scalar2=EPS, op0=ALU.mult, op1=ALU.add, r=["sm"], w=["sm"])
            if S1 < 6:
                return
            A("activation", out=sm[:, 1:2], in_=sm[:, 1:2], func=AF.Sqrt, r=["sm"], w=["sm"])
            if S1 < 7:
                return
            V("reciprocal", out=sm[:, 2:3], in_=sm[:, 1:2], r=["sm"], w=["sm"])
            if S1 < 8:
                return
            A("activation", out=xn[0][:], in_=xt[0][:], func=AF.Copy, scale=sm[:, 2:3], r=["xt0", "sm"], w=["xn0"])
            if S1 < 9:
                return
            b = bank()
            for k in range(8):
                PE("transpose", out=pbT[b][:, k, :], in_=xn[0][:, k * 128:(k + 1) * 128], identity=ident[:], r=["xn0", "ident"], w=[f"pb{b}"], sig=(k == 7))
            if S1 < 10:
                return
            V("tensor_tensor", out=xnTs, in0=pbT[b][:, :, :], in1=gpre[:].unsqueeze(2).broadcast_to([128, 8, 128]), op=ALU.mult,
              r=[f"pb{b}", "gpre"], w=["QT0"])

            def tm_cols(c0, w):
                bb = bank()
                for k in range(8):
                    PE("matmul", pb[bb][:, 0:w], lhsT=xnTs[:, k, :], rhs=Win[:, k, c0:c0 + w], start=(k == 0), stop=(k == 7),
                       r=["QT0"] + [f"Win{k}_{hh}" for hh in sorted({c0 // 1152, (c0 + w - 1) // 1152})], w=[f"pb{bb}"], sig=(k == 7))
                return bb

            def fm_cols(c0):
                bb = bank()
                hh = c0 // 1152
                for k in range(8):
                    PE("matmul", pb[bb][:, 0:128], lhsT=Win[:, k, c0:c0 + 128], rhs=xnTs[:, k, :], start=(k == 0), stop=(k == 7),
                       r=["QT0", f"Win{k}_{hh}"], w=[f"pb{bb}"], sig=(k == 7))
                return bb

            if SS < 2:
                return
            bKV = tm_cols(C_K, 256)
            A("activation", out=kv_s, in_=pb[bKV][R16, 0:256], func=AF.Copy, r=[f"pb{bKV}"], w=["att_o"])
            P.dma("sync", kvb.ap(), kv_s, "kvb", reads=["att_o"], writes=["kvb"])
            P.dma("sync", Kn[127:128], kvb.ap()[:, 0:128].unsqueeze(0), "Kn_b", reads=["kvb", "xnT", "KnH0", "KnH1"], writes=["xnT"], group="samp2")
            for h in range(2):
                P.dma("sync", Vn[h][127:128], kvb.ap()[8 * h:8 * h + 8, 128:256].unsqueeze(0), f"Vn_b{h}", reads=["kvb", f"PT{h}"], writes=[f"PT{h}"], group="samp2")
            for h in range(2):
                big[0] = P.dma("sync", nks_d[8 * h:8 * h + 8].rearrange("s k c -> k s c"), Kn[:, 8 * h:8 * h + 8, :], f"o_nks{h}", reads=["xnT", f"KnH{h}"], deps=[big[0]])
                out_handles.append(big[0])
                big[0] = P.dma("sync", nvs_d[8 * h:8 * h + 8].rearrange("s k c -> k s c"), Vn[h][:], f"o_nvs{h}", reads=[f"PT{h}"], deps=[big[0]])
                out_handles.append(big[0])
            if SS < 3:
                return
            for cc in range(4):
                bq = fm_cols(C_Q + cc * 128)
                V("tensor_scalar", out=Qm[0:64, :, 2 * cc], in0=pb[bq][0:64, 0:SB], scalar1=0.125, scalar2=None, op0=ALU.mult, r=[f"pb{bq}"], w=["th0"])
                V("tensor_scalar", out=Qm[64:128, :, 2 * cc + 1], in0=pb[bq][64:128, 0:SB], scalar1=0.125, scalar2=None, op0=ALU.mult, r=[f"pb{bq}"], w=["th0"])
            for cc in range(4):
                bg = fm_cols(C_GA + cc * 128)
                A("activation", out=tD[:, 0:SB], in_=pb[bg][:, 0:SB], func=AF.Tanh, scale=0.5, r=[f"pb{bg}"], w=["ua"])
                V("scalar_tensor_tensor", out=uaT[:, cc, :], in0=tD[:, 0:SB], scalar=1.0, in1=pb[bg][:, 0:SB], op0=ALU.add, op1=ALU.mult,
                  r=[f"pb{bg}", "ua"], w=["uaT"])
            if SS < 4:
                return
            bXR = tm_cols(C_XR, 512)
            bGR = tm_cols(C_GR, 512)
            cw = lambda tap: rp[tap]
            cb_r, bga_r, bgx_r, lam_r = rp[4], rp[5], rp[6], rp[7]
            A("activation", out=tB[R16], in_=pb[bXR][R16, :], func=AF.Copy, r=[f"pb{bXR}"], w=["xcv0"])
            out_handles.append(P.dma("sync", nconvs_d[:, 0:512], sst_t[1], "o_ncs_a", reads=["tt"]))
            out_handles.append(P.dma("sync", nconvs_d[:, 512:1024], sst_t[2], "o_ncs_c", reads=["BiasF"], group="outs"))
            out_handles.append(P.dma("sync", nconvs_d[:, 1024:1536], tB[R16], "o_ncs_b", reads=["xcv0"], group="outs"))
            V("tensor_tensor", out=xcs[R16], in0=tB[R16], in1=cw(3), op=ALU.mult, r=["xcv0", rpn[3]], w=["t10"])
            for tap in range(3):
                V("tensor_tensor", out=tA[R16], in0=sst_t[tap], in1=cw(tap), op=ALU.mult, r=[sst_n[tap], rpn[tap]], w=["t11"])
                V("tensor_tensor", out=xcs[R16], in0=xcs[R16], in1=tA[R16], op=ALU.add, r=["t10", "t11"], w=["t10"])
            V("tensor_tensor", out=xcs[R16], in0=xcs[R16], in1=cb_r, op=ALU.add, r=["t10", rpn[4]], w=["t10"])
            b = bank()
            for g in range(4):
                PE("transpose", out=pb[b][:, g * 128:(g + 1) * 128], in_=xcs[:, g * 128:(g + 1) * 128], identity=ident32[:],
                   r=["t10", "ident32"], w=[f"pb{b}"], sig=(g == 3))
            V("tensor_copy", out=xcT, in_=pb[b][:].rearrange("p (g t) -> p g t", t=128), r=[f"pb{b}"], w=["u0"])
            bA, bX = bank(), bank()
            for g in range(4):
                PE("matmul", pb[bA][:, g * 128:(g + 1) * 128], lhsT=xcT[:, g, :], rhs=bda[:, g, :], start=True, stop=True, r=["u0", "bda"], w=[f"pb{bA}"], sig=(g == 3))
            for g in range(4):
                PE("matmul", pb[bX][:, g * 128:(g + 1) * 128], lhsT=xcT[:, g, :], rhs=bdx[:, g, :], start=True, stop=True, r=["u0", "bdx"], w=[f"pb{bX}"], sig=(g == 3))
            A("activation", out=tC[R16], in_=lam_r, func=AF.Exp, scale=-1.0, r=[rpn[7]], w=["xcv1"])
            A("activation", out=tC[R16], in_=tC[R16], func=AF.Ln, bias=1.0, r=["xcv1"], w=["xcv1"])
            V("tensor_scalar", out=tC[R16], in0=tC[R16], scalar1=-4.0, scalar2=None, op0=ALU.mult, r=["xcv1"], w=["xcv1"])
            V("tensor_tensor", out=tA[R16], in0=pb[bA][R16, :], in1=bga_r, op=ALU.add, r=[f"pb{bA}", rpn[5]], w=["t11"])
            A("activation", out=tA[R16], in_=tA[R16], func=AF.Tanh, scale=0.5, r=["t11"], w=["t11"])
            V("scalar_tensor_tensor", out=tA[R16], in0=tA[R16], scalar=1.0, in1=tC[R16], op0=ALU.add, op1=ALU.mult, r=["t11", "xcv1"], w=["t11"])
            A("activation", out=tA[R16], in_=tA[R16], func=AF.Exp, r=["t11"], w=["t11"])
            V("tensor_tensor", out=tC[R16], in0=pb[bX][R16, :], in1=bgx_r, op=ALU.add, r=[f"pb{bX}", rpn[6]], w=["xcv1"])
            A("activation", out=tC[R16], in_=tC[R16], func=AF.Tanh, scale=0.5, r=["xcv1"], w=["xcv1"])
            V("scalar_tensor_tensor", out=tC[R16], in0=tC[R16], scalar=1.0, in1=xcs[R16], op0=ALU.add, op1=ALU.mult, r=["xcv1", "t10"], w=["xcv1"])
            A("activation", out=tE[R16], in_=pb[bGR][R16, :], func=AF.Tanh, scale=0.5, r=[f"pb{bGR}"], w=["sgr"])
            V("scalar_tensor_tensor", out=tE[R16], in0=tE[R16], scalar=1.0, in1=pb[bGR][R16, :], op0=ALU.add, op1=ALU.mult, r=["sgr", f"pb{bGR}"], w=["sgr"])
            V("tensor_tensor", out=tD[R16], in0=tA[R16], in1=tA[R16], op=ALU.mult, r=["t11"], w=["ua"])
            A("activation", out=tD[R16], in_=tD[R16], func=AF.Sqrt, scale=-1.0 / 16.0, bias=1.0 / 16.0, r=["ua"], w=["ua"])
            V("tensor_tensor", out=tC[R16], in0=tC[R16], in1=tD[R16], op=ALU.mult, r=["xcv1", "ua"], w=["xcv1"])
            V("tensor_tensor", out=tA[R16], in0=tA[R16], in1=hprev, op=ALU.mult, r=["t11", "th1"], w=["t11"])
            V("scalar_tensor_tensor", out=tC[R16], in0=tC[R16], scalar=2.0, in1=tA[R16], op0=ALU.mult, op1=ALU.add, r=["xcv1", "t11"], w=["xcv1"])
            out_handles.append(P.dma("sync", nrnns_d, tC[R16], "o_nrs", reads=["xcv1"], group="outs"))
            V("scalar_tensor_tensor", out=mixtm[R16], in0=tC[R16], scalar=0.5, in1=tE[R16], op0=ALU.mult, op1=ALU.mult, r=["xcv1", "sgr"], w=["xn0"])
            b = bank()
            for g in range(4):
                PE("transpose", out=pbT[b][:, g, :], in_=mixtm[:, g * 128:(g + 1) * 128], identity=ident[:], r=["xn0", "ident"], w=[f"pb{b}"], sig=(g == 3))
            V("tensor_copy", out=mixTs[:, 0:4, :], in_=pbT[b][:, 0:4, :], r=[f"pb{b}"], w=["xn1"])
            if SS < 5:
                return
            for q4 in range(4):
                b = bank()
                for i4 in range(4):
                    sq = q4 * 4 + i4
                    PE("transpose", out=pb[b][:, i4 * 128:(i4 + 1) * 128], in_=Kn[:, sq, :], identity=ident32[:], r=["xnT", "ident32"], w=[f"pb{b}"], sig=(i4 == 3))
                hh, s0 = q4 // 2, (q4 % 2) * 4
                V("tensor_copy", out=KnT[hh][:, s0:s0 + 4, :], in_=pb[b][:].rearrange("p (s k) -> p s k", k=128), r=[f"pb{b}"], w=[f"a{hh}" if hh == 0 else "a20"])
            bS = bank()
            for sq in range(SB):
                hh, s0 = sq // 8, sq % 8
                PE("matmul", pb[bS][:, sq * 8:(sq + 1) * 8], lhsT=KnT[hh][:, s0, :], rhs=Qm[:, sq, :], start=True, stop=True,
                   r=["a0" if hh == 0 else "a20", "th0"], w=[f"pb{bS}"], sig=(sq == SB - 1))
            V("tensor_tensor", out=sc.rearrange("p (s q) -> p s q", q=8), in0=pb[bS][:, 0:128].rearrange("p (s q) -> p s q", q=8),
              in1=biass[:].unsqueeze(1).broadcast_to([128, SB, 8]), op=ALU.add, r=[f"pb{bS}", "biass"], w=["Ac"])
            A("activation", out=Pt, in_=sc, func=AF.Exp, r=["Ac"], w=["hl"])
            bO = bank()
            for sq in range(SB):
                hh, s0 = sq // 8, sq % 8
                PE("matmul", pb[bO][:, sq * 8:(sq + 1) * 8], lhsT=Vn[hh][:, s0, :], rhs=Pt[:, sq * 8:(sq + 1) * 8], start=True, stop=True,
                   r=[f"PT{hh}", "hl"], w=[f"pb{bO}"], sig=(sq == SB - 1))
            bD = bank()
            PE("matmul", pb[bD][:, 0:128], lhsT=ones32[:], rhs=Pt, start=True, stop=True, r=["ones32", "hl"], w=[f"pb{bD}"], sig=True)
            V("tensor_copy", out=sm[:, 8:16].rearrange("p (c g) -> p c g", g=2), in_=sinkexp2[:].rearrange("p g c -> p c g"), r=["sinkexp2", "sm"], w=["sm"])
            V("scalar_tensor_tensor", out=rds.rearrange("p (s q) -> p s q", q=8), in0=pb[bD][:, 0:128].rearrange("p (s q) -> p s q", q=8),
              scalar=2.0, in1=sm[:, 8:16].unsqueeze(1).broadcast_to([128, SB, 8]), op0=ALU.mult, op1=ALU.add,
              r=[f"pb{bD}", "sm"], w=["Ac"])
            V("reciprocal", out=rds, in_=rds, r=["Ac"], w=["Ac"])
            V("tensor_tensor", out=attn, in0=pb[bO][:, 0:128], in1=rds, op=ALU.mult, r=[f"pb{bO}", "Ac"], w=["hl"])
            av = attn.rearrange("p (s c g) -> p c s g", c=4, g=2)
            V("tensor_tensor", out=mixTs[0:64, 4:8, 0:SB], in0=av[0:64, :, :, 0], in1=uaT[0:64, :, :], op=ALU.mult, r=["hl", "uaT"], w=["xn1"])
            V("tensor_tensor", out=mixTs[64:128, 4:8, 0:SB], in0=av[64:128, :, :, 1], in1=uaT[64:128, :, :], op=ALU.mult, r=["hl", "uaT"], w=["xn1"])
            if SS < 6:
                return
            bY = [bank(), bank()]
            for kk in range(8):
                for hf in range(2):
                    PE("matmul", pb[bY[hf]][:], lhsT=mixTs[:, kk, :], rhs=Wout[:, kk, hf * 512:(hf + 1) * 512], start=(kk == 0), stop=(kk == 7),
                       r=["xn1", f"Wout{kk}"], w=[f"pb{bY[hf]}"], sig=(kk == 7))
            for hf in range(2):
                A("activation", out=junk2[:], in_=pb[bY[hf]][:], func=AF.Square, accum_out=sm[:, 3 + hf:4 + hf], r=[f"pb{bY[hf]}", "sm"], w=["junk2", "sm"])
            V("tensor_tensor", out=sm[:, 5:6], in0=sm[:, 3:4], in1=sm[:, 4:5], op=ALU.add, r=["sm"], w=["sm"])
            V("tensor_scalar", out=sm[:, 5:6], in0=sm[:, 5:6], scalar1=1.0 / D, scalar2=EPS, op0=ALU.mult, op1=ALU.add, r=["sm"], w=["sm"])
            A("activation", out=sm[:, 5:6], in_=sm[:, 5:6], func=AF.Sqrt, r=["sm"], w=["sm"])
            V("reciprocal", out=sm[:, 6:7], in_=sm[:, 5:6], r=["sm"], w=["sm"])
            for hf in range(2):
                V("scalar_tensor_tensor", out=tt[:, hf * 512:(hf + 1) * 512], in0=pb[bY[hf]][:], scalar=sm[:, 6:7], in1=gpost[:, hf * 512:(hf + 1) * 512],
                  op0=ALU.mult, op1=ALU.mult, r=[f"pb{bY[hf]}", "sm", "gpost"], w=["tt"])
            V("tensor_tensor", out=xt[0][R16, :], in0=tt[R16, :], in1=xt[0][R16, :], op=ALU.add, r=["tt", "xt0"], w=["xt0"])
            out_handles.append(P.dma("sync", ys_d, xt[0][R16, :], "o_ys", reads=["xt0"]))


        if STAGE >= 5:
            for i in range(NT):
                s = i % 2
                c0 = i * 128
                blk0 = (i // 4) * 512
                P.dma("sync", xt[s][:], xc_d[PRE + (i + 1) * 128:PRE + (i + 2) * 128, :], f"xt{s}", writes=[f"xt{s}"])
                for g in range(4):
                    V("scalar_tensor_tensor", out=mixT[s][:, g, :], in0=P2[:, g, c0:c0 + 128], scalar=h0[:, g:g + 1], in1=P1[:, g, c0:c0 + 128],
                                                            op0=ALU.mult, op1=ALU.add,
                      r=[f"P1_{g}_{blk0}", f"P2_{g}_{blk0}", "h0"], w=[f"mixT{s}"])
                bY = [bank(), bank()]
                for kk in range(8):
                    lhs = mixT[s][:, kk, :] if kk < 4 else attT[:, kk - 4, c0:c0 + 128]
                    rn = [f"mixT{s}"] if kk < 4 else [f"attT{i + 1}"]
                    for hf in range(2):
                        PE("matmul", pb[bY[hf]][:], lhsT=lhs, rhs=Wout[:, kk, hf * 512:(hf + 1) * 512],
                                                                     start=(kk == 0), stop=(kk == 7),
                           r=rn + [f"Wout{kk}"], w=[f"pb{bY[hf]}"], sig=(kk == 7))
                for hf in range(2):
                    A("activation", out=junk2[:], in_=pb[bY[hf]][:], func=AF.Square, accum_out=ss2[:, i, hf:hf + 1],
                      r=[f"pb{bY[hf]}"], w=["junk2", f"ss2_{i}_{hf}"])
                G("tensor_tensor", out=ms2[:, i:i + 1], in0=ss2[:, i, 0:1], in1=ss2[:, i, 1:2], op=ALU.add,
                  r=[f"ss2_{i}_0", f"ss2_{i}_1"], w=[f"ms2_{i}"])
                G("tensor_scalar", out=ms2[:, i:i + 1], in0=ms2[:, i:i + 1], scalar1=1.0 / D, scalar2=EPS, op0=ALU.mult, op1=ALU.add,
                  r=[f"ms2_{i}"], w=[f"ms2_{i}"])
                G("tensor_tensor", out=rstd2[:, i:i + 1], in0=ms2[:, i:i + 1], in1=cst[:, 0:1], op=ALU.pow,
                  r=[f"ms2_{i}", "cst"], w=[f"rstd2_{i}"])
                for hf in range(2):
                    V("scalar_tensor_tensor", out=tt[:, hf * 512:(hf + 1) * 512], in0=pb[bY[hf]][:], scalar=rstd2[:, i:i + 1],
                                                              in1=gpost[:, hf * 512:(hf + 1) * 512], op0=ALU.mult, op1=ALU.mult,
                      r=[f"pb{bY[hf]}", f"rstd2_{i}", "gpost"], w=["tt"])
                G("tensor_tensor", out=xt[s][:], in0=tt[:], in1=xt[s][:], op=ALU.add, r=["tt", f"xt{s}"], w=[f"xt{s}"])
                out_handles.append(P.dma("sync", y_d[c0:c0 + 128, :], xt[s][:], f"o_y{s}", reads=[f"xt{s}"]))

        if STAGE >= 6:
            G("memset", th[0][:, 0:128], 0.0, w=["th0"])
            G("memset", t1_b[:], 0.0, w=["t10", "t11"])
            G("memset", xn[1][:], 0.0, w=["xn1"])
            sample_path()

        P.wait_all("sync", out_handles)
        P.emit()
    return nc


def _t5_bucket(dist):
    dist = np.maximum(dist, 0)
    max_exact = 16
    d = np.maximum(dist, 1).astype(np.float32)
    large = max_exact + (np.log(d / np.float32(max_exact)) / np.float32(np.log(128 / max_exact)) * np.float32(32 - max_exact)).astype(np.int32)
    large = np.minimum(large, 31)
    return np.where(dist < max_exact, dist, large)


_NC_CACHE = {}


def kernel(x_prompt, x_sample, state_conv, state_rnn, cache_k_win, cache_v_win,
           norm_pre, norm_post, w_in, conv_w, conv_b, w_gate_a, b_gate_a, w_gate_x, b_gate_x,
           lru_lambda, attn_sinks, rel_bias, w_out):
    f32 = np.float32
    x_prompt = np.asarray(x_prompt, f32)
    w_in0 = np.asarray(w_in, f32)[0]
    w_out0 = np.asarray(w_out, f32)[0]
    qperm = np.concatenate([np.arange(h * 64, h * 64 + 64) for h in LPOS])
    cols = np.concatenate([np.arange(0, 1024), 1024 + qperm, np.arange(1536, 1792), 1792 + qperm])
    w_in_p = np.ascontiguousarray(w_in0[:, cols])
    rows = np.concatenate([np.arange(0, 512), 512 + qperm])
    w_out_p = np.ascontiguousarray(w_out0[rows, :])

    def pg(v):
        return np.ascontiguousarray(np.asarray(v, f32).reshape(4, 128).T)

    gpre = np.ascontiguousarray(np.asarray(norm_pre, f32)[0].reshape(8, 128).T)
    gpost = np.ascontiguousarray(np.broadcast_to(np.asarray(norm_post, f32)[0][None, :], (128, D)))
    cw = np.asarray(conv_w, f32)[0]
    convw = np.ascontiguousarray(cw.reshape(4, 4, 128).transpose(2, 1, 0).reshape(128, 16))
    vec4 = np.ascontiguousarray(np.stack([pg(np.asarray(conv_b)[0]), pg(np.asarray(b_gate_a)[0]), pg(np.asarray(b_gate_x)[0]),
                                          pg(np.asarray(lru_lambda)[0])], axis=1).reshape(128, 16))

    def blockdiag(w):
        w = np.asarray(w, f32)[0]
        o = np.zeros((128, 4, 128), f32)
        for g in range(4):
            for h in range(2):
                o[h * 64:(h + 1) * 64, g, h * 64:(h + 1) * 64] = w[2 * g + h]
        return o.reshape(128, 512)

    bda = blockdiag(w_gate_a)
    bdx = blockdiag(w_gate_x)
    hd = np.array([[g * 4 + cc for cc in range(4)] for g in range(2)])
    sinks = np.ascontiguousarray(np.broadcast_to(np.asarray(attn_sinks, f32)[0][hd].reshape(1, 8), (128, 8)))
    rb = np.asarray(rel_bias, f32)
    kk = np.arange(128)[:, None]
    qq = np.arange(128)[None, :]
    biasg = np.zeros((128, 2, 2, 4, 128), f32)
    maskc = np.zeros((128, 2, 128), f32)
    for blk in range(2):
        dist = qq + (128 if blk == 0 else 0) - kk
        valid = (dist >= 0) & (dist < 128)
        bkt = _t5_bucket(np.clip(dist, 0, 127))
        for g in range(2):
            for cc in range(4):
                biasg[:, blk, g, cc, :] = rb[bkt, hd[g, cc]]
        maskc[:, blk, :] = np.where(valid, 0.0, NEG)
    biasg = biasg.reshape(128, 2048)
    maskc = maskc.reshape(128, 256)
    ident = np.eye(128, dtype=f32)

    xs_all = np.asarray(x_sample, f32)[:, 0, :]
    sconv_all = np.asarray(state_conv, f32)[0].reshape(128, 1536)
    srnn_all = np.asarray(state_rnn, f32)[0]
    ck_all = np.asarray(cache_k_win, f32)[0].reshape(128, 128, 128)
    cv_all = np.asarray(cache_v_win, f32)[0].reshape(128, 128, 128)
    rowv = np.concatenate([cw.reshape(-1), np.asarray(conv_b, f32)[0], np.asarray(b_gate_a, f32)[0], np.asarray(b_gate_x, f32)[0],
                           np.asarray(lru_lambda, f32)[0]])
    rowp = np.ascontiguousarray(np.broadcast_to(rowv[None, :], (SB, 4096)))
    posh = np.array([g * 4 + cc for cc in range(4) for g in range(2)])
    biass = np.ascontiguousarray(rb[_t5_bucket(127 - np.arange(128))][:, posh])

    def padrows(a):
        return np.concatenate([a.reshape(SB * 128, 128), np.zeros((128, 128), f32)], axis=0)

    in_maps = []
    for c in range(NCORES):
        b, j = c // 4, c % 4
        xc = np.zeros((PRE + CH + 128, D), f32)
        flag = np.zeros((128, 3), f32)
        for kf in range(3):
            cj = j - 3 + kf
            if cj >= 0:
                xc[kf * CH:(kf + 1) * CH] = x_prompt[b, cj * CH:(cj + 1) * CH]
                flag[:, kf] = 1.0
        xc[PRE + 128:] = x_prompt[b, j * CH:(j + 1) * CH]
        xc[PRE:PRE + 128] = xc[PRE - 128:PRE]
        hmask = np.full((128, 1), 0.0 if j > 0 else NEG, f32)
        sel = np.zeros((128, 8), f32)
        for r in range(NCORES):
            if r // 4 == b and r < c:
                sel[:, r] = 1.0
        in_maps.append({"xc": xc, "w_in": w_in_p, "w_out": w_out_p, "gpre": gpre, "gpost": gpost, "convw": convw, "vec4": vec4,
                        "bda": bda, "bdx": bdx, "sinks": sinks, "biasg": biasg, "maskc": maskc, "hmask": hmask, "sel": sel, "flag": flag,
                        "ident": ident, "xs": np.concatenate([xs_all[c * SB:(c + 1) * SB], np.zeros((128 - SB, D), f32)], axis=0),
                        "sconv": np.ascontiguousarray(sconv_all[c * SB:(c + 1) * SB]), "srnn": np.ascontiguousarray(srnn_all[c * SB:(c + 1) * SB]),
                        "ck": padrows(ck_all[c * SB:(c + 1) * SB]), "cv": padrows(cv_all[c * SB:(c + 1) * SB]),
                        "rowp": rowp, "biass": biass})

    if "nc" not in _NC_CACHE:
        _NC_CACHE["nc"] = build_program()
    nc = _NC_CACHE["nc"]
    res = run_bass_kernel_spmd(nc, in_maps, core_ids=list(range(NCORES)))
    R = res.results

    y_prompt = np.stack([np.concatenate([R[b * 4 + j]["y"] for j in range(4)], axis=0) for b in range(2)], axis=0)
    new_conv_p = np.stack([R[3]["nconv"], R[7]["nconv"]])[None]
    new_rnn_p = np.stack([R[3]["nrnn"], R[7]["nrnn"]])[None]
    new_k_p = np.stack([R[3]["nk"].reshape(128, 2, 64), R[7]["nk"].reshape(128, 2, 64)])[None]
    new_v_p = np.stack([R[3]["nv"].reshape(128, 2, 64), R[7]["nv"].reshape(128, 2, 64)])[None]
    cat = lambda k: np.concatenate([R[c][k] for c in range(NCORES)], axis=0)
    y_sample = cat("ys").reshape(128, 1, D).astype(f32)
    new_conv_s = cat("nconvs").reshape(1, 128, 3, 512).astype(f32)
    new_rnn_s = cat("nrnns").reshape(1, 128, 512).astype(f32)
    new_k_s = cat("nks").reshape(1, 128, 128, 2, 64).astype(f32)
    new_v_s = cat("nvs").reshape(1, 128, 128, 2, 64).astype(f32)
    return (y_prompt.astype(f32), y_sample, new_conv_p.astype(f32), new_rnn_p.astype(f32), new_k_p.astype(f32), new_v_p.astype(f32),
            new_conv_s, new_rnn_s, new_k_s, new_v_s)
```

```python
import contextlib
import numpy as np
import concourse.bass as bass
import concourse.mybir as mybir
from concourse.bass_utils import run_bass_kernel_spmd

F32 = mybir.dt.float32
BF16 = mybir.dt.bfloat16
ALU = mybir.AluOpType
AF = mybir.ActivationFunctionType

NCORES = 8
D = 1024
D_IN = 2304
SEQ = 8192
CH = 2048
NT = 16
EPS = 1e-6
NEG = -1e30
LPOS = [0, 4, 1, 5, 2, 6, 3, 7]
C_XR, C_GR, C_Q, C_K, C_V, C_GA = 0, 512, 1024, 1536, 1664, 1792
SB = 16
PRE = 3 * CH
NPT = PRE // 128

ENGS = ("sync", "scalar", "vector", "gpsimd", "tensor")
CC_INC = 1
GROUPS = {"setup": 11, "W1": 9, "W2": 8, "W3": 8, "samp": 18, "samp2": 3, "outs": 14}
GROUPS_SEEN = {}
STAGE = 99
SUB = 99
NBLK = 99
NOOUT = 0
NOLAST = 0
SS = 99
DM = 255
S1 = 99


class Prog:
    def __init__(self, nc, es):
        self.nc = nc
        self.es = es
        self.q = {e: [] for e in ENGS}
        self.sig = {e: 0 for e in ENGS}
        self.pending = {e: False for e in ENGS}
        self.waited = {}
        self.bufs = {}
        self.dma_cnt = {}
        self.grp_seen = {}
        self.sems = {}

    def sem(self, key):
        if key not in self.sems:
            name = "s_" + "_".join(str(k) for k in key)
            self.sems[key] = self.es.enter_context(self.nc.semaphore(name))
        return self.sems[key]

    def _deps(self, eng, reads, writes, extra, skip_key=None):
        deps = set(extra)
        for r in reads:
            b = self.bufs.get(r)
            if b and b["w"] is not None:
                deps.add(b["w"])
            if b and r.startswith("pb"):
                deps.update(h for h in b["r"] if h[1] != eng)
        for w in writes:
            b = self.bufs.get(w)
            if b:
                if b["w"] is not None:
                    deps.add(b["w"])
                deps.update(b["r"])
        best = {}
        for d in deps:
            if d is None:
                continue
            key = d[:2]
            if eng == "tensor" and key == ("eng", "tensor"):
                continue
            if key == skip_key:
                continue
            if d[2] > best.get(key, 0):
                best[key] = d[2]
        waits = []
        for key, val in best.items():
            if self.waited.get((eng, key), 0) >= val:
                continue
            self.waited[(eng, key)] = val
            waits.append((key, val))
        return waits

    def _track(self, h, reads, writes):
        for r in reads:
            b = self.bufs.setdefault(r, {"w": None, "r": []})
            b["r"].append(h)
        for w in writes:
            self.bufs[w] = {"w": h, "r": []}

    def op(self, eng, fn, reads=(), writes=(), signal=True, deps=()):
        waits = self._deps(eng, reads, writes, deps)
        if signal:
            self.sig[eng] += 1
            h = ("eng", eng, self.sig[eng])
            self.pending[eng] = False
        else:
            h = ("eng", eng, self.sig[eng] + 1)
            self.pending[eng] = True
        self._track(h, reads, writes)
        semw = [(self.sem(k), v) for k, v in waits]
        mysem = self.sem(("eng", eng)) if signal else None

        def emit(e):
            for s, v in semw:
                e.wait_ge(s, v)
            ins = fn(e)
            if mysem is not None:
                ins.then_inc(mysem, 1)

        self.q[eng].append(emit)
        return h

    def dma(self, eng, out, in_, slot, reads=(), writes=(), deps=(), group=None, **kw):
        waits = self._deps(eng, reads, writes, deps, skip_key=(("dma", group) if group is not None else None))
        if group is not None:
            slot = group
            self.grp_seen[group] = self.grp_seen.get(group, 0) + 1
            cnt = 16 * GROUPS[group]
        else:
            cnt = self.dma_cnt.get(slot, 0) + 16
        self.dma_cnt[slot] = cnt
        h = ("dma", slot, cnt)
        self._track(h, reads, writes)
        semw = [(self.sem(k), v) for k, v in waits]
        mysem = self.sem(("dma", slot))

        def emit(e):
            for s, v in semw:
                e.wait_ge(s, v)
            e.dma_start(out=out, in_=in_, **kw).then_inc(mysem, 16)

        self.q[eng].append(emit)
        return h

    def cc(self, eng, fn, reads=(), writes=(), inc=None):
        inc = CC_INC if inc is None else inc
        waits = self._deps(eng, reads, writes, ())
        cnt = self.dma_cnt.get("cc", 0) + inc
        self.dma_cnt["cc"] = cnt
        h = ("dma", "cc", cnt)
        self._track(h, reads, writes)
        semw = [(self.sem(k), v) for k, v in waits]
        mysem = self.sem(("dma", "cc"))

        def emit(e):
            for s, v in semw:
                e.wait_ge(s, v)
            fn(e).then_inc(mysem, 1)

        self.q[eng].append(emit)
        return h

    def wait_all(self, eng, handles):
        waits = self._deps(eng, (), (), handles)
        semw = [(self.sem(k), v) for k, v in waits]

        def emit(e):
            for s, v in semw:
                e.wait_ge(s, v)

        self.q[eng].append(emit)

    def emit(self):
        assert not self.pending["tensor"], "PE has unsignalled trailing ops"
        GROUPS_SEEN.clear()
        GROUPS_SEEN.update(self.grp_seen)
        with self.nc.Block() as block:
            @block.sync
            def _(e):
                for f in self.q["sync"]:
                    f(e)

            @block.scalar
            def _(e):
                for f in self.q["scalar"]:
                    f(e)

            @block.vector
            def _(e):
                for f in self.q["vector"]:
                    f(e)

            @block.gpsimd
            def _(e):
                for f in self.q["gpsimd"]:
                    f(e)

            @block.tensor
            def _(e):
                for f in self.q["tensor"]:
                    f(e)


def build_program():
    nc = _build_program()
    if any(GROUPS.get(g) != n for g, n in GROUPS_SEEN.items()):
        GROUPS.update(GROUPS_SEEN)
        nc = _build_program()
        assert all(GROUPS.get(g) == n for g, n in GROUPS_SEEN.items())
    return nc


def _build_program():
    nc = bass.Bass("TRN2", target_bir_lowering=False)

    def din(name, shape):
        return nc.dram_tensor(name, list(shape), F32, kind="ExternalInput").ap()

    def dout(name, shape):
        return nc.dram_tensor(name, list(shape), F32, kind="ExternalOutput").ap()

    xc_d = din("xc", [PRE + CH + 128, D])
    flag_d = din("flag", [128, 3])
    w_in_d = din("w_in", [D, D_IN])
    w_out_d = din("w_out", [D, D])
    gpre_d = din("gpre", [128, 8])
    gpost_d = din("gpost", [128, D])
    convw_d = din("convw", [128, 16])
    vec4_d = din("vec4", [128, 16])
    bda_d = din("bda", [128, 512])
    bdx_d = din("bdx", [128, 512])
    sinks_d = din("sinks", [128, 8])
    biasg_d = din("biasg", [128, 2048])
    maskc_d = din("maskc", [128, 256])
    hmask_d = din("hmask", [128, 1])
    sel_d = din("sel", [128, 8])
    ident_d = din("ident", [128, 128])

    xs_d = din("xs", [128, D])
    sconv_d = din("sconv", [SB, 1536])
    srnn_d = din("srnn", [SB, 512])
    ck_d = din("ck", [SB * 128 + 128, 128])
    cv_d = din("cv", [SB * 128 + 128, 128])
    rowp_d = din("rowp", [SB, 4096])
    biass_d = din("biass", [128, 8])

    y_d = dout("y", [CH, D])
    ys_d = dout("ys", [SB, D])
    nconvs_d = dout("nconvs", [SB, 1536])
    nrnns_d = dout("nrnns", [SB, 512])
    nks_d = dout("nks", [SB, 128, 128])
    nvs_d = dout("nvs", [SB, 128, 128])
    nconv_d = dout("nconv", [3, 512])
    nrnn_d = dout("nrnn", [512])
    nk_d = dout("nk", [128, 128])
    nv_d = dout("nv", [128, 128])

    bounce = nc.dram_tensor("bounce", [128, 8], F32)
    gath = nc.dram_tensor("gath", [NCORES * 128, 8], F32)
    kvb = nc.dram_tensor("kvb", [SB, 256], F32)

    out_handles = []

    with contextlib.ExitStack() as es:
        P = Prog(nc, es)
        def sb(name, shape, dt=F32):
            return es.enter_context(nc.sbuf_tensor("sb_" + name, list(shape), dt))

        Win = sb("Win", [128, 8, D_IN], BF16)
        Wout = sb("Wout", [128, 8, D], BF16)
        P1 = sb("P1", [128, 4, CH], BF16)
        P2 = sb("P2", [128, 4, CH], BF16)
        attT = sb("attT", [128, 4, CH], BF16)
        KT = sb("KT", [128, 128 + 512], BF16)
        Vg = sb("Vg", [128, 5, 132], BF16)
        xt = [sb(f"xt{i}", [128, D]) for i in range(2)]
        xn = [sb(f"xn{i}", [128, D], BF16) for i in range(2)]
        xnT = sb("xnT", [128, 8, 512], BF16)
        xr = sb("xr", [128, 4, 515], BF16)
        xrt = sb("xrt", [128, 4, 4])
        xcv = sb("xcv", [128, 2, 512])
        u_b = sb("u_b", [128, 2, 512])
        a_b = sb("a_b", [128, 2, 512])
        a2_b = sb("a2_b", [128, 2, 512])
        t1_b = sb("t1_b", [128, 2, 512])
        th = [sb(f"th{i}", [128, 512]) for i in range(2)]
        hl = sb("hl", [128, 512])
        Ac = sb("Ac", [128, 512])
        QT = [sb(f"QT{g}", [128, 4, 512], BF16) for g in range(2)]
        BiasT = sb("BiasT", [128, 2, 2, 512], BF16)
        BiasF = sb("BiasF", [128, 2, 512], BF16)
        PT = [sb(f"PT{i}", [128, 2, 2, 512], BF16) for i in range(2)]
        ua = sb("ua", [128, 512])
        sgr = sb("sgr", [128, 512])
        att_o = sb("att_o", [128, 512], BF16)
        mixT = [sb(f"mixT{i}", [128, 4, 128], BF16) for i in range(2)]
        tt = sb("tt", [128, D])
        gpost = sb("gpost", [128, D])
        junk2 = sb("junk2", [128, 512], BF16)
        bda = sb("bda", [128, 4, 128])
        bdx = sb("bdx", [128, 4, 128])
        diagw = sb("diagw", [128, 4, 4, 128], BF16)
        ident = sb("ident", [128, 128], BF16)
        maskc = sb("maskc", [128, 2, 128])
        gpre = sb("gpre", [128, 8])
        convw = sb("convw", [128, 4, 4])
        vec4 = sb("vec4", [128, 4, 4])
        hb = sb("hb", [128, 2, 4])
        chalf = sb("chalf", [128, 4])
        sinks = sb("sinks", [128, 2, 4])
        sinkexp2 = sb("sinkexp2", [128, 2, 4])
        hmask = sb("hmask", [128, 1])
        sel = sb("sel", [128, 8])
        cst = sb("cst", [128, 4])
        ss = sb("ss", [128, 17 + NPT])
        ms = sb("ms", [128, 17 + NPT])
        rstd = sb("rstd", [128, 17 + NPT])
        flag = sb("flag", [128, 3])
        ss2 = sb("ss2", [128, NT, 2])
        ms2 = sb("ms2", [128, NT])
        rstd2 = sb("rstd2", [128, NT])
        car = sb("car", [128, 8])
        dsum = sb("dsum", [128, 4, 2])
        rden = sb("rden", [128, 4, 2])
        G_sb = sb("G_sb", [128, 8, 8])
        Ap = sb("Ap", [128, 8, 4])
        Bp = sb("Bp", [128, 8, 4])
        hscan = sb("hscan", [128, 4, 8])
        h0 = sb("h0", [128, 4])
        hfin = sb("hfin", [128, 4])
        kout = sb("kout", [128, 128])
        vout = sb("vout", [128, 128])
        sth = sb("sth", [128, 8])
        ident32 = sb("ident32", [128, 128])
        ones32 = sb("ones32", [128, 128])
        biass = sb("biass", [128, 8])
        uaT = sb("uaT", [128, 4, SB])
        sm = sb("sm", [128, 16])

        pb = [es.enter_context(nc.psum_tensor(f"pb{i}", [128, 512], F32)) for i in range(8)]
        pbT = [p[:].bitcast(BF16).rearrange("p (k c) -> p k c", c=128) for p in pb]
        bank_ctr = [0]

        def bank():
            i = bank_ctr[0]
            bank_ctr[0] = (i + 1) % 8
            return i

        def mk(eng):
            def f(name, *args, r=(), w=(), sig=True, **kw):
                return P.op(eng, lambda e: getattr(e, name)(*args, **kw), reads=r, writes=w, signal=sig)
            return f
        V, A, G = mk("vector"), mk("scalar"), mk("gpsimd")
        _pe = mk("tensor")

        def PE(name, *args, r=(), w=(), sig=False, **kw):
            return _pe(name, *args, r=r, w=w, sig=sig, **kw)

        def ld(dst, src, name, group="setup"):
            P.dma("sync", dst, src, name, writes=[name], group=group)

        ld(gpre[:], gpre_d, "gpre")
        ld(convw[:].rearrange("p g t -> p (g t)"), convw_d, "convw")
        ld(vec4[:].rearrange("p a g -> p (a g)"), vec4_d, "vec4")
        ld(hmask[:], hmask_d, "hmask")
        ld(sel[:], sel_d, "sel")
        ld(sinks[:].rearrange("p g c -> p (g c)"), sinks_d, "sinks")
        ld(maskc[:].rearrange("p b q -> p (b q)"), maskc_d, "maskc")
        ld(bda[:].rearrange("p g m -> p (g m)"), bda_d, "bda")
        ld(bdx[:].rearrange("p g m -> p (g m)"), bdx_d, "bdx")
        ld(gpost[:], gpost_d, "gpost")
        ld(xt[0][:], biasg_d[:, 0:1024], "xt0", group=None)
        ld(xt[1][:], biasg_d[:, 1024:2048], "xt1", group=None)
        P.dma("gpsimd", ident[:], ident_d, "ident", writes=["ident"], group="W1")
        for k in range(8):
            for h in range(2):
                P.dma("gpsimd", Win[:, k, h * 1152:(h + 1) * 1152], w_in_d[k * 128:(k + 1) * 128, h * 1152:(h + 1) * 1152],
                      f"Win{k}_{h}", writes=[f"Win{k}_{h}"], group=("W1" if h == 0 else "W2"))
        for k in range(8):
            P.dma("gpsimd", Wout[:, k, :], w_out_d[k * 128:(k + 1) * 128, :], f"Wout{k}", writes=[f"Wout{k}"], group="W3")

        G("memset", cst[:, 0:1], -0.5, w=["cst"])
        G("memset", cst[:, 1:2], 0.0, r=["cst"], w=["cst"])
        G("memset", cst[:, 2:3], 1.0 / 16.0, r=["cst"], w=["cst"])
        G("memset", Vg[:], 1.0, w=["Vg"])
        G("memset", QT[0][64:128], 0.0, w=["QT0"])
        G("memset", QT[1][0:64], 0.0, w=["QT1"])
        G("memset", ones32[:], 1.0, w=["ones32"])
        A("activation", out=sth[:, 0:4], in_=vec4[:, 3, :], func=AF.Exp, scale=-1.0, r=["vec4"], w=["sth"])
        A("activation", out=sth[:, 4:8], in_=sth[:, 0:4], func=AF.Ln, bias=1.0, r=["sth"], w=["sth"])
        V("tensor_scalar", out=chalf[:], in0=sth[:, 4:8], scalar1=-4.0, scalar2=None, op0=ALU.mult, r=["sth"], w=["chalf"])
        V("tensor_scalar", out=hb[:], in0=vec4[:, 1:3, :], scalar1=0.5, scalar2=None, op0=ALU.mult, r=["vec4"], w=["hb"])
        A("activation", out=sinkexp2[:], in_=sinks[:], func=AF.Exp, r=["sinks"], w=["sinkexp2"])
        V("tensor_scalar", out=sinkexp2[:], in0=sinkexp2[:], scalar1=2.0, scalar2=None, op0=ALU.mult, r=["sinkexp2"], w=["sinkexp2"])
        for blk in range(2):
            V("tensor_tensor", out=BiasT[:, blk].rearrange("p g (c q) -> p (g c) q", q=128),
                                                 in0=xt[blk][:].rearrange("p (h q) -> p h q", q=128),
                                                 in1=maskc[:, blk, :].unsqueeze(1).broadcast_to([128, 8, 128]), op=ALU.add,
              r=[f"xt{blk}", "maskc"], w=["BiasT"])
        V("tensor_tensor", out=xt[0][:].rearrange("p (h q) -> p h q", q=128), in0=xt[0][:].rearrange("p (h q) -> p h q", q=128),
                                    in1=maskc[:, 0, :].unsqueeze(1).broadcast_to([128, 8, 128]), op=ALU.add,
          r=["xt0", "maskc"], w=["xt0"])
        V("tensor_scalar", out=BiasF[:].rearrange("p g c -> p (g c)"), in0=xt[0][:], scalar1=hmask[:, 0:1], scalar2=None, op0=ALU.add,
          r=["xt0", "hmask"], w=["BiasF"])
        for g in range(4):
            for tap in range(4):
                V("tensor_scalar", out=diagw[:, g, tap, :], in0=ident[:], scalar1=convw[:, g, tap:tap + 1], scalar2=None, op0=ALU.mult,
                  r=["ident", "convw"], w=["diagw"])

        def win_names(c0, c1):
            hs = sorted({c0 // 1152, (c1 - 1) // 1152})
            return hs

        def tile_front(t, j):
            s = t % 2
            row = PRE + t * 128 if t < 17 else (t - 17) * 128
            P.dma("sync", xt[s][:], xc_d[row:row + 128, :], f"xt{s}", writes=[f"xt{s}"])
            A("activation", out=xn[s][:], in_=xt[s][:], func=AF.Square, accum_out=ss[:, t:t + 1],
              r=[f"xt{s}"], w=[f"xn{s}", f"ss{t}"])
            G("tensor_scalar", out=ms[:, t:t + 1], in0=ss[:, t:t + 1], scalar1=1.0 / D, scalar2=EPS, op0=ALU.mult, op1=ALU.add,
              r=[f"ss{t}"], w=[f"ms{t}"])
            G("tensor_tensor", out=rstd[:, t:t + 1], in0=ms[:, t:t + 1], in1=cst[:, 0:1], op=ALU.pow,
              r=[f"ms{t}", "cst"], w=[f"rstd{t}"])
            G("tensor_scalar", out=xn[s][:], in0=xt[s][:], scalar1=rstd[:, t:t + 1], scalar2=0.0, op0=ALU.mult, op1=ALU.add,
              r=[f"xt{s}", f"rstd{t}"], w=[f"xn{s}"])
            b = bank()
            for k in range(8):
                PE("transpose", out=pbT[b][:, k, :], in_=xn[s][:, k * 128:(k + 1) * 128], identity=ident[:],
                   r=[f"xn{s}", "ident"], w=[f"pb{b}"], sig=(k == 7))
            V("tensor_tensor", out=xnT[:, :, j * 128:(j + 1) * 128], in0=pbT[b][:, :, :],
                                        in1=gpre[:].unsqueeze(2).broadcast_to([128, 8, 128]), op=ALU.mult,
              r=[f"pb{b}", "gpre"], w=["xnT"])

        def fm_chunk(c0, N):
            b = bank()
            h = c0 // 1152
            for k in range(8):
                PE("matmul", pb[b][:, 0:N], lhsT=Win[:, k, c0:c0 + 128], rhs=xnT[:, k, 0:N], start=(k == 0), stop=(k == 7),
                   r=[f"Win{k}_{h}", "xnT"], w=[f"pb{b}"], sig=(k == 7))
            return b

        thc = [0]

        def th_slot():
            thc[0] ^= 1
            return thc[0]

        def rnn_chain(N, t0, pre=False):
            for gp in range(2):
                gs = (2 * gp, 2 * gp + 1)
                for gi, g in enumerate(gs):
                    b = bank()
                    for tap in range(4):
                        PE("matmul", pb[b][:, 0:N], lhsT=diagw[:, g, tap, :], rhs=xr[:, g, tap:tap + N],
                                                                 start=(tap == 0), stop=(tap == 3),
                           r=["diagw", "xr"], w=[f"pb{b}"], sig=(tap == 3))
                    A("activation", out=xcv[:, gi, 0:N], in_=pb[b][:, 0:N], func=AF.Identity, bias=vec4[:, 0, g:g + 1],
                      r=[f"pb{b}", "vec4"], w=[f"xcv{gi}"])
                    if pre:
                        continue
                    b = fm_chunk(C_GR + g * 128, N)
                    s = th_slot()
                    A("activation", out=th[s][:, 0:N], in_=pb[b][:, 0:N], func=AF.Tanh, scale=0.5, r=[f"pb{b}"], w=[f"th{s}"])
                    V("scalar_tensor_tensor", out=u_b[:, gi, 0:N], in0=th[s][:, 0:N], scalar=1.0, in1=pb[b][:, 0:N],
                                                                        op0=ALU.add, op1=ALU.mult,
                      r=[f"pb{b}", f"th{s}"], w=[f"u{gi}"])
                for gi, g in enumerate(gs):
                    bA = bank()
                    PE("matmul", pb[bA][:, 0:N], lhsT=bda[:, g, :], rhs=xcv[:, gi, 0:N], start=True, stop=True,
                       r=["bda", f"xcv{gi}"], w=[f"pb{bA}"], sig=True)
                    bX = bank()
                    PE("matmul", pb[bX][:, 0:N], lhsT=bdx[:, g, :], rhs=xcv[:, gi, 0:N], start=True, stop=True,
                       r=["bdx", f"xcv{gi}"], w=[f"pb{bX}"], sig=True)
                    s = th_slot()
                    A("activation", out=th[s][:, 0:N], in_=pb[bA][:, 0:N], func=AF.Tanh, scale=0.5, bias=hb[:, 0, g:g + 1],
                      r=[f"pb{bA}", "hb"], w=[f"th{s}"])
                    A("activation", out=a_b[:, gi, 0:N], in_=th[s][:, 0:N], func=AF.Exp, scale=chalf[:, g:g + 1], bias=chalf[:, g:g + 1],
                      r=[f"th{s}", "chalf"], w=[f"a{gi}"])
                    G("tensor_tensor", out=a2_b[:, gi, 0:N], in0=a_b[:, gi, 0:N], in1=a_b[:, gi, 0:N], op=ALU.mult,
                      r=[f"a{gi}"], w=[f"a2{gi}"])
                    s2 = th_slot()
                    A("activation", out=th[s2][:, 0:N], in_=pb[bX][:, 0:N], func=AF.Tanh, scale=0.5, bias=hb[:, 1, g:g + 1],
                      r=[f"pb{bX}", "hb"], w=[f"th{s2}"])
                    V("scalar_tensor_tensor", out=t1_b[:, gi, 0:N], in0=th[s2][:, 0:N], scalar=1.0, in1=xcv[:, gi, 0:N],
                                                                     op0=ALU.add, op1=ALU.mult,
                      r=[f"th{s2}", f"xcv{gi}"], w=[f"t1{gi}"])
                rnn_tail.append((gs, N, t0, pre))
                if SUB >= 4:
                    flush_rnn_tail()
                else:
                    rnn_tail.clear()

        front_done = set()

        def front(t0, nt):
            if (t0, nt) in front_done:
                return
            front_done.add((t0, nt))
            for j in range(nt):
                tile_front(t0 + j, j)

        def block(bi, t0, nt, pre=False, nxt=None):
            N = nt * 128
            halo = (bi == 0)
            last = (bi == 4) and not NOLAST
            front(t0, nt)
            if not halo and Nprev[0] > 0:
                npv = Nprev[0]
                V("tensor_copy", out=xr[:, :, 0:3], in_=xr[:, :, npv:npv + 3], r=["xr"], w=["xr"])
            for g in range(4):
                b = fm_chunk(C_XR + g * 128, N)
                A("activation", out=xr[:, g, 3:3 + N], in_=pb[b][:, 0:N], func=AF.Copy, r=[f"pb{b}"], w=["xr"])
                if last:
                    A("activation", out=xrt[:, g, :], in_=pb[b][:, N - 4:N], func=AF.Copy, r=[f"pb{b}"], w=["xrt"])
            Nprev[0] = N
            if pre:
                if nxt is not None:
                    front(*nxt)
                rnn_chain(N, t0, pre=True)
                return
            b = fm_chunk(C_K, N)
            if not halo:
                kpv, vpv = KTprev[0], Vprev[0]
                if kpv > 0:
                    V("tensor_copy", out=KT[:, 0:128], in_=KT[:, kpv:kpv + 128], r=["KT"], w=["KT"])
                    V("tensor_copy", out=Vg[:, 0, :], in_=Vg[:, vpv, :], r=["Vg"], w=["Vg"])
                A("activation", out=KT[:, 128:128 + N], in_=pb[b][:, 0:N], func=AF.Copy, r=[f"pb{b}"], w=["KT"])
                KTprev[0] = N
                Vprev[0] = nt
            else:
                A("activation", out=KT[:, 0:128], in_=pb[b][:, 0:N], func=AF.Copy, r=[f"pb{b}"], w=["KT"])
                KTprev[0] = 0
                Vprev[0] = 0
            for j in range(nt):
                t = t0 + j
                vs = 0 if halo else j + 1
                bV = bank()
                for k in range(8):
                    PE("matmul", pb[bV][:, 0:128], lhsT=xnT[:, k, j * 128:(j + 1) * 128], rhs=Win[:, k, C_V:C_V + 128],
                                                    start=(k == 0), stop=(k == 7),
                       r=[f"Win{k}_1", "xnT"], w=[f"pb{bV}"], sig=(k == 7))
                V("tensor_copy", out=Vg[:, vs, :].rearrange("p (g e) -> p g e", e=66)[:, :, 0:64],
                                                 in_=pb[bV][:, 0:128].rearrange("p (g d) -> p g d", d=64),
                  r=[f"pb{bV}"], w=["Vg"])
                if t == NT and not NOLAST:
                    A("activation", out=vout[:], in_=pb[bV][:, 0:128], func=AF.Copy, r=[f"pb{bV}"], w=["vout"])
                    bK = bank()
                    for k in range(8):
                        PE("matmul", pb[bK][:, 0:128], lhsT=xnT[:, k, j * 128:(j + 1) * 128], rhs=Win[:, k, C_K:C_K + 128],
                                                        start=(k == 0), stop=(k == 7),
                           r=[f"Win{k}_1", "xnT"], w=[f"pb{bK}"], sig=(k == 7))
                    A("activation", out=kout[:], in_=pb[bK][:, 0:128], func=AF.Copy, r=[f"pb{bK}"], w=["kout"])
            if halo or SUB < 2:
                return
            for cc in range(4):
                b = fm_chunk(C_Q + cc * 128, N)
                V("tensor_scalar", out=QT[0][0:64, cc, 0:N], in0=pb[b][0:64, 0:N], scalar1=0.125, scalar2=None, op0=ALU.mult,
                  r=[f"pb{b}"], w=["QT0"])
                V("tensor_scalar", out=QT[1][64:128, cc, 0:N], in0=pb[b][64:128, 0:N], scalar1=0.125, scalar2=None, op0=ALU.mult,
                  r=[f"pb{b}"], w=["QT1"])
            if SUB < 3:
                return
            rnn_chain(N, t0)
            if SUB < 5:
                return
            for j in range(nt):
                t = t0 + j
                bG = bank()
                for k in range(8):
                    PE("matmul", pb[bG][:, 0:512], lhsT=xnT[:, k, j * 128:(j + 1) * 128], rhs=Win[:, k, C_GA:C_GA + 512],
                                                    start=(k == 0), stop=(k == 7),
                       r=[f"Win{k}_1", "xnT"], w=[f"pb{bG}"], sig=(k == 7))
                s = th_slot()
                A("activation", out=th[s][:], in_=pb[bG][:], func=AF.Tanh, scale=0.5, r=[f"pb{bG}"], w=[f"th{s}"])
                V("scalar_tensor_tensor", out=ua[:], in0=th[s][:], scalar=1.0, in1=pb[bG][:], op0=ALU.add, op1=ALU.mult,
                  r=[f"pb{bG}", f"th{s}"], w=["ua"])
                if SUB >= 6:
                    attention(t, j)

        def attention(t, j):
            ps = t % 2
            for blk in range(2):
                kc = (j + blk) * 128
                for g in range(2):
                    b = bank()
                    bias_ap = BiasF[:, g, :] if (t == 1 and blk == 0) else BiasT[:, blk, g, :]
                    bname = "BiasF" if (t == 1 and blk == 0) else "BiasT"
                    PE("matmul", pb[b][:].rearrange("p (c q) -> p c q", q=128), lhsT=KT[:, kc:kc + 128],
                                                           rhs=QT[g][:, :, j * 128:(j + 1) * 128], start=True, stop=False,
                       r=["KT", f"QT{g}"], w=[f"pb{b}"])
                    PE("matmul", pb[b][:], lhsT=ident[:], rhs=bias_ap, start=False, stop=True,
                       r=["ident", bname], w=[f"pb{b}"], sig=True)
                    A("activation", out=PT[ps][:, blk, g, :], in_=pb[b][:], func=AF.Exp, r=[f"pb{b}"], w=[f"PT{ps}"])
            if SUB < 7:
                return
            bO = []
            for g in range(2):
                b = bank()
                bO.append(b)
                for cc in range(4):
                    for blk in range(2):
                        PE("matmul", pb[b][:, cc * 66:cc * 66 + 66], lhsT=PT[ps][:, blk, g, cc * 128:(cc + 1) * 128],
                                                                        rhs=Vg[:, j + blk, g * 66:(g + 1) * 66], start=(blk == 0), stop=(blk == 1),
                           r=[f"PT{ps}", "Vg"], w=[f"pb{b}"], sig=(cc == 3 and blk == 1))
                V("scalar_tensor_tensor", out=dsum[:, :, g], in0=pb[b][:, 0:264].rearrange("p (c e) -> p c e", e=66)[:, :, 64],
                                                             scalar=2.0, in1=sinkexp2[:, g, :], op0=ALU.mult, op1=ALU.add,
                  r=[f"pb{b}", "sinkexp2"], w=["dsum"])
            if SUB < 8:
                return
            V("reciprocal", out=rden[:], in_=dsum[:], r=["dsum"], w=["rden"])
            G("tensor_tensor", out=sgr[:].rearrange("p (c g d) -> p c g d", g=2, d=64), in0=ua[:].rearrange("p (c g d) -> p c g d", g=2, d=64),
                                        in1=rden[:].unsqueeze(3).broadcast_to([128, 4, 2, 64]), op=ALU.mult,
              r=["ua", "rden"], w=["sgr"])
            for g in range(2):
                b = bO[g]
                V("tensor_tensor", out=att_o[:].rearrange("p (c g d) -> p c g d", g=2, d=64)[:, :, g, :],
                                                      in0=pb[b][:, 0:264].rearrange("p (c e) -> p c e", e=66)[:, :, 0:64],
                                                      in1=sgr[:].rearrange("p (c g d) -> p c g d", g=2, d=64)[:, :, g, :], op=ALU.mult,
                  r=[f"pb{b}", "sgr"], w=["att_o"])
            b = bank()
            for cc in range(4):
                PE("transpose", out=pbT[b][:, cc, :], in_=att_o[:, cc * 128:(cc + 1) * 128], identity=ident[:],
                   r=["att_o", "ident"], w=[f"pb{b}"], sig=(cc == 3))
            A("activation", out=attT[:, :, (t - 1) * 128:t * 128], in_=pbT[b][:, 0:4, :], func=AF.Copy, r=[f"pb{b}"], w=[f"attT{t}"])

        rnn_tail = []
        first_scan = [True]
        first_A = [True]

        def flush_rnn_tail():
            for gs, N, t0, pre in rnn_tail:
                for gi, g in enumerate(gs):
                    A("activation", out=a2_b[:, gi, 0:N], in_=a2_b[:, gi, 0:N], func=AF.Sqrt, scale=-1.0 / 16.0, bias=1.0 / 16.0,
                      r=[f"a2{gi}"], w=[f"a2{gi}"])
            for gs, N, t0, pre in rnn_tail:
                c0 = (t0 - 1) * 128
                for gi, g in enumerate(gs):
                    G("tensor_tensor", out=t1_b[:, gi, 0:N], in0=t1_b[:, gi, 0:N], in1=a2_b[:, gi, 0:N], op=ALU.mult,
                      r=[f"t1{gi}", f"a2{gi}"], w=[f"t1{gi}"])
                    fs = first_scan[0]
                    fsA = first_A[0]
                    V("tensor_tensor_scan", out=hl[:, 0:N], data0=a_b[:, gi, 0:N], data1=t1_b[:, gi, 0:N],
                                                                              initial=(0.0 if fs else car[:, 4 + g:5 + g]), op0=ALU.mult, op1=ALU.add,
                      r=[f"a{gi}", f"t1{gi}", "car"], w=["hl"])
                    V("tensor_copy", out=car[:, 4 + g:5 + g], in_=hl[:, N - 1:N], r=["hl", "car"], w=["car"])
                    if pre:
                        continue
                    V("tensor_tensor_scan", out=Ac[:, 0:N], data0=a_b[:, gi, 0:N], data1=cst[:, 1:2].broadcast_to([128, N]),
                                                                              initial=(1.0 if fsA else car[:, g:g + 1]), op0=ALU.mult, op1=ALU.add,
                      r=[f"a{gi}", "cst", "car"], w=["Ac"])
                    V("tensor_copy", out=car[:, g:g + 1], in_=Ac[:, N - 1:N], r=["Ac", "car"], w=["car"])
                    G("tensor_tensor", out=P1[:, g, c0:c0 + N], in0=hl[:, 0:N], in1=u_b[:, gi, 0:N], op=ALU.mult,
                      r=["hl", f"u{gi}"], w=[f"P1_{g}_{c0}"])
                    G("tensor_tensor", out=P2[:, g, c0:c0 + N], in0=Ac[:, 0:N], in1=u_b[:, gi, 0:N], op=ALU.mult,
                      r=["Ac", f"u{gi}"], w=[f"P2_{g}_{c0}"])
                if gs[1] == 3:
                    first_scan[0] = False
                    if not pre:
                        first_A[0] = False
            rnn_tail.clear()

        Nprev = [0]
        KTprev = [0]
        Vprev = [0]
        G("memset", xr[:], 0.0, w=["xr"])
        ld(flag[:], flag_d, "flag")
        for pb_i in range(NPT // 4):
            nxt = (17 + 4 * (pb_i + 1), 4) if pb_i + 1 < NPT // 4 else (0, 1)
            block(100 + pb_i, 17 + 4 * pb_i, 4, pre=True, nxt=nxt)
            if pb_i % 4 == 3:
                kf = pb_i // 4
                V("tensor_scalar", out=car[:, 4:8], in0=car[:, 4:8], scalar1=flag[:, kf:kf + 1], scalar2=None, op0=ALU.mult,
                  r=["car", "flag"], w=["car"])
        Nprev[0] = 0
        blocks = [(0, 0, 1), (1, 1, 4), (2, 5, 4), (3, 9, 4), (4, 13, 4)]
        for bi, t0, nt in blocks:
            if STAGE >= (1 if bi == 0 else 2 if bi == 1 else 3) and bi <= NBLK:
                block(bi, t0, nt)

        if STAGE >= 3 and not NOOUT:
            for g in range(4):
                out_handles.append(P.dma("sync", nconv_d[:, g * 128:(g + 1) * 128].rearrange("t p -> p t"), xrt[:, g, 1:4], f"o_nconv{g}", reads=["xrt"],
                                         allow_slow_non_contiguous=True, group="outs"))
            out_handles.append(P.dma("sync", nk_d, kout[:], "o_nk", reads=["kout"], group="outs"))
            out_handles.append(P.dma("sync", nv_d, vout[:], "o_nv", reads=["vout"], group="outs"))

        if STAGE >= 4:
            G("memset", h0[:], 0.0, w=["h0"])
            V("tensor_scalar", out=hfin[:], in0=car[:, 4:8], scalar1=2.0, scalar2=None, op0=ALU.mult, r=["car"], w=["hfin"])
            out_handles.append(P.dma("sync", nrnn_d.rearrange("(g p) -> p g", p=128), hfin[:], "o_nrnn", reads=["hfin"],
                                     allow_slow_non_contiguous=True, group="outs"))

        def sample_path():
            F = lambda ap: ap.bitcast(F32)
            xnTs = QT[0][:].rearrange("p c n -> p (c n)")[:, 0:1024].rearrange("p (k t) -> p k t", t=128)
            Kn = F(xnT[:].rearrange("p k n -> p (k n)")).rearrange("p (s c) -> p s c", c=128)
            Vn = [F(PT[h][:].rearrange("p a b n -> p (a b n)")).rearrange("p (s c) -> p s c", c=128) for h in range(2)]
            KnT = [a_b[:].rearrange("p a n -> p (a n)").rearrange("p (s c) -> p s c", c=128),
                   a2_b[:].rearrange("p a n -> p (a n)").rearrange("p (s c) -> p s c", c=128)]
            xcT = u_b[:].rearrange("p a n -> p (a n)")[:, 0:512].rearrange("p (g t) -> p g t", t=128)
            Qm = th[0][:, 0:128].rearrange("p (s q) -> p s q", q=8)
            Pt = hl[:, 0:128]
            sc = Ac[:, 0:128]
            rds = Ac[:, 128:256]
            attn = hl[:, 128:256]
            xcs = t1_b[:, 0, :]
            tA = t1_b[:, 1, :]
            tB = xcv[:, 0, :]
            tC = xcv[:, 1, :]
            tD = ua[:]
            tE = sgr[:]
            mixtm = xn[0][:, 0:512]
            mixTs = xn[1][:].rearrange("p (k t) -> p k t", t=128)
            R16 = slice(0, SB)
            flat32 = lambda t, pat: F(t[:].rearrange(pat))
            hosts = [(xt[1][:], "xt1"), (flat32(QT[1], "p c n -> p (c n)"), "QT1"), (flat32(BiasT, "p a b n -> p (a b n)"), "BiasT"),
                     (flat32(diagw, "p g t m -> p (g t m)"), "diagw")]
            rp, rpn = [], []
            for hap, hname in hosts:
                for i2 in range(2):
                    rp.append(hap[R16, i2 * 512:(i2 + 1) * 512])
                    rpn.append(hname)
            for i8 in range(8):
                P.dma("sync", rp[i8], rowp_d[:, i8 * 512:(i8 + 1) * 512], f"rowp{i8}", writes=[rpn[i8]], group="samp")
            sst_t = [tt[R16, 0:512], tt[R16, 512:1024], F(BiasF[:].rearrange("p g n -> p (g n)"))[R16, 0:512]]
            sst_n = ["tt", "tt", "BiasF"]
            for t3 in range(3):
                P.dma("sync", sst_t[t3], sconv_d[:, t3 * 512:(t3 + 1) * 512], f"sst{t3}", writes=[sst_n[t3]], group="samp")
            hprev = th[1][R16, :]
            P.dma("sync", hprev, srnn_d, "hprev", writes=["th1"], group="samp")
            kv_s = F(att_o[:])[R16, :]
            ld(ident32[:], ident_d, "ident32", group="samp")
            ld(biass[:], biass_d, "biass", group="samp")
            def win(src, h):
                return src[1 + 8 * h * 128:1 + (8 * h + 8) * 128, :].rearrange("(s k) c -> k s c", k=128)
            big = [None]
            for h in range(2):
                big[0] = P.dma("sync", Kn[:, 8 * h:8 * h + 8, :], win(ck_d, h), f"Kn_a{h}", writes=["xnT", f"KnH{h}"], deps=[big[0]])
                big[0] = P.dma("sync", Vn[h][:], win(cv_d, h), f"Vn_a{h}", writes=[f"PT{h}"], deps=[big[0]])
            if SS < 1:
                return
            if S1 < 1:
                return
            pass
            if S1 < 2:
                return
            P.dma("sync", xt[0][:], xs_d, "xt0", writes=["xt0"])
            if S1 < 3:
                return
            pass
            if S1 < 4:
                return
            A("activation", out=xn[0][:], in_=xt[0][:], func=AF.Square, accum_out=sm[:, 0:1], r=["xt0", "sm"], w=["xn0", "sm"])
            if S1 < 5:
                return
            V("tensor_scalar", out=sm[:, 1:2], in0=sm[:, 0:1], scalar1=1.0 / D, scalar2=EPS, op0=ALU.mult, op1=ALU.add, r=["sm"], w=["sm"])
            if S1 < 6:
                return
            A("activation", out=sm[:, 1:2], in_=sm[:, 1:2], func=AF.Sqrt, r=["sm"], w=["sm"])
            if S1 < 7:
                return
            V("reciprocal", out=sm[:, 2:3], in_=sm[:, 1:2], r=["sm"], w=["sm"])
            if S1 < 8:
                return
            A("activation", out=xn[0][:], in_=xt[0][:], func=AF.Copy, scale=sm[:, 2:3], r=["xt0", "sm"], w=["xn0"])
            if S1 < 9:
                return
            b = bank()
            for k in range(8):
                PE("transpose", out=pbT[b][:, k, :], in_=xn[0][:, k * 128:(k + 1) * 128], identity=ident[:], r=["xn0", "ident"], w=[f"pb{b}"], sig=(k == 7))
            if S1 < 10:
                return
            V("tensor_tensor", out=xnTs, in0=pbT[b][:, :, :], in1=gpre[:].unsqueeze(2).broadcast_to([128, 8, 128]), op=ALU.mult,
              r=[f"pb{b}", "gpre"], w=["QT0"])

            def tm_cols(c0, w):
                bb = bank()
                for k in range(8):
                    PE("matmul", pb[bb][:, 0:w], lhsT=xnTs[:, k, :], rhs=Win[:, k, c0:c0 + w], start=(k == 0), stop=(k == 7),
                       r=["QT0"] + [f"Win{k}_{hh}" for hh in sorted({c0 // 1152, (c0 + w - 1) // 1152})], w=[f"pb{bb}"], sig=(k == 7))
                return bb

            def fm_cols(c0):
                bb = bank()
                hh = c0 // 1152
                for k in range(8):
                    PE("matmul", pb[bb][:, 0:128], lhsT=Win[:, k, c0:c0 + 128], rhs=xnTs[:, k, :], start=(k == 0), stop=(k == 7),
                       r=["QT0", f"Win{k}_{hh}"], w=[f"pb{bb}"], sig=(k == 7))
                return bb

            if SS < 2:
                return
            bKV = tm_cols(C_K, 256)
            A("activation", out=kv_s, in_=pb[bKV][R16, 0:256], func=AF.Copy, r=[f"pb{bKV}"], w=["att_o"])
            P.dma("sync", kvb.ap(), kv_s, "kvb", reads=["att_o"], writes=["kvb"])
            P.dma("sync", Kn[127:128], kvb.ap()[:, 0:128].unsqueeze(0), "Kn_b", reads=["kvb", "xnT", "KnH0", "KnH1"], writes=["xnT"], group="samp2")
            for h in range(2):
                P.dma("sync", Vn[h][127:128], kvb.ap()[8 * h:8 * h + 8, 128:256].unsqueeze(0), f"Vn_b{h}", reads=["kvb", f"PT{h}"], writes=[f"PT{h}"], group="samp2")
            for h in range(2):
                big[0] = P.dma("sync", nks_d[8 * h:8 * h + 8].rearrange("s k c -> k s c"), Kn[:, 8 * h:8 * h + 8, :], f"o_nks{h}", reads=["xnT", f"KnH{h}"], deps=[big[0]])
                out_handles.append(big[0])
                big[0] = P.dma("sync", nvs_d[8 * h:8 * h + 8].rearrange("s k c -> k s c"), Vn[h][:], f"o_nvs{h}", reads=[f"PT{h}"], deps=[big[0]])
                out_handles.append(big[0])
            if SS < 3:
                return
            for cc in range(4):
                bq = fm_cols(C_Q + cc * 128)
                V("tensor_scalar", out=Qm[0:64, :, 2 * cc], in0=pb[bq][0:64, 0:SB], scalar1=0.125, scalar2=None, op0=ALU.mult, r=[f"pb{bq}"], w=["th0"])
                V("tensor_scalar", out=Qm[64:128, :, 2 * cc + 1], in0=pb[bq][64:128, 0:SB], scalar1=0.125, scalar2=None, op0=ALU.mult, r=[f"pb{bq}"], w=["th0"])
            for cc in range(4):
                bg = fm_cols(C_GA + cc * 128)
                A("activation", out=tD[:, 0:SB], in_=pb[bg][:, 0:SB], func=AF.Tanh, scale=0.5, r=[f"pb{bg}"], w=["ua"])
                V("scalar_tensor_tensor", out=uaT[:, cc, :], in0=tD[:, 0:SB], scalar=1.0, in1=pb[bg][:, 0:SB], op0=ALU.add, op1=ALU.mult,
                  r=[f"pb{bg}", "ua"], w=["uaT"])
            if SS < 4:
                return
            bXR = tm_cols(C_XR, 512)
            bGR = tm_cols(C_GR, 512)
            cw = lambda tap: rp[tap]
            cb_r, bga_r, bgx_r, lam_r = rp[4], rp[5], rp[6], rp[7]
            A("activation", out=tB[R16], in_=pb[bXR][R16, :], func=AF.Copy, r=[f"pb{bXR}"], w=["xcv0"])
            out_handles.append(P.dma("sync", nconvs_d[:, 0:512], sst_t[1], "o_ncs_a", reads=["tt"]))
            out_handles.append(P.dma("sync", nconvs_d[:, 512:1024], sst_t[2], "o_ncs_c", reads=["BiasF"], group="outs"))
            out_handles.append(P.dma("sync", nconvs_d[:, 1024:1536], tB[R16], "o_ncs_b", reads=["xcv0"], group="outs"))
            V("tensor_tensor", out=xcs[R16], in0=tB[R16], in1=cw(3), op=ALU.mult, r=["xcv0", rpn[3]], w=["t10"])
            for tap in range(3):
                V("tensor_tensor", out=tA[R16], in0=sst_t[tap], in1=cw(tap), op=ALU.mult, r=[sst_n[tap], rpn[tap]], w=["t11"])
                V("tensor_tensor", out=xcs[R16], in0=xcs[R16], in1=tA[R16], op=ALU.add, r=["t10", "t11"], w=["t10"])
            V("tensor_tensor", out=xcs[R16], in0=xcs[R16], in1=cb_r, op=ALU.add, r=["t10", rpn[4]], w=["t10"])
            b = bank()
            for g in range(4):
                PE("transpose", out=pb[b][:, g * 128:(g + 1) * 128], in_=xcs[:, g * 128:(g + 1) * 128], identity=ident32[:],
                   r=["t10", "ident32"], w=[f"pb{b}"], sig=(g == 3))
            V("tensor_copy", out=xcT, in_=pb[b][:].rearrange("p (g t) -> p g t", t=128), r=[f"pb{b}"], w=["u0"])
            bA, bX = bank(), bank()
            for g in range(4):
                PE("matmul", pb[bA][:, g * 128:(g + 1) * 128], lhsT=xcT[:, g, :], rhs=bda[:, g, :], start=True, stop=True, r=["u0", "bda"], w=[f"pb{bA}"], sig=(g == 3))
            for g in range(4):
                PE("matmul", pb[bX][:, g * 128:(g + 1) * 128], lhsT=xcT[:, g, :], rhs=bdx[:, g, :], start=True, stop=True, r=["u0", "bdx"], w=[f"pb{bX}"], sig=(g == 3))
            A("activation", out=tC[R16], in_=lam_r, func=AF.Exp, scale=-1.0, r=[rpn[7]], w=["xcv1"])
            A("activation", out=tC[R16], in_=tC[R16], func=AF.Ln, bias=1.0, r=["xcv1"], w=["xcv1"])
            V("tensor_scalar", out=tC[R16], in0=tC[R16], scalar1=-4.0, scalar2=None, op0=ALU.mult, r=["xcv1"], w=["xcv1"])
            V("tensor_tensor", out=tA[R16], in0=pb[bA][R16, :], in1=bga_r, op=ALU.add, r=[f"pb{bA}", rpn[5]], w=["t11"])
            A("activation", out=tA[R16], in_=tA[R16], func=AF.Tanh, scale=0.5, r=["t11"], w=["t11"])
            V("scalar_tensor_tensor", out=tA[R16], in0=tA[R16], scalar=1.0, in1=tC[R16], op0=ALU.add, op1=ALU.mult, r=["t11", "xcv1"], w=["t11"])
            A("activation", out=tA[R16], in_=tA[R16], func=AF.Exp, r=["t11"], w=["t11"])
            V("tensor_tensor", out=tC[R16], in0=pb[bX][R16, :], in1=bgx_r, op=ALU.add, r=[f"pb{bX}", rpn[6]], w=["xcv1"])
            A("activation", out=tC[R16], in_=tC[R16], func=AF.Tanh, scale=0.5, r=["xcv1"], w=["xcv1"])
            V("scalar_tensor_tensor", out=tC[R16], in0=tC[R16], scalar=1.0, in1=xcs[R16], op0=ALU.add, op1=ALU.mult, r=["xcv1", "t10"], w=["xcv1"])
            A("activation", out=tE[R16], in_=pb[bGR][R16, :], func=AF.Tanh, scale=0.5, r=[f"pb{bGR}"], w=["sgr"])
            V("scalar_tensor_tensor", out=tE[R16], in0=tE[R16], scalar=1.0, in1=pb[bGR][R16, :], op0=ALU.add, op1=ALU.mult, r=["sgr", f"pb{bGR}"], w=["sgr"])
            V("tensor_tensor", out=tD[R16], in0=tA[R16], in1=tA[R16], op=ALU.mult, r=["t11"], w=["ua"])
            A("activation", out=tD[R16], in_=tD[R16], func=AF.Sqrt, scale=-1.0 / 16.0, bias=1.0 / 16.0, r=["ua"], w=["ua"])
            V("tensor_tensor", out=tC[R16], in0=tC[R16], in1=tD[R16], op=ALU.mult, r=["xcv1", "ua"], w=["xcv1"])
            V("tensor_tensor", out=tA[R16], in0=tA[R16], in1=hprev, op=ALU.mult, r=["t11", "th1"], w=["t11"])
            V("scalar_tensor_tensor", out=tC[R16], in0=tC[R16], scalar=2.0, in1=tA[R16], op0=ALU.mult, op1=ALU.add, r=["xcv1", "t11"], w=["xcv1"])
            out_handles.append(P.dma("sync", nrnns_d, tC[R16], "o_nrs", reads=["xcv1"], group="outs"))
            V("scalar_tensor_tensor", out=mixtm[R16], in0=tC[R16], scalar=0.5, in1=tE[R16], op0=ALU.mult, op1=ALU.mult, r=["xcv1", "sgr"], w=["xn0"])
            b = bank()
            for g in range(4):
                PE("transpose", out=pbT[b][:, g, :], in_=mixtm[:, g * 128:(g + 1) * 128], identity=ident[:], r=["xn0", "ident"], w=[f"pb{b}"], sig=(g == 3))
            V("tensor_copy", out=mixTs[:, 0:4, :], in_=pbT[b][:, 0:4, :], r=[f"pb{b}"], w=["xn1"])
            if SS < 5:
                return
            for q4 in range(4):
                b = bank()
                for i4 in range(4):
                    sq = q4 * 4 + i4
                    PE("transpose", out=pb[b][:, i4 * 128:(i4 + 1) * 128], in_=Kn[:, sq, :], identity=ident32[:], r=["xnT", "ident32"], w=[f"pb{b}"], sig=(i4 == 3))
                hh, s0 = q4 // 2, (q4 % 2) * 4
                V("tensor_copy", out=KnT[hh][:, s0:s0 + 4, :], in_=pb[b][:].rearrange("p (s k) -> p s k", k=128), r=[f"pb{b}"], w=[f"a{hh}" if hh == 0 else "a20"])
            bS = bank()
            for sq in range(SB):
                hh, s0 = sq // 8, sq % 8
                PE("matmul", pb[bS][:, sq * 8:(sq + 1) * 8], lhsT=KnT[hh][:, s0, :], rhs=Qm[:, sq, :], start=True, stop=True,
                   r=["a0" if hh == 0 else "a20", "th0"], w=[f"pb{bS}"], sig=(sq == SB - 1))
            V("tensor_tensor", out=sc.rearrange("p (s q) -> p s q", q=8), in0=pb[bS][:, 0:128].rearrange("p (s q) -> p s q", q=8),
              in1=biass[:].unsqueeze(1).broadcast_to([128, SB, 8]), op=ALU.add, r=[f"pb{bS}", "biass"], w=["Ac"])
            A("activation", out=Pt, in_=sc, func=AF.Exp, r=["Ac"], w=["hl"])
            bO = bank()
            for sq in range(SB):
                hh, s0 = sq // 8, sq % 8
                PE("matmul", pb[bO][:, sq * 8:(sq + 1) * 8], lhsT=Vn[hh][:, s0, :], rhs=Pt[:, sq * 8:(sq + 1) * 8], start=True, stop=True,
                   r=[f"PT{hh}", "hl"], w=[f"pb{bO}"], sig=(sq == SB - 1))
            bD = bank()
            PE("matmul", pb[bD][:, 0:128], lhsT=ones32[:], rhs=Pt, start=True, stop=True, r=["ones32", "hl"], w=[f"pb{bD}"], sig=True)
            V("tensor_copy", out=sm[:, 8:16].rearrange("p (c g) -> p c g", g=2), in_=sinkexp2[:].rearrange("p g c -> p c g"), r=["sinkexp2", "sm"], w=["sm"])
            V("scalar_tensor_tensor", out=rds.rearrange("p (s q) -> p s q", q=8), in0=pb[bD][:, 0:128].rearrange("p (s q) -> p s q", q=8),
              scalar=2.0, in1=sm[:, 8:16].unsqueeze(1).broadcast_to([128, SB, 8]), op0=ALU.mult, op1=ALU.add,
              r=[f"pb{bD}", "sm"], w=["Ac"])
            V("reciprocal", out=rds, in_=rds, r=["Ac"], w=["Ac"])
            V("tensor_tensor", out=attn, in0=pb[bO][:, 0:128], in1=rds, op=ALU.mult, r=[f"pb{bO}", "Ac"], w=["hl"])
            av = attn.rearrange("p (s c g) -> p c s g", c=4, g=2)
            V("tensor_tensor", out=mixTs[0:64, 4:8, 0:SB], in0=av[0:64, :, :, 0], in1=uaT[0:64, :, :], op=ALU.mult, r=["hl", "uaT"], w=["xn1"])
            V("tensor_tensor", out=mixTs[64:128, 4:8, 0:SB], in0=av[64:128, :, :, 1], in1=uaT[64:128, :, :], op=ALU.mult, r=["hl", "uaT"], w=["xn1"])
            if SS < 6:
                return
            bY = [bank(), bank()]
            for kk in range(8):
                for hf in range(2):
                    PE("matmul", pb[bY[hf]][:], lhsT=mixTs[:, kk, :], rhs=Wout[:, kk, hf * 512:(hf + 1) * 512], start=(kk == 0), stop=(kk == 7),
                       r=["xn1", f"Wout{kk}"], w=[f"pb{bY[hf]}"], sig=(kk == 7))
            for hf in range(2):
                A("activation", out=junk2[:], in_=pb[bY[hf]][:], func=AF.Square, accum_out=sm[:, 3 + hf:4 + hf], r=[f"pb{bY[hf]}", "sm"], w=["junk2", "sm"])
            V("tensor_tensor", out=sm[:, 5:6], in0=sm[:, 3:4], in1=sm[:, 4:5], op=ALU.add, r=["sm"], w=["sm"])
            V("tensor_scalar", out=sm[:, 5:6], in0=sm[:, 5:6], scalar1=1.0 / D, scalar2=EPS, op0=ALU.mult, op1=ALU.add, r=["sm"], w=["sm"])
            A("activation", out=sm[:, 5:6], in_=sm[:, 5:6], func=AF.Sqrt, r=["sm"], w=["sm"])
            V("reciprocal", out=sm[:, 6:7], in_=sm[:, 5:6], r=["sm"], w=["sm"])
            for hf in range(2):
                V("scalar_tensor_tensor", out=tt[:, hf * 512:(hf + 1) * 512], in0=pb[bY[hf]][:], scalar=sm[:, 6:7], in1=gpost[:, hf * 512:(hf + 1) * 512],
                  op0=ALU.mult, op1=ALU.mult, r=[f"pb{bY[hf]}", "sm", "gpost"], w=["tt"])
            V("tensor_tensor", out=xt[0][R16, :], in0=tt[R16, :], in1=xt[0][R16, :], op=ALU.add, r=["tt", "xt0"], w=["xt0"])
            out_handles.append(P.dma("sync", ys_d, xt[0][R16, :], "o_ys", reads=["xt0"]))


        if STAGE >= 5:
            F2 = lambda ap: ap.bitcast(F32)
            xsl = [(xt[0][:], ["xt0"]), (xt[1][:], ["xt1"]),
                   (F2(PT[0][:].rearrange("p a b n -> p (a b n)")), ["PT0"]), (F2(PT[1][:].rearrange("p a b n -> p (a b n)")), ["PT1"])]
            tsl = [(tt[:], ["tt"]), (a_b[:].rearrange("p a n -> p (a n)"), ["a0", "a1"]), (a2_b[:].rearrange("p a n -> p (a n)"), ["a20", "a21"])]
            for i in range(NT):
                s = i % 2
                xa, xnm = xsl[i % 4]
                ta, tnm = tsl[i % 3]
                c0 = i * 128
                blk0 = (i // 4) * 512
                P.dma("sync", xa, xc_d[PRE + (i + 1) * 128:PRE + (i + 2) * 128, :], f"x2_{i % 4}", writes=xnm)
                for g in range(4):
                    V("scalar_tensor_tensor", out=mixT[s][:, g, :], in0=P2[:, g, c0:c0 + 128], scalar=h0[:, g:g + 1], in1=P1[:, g, c0:c0 + 128],
                                                            op0=ALU.mult, op1=ALU.add,
                      r=[f"P1_{g}_{blk0}", f"P2_{g}_{blk0}", "h0"], w=[f"mixT{s}"])
                bY = [bank(), bank()]
                for kk in range(8):
                    lhs = mixT[s][:, kk, :] if kk < 4 else attT[:, kk - 4, c0:c0 + 128]
                    rn = [f"mixT{s}"] if kk < 4 else [f"attT{i + 1}"]
                    for hf in range(2):
                        PE("matmul", pb[bY[hf]][:], lhsT=lhs, rhs=Wout[:, kk, hf * 512:(hf + 1) * 512],
                                                                     start=(kk == 0), stop=(kk == 7),
                           r=rn + [f"Wout{kk}"], w=[f"pb{bY[hf]}"], sig=(kk == 7))
                for hf in range(2):
                    A("activation", out=junk2[:], in_=pb[bY[hf]][:], func=AF.Square, accum_out=ss2[:, i, hf:hf + 1],
                      r=[f"pb{bY[hf]}"], w=["junk2", f"ss2_{i}_{hf}"])
                G("tensor_tensor", out=ms2[:, i:i + 1], in0=ss2[:, i, 0:1], in1=ss2[:, i, 1:2], op=ALU.add,
                  r=[f"ss2_{i}_0", f"ss2_{i}_1"], w=[f"ms2_{i}"])
                G("tensor_scalar", out=ms2[:, i:i + 1], in0=ms2[:, i:i + 1], scalar1=1.0 / D, scalar2=EPS, op0=ALU.mult, op1=ALU.add,
                  r=[f"ms2_{i}"], w=[f"ms2_{i}"])
                G("tensor_tensor", out=rstd2[:, i:i + 1], in0=ms2[:, i:i + 1], in1=cst[:, 0:1], op=ALU.pow,
                  r=[f"ms2_{i}", "cst"], w=[f"rstd2_{i}"])
                for hf in range(2):
                    V("scalar_tensor_tensor", out=ta[:, hf * 512:(hf + 1) * 512], in0=pb[bY[hf]][:], scalar=rstd2[:, i:i + 1],
                                                              in1=gpost[:, hf * 512:(hf + 1) * 512], op0=ALU.mult, op1=ALU.mult,
                      r=[f"pb{bY[hf]}", f"rstd2_{i}", "gpost"] + tnm, w=tnm)
                G("tensor_tensor", out=xa, in0=ta, in1=xa, op=ALU.add, r=tnm + xnm, w=xnm)
                out_handles.append(P.dma("sync", y_d[c0:c0 + 128, :], xa, f"o_y{i % 4}", reads=xnm))

        if STAGE >= 6:
            G("memset", th[0][:, 0:128], 0.0, w=["th0"])
            G("memset", t1_b[:], 0.0, w=["t10", "t11"])
            G("memset", xn[1][:], 0.0, w=["xn1"])
            sample_path()

        P.wait_all("sync", out_handles)
        P.emit()
    return nc


def _t5_bucket(dist):
    dist = np.maximum(dist, 0)
    max_exact = 16
    d = np.maximum(dist, 1).astype(np.float32)
    large = max_exact + (np.log(d / np.float32(max_exact)) / np.float32(np.log(128 / max_exact)) * np.float32(32 - max_exact)).astype(np.int32)
    large = np.minimum(large, 31)
    return np.where(dist < max_exact, dist, large)


_NC_CACHE = {}


def kernel(x_prompt, x_sample, state_conv, state_rnn, cache_k_win, cache_v_win,
           norm_pre, norm_post, w_in, conv_w, conv_b, w_gate_a, b_gate_a, w_gate_x, b_gate_x,
           lru_lambda, attn_sinks, rel_bias, w_out):
    f32 = np.float32
    x_prompt = np.asarray(x_prompt, f32)
    w_in0 = np.asarray(w_in, f32)[0]
    w_out0 = np.asarray(w_out, f32)[0]
    qperm = np.concatenate([np.arange(h * 64, h * 64 + 64) for h in LPOS])
    cols = np.concatenate([np.arange(0, 1024), 1024 + qperm, np.arange(1536, 1792), 1792 + qperm])
    w_in_p = np.ascontiguousarray(w_in0[:, cols])
    rows = np.concatenate([np.arange(0, 512), 512 + qperm])
    w_out_p = np.ascontiguousarray(w_out0[rows, :])

    def pg(v):
        return np.ascontiguousarray(np.asarray(v, f32).reshape(4, 128).T)

    gpre = np.ascontiguousarray(np.asarray(norm_pre, f32)[0].reshape(8, 128).T)
    gpost = np.ascontiguousarray(np.broadcast_to(np.asarray(norm_post, f32)[0][None, :], (128, D)))
    cw = np.asarray(conv_w, f32)[0]
    convw = np.ascontiguousarray(cw.reshape(4, 4, 128).transpose(2, 1, 0).reshape(128, 16))
    vec4 = np.ascontiguousarray(np.stack([pg(np.asarray(conv_b)[0]), pg(np.asarray(b_gate_a)[0]), pg(np.asarray(b_gate_x)[0]),
                                          pg(np.asarray(lru_lambda)[0])], axis=1).reshape(128, 16))

    def blockdiag(w):
        w = np.asarray(w, f32)[0]
        o = np.zeros((128, 4, 128), f32)
        for g in range(4):
            for h in range(2):
                o[h * 64:(h + 1) * 64, g, h * 64:(h + 1) * 64] = w[2 * g + h]
        return o.reshape(128, 512)

    bda = blockdiag(w_gate_a)
    bdx = blockdiag(w_gate_x)
    hd = np.array([[g * 4 + cc for cc in range(4)] for g in range(2)])
    sinks = np.ascontiguousarray(np.broadcast_to(np.asarray(attn_sinks, f32)[0][hd].reshape(1, 8), (128, 8)))
    rb = np.asarray(rel_bias, f32)
    kk = np.arange(128)[:, None]
    qq = np.arange(128)[None, :]
    biasg = np.zeros((128, 2, 2, 4, 128), f32)
    maskc = np.zeros((128, 2, 128), f32)
    for blk in range(2):
        dist = qq + (128 if blk == 0 else 0) - kk
        valid = (dist >= 0) & (dist < 128)
        bkt = _t5_bucket(np.clip(dist, 0, 127))
        for g in range(2):
            for cc in range(4):
                biasg[:, blk, g, cc, :] = rb[bkt, hd[g, cc]]
        maskc[:, blk, :] = np.where(valid, 0.0, NEG)
    biasg = biasg.reshape(128, 2048)
    maskc = maskc.reshape(128, 256)
    ident = np.eye(128, dtype=f32)

    xs_all = np.asarray(x_sample, f32)[:, 0, :]
    sconv_all = np.asarray(state_conv, f32)[0].reshape(128, 1536)
    srnn_all = np.asarray(state_rnn, f32)[0]
    ck_all = np.asarray(cache_k_win, f32)[0].reshape(128, 128, 128)
    cv_all = np.asarray(cache_v_win, f32)[0].reshape(128, 128, 128)
    rowv = np.concatenate([cw.reshape(-1), np.asarray(conv_b, f32)[0], np.asarray(b_gate_a, f32)[0], np.asarray(b_gate_x, f32)[0],
                           np.asarray(lru_lambda, f32)[0]])
    rowp = np.ascontiguousarray(np.broadcast_to(rowv[None, :], (SB, 4096)))
    posh = np.array([g * 4 + cc for cc in range(4) for g in range(2)])
    biass = np.ascontiguousarray(rb[_t5_bucket(127 - np.arange(128))][:, posh])

    def padrows(a):
        return np.concatenate([a.reshape(SB * 128, 128), np.zeros((128, 128), f32)], axis=0)

    in_maps = []
    for c in range(NCORES):
        b, j = c // 4, c % 4
        xc = np.zeros((PRE + CH + 128, D), f32)
        flag = np.zeros((128, 3), f32)
        for kf in range(3):
            cj = j - 3 + kf
            if cj >= 0:
                xc[kf * CH:(kf + 1) * CH] = x_prompt[b, cj * CH:(cj + 1) * CH]
                flag[:, kf] = 1.0
        xc[PRE + 128:] = x_prompt[b, j * CH:(j + 1) * CH]
        xc[PRE:PRE + 128] = xc[PRE - 128:PRE]
        hmask = np.full((128, 1), 0.0 if j > 0 else NEG, f32)
        sel = np.zeros((128, 8), f32)
        for r in range(NCORES):
            if r // 4 == b and r < c:
                sel[:, r] = 1.0
        in_maps.append({"xc": xc, "w_in": w_in_p, "w_out": w_out_p, "gpre": gpre, "gpost": gpost, "convw": convw, "vec4": vec4,
                        "bda": bda, "bdx": bdx, "sinks": sinks, "biasg": biasg, "maskc": maskc, "hmask": hmask, "sel": sel, "flag": flag,
                        "ident": ident, "xs": np.concatenate([xs_all[c * SB:(c + 1) * SB], np.zeros((128 - SB, D), f32)], axis=0),
                        "sconv": np.ascontiguousarray(sconv_all[c * SB:(c + 1) * SB]), "srnn": np.ascontiguousarray(srnn_all[c * SB:(c + 1) * SB]),
                        "ck": padrows(ck_all[c * SB:(c + 1) * SB]), "cv": padrows(cv_all[c * SB:(c + 1) * SB]),
                        "rowp": rowp, "biass": biass})

    if "nc" not in _NC_CACHE:
        _NC_CACHE["nc"] = build_program()
    nc = _NC_CACHE["nc"]
    res = run_bass_kernel_spmd(nc, in_maps, core_ids=list(range(NCORES)))
    R = res.results

    y_prompt = np.stack([np.concatenate([R[b * 4 + j]["y"] for j in range(4)], axis=0) for b in range(2)], axis=0)
    new_conv_p = np.stack([R[3]["nconv"], R[7]["nconv"]])[None]
    new_rnn_p = np.stack([R[3]["nrnn"], R[7]["nrnn"]])[None]
    new_k_p = np.stack([R[3]["nk"].reshape(128, 2, 64), R[7]["nk"].reshape(128, 2, 64)])[None]
    new_v_p = np.stack([R[3]["nv"].reshape(128, 2, 64), R[7]["nv"].reshape(128, 2, 64)])[None]
    cat = lambda k: np.concatenate([R[c][k] for c in range(NCORES)], axis=0)
    y_sample = cat("ys").reshape(128, 1, D).astype(f32)
    new_conv_s = cat("nconvs").reshape(1, 128, 3, 512).astype(f32)
    new_rnn_s = cat("nrnns").reshape(1, 128, 512).astype(f32)
    new_k_s = cat("nks").reshape(1, 128, 128, 2, 64).astype(f32)
    new_v_s = cat("nvs").reshape(1, 128, 128, 2, 64).astype(f32)
    return (y_prompt.astype(f32), y_sample, new_conv_p.astype(f32), new_rnn_p.astype(f32), new_k_p.astype(f32), new_v_p.astype(f32),
            new_conv_s, new_rnn_s, new_k_s, new_v_s)
```

```python
import contextlib
import numpy as np
import concourse.bass as bass
import concourse.mybir as mybir
from concourse.bass_utils import run_bass_kernel_spmd

F32 = mybir.dt.float32
BF16 = mybir.dt.bfloat16
ALU = mybir.AluOpType
AF = mybir.ActivationFunctionType

NCORES = 8
D = 1024
D_IN = 2304
SEQ = 8192
CH = 2048
NT = 16
EPS = 1e-6
NEG = -1e30
LPOS = [0, 4, 1, 5, 2, 6, 3, 7]
C_XR, C_GR, C_Q, C_K, C_V, C_GA = 0, 512, 1024, 1536, 1664, 1792
SB = 16
PRE = 3 * CH
NPT = PRE // 128

ENGS = ("sync", "scalar", "vector", "gpsimd", "tensor")
CC_INC = 1
GROUPS = {"setup": 11, "W1": 9, "W2": 8, "W3": 8, "samp": 18, "samp2": 3, "outs": 14}
GROUPS_SEEN = {}
STAGE = 99
SUB = 99
NBLK = 99
NOOUT = 0
NOLAST = 0
SS = 99
DM = 255
S1 = 99


class Prog:
    def __init__(self, nc, es):
        self.nc = nc
        self.es = es
        self.q = {e: [] for e in ENGS}
        self.sig = {e: 0 for e in ENGS}
        self.pending = {e: False for e in ENGS}
        self.waited = {}
        self.bufs = {}
        self.dma_cnt = {}
        self.grp_seen = {}
        self.sems = {}

    def sem(self, key):
        if key not in self.sems:
            name = "s_" + "_".join(str(k) for k in key)
            self.sems[key] = self.es.enter_context(self.nc.semaphore(name))
        return self.sems[key]

    def _deps(self, eng, reads, writes, extra, skip_key=None):
        deps = set(extra)
        for r in reads:
            b = self.bufs.get(r)
            if b and b["w"] is not None:
                deps.add(b["w"])
            if b and r.startswith("pb"):
                deps.update(h for h in b["r"] if h[1] != eng)
        for w in writes:
            b = self.bufs.get(w)
            if b:
                if b["w"] is not None:
                    deps.add(b["w"])
                deps.update(b["r"])
        best = {}
        for d in deps:
            if d is None:
                continue
            key = d[:2]
            if eng == "tensor" and key == ("eng", "tensor"):
                continue
            if key == skip_key:
                continue
            if d[2] > best.get(key, 0):
                best[key] = d[2]
        waits = []
        for key, val in best.items():
            if self.waited.get((eng, key), 0) >= val:
                continue
            self.waited[(eng, key)] = val
            waits.append((key, val))
        return waits

    def _track(self, h, reads, writes):
        for r in reads:
            b = self.bufs.setdefault(r, {"w": None, "r": []})
            b["r"].append(h)
        for w in writes:
            self.bufs[w] = {"w": h, "r": []}

    def op(self, eng, fn, reads=(), writes=(), signal=True, deps=()):
        waits = self._deps(eng, reads, writes, deps)
        if signal:
            self.sig[eng] += 1
            h = ("eng", eng, self.sig[eng])
            self.pending[eng] = False
        else:
            h = ("eng", eng, self.sig[eng] + 1)
            self.pending[eng] = True
        self._track(h, reads, writes)
        semw = [(self.sem(k), v) for k, v in waits]
        mysem = self.sem(("eng", eng)) if signal else None

        def emit(e):
            for s, v in semw:
                e.wait_ge(s, v)
            ins = fn(e)
            if mysem is not None:
                ins.then_inc(mysem, 1)

        self.q[eng].append(emit)
        return h

    def dma(self, eng, out, in_, slot, reads=(), writes=(), deps=(), group=None, **kw):
        waits = self._deps(eng, reads, writes, deps, skip_key=(("dma", group) if group is not None else None))
        if group is not None:
            slot = group
            self.grp_seen[group] = self.grp_seen.get(group, 0) + 1
            cnt = 16 * GROUPS[group]
        else:
            cnt = self.dma_cnt.get(slot, 0) + 16
        self.dma_cnt[slot] = cnt
        h = ("dma", slot, cnt)
        self._track(h, reads, writes)
        semw = [(self.sem(k), v) for k, v in waits]
        mysem = self.sem(("dma", slot))

        def emit(e):
            for s, v in semw:
                e.wait_ge(s, v)
            e.dma_start(out=out, in_=in_, **kw).then_inc(mysem, 16)

        self.q[eng].append(emit)
        return h

    def cc(self, eng, fn, reads=(), writes=(), inc=None):
        inc = CC_INC if inc is None else inc
        waits = self._deps(eng, reads, writes, ())
        cnt = self.dma_cnt.get("cc", 0) + inc
        self.dma_cnt["cc"] = cnt
        h = ("dma", "cc", cnt)
        self._track(h, reads, writes)
        semw = [(self.sem(k), v) for k, v in waits]
        mysem = self.sem(("dma", "cc"))

        def emit(e):
            for s, v in semw:
                e.wait_ge(s, v)
            fn(e).then_inc(mysem, 1)

        self.q[eng].append(emit)
        return h

    def wait_all(self, eng, handles):
        waits = self._deps(eng, (), (), handles)
        semw = [(self.sem(k), v) for k, v in waits]

        def emit(e):
            for s, v in semw:
                e.wait_ge(s, v)

        self.q[eng].append(emit)

    def emit(self):
        assert not self.pending["tensor"], "PE has unsignalled trailing ops"
        GROUPS_SEEN.clear()
        GROUPS_SEEN.update(self.grp_seen)
        with self.nc.Block() as block:
            @block.sync
            def _(e):
                for f in self.q["sync"]:
                    f(e)

            @block.scalar
            def _(e):
                for f in self.q["scalar"]:
                    f(e)

            @block.vector
            def _(e):
                for f in self.q["vector"]:
                    f(e)

            @block.gpsimd
            def _(e):
                for f in self.q["gpsimd"]:
                    f(e)

            @block.tensor
            def _(e):
                for f in self.q["tensor"]:
                    f(e)


def build_program():
    nc = _build_program()
    if any(GROUPS.get(g) != n for g, n in GROUPS_SEEN.items()):
        GROUPS.update(GROUPS_SEEN)
        nc = _build_program()
        assert all(GROUPS.get(g) == n for g, n in GROUPS_SEEN.items())
    return nc


def _build_program():
    nc = bass.Bass("TRN2", target_bir_lowering=False)

    def din(name, shape):
        return nc.dram_tensor(name, list(shape), F32, kind="ExternalInput").ap()

    def dout(name, shape):
        return nc.dram_tensor(name, list(shape), F32, kind="ExternalOutput").ap()

    xc_d = din("xc", [PRE + CH + 128, D])
    flag_d = din("flag", [128, 3])
    w_in_d = din("w_in", [D, D_IN])
    w_out_d = din("w_out", [D, D])
    gpre_d = din("gpre", [128, 8])
    gpost_d = din("gpost", [128, D])
    convw_d = din("convw", [128, 16])
    vec4_d = din("vec4", [128, 16])
    bda_d = din("bda", [128, 512])
    bdx_d = din("bdx", [128, 512])
    sinks_d = din("sinks", [128, 8])
    biasg_d = din("biasg", [128, 2048])
    maskc_d = din("maskc", [128, 256])
    hmask_d = din("hmask", [128, 1])
    sel_d = din("sel", [128, 8])
    ident_d = din("ident", [128, 128])

    xs_d = din("xs", [128, D])
    sconv_d = din("sconv", [SB, 1536])
    srnn_d = din("srnn", [SB, 512])
    ck_d = din("ck", [SB * 128 + 128, 128])
    cv_d = din("cv", [SB * 128 + 128, 128])
    rowp_d = din("rowp", [SB, 4096])
    biass_d = din("biass", [128, 8])

    y_d = dout("y", [CH, D])
    ys_d = dout("ys", [SB, D])
    nconvs_d = dout("nconvs", [SB, 1536])
    nrnns_d = dout("nrnns", [SB, 512])
    nks_d = dout("nks", [SB, 128, 128])
    nvs_d = dout("nvs", [SB, 128, 128])
    nconv_d = dout("nconv", [3, 512])
    nrnn_d = dout("nrnn", [512])
    nk_d = dout("nk", [128, 128])
    nv_d = dout("nv", [128, 128])

    bounce = nc.dram_tensor("bounce", [128, 8], F32)
    gath = nc.dram_tensor("gath", [NCORES * 128, 8], F32)
    kvb = nc.dram_tensor("kvb", [SB, 256], F32)

    out_handles = []

    with contextlib.ExitStack() as es:
        P = Prog(nc, es)
        def sb(name, shape, dt=F32):
            return es.enter_context(nc.sbuf_tensor("sb_" + name, list(shape), dt))

        Win = sb("Win", [128, 8, D_IN], BF16)
        Wout = sb("Wout", [128, 8, D], BF16)
        P1 = sb("P1", [128, 4, CH], BF16)
        P2 = sb("P2", [128, 4, CH], BF16)
        attT = sb("attT", [128, 4, CH], BF16)
        KT = sb("KT", [128, 128 + 512], BF16)
        Vg = sb("Vg", [128, 5, 132], BF16)
        xt = [sb(f"xt{i}", [128, D]) for i in range(2)]
        xn = [sb(f"xn{i}", [128, D], BF16) for i in range(2)]
        xnT = sb("xnT", [128, 8, 512], BF16)
        xr = sb("xr", [128, 4, 515], BF16)
        xrt = sb("xrt", [128, 4, 4])
        xcv = sb("xcv", [128, 2, 512])
        u_b = sb("u_b", [128, 2, 512])
        a_b = sb("a_b", [128, 2, 512])
        a2_b = sb("a2_b", [128, 2, 512])
        t1_b = sb("t1_b", [128, 2, 512])
        th = [sb(f"th{i}", [128, 512]) for i in range(2)]
        hl = sb("hl", [128, 512])
        Ac = sb("Ac", [128, 512])
        QT = [sb(f"QT{g}", [128, 4, 512], BF16) for g in range(2)]
        BiasT = sb("BiasT", [128, 2, 2, 512], BF16)
        BiasF = sb("BiasF", [128, 2, 512], BF16)
        PT = [sb(f"PT{i}", [128, 2, 2, 512], BF16) for i in range(2)]
        ua = sb("ua", [128, 512])
        sgr = sb("sgr", [128, 512])
        att_o = sb("att_o", [128, 512], BF16)
        mixT = [sb(f"mixT{i}", [128, 4, 128], BF16) for i in range(2)]
        tt = sb("tt", [128, D])
        gpost = sb("gpost", [128, D])
        junk2 = sb("junk2", [128, 512], BF16)
        bda = sb("bda", [128, 4, 128])
        bdx = sb("bdx", [128, 4, 128])
        diagw = sb("diagw", [128, 4, 4, 128], BF16)
        ident = sb("ident", [128, 128], BF16)
        maskc = sb("maskc", [128, 2, 128])
        gpre = sb("gpre", [128, 8])
        convw = sb("convw", [128, 4, 4])
        vec4 = sb("vec4", [128, 4, 4])
        hb = sb("hb", [128, 2, 4])
        chalf = sb("chalf", [128, 4])
        sinks = sb("sinks", [128, 2, 4])
        sinkexp2 = sb("sinkexp2", [128, 2, 4])
        hmask = sb("hmask", [128, 1])
        sel = sb("sel", [128, 8])
        cst = sb("cst", [128, 4])
        ss = sb("ss", [128, 17 + NPT])
        ms = sb("ms", [128, 17 + NPT])
        rstd = sb("rstd", [128, 17 + NPT])
        flag = sb("flag", [128, 3])
        ss2 = sb("ss2", [128, NT, 2])
        ms2 = sb("ms2", [128, NT])
        rstd2 = sb("rstd2", [128, NT])
        car = sb("car", [128, 8])
        dsum = sb("dsum", [128, 4, 2])
        rden = sb("rden", [128, 4, 2])
        G_sb = sb("G_sb", [128, 8, 8])
        Ap = sb("Ap", [128, 8, 4])
        Bp = sb("Bp", [128, 8, 4])
        hscan = sb("hscan", [128, 4, 8])
        h0 = sb("h0", [128, 4])
        hfin = sb("hfin", [128, 4])
        kout = sb("kout", [128, 128])
        vout = sb("vout", [128, 128])
        sth = sb("sth", [128, 8])
        ident32 = sb("ident32", [128, 128])
        ones32 = sb("ones32", [128, 128])
        biass = sb("biass", [128, 8])
        uaT = sb("uaT", [128, 4, SB])
        sm = sb("sm", [128, 16])

        pb = [es.enter_context(nc.psum_tensor(f"pb{i}", [128, 512], F32)) for i in range(8)]
        pbT = [p[:].bitcast(BF16).rearrange("p (k c) -> p k c", c=128) for p in pb]
        bank_ctr = [0]

        def bank():
            i = bank_ctr[0]
            bank_ctr[0] = (i + 1) % 8
            return i

        def mk(eng):
            def f(name, *args, r=(), w=(), sig=True, **kw):
                return P.op(eng, lambda e: getattr(e, name)(*args, **kw), reads=r, writes=w, signal=sig)
            return f
        V, A, G = mk("vector"), mk("scalar"), mk("gpsimd")
        _pe = mk("tensor")

        def PE(name, *args, r=(), w=(), sig=False, **kw):
            return _pe(name, *args, r=r, w=w, sig=sig, **kw)

        def ld(dst, src, name, group="setup"):
            P.dma("sync", dst, src, name, writes=[name], group=group)

        ld(gpre[:], gpre_d, "gpre")
        ld(convw[:].rearrange("p g t -> p (g t)"), convw_d, "convw")
        ld(vec4[:].rearrange("p a g -> p (a g)"), vec4_d, "vec4")
        ld(hmask[:], hmask_d, "hmask")
        ld(sel[:], sel_d, "sel")
        ld(sinks[:].rearrange("p g c -> p (g c)"), sinks_d, "sinks")
        ld(maskc[:].rearrange("p b q -> p (b q)"), maskc_d, "maskc")
        ld(bda[:].rearrange("p g m -> p (g m)"), bda_d, "bda")
        ld(bdx[:].rearrange("p g m -> p (g m)"), bdx_d, "bdx")
        ld(gpost[:], gpost_d, "gpost")
        ld(xt[0][:], biasg_d[:, 0:1024], "xt0", group=None)
        ld(xt[1][:], biasg_d[:, 1024:2048], "xt1", group=None)
        P.dma("gpsimd", ident[:], ident_d, "ident", writes=["ident"], group="W1")
        for k in range(8):
            for h in range(2):
                P.dma("gpsimd", Win[:, k, h * 1152:(h + 1) * 1152], w_in_d[k * 128:(k + 1) * 128, h * 1152:(h + 1) * 1152],
                      f"Win{k}_{h}", writes=[f"Win{k}_{h}"], group=("W1" if h == 0 else "W2"))
        for k in range(8):
            P.dma("gpsimd", Wout[:, k, :], w_out_d[k * 128:(k + 1) * 128, :], f"Wout{k}", writes=[f"Wout{k}"], group="W3")

        G("memset", cst[:, 0:1], -0.5, w=["cst"])
        G("memset", cst[:, 1:2], 0.0, r=["cst"], w=["cst"])
        G("memset", cst[:, 2:3], 1.0 / 16.0, r=["cst"], w=["cst"])
        G("memset", Vg[:], 1.0, w=["Vg"])
        G("memset", QT[0][64:128], 0.0, w=["QT0"])
        G("memset", QT[1][0:64], 0.0, w=["QT1"])
        G("memset", ones32[:], 1.0, w=["ones32"])
        A("activation", out=sth[:, 0:4], in_=vec4[:, 3, :], func=AF.Exp, scale=-1.0, r=["vec4"], w=["sth"])
        A("activation", out=sth[:, 4:8], in_=sth[:, 0:4], func=AF.Ln, bias=1.0, r=["sth"], w=["sth"])
        V("tensor_scalar", out=chalf[:], in0=sth[:, 4:8], scalar1=-4.0, scalar2=None, op0=ALU.mult, r=["sth"], w=["chalf"])
        V("tensor_scalar", out=hb[:], in0=vec4[:, 1:3, :], scalar1=0.5, scalar2=None, op0=ALU.mult, r=["vec4"], w=["hb"])
        A("activation", out=sinkexp2[:], in_=sinks[:], func=AF.Exp, r=["sinks"], w=["sinkexp2"])
        V("tensor_scalar", out=sinkexp2[:], in0=sinkexp2[:], scalar1=2.0, scalar2=None, op0=ALU.mult, r=["sinkexp2"], w=["sinkexp2"])
        for blk in range(2):
            V("tensor_tensor", out=BiasT[:, blk].rearrange("p g (c q) -> p (g c) q", q=128),
                                                 in0=xt[blk][:].rearrange("p (h q) -> p h q", q=128),
                                                 in1=maskc[:, blk, :].unsqueeze(1).broadcast_to([128, 8, 128]), op=ALU.add,
              r=[f"xt{blk}", "maskc"], w=["BiasT"])
        V("tensor_tensor", out=xt[0][:].rearrange("p (h q) -> p h q", q=128), in0=xt[0][:].rearrange("p (h q) -> p h q", q=128),
                                    in1=maskc[:, 0, :].unsqueeze(1).broadcast_to([128, 8, 128]), op=ALU.add,
          r=["xt0", "maskc"], w=["xt0"])
        V("tensor_scalar", out=BiasF[:].rearrange("p g c -> p (g c)"), in0=xt[0][:], scalar1=hmask[:, 0:1], scalar2=None, op0=ALU.add,
          r=["xt0", "hmask"], w=["BiasF"])
        for g in range(4):
            for tap in range(4):
                V("tensor_scalar", out=diagw[:, g, tap, :], in0=ident[:], scalar1=convw[:, g, tap:tap + 1], scalar2=None, op0=ALU.mult,
                  r=["ident", "convw"], w=["diagw"])

        def win_names(c0, c1):
            hs = sorted({c0 // 1152, (c1 - 1) // 1152})
            return hs

        def tile_front(t, j):
            s = t % 2
            row = PRE + t * 128 if t < 17 else (t - 17) * 128
            P.dma("sync", xt[s][:], xc_d[row:row + 128, :], f"xt{s}", writes=[f"xt{s}"])
            A("activation", out=xn[s][:], in_=xt[s][:], func=AF.Square, accum_out=ss[:, t:t + 1],
              r=[f"xt{s}"], w=[f"xn{s}", f"ss{t}"])
            G("tensor_scalar", out=ms[:, t:t + 1], in0=ss[:, t:t + 1], scalar1=1.0 / D, scalar2=EPS, op0=ALU.mult, op1=ALU.add,
              r=[f"ss{t}"], w=[f"ms{t}"])
            G("tensor_tensor", out=rstd[:, t:t + 1], in0=ms[:, t:t + 1], in1=cst[:, 0:1], op=ALU.pow,
              r=[f"ms{t}", "cst"], w=[f"rstd{t}"])
            G("tensor_scalar", out=xn[s][:], in0=xt[s][:], scalar1=rstd[:, t:t + 1], scalar2=0.0, op0=ALU.mult, op1=ALU.add,
              r=[f"xt{s}", f"rstd{t}"], w=[f"xn{s}"])
            b = bank()
            for k in range(8):
                PE("transpose", out=pbT[b][:, k, :], in_=xn[s][:, k * 128:(k + 1) * 128], identity=ident[:],
                   r=[f"xn{s}", "ident"], w=[f"pb{b}"], sig=(k == 7))
            V("tensor_tensor", out=xnT[:, :, j * 128:(j + 1) * 128], in0=pbT[b][:, :, :],
                                        in1=gpre[:].unsqueeze(2).broadcast_to([128, 8, 128]), op=ALU.mult,
              r=[f"pb{b}", "gpre"], w=["xnT"])

        def fm_chunk(c0, N):
            b = bank()
            h = c0 // 1152
            for k in range(8):
                PE("matmul", pb[b][:, 0:N], lhsT=Win[:, k, c0:c0 + 128], rhs=xnT[:, k, 0:N], start=(k == 0), stop=(k == 7),
                   r=[f"Win{k}_{h}", "xnT"], w=[f"pb{b}"], sig=(k == 7))
            return b

        thc = [0]

        def th_slot():
            thc[0] ^= 1
            return thc[0]

        def rnn_chain(N, t0, pre=False):
            for gp in range(2):
                gs = (2 * gp, 2 * gp + 1)
                for gi, g in enumerate(gs):
                    b = bank()
                    for tap in range(4):
                        PE("matmul", pb[b][:, 0:N], lhsT=diagw[:, g, tap, :], rhs=xr[:, g, tap:tap + N],
                                                                 start=(tap == 0), stop=(tap == 3),
                           r=["diagw", "xr"], w=[f"pb{b}"], sig=(tap == 3))
                    A("activation", out=xcv[:, gi, 0:N], in_=pb[b][:, 0:N], func=AF.Identity, bias=vec4[:, 0, g:g + 1],
                      r=[f"pb{b}", "vec4"], w=[f"xcv{gi}"])
                    if pre:
                        continue
                    b = fm_chunk(C_GR + g * 128, N)
                    s = th_slot()
                    A("activation", out=th[s][:, 0:N], in_=pb[b][:, 0:N], func=AF.Tanh, scale=0.5, r=[f"pb{b}"], w=[f"th{s}"])
                    V("scalar_tensor_tensor", out=u_b[:, gi, 0:N], in0=th[s][:, 0:N], scalar=1.0, in1=pb[b][:, 0:N],
                                                                        op0=ALU.add, op1=ALU.mult,
                      r=[f"pb{b}", f"th{s}"], w=[f"u{gi}"])
                for gi, g in enumerate(gs):
                    bA = bank()
                    PE("matmul", pb[bA][:, 0:N], lhsT=bda[:, g, :], rhs=xcv[:, gi, 0:N], start=True, stop=True,
                       r=["bda", f"xcv{gi}"], w=[f"pb{bA}"], sig=True)
                    bX = bank()
                    PE("matmul", pb[bX][:, 0:N], lhsT=bdx[:, g, :], rhs=xcv[:, gi, 0:N], start=True, stop=True,
                       r=["bdx", f"xcv{gi}"], w=[f"pb{bX}"], sig=True)
                    s = th_slot()
                    A("activation", out=th[s][:, 0:N], in_=pb[bA][:, 0:N], func=AF.Tanh, scale=0.5, bias=hb[:, 0, g:g + 1],
                      r=[f"pb{bA}", "hb"], w=[f"th{s}"])
                    A("activation", out=a_b[:, gi, 0:N], in_=th[s][:, 0:N], func=AF.Exp, scale=chalf[:, g:g + 1], bias=chalf[:, g:g + 1],
                      r=[f"th{s}", "chalf"], w=[f"a{gi}"])
                    G("tensor_tensor", out=a2_b[:, gi, 0:N], in0=a_b[:, gi, 0:N], in1=a_b[:, gi, 0:N], op=ALU.mult,
                      r=[f"a{gi}"], w=[f"a2{gi}"])
                    s2 = th_slot()
                    A("activation", out=th[s2][:, 0:N], in_=pb[bX][:, 0:N], func=AF.Tanh, scale=0.5, bias=hb[:, 1, g:g + 1],
                      r=[f"pb{bX}", "hb"], w=[f"th{s2}"])
                    V("scalar_tensor_tensor", out=t1_b[:, gi, 0:N], in0=th[s2][:, 0:N], scalar=1.0, in1=xcv[:, gi, 0:N],
                                                                     op0=ALU.add, op1=ALU.mult,
                      r=[f"th{s2}", f"xcv{gi}"], w=[f"t1{gi}"])
                rnn_tail.append((gs, N, t0, pre))
                if SUB >= 4:
                    flush_rnn_tail()
                else:
                    rnn_tail.clear()

        front_done = set()

        def front(t0, nt):
            if (t0, nt) in front_done:
                return
            front_done.add((t0, nt))
            for j in range(nt):
                tile_front(t0 + j, j)

        def block(bi, t0, nt, pre=False, nxt=None):
            N = nt * 128
            halo = (bi == 0)
            last = (bi == 4) and not NOLAST
            front(t0, nt)
            if not halo and Nprev[0] > 0:
                npv = Nprev[0]
                V("tensor_copy", out=xr[:, :, 0:3], in_=xr[:, :, npv:npv + 3], r=["xr"], w=["xr"])
            for g in range(4):
                b = fm_chunk(C_XR + g * 128, N)
                A("activation", out=xr[:, g, 3:3 + N], in_=pb[b][:, 0:N], func=AF.Copy, r=[f"pb{b}"], w=["xr"])
                if last:
                    A("activation", out=xrt[:, g, :], in_=pb[b][:, N - 4:N], func=AF.Copy, r=[f"pb{b}"], w=["xrt"])
            Nprev[0] = N
            if pre:
                if nxt is not None:
                    front(*nxt)
                rnn_chain(N, t0, pre=True)
                return
            b = fm_chunk(C_K, N)
            if not halo:
                kpv, vpv = KTprev[0], Vprev[0]
                if kpv > 0:
                    V("tensor_copy", out=KT[:, 0:128], in_=KT[:, kpv:kpv + 128], r=["KT"], w=["KT"])
                    V("tensor_copy", out=Vg[:, 0, :], in_=Vg[:, vpv, :], r=["Vg"], w=["Vg"])
                A("activation", out=KT[:, 128:128 + N], in_=pb[b][:, 0:N], func=AF.Copy, r=[f"pb{b}"], w=["KT"])
                KTprev[0] = N
                Vprev[0] = nt
            else:
                A("activation", out=KT[:, 0:128], in_=pb[b][:, 0:N], func=AF.Copy, r=[f"pb{b}"], w=["KT"])
                KTprev[0] = 0
                Vprev[0] = 0
            for j in range(nt):
                t = t0 + j
                vs = 0 if halo else j + 1
                bV = bank()
                for k in range(8):
                    PE("matmul", pb[bV][:, 0:128], lhsT=xnT[:, k, j * 128:(j + 1) * 128], rhs=Win[:, k, C_V:C_V + 128],
                                                    start=(k == 0), stop=(k == 7),
                       r=[f"Win{k}_1", "xnT"], w=[f"pb{bV}"], sig=(k == 7))
                V("tensor_copy", out=Vg[:, vs, :].rearrange("p (g e) -> p g e", e=66)[:, :, 0:64],
                                                 in_=pb[bV][:, 0:128].rearrange("p (g d) -> p g d", d=64),
                  r=[f"pb{bV}"], w=["Vg"])
                if t == NT and not NOLAST:
                    A("activation", out=vout[:], in_=pb[bV][:, 0:128], func=AF.Copy, r=[f"pb{bV}"], w=["vout"])
                    bK = bank()
                    for k in range(8):
                        PE("matmul", pb[bK][:, 0:128], lhsT=xnT[:, k, j * 128:(j + 1) * 128], rhs=Win[:, k, C_K:C_K + 128],
                                                        start=(k == 0), stop=(k == 7),
                           r=[f"Win{k}_1", "xnT"], w=[f"pb{bK}"], sig=(k == 7))
                    A("activation", out=kout[:], in_=pb[bK][:, 0:128], func=AF.Copy, r=[f"pb{bK}"], w=["kout"])
            if halo or SUB < 2:
                return
            for cc in range(4):
                b = fm_chunk(C_Q + cc * 128, N)
                V("tensor_scalar", out=QT[0][0:64, cc, 0:N], in0=pb[b][0:64, 0:N], scalar1=0.125, scalar2=None, op0=ALU.mult,
                  r=[f"pb{b}"], w=["QT0"])
                V("tensor_scalar", out=QT[1][64:128, cc, 0:N], in0=pb[b][64:128, 0:N], scalar1=0.125, scalar2=None, op0=ALU.mult,
                  r=[f"pb{b}"], w=["QT1"])
            if SUB < 3:
                return
            rnn_chain(N, t0)
            if SUB < 5:
                return
            for j in range(nt):
                t = t0 + j
                bG = bank()
                for k in range(8):
                    PE("matmul", pb[bG][:, 0:512], lhsT=xnT[:, k, j * 128:(j + 1) * 128], rhs=Win[:, k, C_GA:C_GA + 512],
                                                    start=(k == 0), stop=(k == 7),
                       r=[f"Win{k}_1", "xnT"], w=[f"pb{bG}"], sig=(k == 7))
                s = th_slot()
                A("activation", out=th[s][:], in_=pb[bG][:], func=AF.Tanh, scale=0.5, r=[f"pb{bG}"], w=[f"th{s}"])
                V("scalar_tensor_tensor", out=ua[:], in0=th[s][:], scalar=1.0, in1=pb[bG][:], op0=ALU.add, op1=ALU.mult,
                  r=[f"pb{bG}", f"th{s}"], w=["ua"])
                if SUB >= 6:
                    attention(t, j)

        def attention(t, j):
            ps = t % 2
            for blk in range(2):
                kc = (j + blk) * 128
                for g in range(2):
                    b = bank()
                    bias_ap = BiasF[:, g, :] if (t == 1 and blk == 0) else BiasT[:, blk, g, :]
                    bname = "BiasF" if (t == 1 and blk == 0) else "BiasT"
                    PE("matmul", pb[b][:].rearrange("p (c q) -> p c q", q=128), lhsT=KT[:, kc:kc + 128],
                                                           rhs=QT[g][:, :, j * 128:(j + 1) * 128], start=True, stop=False,
                       r=["KT", f"QT{g}"], w=[f"pb{b}"])
                    PE("matmul", pb[b][:], lhsT=ident[:], rhs=bias_ap, start=False, stop=True,
                       r=["ident", bname], w=[f"pb{b}"], sig=True)
                    A("activation", out=PT[ps][:, blk, g, :], in_=pb[b][:], func=AF.Exp, r=[f"pb{b}"], w=[f"PT{ps}"])
            if SUB < 7:
                return
            bO = []
            for g in range(2):
                b = bank()
                bO.append(b)
                for cc in range(4):
                    for blk in range(2):
                        PE("matmul", pb[b][:, cc * 66:cc * 66 + 66], lhsT=PT[ps][:, blk, g, cc * 128:(cc + 1) * 128],
                                                                        rhs=Vg[:, j + blk, g * 66:(g + 1) * 66], start=(blk == 0), stop=(blk == 1),
                           r=[f"PT{ps}", "Vg"], w=[f"pb{b}"], sig=(cc == 3 and blk == 1))
                V("scalar_tensor_tensor", out=dsum[:, :, g], in0=pb[b][:, 0:264].rearrange("p (c e) -> p c e", e=66)[:, :, 64],
                                                             scalar=2.0, in1=sinkexp2[:, g, :], op0=ALU.mult, op1=ALU.add,
                  r=[f"pb{b}", "sinkexp2"], w=["dsum"])
            if SUB < 8:
                return
            V("reciprocal", out=rden[:], in_=dsum[:], r=["dsum"], w=["rden"])
            G("tensor_tensor", out=sgr[:].rearrange("p (c g d) -> p c g d", g=2, d=64), in0=ua[:].rearrange("p (c g d) -> p c g d", g=2, d=64),
                                        in1=rden[:].unsqueeze(3).broadcast_to([128, 4, 2, 64]), op=ALU.mult,
              r=["ua", "rden"], w=["sgr"])
            for g in range(2):
                b = bO[g]
                V("tensor_tensor", out=att_o[:].rearrange("p (c g d) -> p c g d", g=2, d=64)[:, :, g, :],
                                                      in0=pb[b][:, 0:264].rearrange("p (c e) -> p c e", e=66)[:, :, 0:64],
                                                      in1=sgr[:].rearrange("p (c g d) -> p c g d", g=2, d=64)[:, :, g, :], op=ALU.mult,
                  r=[f"pb{b}", "sgr"], w=["att_o"])
            b = bank()
            for cc in range(4):
                PE("transpose", out=pbT[b][:, cc, :], in_=att_o[:, cc * 128:(cc + 1) * 128], identity=ident[:],
                   r=["att_o", "ident"], w=[f"pb{b}"], sig=(cc == 3))
            A("activation", out=attT[:, :, (t - 1) * 128:t * 128], in_=pbT[b][:, 0:4, :], func=AF.Copy, r=[f"pb{b}"], w=[f"attT{t}"])

        rnn_tail = []
        first_scan = [True]
        first_A = [True]

        def flush_rnn_tail():
            for gs, N, t0, pre in rnn_tail:
                for gi, g in enumerate(gs):
                    A("activation", out=a2_b[:, gi, 0:N], in_=a2_b[:, gi, 0:N], func=AF.Sqrt, scale=-1.0 / 16.0, bias=1.0 / 16.0,
                      r=[f"a2{gi}"], w=[f"a2{gi}"])
            for gs, N, t0, pre in rnn_tail:
                c0 = (t0 - 1) * 128
                for gi, g in enumerate(gs):
                    G("tensor_tensor", out=t1_b[:, gi, 0:N], in0=t1_b[:, gi, 0:N], in1=a2_b[:, gi, 0:N], op=ALU.mult,
                      r=[f"t1{gi}", f"a2{gi}"], w=[f"t1{gi}"])
                    fs = first_scan[0]
                    fsA = first_A[0]
                    V("tensor_tensor_scan", out=hl[:, 0:N], data0=a_b[:, gi, 0:N], data1=t1_b[:, gi, 0:N],
                                                                              initial=(0.0 if fs else car[:, 4 + g:5 + g]), op0=ALU.mult, op1=ALU.add,
                      r=[f"a{gi}", f"t1{gi}", "car"], w=["hl"])
                    V("tensor_copy", out=car[:, 4 + g:5 + g], in_=hl[:, N - 1:N], r=["hl", "car"], w=["car"])
                    if pre:
                        continue
                    V("tensor_tensor_scan", out=Ac[:, 0:N], data0=a_b[:, gi, 0:N], data1=cst[:, 1:2].broadcast_to([128, N]),
                                                                              initial=(1.0 if fsA else car[:, g:g + 1]), op0=ALU.mult, op1=ALU.add,
                      r=[f"a{gi}", "cst", "car"], w=["Ac"])
                    V("tensor_copy", out=car[:, g:g + 1], in_=Ac[:, N - 1:N], r=["Ac", "car"], w=["car"])
                    G("tensor_tensor", out=P1[:, g, c0:c0 + N], in0=hl[:, 0:N], in1=u_b[:, gi, 0:N], op=ALU.mult,
                      r=["hl", f"u{gi}"], w=[f"P1_{g}_{c0}"])
                    G("tensor_tensor", out=P2[:, g, c0:c0 + N], in0=Ac[:, 0:N], in1=u_b[:, gi, 0:N], op=ALU.mult,
                      r=["Ac", f"u{gi}"], w=[f"P2_{g}_{c0}"])
                if gs[1] == 3:
                    first_scan[0] = False
                    if not pre:
                        first_A[0] = False
            rnn_tail.clear()

        Nprev = [0]
        KTprev = [0]
        Vprev = [0]
        G("memset", xr[:], 0.0, w=["xr"])
        ld(flag[:], flag_d, "flag")
        for pb_i in range(NPT // 4):
            nxt = (17 + 4 * (pb_i + 1), 4) if pb_i + 1 < NPT // 4 else (0, 1)
            block(100 + pb_i, 17 + 4 * pb_i, 4, pre=True, nxt=nxt)
            if pb_i % 4 == 3:
                kf = pb_i // 4
                V("tensor_scalar", out=car[:, 4:8], in0=car[:, 4:8], scalar1=flag[:, kf:kf + 1], scalar2=None, op0=ALU.mult,
                  r=["car", "flag"], w=["car"])
        Nprev[0] = 0
        blocks = [(0, 0, 1), (1, 1, 4), (2, 5, 4), (3, 9, 4), (4, 13, 4)]
        for bi, t0, nt in blocks:
            if STAGE >= (1 if bi == 0 else 2 if bi == 1 else 3) and bi <= NBLK:
                block(bi, t0, nt)

        if STAGE >= 3 and not NOOUT:
            for g in range(4):
                out_handles.append(P.dma("sync", nconv_d[:, g * 128:(g + 1) * 128].rearrange("t p -> p t"), xrt[:, g, 1:4], f"o_nconv{g}", reads=["xrt"],
                                         allow_slow_non_contiguous=True, group="outs"))
            out_handles.append(P.dma("sync", nk_d, kout[:], "o_nk", reads=["kout"], group="outs"))
            out_handles.append(P.dma("sync", nv_d, vout[:], "o_nv", reads=["vout"], group="outs"))

        if STAGE >= 4:
            G("memset", h0[:], 0.0, w=["h0"])
            V("tensor_scalar", out=hfin[:], in0=car[:, 4:8], scalar1=2.0, scalar2=None, op0=ALU.mult, r=["car"], w=["hfin"])
            out_handles.append(P.dma("sync", nrnn_d.rearrange("(g p) -> p g", p=128), hfin[:], "o_nrnn", reads=["hfin"],
                                     allow_slow_non_contiguous=True, group="outs"))

        def sample_path():
            F = lambda ap: ap.bitcast(F32)
            xnTs = QT[0][:].rearrange("p c n -> p (c n)")[:, 0:1024].rearrange("p (k t) -> p k t", t=128)
            Kn = F(xnT[:].rearrange("p k n -> p (k n)")).rearrange("p (s c) -> p s c", c=128)
            Vn = [F(PT[h][:].rearrange("p a b n -> p (a b n)")).rearrange("p (s c) -> p s c", c=128) for h in range(2)]
            KnT = [a_b[:].rearrange("p a n -> p (a n)").rearrange("p (s c) -> p s c", c=128),
                   a2_b[:].rearrange("p a n -> p (a n)").rearrange("p (s c) -> p s c", c=128)]
            xcT = u_b[:].rearrange("p a n -> p (a n)")[:, 0:512].rearrange("p (g t) -> p g t", t=128)
            Qm = th[0][:, 0:128].rearrange("p (s q) -> p s q", q=8)
            Pt = hl[:, 0:128]
            sc = Ac[:, 0:128]
            rds = Ac[:, 128:256]
            attn = hl[:, 128:256]
            xcs = t1_b[:, 0, :]
            tA = t1_b[:, 1, :]
            tB = xcv[:, 0, :]
            tC = xcv[:, 1, :]
            tD = ua[:]
            tE = sgr[:]
            mixtm = xn[0][:, 0:512]
            mixTs = xn[1][:].rearrange("p (k t) -> p k t", t=128)
            R16 = slice(0, SB)
            flat32 = lambda t, pat: F(t[:].rearrange(pat))
            hosts = [(xt[1][:], "xt1"), (flat32(QT[1], "p c n -> p (c n)"), "QT1"), (flat32(BiasT, "p a b n -> p (a b n)"), "BiasT"),
                     (flat32(diagw, "p g t m -> p (g t m)"), "diagw")]
            rp, rpn = [], []
            for hap, hname in hosts:
                for i2 in range(2):
                    rp.append(hap[R16, i2 * 512:(i2 + 1) * 512])
                    rpn.append(hname)
            for i8 in range(8):
                P.dma("sync", rp[i8], rowp_d[:, i8 * 512:(i8 + 1) * 512], f"rowp{i8}", writes=[rpn[i8]], group="samp")
            sst_t = [tt[R16, 0:512], tt[R16, 512:1024], F(BiasF[:].rearrange("p g n -> p (g n)"))[R16, 0:512]]
            sst_n = ["tt", "tt", "BiasF"]
            for t3 in range(3):
                P.dma("sync", sst_t[t3], sconv_d[:, t3 * 512:(t3 + 1) * 512], f"sst{t3}", writes=[sst_n[t3]], group="samp")
            hprev = th[1][R16, :]
            P.dma("sync", hprev, srnn_d, "hprev", writes=["th1"], group="samp")
            kv_s = F(att_o[:])[R16, :]
            ld(ident32[:], ident_d, "ident32", group="samp")
            ld(biass[:], biass_d, "biass", group="samp")
            def win(src, h):
                return src[1 + 8 * h * 128:1 + (8 * h + 8) * 128, :].rearrange("(s k) c -> k s c", k=128)
            big = [None]
            for h in range(2):
                big[0] = P.dma("sync", Kn[:, 8 * h:8 * h + 8, :], win(ck_d, h), f"Kn_a{h}", writes=["xnT", f"KnH{h}"], deps=[big[0]])
                big[0] = P.dma("sync", Vn[h][:], win(cv_d, h), f"Vn_a{h}", writes=[f"PT{h}"], deps=[big[0]])
            if SS < 1:
                return
            if S1 < 1:
                return
            pass
            if S1 < 2:
                return
            P.dma("sync", xt[0][:], xs_d, "xt0", writes=["xt0"])
            if S1 < 3:
                return
            pass
            if S1 < 4:
                return
            A("activation", out=xn[0][:], in_=xt[0][:], func=AF.Square, accum_out=sm[:, 0:1], r=["xt0", "sm"], w=["xn0", "sm"])
            if S1 < 5:
                return
            V("tensor_scalar", out=sm[:, 1:2], in0=sm[:, 0:1], scalar1=1.0 / D, scalar2=EPS, op0=ALU.mult, op1=ALU.add, r=["sm"], w=["sm"])
            if S1 < 6:
                return
            A("activation", out=sm[:, 1:2], in_=sm[:, 1:2], func=AF.Sqrt, r=["sm"], w=["sm"])
            if S1 < 7:
                return
            V("reciprocal", out=sm[:, 2:3], in_=sm[:, 1:2], r=["sm"], w=["sm"])
            if S1 < 8:
                return
            A("activation", out=xn[0][:], in_=xt[0][:], func=AF.Copy, scale=sm[:, 2:3], r=["xt0", "sm"], w=["xn0"])
            if S1 < 9:
                return
            b = bank()
            for k in range(8):
                PE("transpose", out=pbT[b][:, k, :], in_=xn[0][:, k * 128:(k + 1) * 128], identity=ident[:], r=["xn0", "ident"], w=[f"pb{b}"], sig=(k == 7))
            if S1 < 10:
                return
            V("tensor_tensor", out=xnTs, in0=pbT[b][:, :, :], in1=gpre[:].unsqueeze(2).broadcast_to([128, 8, 128]), op=ALU.mult,
              r=[f"pb{b}", "gpre"], w=["QT0"])

            def tm_cols(c0, w):
                bb = bank()
                for k in range(8):
                    PE("matmul", pb[bb][:, 0:w], lhsT=xnTs[:, k, :], rhs=Win[:, k, c0:c0 + w], start=(k == 0), stop=(k == 7),
                       r=["QT0"] + [f"Win{k}_{hh}" for hh in sorted({c0 // 1152, (c0 + w - 1) // 1152})], w=[f"pb{bb}"], sig=(k == 7))
                return bb

            def fm_cols(c0):
                bb = bank()
                hh = c0 // 1152
                for k in range(8):
                    PE("matmul", pb[bb][:, 0:128], lhsT=Win[:, k, c0:c0 + 128], rhs=xnTs[:, k, :], start=(k == 0), stop=(k == 7),
                       r=["QT0", f"Win{k}_{hh}"], w=[f"pb{bb}"], sig=(k == 7))
                return bb

            if SS < 2:
                return
            bKV = tm_cols(C_K, 256)
            A("activation", out=kv_s, in_=pb[bKV][R16, 0:256], func=AF.Copy, r=[f"pb{bKV}"], w=["att_o"])
            P.dma("sync", kvb.ap(), kv_s, "kvb", reads=["att_o"], writes=["kvb"])
            P.dma("sync", Kn[127:128], kvb.ap()[:, 0:128].unsqueeze(0), "Kn_b", reads=["kvb", "xnT", "KnH0", "KnH1"], writes=["xnT"], group="samp2")
            for h in range(2):
                P.dma("sync", Vn[h][127:128], kvb.ap()[8 * h:8 * h + 8, 128:256].unsqueeze(0), f"Vn_b{h}", reads=["kvb", f"PT{h}"], writes=[f"PT{h}"], group="samp2")
            for h in range(2):
                big[0] = P.dma("sync", nks_d[8 * h:8 * h + 8].rearrange("s k c -> k s c"), Kn[:, 8 * h:8 * h + 8, :], f"o_nks{h}", reads=["xnT", f"KnH{h}"], deps=[big[0]])
                out_handles.append(big[0])
                big[0] = P.dma("sync", nvs_d[8 * h:8 * h + 8].rearrange("s k c -> k s c"), Vn[h][:], f"o_nvs{h}", reads=[f"PT{h}"], deps=[big[0]])
                out_handles.append(big[0])
            if SS < 3:
                return
            for cc in range(4):
                bq = fm_cols(C_Q + cc * 128)
                V("tensor_scalar", out=Qm[0:64, :, 2 * cc], in0=pb[bq][0:64, 0:SB], scalar1=0.125, scalar2=None, op0=ALU.mult, r=[f"pb{bq}"], w=["th0"])
                V("tensor_scalar", out=Qm[64:128, :, 2 * cc + 1], in0=pb[bq][64:128, 0:SB], scalar1=0.125, scalar2=None, op0=ALU.mult, r=[f"pb{bq}"], w=["th0"])
            for cc in range(4):
                bg = fm_cols(C_GA + cc * 128)
                A("activation", out=tD[:, 0:SB], in_=pb[bg][:, 0:SB], func=AF.Tanh, scale=0.5, r=[f"pb{bg}"], w=["ua"])
                V("scalar_tensor_tensor", out=uaT[:, cc, :], in0=tD[:, 0:SB], scalar=1.0, in1=pb[bg][:, 0:SB], op0=ALU.add, op1=ALU.mult,
                  r=[f"pb{bg}", "ua"], w=["uaT"])
            if SS < 4:
                return
            bXR = tm_cols(C_XR, 512)
            bGR = tm_cols(C_GR, 512)
            cw = lambda tap: rp[tap]
            cb_r, bga_r, bgx_r, lam_r = rp[4], rp[5], rp[6], rp[7]
            A("activation", out=tB[R16], in_=pb[bXR][R16, :], func=AF.Copy, r=[f"pb{bXR}"], w=["xcv0"])
            out_handles.append(P.dma("sync", nconvs_d[:, 0:512], sst_t[1], "o_ncs_a", reads=["tt"]))
            out_handles.append(P.dma("sync", nconvs_d[:, 512:1024], sst_t[2], "o_ncs_c", reads=["BiasF"], group="outs"))
            out_handles.append(P.dma("sync", nconvs_d[:, 1024:1536], tB[R16], "o_ncs_b", reads=["xcv0"], group="outs"))
            V("tensor_tensor", out=xcs[R16], in0=tB[R16], in1=cw(3), op=ALU.mult, r=["xcv0", rpn[3]], w=["t10"])
            for tap in range(3):
                V("tensor_tensor", out=tA[R16], in0=sst_t[tap], in1=cw(tap), op=ALU.mult, r=[sst_n[tap], rpn[tap]], w=["t11"])
                V("tensor_tensor", out=xcs[R16], in0=xcs[R16], in1=tA[R16], op=ALU.add, r=["t10", "t11"], w=["t10"])
            V("tensor_tensor", out=xcs[R16], in0=xcs[R16], in1=cb_r, op=ALU.add, r=["t10", rpn[4]], w=["t10"])
            b = bank()
            for g in range(4):
                PE("transpose", out=pb[b][:, g * 128:(g + 1) * 128], in_=xcs[:, g * 128:(g + 1) * 128], identity=ident32[:],
                   r=["t10", "ident32"], w=[f"pb{b}"], sig=(g == 3))
            V("tensor_copy", out=xcT, in_=pb[b][:].rearrange("p (g t) -> p g t", t=128), r=[f"pb{b}"], w=["u0"])
            bA, bX = bank(), bank()
            for g in range(4):
                PE("matmul", pb[bA][:, g * 128:(g + 1) * 128], lhsT=xcT[:, g, :], rhs=bda[:, g, :], start=True, stop=True, r=["u0", "bda"], w=[f"pb{bA}"], sig=(g == 3))
            for g in range(4):
                PE("matmul", pb[bX][:, g * 128:(g + 1) * 128], lhsT=xcT[:, g, :], rhs=bdx[:, g, :], start=True, stop=True, r=["u0", "bdx"], w=[f"pb{bX}"], sig=(g == 3))
            A("activation", out=tC[R16], in_=lam_r, func=AF.Exp, scale=-1.0, r=[rpn[7]], w=["xcv1"])
            A("activation", out=tC[R16], in_=tC[R16], func=AF.Ln, bias=1.0, r=["xcv1"], w=["xcv1"])
            V("tensor_scalar", out=tC[R16], in0=tC[R16], scalar1=-4.0, scalar2=None, op0=ALU.mult, r=["xcv1"], w=["xcv1"])
            V("tensor_tensor", out=tA[R16], in0=pb[bA][R16, :], in1=bga_r, op=ALU.add, r=[f"pb{bA}", rpn[5]], w=["t11"])
            A("activation", out=tA[R16], in_=tA[R16], func=AF.Tanh, scale=0.5, r=["t11"], w=["t11"])
            V("scalar_tensor_tensor", out=tA[R16], in0=tA[R16], scalar=1.0, in1=tC[R16], op0=ALU.add, op1=ALU.mult, r=["t11", "xcv1"], w=["t11"])
            A("activation", out=tA[R16], in_=tA[R16], func=AF.Exp, r=["t11"], w=["t11"])
            V("tensor_tensor", out=tC[R16], in0=pb[bX][R16, :], in1=bgx_r, op=ALU.add, r=[f"pb{bX}", rpn[6]], w=["xcv1"])
            A("activation", out=tC[R16], in_=tC[R16], func=AF.Tanh, scale=0.5, r=["xcv1"], w=["xcv1"])
            V("scalar_tensor_tensor", out=tC[R16], in0=tC[R16], scalar=1.0, in1=xcs[R16], op0=ALU.add, op1=ALU.mult, r=["xcv1", "t10"], w=["xcv1"])
            A("activation", out=tE[R16], in_=pb[bGR][R16, :], func=AF.Tanh, scale=0.5, r=[f"pb{bGR}"], w=["sgr"])
            V("scalar_tensor_tensor", out=tE[R16], in0=tE[R16], scalar=1.0, in1=pb[bGR][R16, :], op0=ALU.add, op1=ALU.mult, r=["sgr", f"pb{bGR}"], w=["sgr"])
            V("tensor_tensor", out=tD[R16], in0=tA[R16], in1=tA[R16], op=ALU.mult, r=["t11"], w=["ua"])
            A("activation", out=tD[R16], in_=tD[R16], func=AF.Sqrt, scale=-1.0 / 16.0, bias=1.0 / 16.0, r=["ua"], w=["ua"])
            V("tensor_tensor", out=tC[R16], in0=tC[R16], in1=tD[R16], op=ALU.mult, r=["xcv1", "ua"], w=["xcv1"])
            V("tensor_tensor", out=tA[R16], in0=tA[R16], in1=hprev, op=ALU.mult, r=["t11", "th1"], w=["t11"])
            V("scalar_tensor_tensor", out=tC[R16], in0=tC[R16], scalar=2.0, in1=tA[R16], op0=ALU.mult, op1=ALU.add, r=["xcv1", "t11"], w=["xcv1"])
            out_handles.append(P.dma("sync", nrnns_d, tC[R16], "o_nrs", reads=["xcv1"], group="outs"))
            V("scalar_tensor_tensor", out=mixtm[R16], in0=tC[R16], scalar=0.5, in1=tE[R16], op0=ALU.mult, op1=ALU.mult, r=["xcv1", "sgr"], w=["xn0"])
            b = bank()
            for g in range(4):
                PE("transpose", out=pbT[b][:, g, :], in_=mixtm[:, g * 128:(g + 1) * 128], identity=ident[:], r=["xn0", "ident"], w=[f"pb{b}"], sig=(g == 3))
            V("tensor_copy", out=mixTs[:, 0:4, :], in_=pbT[b][:, 0:4, :], r=[f"pb{b}"], w=["xn1"])
            if SS < 5:
                return
            for q4 in range(4):
                b = bank()
                for i4 in range(4):
                    sq = q4 * 4 + i4
                    PE("transpose", out=pb[b][:, i4 * 128:(i4 + 1) * 128], in_=Kn[:, sq, :], identity=ident32[:], r=["xnT", "ident32"], w=[f"pb{b}"], sig=(i4 == 3))
                hh, s0 = q4 // 2, (q4 % 2) * 4
                V("tensor_copy", out=KnT[hh][:, s0:s0 + 4, :], in_=pb[b][:].rearrange("p (s k) -> p s k", k=128), r=[f"pb{b}"], w=[f"a{hh}" if hh == 0 else "a20"])
            bS = bank()
            for sq in range(SB):
                hh, s0 = sq // 8, sq % 8
                PE("matmul", pb[bS][:, sq * 8:(sq + 1) * 8], lhsT=KnT[hh][:, s0, :], rhs=Qm[:, sq, :], start=True, stop=True,
                   r=["a0" if hh == 0 else "a20", "th0"], w=[f"pb{bS}"], sig=(sq == SB - 1))
            V("tensor_tensor", out=sc.rearrange("p (s q) -> p s q", q=8), in0=pb[bS][:, 0:128].rearrange("p (s q) -> p s q", q=8),
              in1=biass[:].unsqueeze(1).broadcast_to([128, SB, 8]), op=ALU.add, r=[f"pb{bS}", "biass"], w=["Ac"])
            A("activation", out=Pt, in_=sc, func=AF.Exp, r=["Ac"], w=["hl"])
            bO = bank()
            for sq in range(SB):
                hh, s0 = sq // 8, sq % 8
                PE("matmul", pb[bO][:, sq * 8:(sq + 1) * 8], lhsT=Vn[hh][:, s0, :], rhs=Pt[:, sq * 8:(sq + 1) * 8], start=True, stop=True,
                   r=[f"PT{hh}", "hl"], w=[f"pb{bO}"], sig=(sq == SB - 1))
            bD = bank()
            PE("matmul", pb[bD][:, 0:128], lhsT=ones32[:], rhs=Pt, start=True, stop=True, r=["ones32", "hl"], w=[f"pb{bD}"], sig=True)
            V("tensor_copy", out=sm[:, 8:16].rearrange("p (c g) -> p c g", g=2), in_=sinkexp2[:].rearrange("p g c -> p c g"), r=["sinkexp2", "sm"], w=["sm"])
            V("scalar_tensor_tensor", out=rds.rearrange("p (s q) -> p s q", q=8), in0=pb[bD][:, 0:128].rearrange("p (s q) -> p s q", q=8),
              scalar=2.0, in1=sm[:, 8:16].unsqueeze(1).broadcast_to([128, SB, 8]), op0=ALU.mult, op1=ALU.add,
              r=[f"pb{bD}", "sm"], w=["Ac"])
            V("reciprocal", out=rds, in_=rds, r=["Ac"], w=["Ac"])
            V("tensor_tensor", out=attn, in0=pb[bO][:, 0:128], in1=rds, op=ALU.mult, r=[f"pb{bO}", "Ac"], w=["hl"])
            av = attn.rearrange("p (s c g) -> p c s g", c=4, g=2)
            V("tensor_tensor", out=mixTs[0:64, 4:8, 0:SB], in0=av[0:64, :, :, 0], in1=uaT[0:64, :, :], op=ALU.mult, r=["hl", "uaT"], w=["xn1"])
            V("tensor_tensor", out=mixTs[64:128, 4:8, 0:SB], in0=av[64:128, :, :, 1], in1=uaT[64:128, :, :], op=ALU.mult, r=["hl", "uaT"], w=["xn1"])
            if SS < 6:
                return
            bY = [bank(), bank()]
            for kk in range(8):
                for hf in range(2):
                    PE("matmul", pb[bY[hf]][:], lhsT=mixTs[:, kk, :], rhs=Wout[:, kk, hf * 512:(hf + 1) * 512], start=(kk == 0), stop=(kk == 7),
                       r=["xn1", f"Wout{kk}"], w=[f"pb{bY[hf]}"], sig=(kk == 7))
            for hf in range(2):
                A("activation", out=junk2[:], in_=pb[bY[hf]][:], func=AF.Square, accum_out=sm[:, 3 + hf:4 + hf], r=[f"pb{bY[hf]}", "sm"], w=["junk2", "sm"])
            V("tensor_tensor", out=sm[:, 5:6], in0=sm[:, 3:4], in1=sm[:, 4:5], op=ALU.add, r=["sm"], w=["sm"])
            V("tensor_scalar", out=sm[:, 5:6], in0=sm[:, 5:6], scalar1=1.0 / D, scalar2=EPS, op0=ALU.mult, op1=ALU.add, r=["sm"], w=["sm"])
            A("activation", out=sm[:, 5:6], in_=sm[:, 5:6], func=AF.Sqrt, r=["sm"], w=["sm"])
            V("reciprocal", out=sm[:, 6:7], in_=sm[:, 5:6], r=["sm"], w=["sm"])
            for hf in range(2):
                V("scalar_tensor_tensor", out=tt[:, hf * 512:(hf + 1) * 512], in0=pb[bY[hf]][:], scalar=sm[:, 6:7], in1=gpost[:, hf * 512:(hf + 1) * 512],
                  op0=ALU.mult, op1=ALU.mult, r=[f"pb{bY[hf]}", "sm", "gpost"], w=["tt"])
            V("tensor_tensor", out=xt[0][R16, :], in0=tt[R16, :], in1=xt[0][R16, :], op=ALU.add, r=["tt", "xt0"], w=["xt0"])
            out_handles.append(P.dma("sync", ys_d, xt[0][R16, :], "o_ys", reads=["xt0"]))


        if STAGE >= 5:
            F2 = lambda ap: ap.bitcast(F32)
            xsl = [(xt[0][:], ["xt0"]), (xt[1][:], ["xt1"]),
                   (F2(PT[0][:].rearrange("p a b n -> p (a b n)")), ["PT0"]), (F2(PT[1][:].rearrange("p a b n -> p (a b n)")), ["PT1"])]
            tsl = [(tt[:], ["tt"]), (a_b[:].rearrange("p a n -> p (a n)"), ["a0", "a1"]), (a2_b[:].rearrange("p a n -> p (a n)"), ["a20", "a21"])]
            bYs = {}

            def p2_front(i):
                s = i % 2
                xa, xnm = xsl[i % 4]
                c0 = i * 128
                blk0 = (i // 4) * 512
                P.dma("sync", xa, xc_d[PRE + (i + 1) * 128:PRE + (i + 2) * 128, :], f"x2_{i % 4}", writes=xnm)
                for g in range(4):
                    V("scalar_tensor_tensor", out=mixT[s][:, g, :], in0=P2[:, g, c0:c0 + 128], scalar=h0[:, g:g + 1], in1=P1[:, g, c0:c0 + 128],
                      op0=ALU.mult, op1=ALU.add, r=[f"P1_{g}_{blk0}", f"P2_{g}_{blk0}", "h0"], w=[f"mixT{s}"])
                bY = [bank(), bank()]
                bYs[i] = bY
                for kk in range(8):
                    lhs = mixT[s][:, kk, :] if kk < 4 else attT[:, kk - 4, c0:c0 + 128]
                    rn = [f"mixT{s}"] if kk < 4 else [f"attT{i + 1}"]
                    for hf in range(2):
                        PE("matmul", pb[bY[hf]][:], lhsT=lhs, rhs=Wout[:, kk, hf * 512:(hf + 1) * 512], start=(kk == 0), stop=(kk == 7),
                           r=rn + [f"Wout{kk}"], w=[f"pb{bY[hf]}"], sig=(kk == 7))

            def p2_back(i):
                xa, xnm = xsl[i % 4]
                ta, tnm = tsl[i % 3]
                c0 = i * 128
                bY = bYs[i]
                for hf in range(2):
                    A("activation", out=junk2[:], in_=pb[bY[hf]][:], func=AF.Square, accum_out=ss2[:, i, hf:hf + 1],
                      r=[f"pb{bY[hf]}"], w=["junk2", f"ss2_{i}_{hf}"])
                G("tensor_tensor", out=ms2[:, i:i + 1], in0=ss2[:, i, 0:1], in1=ss2[:, i, 1:2], op=ALU.add,
                  r=[f"ss2_{i}_0", f"ss2_{i}_1"], w=[f"ms2_{i}"])
                G("tensor_scalar", out=ms2[:, i:i + 1], in0=ms2[:, i:i + 1], scalar1=1.0 / D, scalar2=EPS, op0=ALU.mult, op1=ALU.add,
                  r=[f"ms2_{i}"], w=[f"ms2_{i}"])
                G("tensor_tensor", out=rstd2[:, i:i + 1], in0=ms2[:, i:i + 1], in1=cst[:, 0:1], op=ALU.pow,
                  r=[f"ms2_{i}", "cst"], w=[f"rstd2_{i}"])
                for hf in range(2):
                    V("scalar_tensor_tensor", out=ta[:, hf * 512:(hf + 1) * 512], in0=pb[bY[hf]][:], scalar=rstd2[:, i:i + 1],
                      in1=gpost[:, hf * 512:(hf + 1) * 512], op0=ALU.mult, op1=ALU.mult,
                      r=[f"pb{bY[hf]}", f"rstd2_{i}", "gpost"] + tnm, w=tnm)
                G("tensor_tensor", out=xa, in0=ta, in1=xa, op=ALU.add, r=tnm + xnm, w=xnm)
                out_handles.append(P.dma("sync", y_d[c0:c0 + 128, :], xa, f"o_y{i % 4}", reads=xnm))

            p2_front(0)
            for i in range(NT):
                if i + 1 < NT:
                    p2_front(i + 1)
                p2_back(i)

        if STAGE >= 6:
            G("memset", th[0][:, 0:128], 0.0, w=["th0"])
            G("memset", t1_b[:], 0.0, w=["t10", "t11"])
            G("memset", xn[1][:], 0.0, w=["xn1"])
            sample_path()

        P.wait_all("sync", out_handles)
        P.emit()
    return nc


def _t5_bucket(dist):
    dist = np.maximum(dist, 0)
    max_exact = 16
    d = np.maximum(dist, 1).astype(np.float32)
    large = max_exact + (np.log(d / np.float32(max_exact)) / np.float32(np.log(128 / max_exact)) * np.float32(32 - max_exact)).astype(np.int32)
    large = np.minimum(large, 31)
    return np.where(dist < max_exact, dist, large)


_NC_CACHE = {}


def kernel(x_prompt, x_sample, state_conv, state_rnn, cache_k_win, cache_v_win,
           norm_pre, norm_post, w_in, conv_w, conv_b, w_gate_a, b_gate_a, w_gate_x, b_gate_x,
           lru_lambda, attn_sinks, rel_bias, w_out):
    f32 = np.float32
    x_prompt = np.asarray(x_prompt, f32)
    w_in0 = np.asarray(w_in, f32)[0]
    w_out0 = np.asarray(w_out, f32)[0]
    qperm = np.concatenate([np.arange(h * 64, h * 64 + 64) for h in LPOS])
    cols = np.concatenate([np.arange(0, 1024), 1024 + qperm, np.arange(1536, 1792), 1792 + qperm])
    w_in_p = np.ascontiguousarray(w_in0[:, cols])
    rows = np.concatenate([np.arange(0, 512), 512 + qperm])
    w_out_p = np.ascontiguousarray(w_out0[rows, :])

    def pg(v):
        return np.ascontiguousarray(np.asarray(v, f32).reshape(4, 128).T)

    gpre = np.ascontiguousarray(np.asarray(norm_pre, f32)[0].reshape(8, 128).T)
    gpost = np.ascontiguousarray(np.broadcast_to(np.asarray(norm_post, f32)[0][None, :], (128, D)))
    cw = np.asarray(conv_w, f32)[0]
    convw = np.ascontiguousarray(cw.reshape(4, 4, 128).transpose(2, 1, 0).reshape(128, 16))
    vec4 = np.ascontiguousarray(np.stack([pg(np.asarray(conv_b)[0]), pg(np.asarray(b_gate_a)[0]), pg(np.asarray(b_gate_x)[0]),
                                          pg(np.asarray(lru_lambda)[0])], axis=1).reshape(128, 16))

    def blockdiag(w):
        w = np.asarray(w, f32)[0]
        o = np.zeros((128, 4, 128), f32)
        for g in range(4):
            for h in range(2):
                o[h * 64:(h + 1) * 64, g, h * 64:(h + 1) * 64] = w[2 * g + h]
        return o.reshape(128, 512)

    bda = blockdiag(w_gate_a)
    bdx = blockdiag(w_gate_x)
    hd = np.array([[g * 4 + cc for cc in range(4)] for g in range(2)])
    sinks = np.ascontiguousarray(np.broadcast_to(np.asarray(attn_sinks, f32)[0][hd].reshape(1, 8), (128, 8)))
    rb = np.asarray(rel_bias, f32)
    kk = np.arange(128)[:, None]
    qq = np.arange(128)[None, :]
    biasg = np.zeros((128, 2, 2, 4, 128), f32)
    maskc = np.zeros((128, 2, 128), f32)
    for blk in range(2):
        dist = qq + (128 if blk == 0 else 0) - kk
        valid = (dist >= 0) & (dist < 128)
        bkt = _t5_bucket(np.clip(dist, 0, 127))
        for g in range(2):
            for cc in range(4):
                biasg[:, blk, g, cc, :] = rb[bkt, hd[g, cc]]
        maskc[:, blk, :] = np.where(valid, 0.0, NEG)
    biasg = biasg.reshape(128, 2048)
    maskc = maskc.reshape(128, 256)
    ident = np.eye(128, dtype=f32)

    xs_all = np.asarray(x_sample, f32)[:, 0, :]
    sconv_all = np.asarray(state_conv, f32)[0].reshape(128, 1536)
    srnn_all = np.asarray(state_rnn, f32)[0]
    ck_all = np.asarray(cache_k_win, f32)[0].reshape(128, 128, 128)
    cv_all = np.asarray(cache_v_win, f32)[0].reshape(128, 128, 128)
    rowv = np.concatenate([cw.reshape(-1), np.asarray(conv_b, f32)[0], np.asarray(b_gate_a, f32)[0], np.asarray(b_gate_x, f32)[0],
                           np.asarray(lru_lambda, f32)[0]])
    rowp = np.ascontiguousarray(np.broadcast_to(rowv[None, :], (SB, 4096)))
    posh = np.array([g * 4 + cc for cc in range(4) for g in range(2)])
    biass = np.ascontiguousarray(rb[_t5_bucket(127 - np.arange(128))][:, posh])

    def padrows(a):
        return np.concatenate([a.reshape(SB * 128, 128), np.zeros((128, 128), f32)], axis=0)

    in_maps = []
    for c in range(NCORES):
        b, j = c // 4, c % 4
        xc = np.zeros((PRE + CH + 128, D), f32)
        flag = np.zeros((128, 3), f32)
        for kf in range(3):
            cj = j - 3 + kf
            if cj >= 0:
                xc[kf * CH:(kf + 1) * CH] = x_prompt[b, cj * CH:(cj + 1) * CH]
                flag[:, kf] = 1.0
        xc[PRE + 128:] = x_prompt[b, j * CH:(j + 1) * CH]
        xc[PRE:PRE + 128] = xc[PRE - 128:PRE]
        hmask = np.full((128, 1), 0.0 if j > 0 else NEG, f32)
        sel = np.zeros((128, 8), f32)
        for r in range(NCORES):
            if r // 4 == b and r < c:
                sel[:, r] = 1.0
        in_maps.append({"xc": xc, "w_in": w_in_p, "w_out": w_out_p, "gpre": gpre, "gpost": gpost, "convw": convw, "vec4": vec4,
                        "bda": bda, "bdx": bdx, "sinks": sinks, "biasg": biasg, "maskc": maskc, "hmask": hmask, "sel": sel, "flag": flag,
                        "ident": ident, "xs": np.concatenate([xs_all[c * SB:(c + 1) * SB], np.zeros((128 - SB, D), f32)], axis=0),
                        "sconv": np.ascontiguousarray(sconv_all[c * SB:(c + 1) * SB]), "srnn": np.ascontiguousarray(srnn_all[c * SB:(c + 1) * SB]),
                        "ck": padrows(ck_all[c * SB:(c + 1) * SB]), "cv": padrows(cv_all[c * SB:(c + 1) * SB]),
                        "rowp": rowp, "biass": biass})

    if "nc" not in _NC_CACHE:
        _NC_CACHE["nc"] = build_program()
    nc = _NC_CACHE["nc"]
    res = run_bass_kernel_spmd(nc, in_maps, core_ids=list(range(NCORES)))
    R = res.results

    y_prompt = np.stack([np.concatenate([R[b * 4 + j]["y"] for j in range(4)], axis=0) for b in range(2)], axis=0)
    new_conv_p = np.stack([R[3]["nconv"], R[7]["nconv"]])[None]
    new_rnn_p = np.stack([R[3]["nrnn"], R[7]["nrnn"]])[None]
    new_k_p = np.stack([R[3]["nk"].reshape(128, 2, 64), R[7]["nk"].reshape(128, 2, 64)])[None]
    new_v_p = np.stack([R[3]["nv"].reshape(128, 2, 64), R[7]["nv"].reshape(128, 2, 64)])[None]
    cat = lambda k: np.concatenate([R[c][k] for c in range(NCORES)], axis=0)
    y_sample = cat("ys").reshape(128, 1, D).astype(f32)
    new_conv_s = cat("nconvs").reshape(1, 128, 3, 512).astype(f32)
    new_rnn_s = cat("nrnns").reshape(1, 128, 512).astype(f32)
    new_k_s = cat("nks").reshape(1, 128, 128, 2, 64).astype(f32)
    new_v_s = cat("nvs").reshape(1, 128, 128, 2, 64).astype(f32)
    return (y_prompt.astype(f32), y_sample, new_conv_p.astype(f32), new_rnn_p.astype(f32), new_k_p.astype(f32), new_v_p.astype(f32),
            new_conv_s, new_rnn_s, new_k_s, new_v_s)
```

```python
import contextlib
import numpy as np
import concourse.bass as bass
import concourse.mybir as mybir
from concourse.bass_utils import run_bass_kernel_spmd

F32 = mybir.dt.float32
BF16 = mybir.dt.bfloat16
ALU = mybir.AluOpType
AF = mybir.ActivationFunctionType

NCORES = 8
D = 1024
D_IN = 2304
SEQ = 8192
CH = 2048
NT = 16
EPS = 1e-6
NEG = -1e30
LPOS = [0, 4, 1, 5, 2, 6, 3, 7]
C_XR, C_GR, C_Q, C_K, C_V, C_GA = 0, 512, 1024, 1536, 1664, 1792
SB = 16
PRE = 3 * CH
NPT = PRE // 128

ENGS = ("sync", "scalar", "vector", "gpsimd", "tensor")
CC_INC = 1
GROUPS = {"setup": 11, "W1": 9, "W2": 8, "W3": 8, "samp": 18, "samp2": 3, "outs": 14}
GROUPS_SEEN = {}
STAGE = 99
SUB = 99
NBLK = 99
NOOUT = 0
NOLAST = 0
SS = 99
DM = 255
S1 = 99


class Prog:
    def __init__(self, nc, es):
        self.nc = nc
        self.es = es
        self.q = {e: [] for e in ENGS}
        self.sig = {e: 0 for e in ENGS}
        self.pending = {e: False for e in ENGS}
        self.waited = {}
        self.bufs = {}
        self.dma_cnt = {}
        self.grp_seen = {}
        self.sems = {}

    def sem(self, key):
        if key not in self.sems:
            name = "s_" + "_".join(str(k) for k in key)
            self.sems[key] = self.es.enter_context(self.nc.semaphore(name))
        return self.sems[key]

    def _deps(self, eng, reads, writes, extra, skip_key=None):
        deps = set(extra)
        for r in reads:
            b = self.bufs.get(r)
            if b and b["w"] is not None:
                deps.add(b["w"])
            if b and r.startswith("pb"):
                deps.update(h for h in b["r"] if h[1] != eng)
        for w in writes:
            b = self.bufs.get(w)
            if b:
                if b["w"] is not None:
                    deps.add(b["w"])
                deps.update(b["r"])
        best = {}
        for d in deps:
            if d is None:
                continue
            key = d[:2]
            if eng == "tensor" and key == ("eng", "tensor"):
                continue
            if key == skip_key:
                continue
            if d[2] > best.get(key, 0):
                best[key] = d[2]
        waits = []
        for key, val in best.items():
            if self.waited.get((eng, key), 0) >= val:
                continue
            self.waited[(eng, key)] = val
            waits.append((key, val))
        return waits

    def _track(self, h, reads, writes):
        for r in reads:
            b = self.bufs.setdefault(r, {"w": None, "r": []})
            b["r"].append(h)
        for w in writes:
            self.bufs[w] = {"w": h, "r": []}

    def op(self, eng, fn, reads=(), writes=(), signal=True, deps=()):
        waits = self._deps(eng, reads, writes, deps)
        if signal:
            self.sig[eng] += 1
            h = ("eng", eng, self.sig[eng])
            self.pending[eng] = False
        else:
            h = ("eng", eng, self.sig[eng] + 1)
            self.pending[eng] = True
        self._track(h, reads, writes)
        semw = [(self.sem(k), v) for k, v in waits]
        mysem = self.sem(("eng", eng)) if signal else None

        def emit(e):
            for s, v in semw:
                e.wait_ge(s, v)
            ins = fn(e)
            if mysem is not None:
                ins.then_inc(mysem, 1)

        self.q[eng].append(emit)
        return h

    def dma(self, eng, out, in_, slot, reads=(), writes=(), deps=(), group=None, **kw):
        waits = self._deps(eng, reads, writes, deps, skip_key=(("dma", group) if group is not None else None))
        if group is not None:
            slot = group
            self.grp_seen[group] = self.grp_seen.get(group, 0) + 1
            cnt = 16 * GROUPS[group]
        else:
            cnt = self.dma_cnt.get(slot, 0) + 16
        self.dma_cnt[slot] = cnt
        h = ("dma", slot, cnt)
        self._track(h, reads, writes)
        semw = [(self.sem(k), v) for k, v in waits]
        mysem = self.sem(("dma", slot))

        def emit(e):
            for s, v in semw:
                e.wait_ge(s, v)
            e.dma_start(out=out, in_=in_, **kw).then_inc(mysem, 16)

        self.q[eng].append(emit)
        return h

    def cc(self, eng, fn, reads=(), writes=(), inc=None):
        inc = CC_INC if inc is None else inc
        waits = self._deps(eng, reads, writes, ())
        cnt = self.dma_cnt.get("cc", 0) + inc
        self.dma_cnt["cc"] = cnt
        h = ("dma", "cc", cnt)
        self._track(h, reads, writes)
        semw = [(self.sem(k), v) for k, v in waits]
        mysem = self.sem(("dma", "cc"))

        def emit(e):
            for s, v in semw:
                e.wait_ge(s, v)
            fn(e).then_inc(mysem, 1)

        self.q[eng].append(emit)
        return h

    def wait_all(self, eng, handles):
        waits = self._deps(eng, (), (), handles)
        semw = [(self.sem(k), v) for k, v in waits]

        def emit(e):
            for s, v in semw:
                e.wait_ge(s, v)

        self.q[eng].append(emit)

    def emit(self):
        assert not self.pending["tensor"], "PE has unsignalled trailing ops"
        GROUPS_SEEN.clear()
        GROUPS_SEEN.update(self.grp_seen)
        with self.nc.Block() as block:
            @block.sync
            def _(e):
                for f in self.q["sync"]:
                    f(e)

            @block.scalar
            def _(e):
                for f in self.q["scalar"]:
                    f(e)

            @block.vector
            def _(e):
                for f in self.q["vector"]:
                    f(e)

            @block.gpsimd
            def _(e):
                for f in self.q["gpsimd"]:
                    f(e)

            @block.tensor
            def _(e):
                for f in self.q["tensor"]:
                    f(e)


def build_program():
    nc = _build_program()
    if any(GROUPS.get(g) != n for g, n in GROUPS_SEEN.items()):
        GROUPS.update(GROUPS_SEEN)
        nc = _build_program()
        assert all(GROUPS.get(g) == n for g, n in GROUPS_SEEN.items())
    return nc


def _build_program():
    nc = bass.Bass("TRN2", target_bir_lowering=False)

    def din(name, shape):
        return nc.dram_tensor(name, list(shape), F32, kind="ExternalInput").ap()

    def dout(name, shape):
        return nc.dram_tensor(name, list(shape), F32, kind="ExternalOutput").ap()

    xc_d = din("xc", [PRE + CH + 128, D])
    flag_d = din("flag", [128, 3])
    w_in_d = din("w_in", [D, D_IN])
    w_out_d = din("w_out", [D, D])
    gpre_d = din("gpre", [128, 8])
    gpost_d = din("gpost", [128, D])
    convw_d = din("convw", [128, 16])
    vec4_d = din("vec4", [128, 16])
    bda_d = din("bda", [128, 512])
    bdx_d = din("bdx", [128, 512])
    sinks_d = din("sinks", [128, 8])
    biasg_d = din("biasg", [128, 2048])
    maskc_d = din("maskc", [128, 256])
    hmask_d = din("hmask", [128, 1])
    sel_d = din("sel", [128, 8])
    ident_d = din("ident", [128, 128])

    xs_d = din("xs", [128, D])
    sconv_d = din("sconv", [SB, 1536])
    srnn_d = din("srnn", [SB, 512])
    ck_d = din("ck", [SB * 128 + 128, 128])
    cv_d = din("cv", [SB * 128 + 128, 128])
    rowp_d = din("rowp", [SB, 4096])
    biass_d = din("biass", [128, 8])

    y_d = dout("y", [CH, D])
    ys_d = dout("ys", [SB, D])
    nconvs_d = dout("nconvs", [SB, 1536])
    nrnns_d = dout("nrnns", [SB, 512])
    nks_d = dout("nks", [SB, 128, 128])
    nvs_d = dout("nvs", [SB, 128, 128])
    nconv_d = dout("nconv", [3, 512])
    nrnn_d = dout("nrnn", [512])
    nk_d = dout("nk", [128, 128])
    nv_d = dout("nv", [128, 128])

    bounce = nc.dram_tensor("bounce", [128, 8], F32)
    gath = nc.dram_tensor("gath", [NCORES * 128, 8], F32)
    kvb = nc.dram_tensor("kvb", [SB, 256], F32)

    out_handles = []

    with contextlib.ExitStack() as es:
        P = Prog(nc, es)
        def sb(name, shape, dt=F32):
            return es.enter_context(nc.sbuf_tensor("sb_" + name, list(shape), dt))

        Win = sb("Win", [128, 8, D_IN], BF16)
        Wout = sb("Wout", [128, 8, D], BF16)
        P1 = sb("P1", [128, 4, CH], BF16)
        P2 = sb("P2", [128, 4, CH], BF16)
        attT = sb("attT", [128, 4, CH], BF16)
        KT = sb("KT", [128, 128 + 512], BF16)
        Vg = sb("Vg", [128, 5, 132], BF16)
        xt = [sb(f"xt{i}", [128, D]) for i in range(2)]
        xn = [sb(f"xn{i}", [128, D], BF16) for i in range(2)]
        xnT = sb("xnT", [128, 8, 512], BF16)
        xr = sb("xr", [128, 4, 515], BF16)
        xrt = sb("xrt", [128, 4, 4])
        xcv = sb("xcv", [128, 2, 512])
        u_b = sb("u_b", [128, 2, 512])
        a_b = sb("a_b", [128, 2, 512])
        a2_b = sb("a2_b", [128, 2, 512])
        t1_b = sb("t1_b", [128, 2, 512])
        th = [sb(f"th{i}", [128, 512]) for i in range(2)]
        hl = sb("hl", [128, 512])
        Ac = sb("Ac", [128, 512])
        QT = [sb(f"QT{g}", [128, 4, 512], BF16) for g in range(2)]
        BiasT = sb("BiasT", [128, 2, 2, 512], BF16)
        BiasF = sb("BiasF", [128, 2, 512], BF16)
        PT = [sb(f"PT{i}", [128, 2, 2, 512], BF16) for i in range(2)]
        ua = sb("ua", [128, 512])
        sgr = sb("sgr", [128, 512])
        att_o = sb("att_o", [128, 512], BF16)
        mixT = [sb(f"mixT{i}", [128, 4, 128], BF16) for i in range(2)]
        tt = sb("tt", [128, D])
        gpost = sb("gpost", [128, D])
        junk2 = sb("junk2", [128, 512], BF16)
        bda = sb("bda", [128, 4, 128])
        bdx = sb("bdx", [128, 4, 128])
        diagw = sb("diagw", [128, 4, 4, 128], BF16)
        ident = sb("ident", [128, 128], BF16)
        maskc = sb("maskc", [128, 2, 128])
        gpre = sb("gpre", [128, 8])
        convw = sb("convw", [128, 4, 4])
        vec4 = sb("vec4", [128, 4, 4])
        hb = sb("hb", [128, 2, 4])
        chalf = sb("chalf", [128, 4])
        sinks = sb("sinks", [128, 2, 4])
        sinkexp2 = sb("sinkexp2", [128, 2, 4])
        hmask = sb("hmask", [128, 1])
        sel = sb("sel", [128, 8])
        cst = sb("cst", [128, 4])
        ss = sb("ss", [128, 17 + NPT])
        ms = sb("ms", [128, 17 + NPT])
        rstd = sb("rstd", [128, 17 + NPT])
        flag = sb("flag", [128, 3])
        ss2 = sb("ss2", [128, NT, 2])
        ms2 = sb("ms2", [128, NT])
        rstd2 = sb("rstd2", [128, NT])
        car = sb("car", [128, 8])
        dsum = sb("dsum", [128, 4, 2])
        rden = sb("rden", [128, 4, 2])
        G_sb = sb("G_sb", [128, 8, 8])
        Ap = sb("Ap", [128, 8, 4])
        Bp = sb("Bp", [128, 8, 4])
        hscan = sb("hscan", [128, 4, 8])
        h0 = sb("h0", [128, 4])
        hfin = sb("hfin", [128, 4])
        kout = sb("kout", [128, 128])
        vout = sb("vout", [128, 128])
        sth = sb("sth", [128, 8])
        ident32 = sb("ident32", [128, 128])
        ones32 = sb("ones32", [128, 128])
        biass = sb("biass", [128, 8])
        uaT = sb("uaT", [128, 4, SB])
        sm = sb("sm", [128, 16])

        _p1f = P1[:].rearrange("p g n -> p (g n)").bitcast(F32)
        _p2f = P2[:].rearrange("p g n -> p (g n)").bitcast(F32)
        preX = _p1f[:, 0:2048].rearrange("p (g n) -> p g n", n=512)
        preA = _p1f[:, 2048:4096].rearrange("p (g n) -> p g n", n=512)
        preA2 = _p2f[:, 0:2048].rearrange("p (g n) -> p g n", n=512)
        preT1 = _p2f[:, 2048:4096].rearrange("p (g n) -> p g n", n=512)
        pb = [es.enter_context(nc.psum_tensor(f"pb{i}", [128, 512], F32)) for i in range(8)]
        pbT = [p[:].bitcast(BF16).rearrange("p (k c) -> p k c", c=128) for p in pb]
        bank_ctr = [0]

        def bank():
            i = bank_ctr[0]
            bank_ctr[0] = (i + 1) % 8
            return i

        def mk(eng):
            def f(name, *args, r=(), w=(), sig=True, **kw):
                return P.op(eng, lambda e: getattr(e, name)(*args, **kw), reads=r, writes=w, signal=sig)
            return f
        V, A, G = mk("vector"), mk("scalar"), mk("gpsimd")
        _pe = mk("tensor")

        def PE(name, *args, r=(), w=(), sig=False, **kw):
            return _pe(name, *args, r=r, w=w, sig=sig, **kw)

        def ld(dst, src, name, group="setup"):
            P.dma("sync", dst, src, name, writes=[name], group=group)

        ld(gpre[:], gpre_d, "gpre")
        ld(convw[:].rearrange("p g t -> p (g t)"), convw_d, "convw")
        ld(vec4[:].rearrange("p a g -> p (a g)"), vec4_d, "vec4")
        ld(hmask[:], hmask_d, "hmask")
        ld(sel[:], sel_d, "sel")
        ld(sinks[:].rearrange("p g c -> p (g c)"), sinks_d, "sinks")
        ld(maskc[:].rearrange("p b q -> p (b q)"), maskc_d, "maskc")
        ld(bda[:].rearrange("p g m -> p (g m)"), bda_d, "bda")
        ld(bdx[:].rearrange("p g m -> p (g m)"), bdx_d, "bdx")
        ld(gpost[:], gpost_d, "gpost")
        ld(xt[0][:], biasg_d[:, 0:1024], "xt0", group=None)
        ld(xt[1][:], biasg_d[:, 1024:2048], "xt1", group=None)
        P.dma("gpsimd", ident[:], ident_d, "ident", writes=["ident"], group="W1")
        for k in range(8):
            for h in range(2):
                P.dma("gpsimd", Win[:, k, h * 1152:(h + 1) * 1152], w_in_d[k * 128:(k + 1) * 128, h * 1152:(h + 1) * 1152],
                      f"Win{k}_{h}", writes=[f"Win{k}_{h}"], group=("W1" if h == 0 else "W2"))
        for k in range(8):
            P.dma("gpsimd", Wout[:, k, :], w_out_d[k * 128:(k + 1) * 128, :], f"Wout{k}", writes=[f"Wout{k}"], group="W3")

        G("memset", cst[:, 0:1], -0.5, w=["cst"])
        G("memset", cst[:, 1:2], 0.0, r=["cst"], w=["cst"])
        G("memset", cst[:, 2:3], 1.0 / 16.0, r=["cst"], w=["cst"])
        G("memset", Vg[:], 1.0, w=["Vg"])
        G("memset", QT[0][64:128], 0.0, w=["QT0"])
        G("memset", QT[1][0:64], 0.0, w=["QT1"])
        G("memset", ones32[:], 1.0, w=["ones32"])
        A("activation", out=sth[:, 0:4], in_=vec4[:, 3, :], func=AF.Exp, scale=-1.0, r=["vec4"], w=["sth"])
        A("activation", out=sth[:, 4:8], in_=sth[:, 0:4], func=AF.Ln, bias=1.0, r=["sth"], w=["sth"])
        V("tensor_scalar", out=chalf[:], in0=sth[:, 4:8], scalar1=-4.0, scalar2=None, op0=ALU.mult, r=["sth"], w=["chalf"])
        V("tensor_scalar", out=hb[:], in0=vec4[:, 1:3, :], scalar1=0.5, scalar2=None, op0=ALU.mult, r=["vec4"], w=["hb"])
        A("activation", out=sinkexp2[:], in_=sinks[:], func=AF.Exp, r=["sinks"], w=["sinkexp2"])
        V("tensor_scalar", out=sinkexp2[:], in0=sinkexp2[:], scalar1=2.0, scalar2=None, op0=ALU.mult, r=["sinkexp2"], w=["sinkexp2"])
        for blk in range(2):
            V("tensor_tensor", out=BiasT[:, blk].rearrange("p g (c q) -> p (g c) q", q=128),
                                                 in0=xt[blk][:].rearrange("p (h q) -> p h q", q=128),
                                                 in1=maskc[:, blk, :].unsqueeze(1).broadcast_to([128, 8, 128]), op=ALU.add,
              r=[f"xt{blk}", "maskc"], w=["BiasT"])
        V("tensor_tensor", out=xt[0][:].rearrange("p (h q) -> p h q", q=128), in0=xt[0][:].rearrange("p (h q) -> p h q", q=128),
                                    in1=maskc[:, 0, :].unsqueeze(1).broadcast_to([128, 8, 128]), op=ALU.add,
          r=["xt0", "maskc"], w=["xt0"])
        V("tensor_scalar", out=BiasF[:].rearrange("p g c -> p (g c)"), in0=xt[0][:], scalar1=hmask[:, 0:1], scalar2=None, op0=ALU.add,
          r=["xt0", "hmask"], w=["BiasF"])
        for g in range(4):
            for tap in range(4):
                V("tensor_scalar", out=diagw[:, g, tap, :], in0=ident[:], scalar1=convw[:, g, tap:tap + 1], scalar2=None, op0=ALU.mult,
                  r=["ident", "convw"], w=["diagw"])

        def win_names(c0, c1):
            hs = sorted({c0 // 1152, (c1 - 1) // 1152})
            return hs

        def tile_front(t, j):
            s = t % 2
            row = PRE + t * 128 if t < 17 else (t - 17) * 128
            P.dma("sync", xt[s][:], xc_d[row:row + 128, :], f"xt{s}", writes=[f"xt{s}"])
            A("activation", out=xn[s][:], in_=xt[s][:], func=AF.Square, accum_out=ss[:, t:t + 1],
              r=[f"xt{s}"], w=[f"xn{s}", f"ss{t}"])
            G("tensor_scalar", out=ms[:, t:t + 1], in0=ss[:, t:t + 1], scalar1=1.0 / D, scalar2=EPS, op0=ALU.mult, op1=ALU.add,
              r=[f"ss{t}"], w=[f"ms{t}"])
            G("tensor_tensor", out=rstd[:, t:t + 1], in0=ms[:, t:t + 1], in1=cst[:, 0:1], op=ALU.pow,
              r=[f"ms{t}", "cst"], w=[f"rstd{t}"])
            G("tensor_scalar", out=xn[s][:], in0=xt[s][:], scalar1=rstd[:, t:t + 1], scalar2=0.0, op0=ALU.mult, op1=ALU.add,
              r=[f"xt{s}", f"rstd{t}"], w=[f"xn{s}"])
            b = bank()
            for k in range(8):
                PE("transpose", out=pbT[b][:, k, :], in_=xn[s][:, k * 128:(k + 1) * 128], identity=ident[:],
                   r=[f"xn{s}", "ident"], w=[f"pb{b}"], sig=(k == 7))
            V("tensor_tensor", out=xnT[:, :, j * 128:(j + 1) * 128], in0=pbT[b][:, :, :],
                                        in1=gpre[:].unsqueeze(2).broadcast_to([128, 8, 128]), op=ALU.mult,
              r=[f"pb{b}", "gpre"], w=["xnT"])

        def fm_chunk(c0, N):
            b = bank()
            h = c0 // 1152
            for k in range(8):
                PE("matmul", pb[b][:, 0:N], lhsT=Win[:, k, c0:c0 + 128], rhs=xnT[:, k, 0:N], start=(k == 0), stop=(k == 7),
                   r=[f"Win{k}_{h}", "xnT"], w=[f"pb{b}"], sig=(k == 7))
            return b

        thc = [0]

        def th_slot():
            thc[0] ^= 1
            return thc[0]

        def rnn_chain(N, t0, pre=False):
            if pre:
                XCV, AB, A2B, T1B = preX, preA, preA2, preT1
                groups = [(0, 1, 2, 3)]
            else:
                XCV, AB, A2B, T1B = xcv, a_b, a2_b, t1_b
                groups = [(0, 1), (2, 3)]
            for gs in groups:
                for gi, g in enumerate(gs):
                    b = bank()
                    for tap in range(4):
                        PE("matmul", pb[b][:, 0:N], lhsT=diagw[:, g, tap, :], rhs=xr[:, g, tap:tap + N],
                                                                 start=(tap == 0), stop=(tap == 3),
                           r=["diagw", "xr"], w=[f"pb{b}"], sig=(tap == 3))
                    A("activation", out=XCV[:, gi, 0:N], in_=pb[b][:, 0:N], func=AF.Identity, bias=vec4[:, 0, g:g + 1],
                      r=[f"pb{b}", "vec4"], w=[f"xcv{gi}"])
                    if pre:
                        continue
                    b = fm_chunk(C_GR + g * 128, N)
                    s = th_slot()
                    A("activation", out=th[s][:, 0:N], in_=pb[b][:, 0:N], func=AF.Tanh, scale=0.5, r=[f"pb{b}"], w=[f"th{s}"])
                    V("scalar_tensor_tensor", out=u_b[:, gi, 0:N], in0=th[s][:, 0:N], scalar=1.0, in1=pb[b][:, 0:N],
                                                                        op0=ALU.add, op1=ALU.mult,
                      r=[f"pb{b}", f"th{s}"], w=[f"u{gi}"])
                for gi, g in enumerate(gs):
                    bA = bank()
                    PE("matmul", pb[bA][:, 0:N], lhsT=bda[:, g, :], rhs=XCV[:, gi, 0:N], start=True, stop=True,
                       r=["bda", f"xcv{gi}"], w=[f"pb{bA}"], sig=True)
                    bX = bank()
                    PE("matmul", pb[bX][:, 0:N], lhsT=bdx[:, g, :], rhs=XCV[:, gi, 0:N], start=True, stop=True,
                       r=["bdx", f"xcv{gi}"], w=[f"pb{bX}"], sig=True)
                    s = th_slot()
                    A("activation", out=th[s][:, 0:N], in_=pb[bA][:, 0:N], func=AF.Tanh, scale=0.5, bias=hb[:, 0, g:g + 1],
                      r=[f"pb{bA}", "hb"], w=[f"th{s}"])
                    A("activation", out=AB[:, gi, 0:N], in_=th[s][:, 0:N], func=AF.Exp, scale=chalf[:, g:g + 1], bias=chalf[:, g:g + 1],
                      r=[f"th{s}", "chalf"], w=[f"a{gi}"])
                    G("tensor_tensor", out=A2B[:, gi, 0:N], in0=AB[:, gi, 0:N], in1=AB[:, gi, 0:N], op=ALU.mult,
                      r=[f"a{gi}"], w=[f"a2{gi}"])
                    s2 = th_slot()
                    A("activation", out=th[s2][:, 0:N], in_=pb[bX][:, 0:N], func=AF.Tanh, scale=0.5, bias=hb[:, 1, g:g + 1],
                      r=[f"pb{bX}", "hb"], w=[f"th{s2}"])
                    V("scalar_tensor_tensor", out=T1B[:, gi, 0:N], in0=th[s2][:, 0:N], scalar=1.0, in1=XCV[:, gi, 0:N],
                                                                     op0=ALU.add, op1=ALU.mult,
                      r=[f"th{s2}", f"xcv{gi}"], w=[f"t1{gi}"])
                rnn_tail.append((gs, N, t0, pre, AB, A2B, T1B))
                if SUB >= 4:
                    flush_rnn_tail()
                else:
                    rnn_tail.clear()

        front_done = set()

        def front(t0, nt):
            if (t0, nt) in front_done:
                return
            front_done.add((t0, nt))
            for j in range(nt):
                tile_front(t0 + j, j)

        def block(bi, t0, nt, pre=False, nxt=None):
            N = nt * 128
            halo = (bi == 0)
            last = (bi == 4) and not NOLAST
            front(t0, nt)
            if not halo and Nprev[0] > 0:
                npv = Nprev[0]
                V("tensor_copy", out=xr[:, :, 0:3], in_=xr[:, :, npv:npv + 3], r=["xr"], w=["xr"])
            for g in range(4):
                b = fm_chunk(C_XR + g * 128, N)
                A("activation", out=xr[:, g, 3:3 + N], in_=pb[b][:, 0:N], func=AF.Copy, r=[f"pb{b}"], w=["xr"])
                if last:
                    A("activation", out=xrt[:, g, :], in_=pb[b][:, N - 4:N], func=AF.Copy, r=[f"pb{b}"], w=["xrt"])
            Nprev[0] = N
            if pre:
                if nxt is not None:
                    front(*nxt)
                rnn_chain(N, t0, pre=True)
                return
            b = fm_chunk(C_K, N)
            if not halo:
                kpv, vpv = KTprev[0], Vprev[0]
                if kpv > 0:
                    V("tensor_copy", out=KT[:, 0:128], in_=KT[:, kpv:kpv + 128], r=["KT"], w=["KT"])
                    V("tensor_copy", out=Vg[:, 0, :], in_=Vg[:, vpv, :], r=["Vg"], w=["Vg"])
                A("activation", out=KT[:, 128:128 + N], in_=pb[b][:, 0:N], func=AF.Copy, r=[f"pb{b}"], w=["KT"])
                KTprev[0] = N
                Vprev[0] = nt
            else:
                A("activation", out=KT[:, 0:128], in_=pb[b][:, 0:N], func=AF.Copy, r=[f"pb{b}"], w=["KT"])
                KTprev[0] = 0
                Vprev[0] = 0
            for j in range(nt):
                t = t0 + j
                vs = 0 if halo else j + 1
                bV = bank()
                for k in range(8):
                    PE("matmul", pb[bV][:, 0:128], lhsT=xnT[:, k, j * 128:(j + 1) * 128], rhs=Win[:, k, C_V:C_V + 128],
                                                    start=(k == 0), stop=(k == 7),
                       r=[f"Win{k}_1", "xnT"], w=[f"pb{bV}"], sig=(k == 7))
                V("tensor_copy", out=Vg[:, vs, :].rearrange("p (g e) -> p g e", e=66)[:, :, 0:64],
                                                 in_=pb[bV][:, 0:128].rearrange("p (g d) -> p g d", d=64),
                  r=[f"pb{bV}"], w=["Vg"])
                if t == NT and not NOLAST:
                    A("activation", out=vout[:], in_=pb[bV][:, 0:128], func=AF.Copy, r=[f"pb{bV}"], w=["vout"])
                    bK = bank()
                    for k in range(8):
                        PE("matmul", pb[bK][:, 0:128], lhsT=xnT[:, k, j * 128:(j + 1) * 128], rhs=Win[:, k, C_K:C_K + 128],
                                                        start=(k == 0), stop=(k == 7),
                           r=[f"Win{k}_1", "xnT"], w=[f"pb{bK}"], sig=(k == 7))
                    A("activation", out=kout[:], in_=pb[bK][:, 0:128], func=AF.Copy, r=[f"pb{bK}"], w=["kout"])
            if halo or SUB < 2:
                return
            for cc in range(4):
                b = fm_chunk(C_Q + cc * 128, N)
                V("tensor_scalar", out=QT[0][0:64, cc, 0:N], in0=pb[b][0:64, 0:N], scalar1=0.125, scalar2=None, op0=ALU.mult,
                  r=[f"pb{b}"], w=["QT0"])
                V("tensor_scalar", out=QT[1][64:128, cc, 0:N], in0=pb[b][64:128, 0:N], scalar1=0.125, scalar2=None, op0=ALU.mult,
                  r=[f"pb{b}"], w=["QT1"])
            if SUB < 3:
                return
            rnn_chain(N, t0)
            if SUB < 5:
                return
            for j in range(nt):
                t = t0 + j
                bG = bank()
                for k in range(8):
                    PE("matmul", pb[bG][:, 0:512], lhsT=xnT[:, k, j * 128:(j + 1) * 128], rhs=Win[:, k, C_GA:C_GA + 512],
                                                    start=(k == 0), stop=(k == 7),
                       r=[f"Win{k}_1", "xnT"], w=[f"pb{bG}"], sig=(k == 7))
                s = th_slot()
                A("activation", out=th[s][:], in_=pb[bG][:], func=AF.Tanh, scale=0.5, r=[f"pb{bG}"], w=[f"th{s}"])
                V("scalar_tensor_tensor", out=ua[:], in0=th[s][:], scalar=1.0, in1=pb[bG][:], op0=ALU.add, op1=ALU.mult,
                  r=[f"pb{bG}", f"th{s}"], w=["ua"])
                if SUB >= 6:
                    attention(t, j)

        def attention(t, j):
            ps = t % 2
            for blk in range(2):
                kc = (j + blk) * 128
                for g in range(2):
                    b = bank()
                    bias_ap = BiasF[:, g, :] if (t == 1 and blk == 0) else BiasT[:, blk, g, :]
                    bname = "BiasF" if (t == 1 and blk == 0) else "BiasT"
                    PE("matmul", pb[b][:].rearrange("p (c q) -> p c q", q=128), lhsT=KT[:, kc:kc + 128],
                                                           rhs=QT[g][:, :, j * 128:(j + 1) * 128], start=True, stop=False,
                       r=["KT", f"QT{g}"], w=[f"pb{b}"])
                    PE("matmul", pb[b][:], lhsT=ident[:], rhs=bias_ap, start=False, stop=True,
                       r=["ident", bname], w=[f"pb{b}"], sig=True)
                    A("activation", out=PT[ps][:, blk, g, :], in_=pb[b][:], func=AF.Exp, r=[f"pb{b}"], w=[f"PT{ps}"])
            if SUB < 7:
                return
            bO = []
            for g in range(2):
                b = bank()
                bO.append(b)
                for cc in range(4):
                    for blk in range(2):
                        PE("matmul", pb[b][:, cc * 66:cc * 66 + 66], lhsT=PT[ps][:, blk, g, cc * 128:(cc + 1) * 128],
                                                                        rhs=Vg[:, j + blk, g * 66:(g + 1) * 66], start=(blk == 0), stop=(blk == 1),
                           r=[f"PT{ps}", "Vg"], w=[f"pb{b}"], sig=(cc == 3 and blk == 1))
                V("scalar_tensor_tensor", out=dsum[:, :, g], in0=pb[b][:, 0:264].rearrange("p (c e) -> p c e", e=66)[:, :, 64],
                                                             scalar=2.0, in1=sinkexp2[:, g, :], op0=ALU.mult, op1=ALU.add,
                  r=[f"pb{b}", "sinkexp2"], w=["dsum"])
            if SUB < 8:
                return
            V("reciprocal", out=rden[:], in_=dsum[:], r=["dsum"], w=["rden"])
            G("tensor_tensor", out=sgr[:].rearrange("p (c g d) -> p c g d", g=2, d=64), in0=ua[:].rearrange("p (c g d) -> p c g d", g=2, d=64),
                                        in1=rden[:].unsqueeze(3).broadcast_to([128, 4, 2, 64]), op=ALU.mult,
              r=["ua", "rden"], w=["sgr"])
            for g in range(2):
                b = bO[g]
                V("tensor_tensor", out=att_o[:].rearrange("p (c g d) -> p c g d", g=2, d=64)[:, :, g, :],
                                                      in0=pb[b][:, 0:264].rearrange("p (c e) -> p c e", e=66)[:, :, 0:64],
                                                      in1=sgr[:].rearrange("p (c g d) -> p c g d", g=2, d=64)[:, :, g, :], op=ALU.mult,
                  r=[f"pb{b}", "sgr"], w=["att_o"])
            b = bank()
            for cc in range(4):
                PE("transpose", out=pbT[b][:, cc, :], in_=att_o[:, cc * 128:(cc + 1) * 128], identity=ident[:],
                   r=["att_o", "ident"], w=[f"pb{b}"], sig=(cc == 3))
            A("activation", out=attT[:, :, (t - 1) * 128:t * 128], in_=pbT[b][:, 0:4, :], func=AF.Copy, r=[f"pb{b}"], w=[f"attT{t}"])

        rnn_tail = []
        first_scan = [True]
        first_A = [True]

        def flush_rnn_tail():
            for gs, N, t0, pre, AB, A2B, T1B in rnn_tail:
                for gi, g in enumerate(gs):
                    A("activation", out=A2B[:, gi, 0:N], in_=A2B[:, gi, 0:N], func=AF.Sqrt, scale=-1.0 / 16.0, bias=1.0 / 16.0,
                      r=[f"a2{gi}"], w=[f"a2{gi}"])
            for gs, N, t0, pre, AB, A2B, T1B in rnn_tail:
                c0 = (t0 - 1) * 128
                for gi, g in enumerate(gs):
                    G("tensor_tensor", out=T1B[:, gi, 0:N], in0=T1B[:, gi, 0:N], in1=A2B[:, gi, 0:N], op=ALU.mult,
                      r=[f"t1{gi}", f"a2{gi}"], w=[f"t1{gi}"])
                    fs = first_scan[0]
                    fsA = first_A[0]
                    V("tensor_tensor_scan", out=hl[:, 0:N], data0=AB[:, gi, 0:N], data1=T1B[:, gi, 0:N],
                                                                              initial=(0.0 if fs else car[:, 4 + g:5 + g]), op0=ALU.mult, op1=ALU.add,
                      r=[f"a{gi}", f"t1{gi}", "car"], w=["hl"])
                    V("tensor_copy", out=car[:, 4 + g:5 + g], in_=hl[:, N - 1:N], r=["hl", "car"], w=["car"])
                    if pre:
                        continue
                    V("tensor_tensor_scan", out=Ac[:, 0:N], data0=AB[:, gi, 0:N], data1=cst[:, 1:2].broadcast_to([128, N]),
                                                                              initial=(1.0 if fsA else car[:, g:g + 1]), op0=ALU.mult, op1=ALU.add,
                      r=[f"a{gi}", "cst", "car"], w=["Ac"])
                    V("tensor_copy", out=car[:, g:g + 1], in_=Ac[:, N - 1:N], r=["Ac", "car"], w=["car"])
                    G("tensor_tensor", out=P1[:, g, c0:c0 + N], in0=hl[:, 0:N], in1=u_b[:, gi, 0:N], op=ALU.mult,
                      r=["hl", f"u{gi}"], w=[f"P1_{g}_{c0}"])
                    G("tensor_tensor", out=P2[:, g, c0:c0 + N], in0=Ac[:, 0:N], in1=u_b[:, gi, 0:N], op=ALU.mult,
                      r=["Ac", f"u{gi}"], w=[f"P2_{g}_{c0}"])
                if gs[-1] == 3:
                    first_scan[0] = False
                    if not pre:
                        first_A[0] = False
            rnn_tail.clear()

        Nprev = [0]
        KTprev = [0]
        Vprev = [0]
        G("memset", xr[:], 0.0, w=["xr"])
        ld(flag[:], flag_d, "flag")
        for pb_i in range(NPT // 4):
            nxt = (17 + 4 * (pb_i + 1), 4) if pb_i + 1 < NPT // 4 else (0, 1)
            block(100 + pb_i, 17 + 4 * pb_i, 4, pre=True, nxt=nxt)
            if pb_i % 4 == 3:
                kf = pb_i // 4
                V("tensor_scalar", out=car[:, 4:8], in0=car[:, 4:8], scalar1=flag[:, kf:kf + 1], scalar2=None, op0=ALU.mult,
                  r=["car", "flag"], w=["car"])
        Nprev[0] = 0
        blocks = [(0, 0, 1), (1, 1, 4), (2, 5, 4), (3, 9, 4), (4, 13, 4)]
        for bi, t0, nt in blocks:
            if STAGE >= (1 if bi == 0 else 2 if bi == 1 else 3) and bi <= NBLK:
                block(bi, t0, nt)

        if STAGE >= 3 and not NOOUT:
            for g in range(4):
                out_handles.append(P.dma("sync", nconv_d[:, g * 128:(g + 1) * 128].rearrange("t p -> p t"), xrt[:, g, 1:4], f"o_nconv{g}", reads=["xrt"],
                                         allow_slow_non_contiguous=True, group="outs"))
            out_handles.append(P.dma("sync", nk_d, kout[:], "o_nk", reads=["kout"], group="outs"))
            out_handles.append(P.dma("sync", nv_d, vout[:], "o_nv", reads=["vout"], group="outs"))

        if STAGE >= 4:
            G("memset", h0[:], 0.0, w=["h0"])
            V("tensor_scalar", out=hfin[:], in0=car[:, 4:8], scalar1=2.0, scalar2=None, op0=ALU.mult, r=["car"], w=["hfin"])
            out_handles.append(P.dma("sync", nrnn_d.rearrange("(g p) -> p g", p=128), hfin[:], "o_nrnn", reads=["hfin"],
                                     allow_slow_non_contiguous=True, group="outs"))

        def sample_path():
            F = lambda ap: ap.bitcast(F32)
            xnTs = QT[0][:].rearrange("p c n -> p (c n)")[:, 0:1024].rearrange("p (k t) -> p k t", t=128)
            Kn = F(xnT[:].rearrange("p k n -> p (k n)")).rearrange("p (s c) -> p s c", c=128)
            Vn = [F(PT[h][:].rearrange("p a b n -> p (a b n)")).rearrange("p (s c) -> p s c", c=128) for h in range(2)]
            KnT = [a_b[:].rearrange("p a n -> p (a n)").rearrange("p (s c) -> p s c", c=128),
                   a2_b[:].rearrange("p a n -> p (a n)").rearrange("p (s c) -> p s c", c=128)]
            xcT = u_b[:].rearrange("p a n -> p (a n)")[:, 0:512].rearrange("p (g t) -> p g t", t=128)
            Qm = th[0][:, 0:128].rearrange("p (s q) -> p s q", q=8)
            Pt = hl[:, 0:128]
            sc = Ac[:, 0:128]
            rds = Ac[:, 128:256]
            attn = hl[:, 128:256]
            xcs = t1_b[:, 0, :]
            tA = t1_b[:, 1, :]
            tB = xcv[:, 0, :]
            tC = xcv[:, 1, :]
            tD = ua[:]
            tE = sgr[:]
            mixtm = xn[0][:, 0:512]
            mixTs = xn[1][:].rearrange("p (k t) -> p k t", t=128)
            R16 = slice(0, SB)
            flat32 = lambda t, pat: F(t[:].rearrange(pat))
            hosts = [(xt[1][:], "xt1"), (flat32(QT[1], "p c n -> p (c n)"), "QT1"), (flat32(BiasT, "p a b n -> p (a b n)"), "BiasT"),
                     (flat32(diagw, "p g t m -> p (g t m)"), "diagw")]
            rp, rpn = [], []
            for hap, hname in hosts:
                for i2 in range(2):
                    rp.append(hap[R16, i2 * 512:(i2 + 1) * 512])
                    rpn.append(hname)
            for i8 in range(8):
                P.dma("sync", rp[i8], rowp_d[:, i8 * 512:(i8 + 1) * 512], f"rowp{i8}", writes=[rpn[i8]], group="samp")
            sst_t = [tt[R16, 0:512], tt[R16, 512:1024], F(BiasF[:].rearrange("p g n -> p (g n)"))[R16, 0:512]]
            sst_n = ["tt", "tt", "BiasF"]
            for t3 in range(3):
                P.dma("sync", sst_t[t3], sconv_d[:, t3 * 512:(t3 + 1) * 512], f"sst{t3}", writes=[sst_n[t3]], group="samp")
            hprev = th[1][R16, :]
            P.dma("sync", hprev, srnn_d, "hprev", writes=["th1"], group="samp")
            kv_s = F(att_o[:])[R16, :]
            ld(ident32[:], ident_d, "ident32", group="samp")
            ld(biass[:], biass_d, "biass", group="samp")
            def win(src, h):
                return src[1 + 8 * h * 128:1 + (8 * h + 8) * 128, :].rearrange("(s k) c -> k s c", k=128)
            big = [None]
            for h in range(2):
                big[0] = P.dma("sync", Kn[:, 8 * h:8 * h + 8, :], win(ck_d, h), f"Kn_a{h}", writes=["xnT", f"KnH{h}"], deps=[big[0]])
                big[0] = P.dma("sync", Vn[h][:], win(cv_d, h), f"Vn_a{h}", writes=[f"PT{h}"], deps=[big[0]])
            if SS < 1:
                return
            if S1 < 1:
                return
            pass
            if S1 < 2:
                return
            P.dma("sync", xt[0][:], xs_d, "xt0", writes=["xt0"])
            if S1 < 3:
                return
            pass
            if S1 < 4:
                return
            A("activation", out=xn[0][:], in_=xt[0][:], func=AF.Square, accum_out=sm[:, 0:1], r=["xt0", "sm"], w=["xn0", "sm"])
            if S1 < 5:
                return
            V("tensor_scalar", out=sm[:, 1:2], in0=sm[:, 0:1], scalar1=1.0 / D, scalar2=EPS, op0=ALU.mult, op1=ALU.add, r=["sm"], w=["sm"])
            if S1 < 6:
                return
            A("activation", out=sm[:, 1:2], in_=sm[:, 1:2], func=AF.Sqrt, r=["sm"], w=["sm"])
            if S1 < 7:
                return
            V("reciprocal", out=sm[:, 2:3], in_=sm[:, 1:2], r=["sm"], w=["sm"])
            if S1 < 8:
                return
            A("activation", out=xn[0][:], in_=xt[0][:], func=AF.Copy, scale=sm[:, 2:3], r=["xt0", "sm"], w=["xn0"])
            if S1 < 9:
                return
            b = bank()
            for k in range(8):
                PE("transpose", out=pbT[b][:, k, :], in_=xn[0][:, k * 128:(k + 1) * 128], identity=ident[:], r=["xn0", "ident"], w=[f"pb{b}"], sig=(k == 7))
            if S1 < 10:
                return
            V("tensor_tensor", out=xnTs, in0=pbT[b][:, :, :], in1=gpre[:].unsqueeze(2).broadcast_to([128, 8, 128]), op=ALU.mult,
              r=[f"pb{b}", "gpre"], w=["QT0"])

            def tm_cols(c0, w):
                bb = bank()
                for k in range(8):
                    PE("matmul", pb[bb][:, 0:w], lhsT=xnTs[:, k, :], rhs=Win[:, k, c0:c0 + w], start=(k == 0), stop=(k == 7),
                       r=["QT0"] + [f"Win{k}_{hh}" for hh in sorted({c0 // 1152, (c0 + w - 1) // 1152})], w=[f"pb{bb}"], sig=(k == 7))
                return bb

            def fm_cols(c0):
                bb = bank()
                hh = c0 // 1152
                for k in range(8):
                    PE("matmul", pb[bb][:, 0:128], lhsT=Win[:, k, c0:c0 + 128], rhs=xnTs[:, k, :], start=(k == 0), stop=(k == 7),
                       r=["QT0", f"Win{k}_{hh}"], w=[f"pb{bb}"], sig=(k == 7))
                return bb

            if SS < 2:
                return
            bKV = tm_cols(C_K, 256)
            A("activation", out=kv_s, in_=pb[bKV][R16, 0:256], func=AF.Copy, r=[f"pb{bKV}"], w=["att_o"])
            P.dma("sync", kvb.ap(), kv_s, "kvb", reads=["att_o"], writes=["kvb"])
            P.dma("sync", Kn[127:128], kvb.ap()[:, 0:128].unsqueeze(0), "Kn_b", reads=["kvb", "xnT", "KnH0", "KnH1"], writes=["xnT"], group="samp2")
            for h in range(2):
                P.dma("sync", Vn[h][127:128], kvb.ap()[8 * h:8 * h + 8, 128:256].unsqueeze(0), f"Vn_b{h}", reads=["kvb", f"PT{h}"], writes=[f"PT{h}"], group="samp2")
            for h in range(2):
                big[0] = P.dma("sync", nks_d[8 * h:8 * h + 8].rearrange("s k c -> k s c"), Kn[:, 8 * h:8 * h + 8, :], f"o_nks{h}", reads=["xnT", f"KnH{h}"], deps=[big[0]])
                out_handles.append(big[0])
                big[0] = P.dma("sync", nvs_d[8 * h:8 * h + 8].rearrange("s k c -> k s c"), Vn[h][:], f"o_nvs{h}", reads=[f"PT{h}"], deps=[big[0]])
                out_handles.append(big[0])
            if SS < 3:
                return
            for cc in range(4):
                bq = fm_cols(C_Q + cc * 128)
                V("tensor_scalar", out=Qm[0:64, :, 2 * cc], in0=pb[bq][0:64, 0:SB], scalar1=0.125, scalar2=None, op0=ALU.mult, r=[f"pb{bq}"], w=["th0"])
                V("tensor_scalar", out=Qm[64:128, :, 2 * cc + 1], in0=pb[bq][64:128, 0:SB], scalar1=0.125, scalar2=None, op0=ALU.mult, r=[f"pb{bq}"], w=["th0"])
            for cc in range(4):
                bg = fm_cols(C_GA + cc * 128)
                A("activation", out=tD[:, 0:SB], in_=pb[bg][:, 0:SB], func=AF.Tanh, scale=0.5, r=[f"pb{bg}"], w=["ua"])
                V("scalar_tensor_tensor", out=uaT[:, cc, :], in0=tD[:, 0:SB], scalar=1.0, in1=pb[bg][:, 0:SB], op0=ALU.add, op1=ALU.mult,
                  r=[f"pb{bg}", "ua"], w=["uaT"])
            if SS < 4:
                return
            bXR = tm_cols(C_XR, 512)
            bGR = tm_cols(C_GR, 512)
            cw = lambda tap: rp[tap]
            cb_r, bga_r, bgx_r, lam_r = rp[4], rp[5], rp[6], rp[7]
            A("activation", out=tB[R16], in_=pb[bXR][R16, :], func=AF.Copy, r=[f"pb{bXR}"], w=["xcv0"])
            out_handles.append(P.dma("sync", nconvs_d[:, 0:512], sst_t[1], "o_ncs_a", reads=["tt"]))
            out_handles.append(P.dma("sync", nconvs_d[:, 512:1024], sst_t[2], "o_ncs_c", reads=["BiasF"], group="outs"))
            out_handles.append(P.dma("sync", nconvs_d[:, 1024:1536], tB[R16], "o_ncs_b", reads=["xcv0"], group="outs"))
            V("tensor_tensor", out=xcs[R16], in0=tB[R16], in1=cw(3), op=ALU.mult, r=["xcv0", rpn[3]], w=["t10"])
            for tap in range(3):
                V("tensor_tensor", out=tA[R16], in0=sst_t[tap], in1=cw(tap), op=ALU.mult, r=[sst_n[tap], rpn[tap]], w=["t11"])
                V("tensor_tensor", out=xcs[R16], in0=xcs[R16], in1=tA[R16], op=ALU.add, r=["t10", "t11"], w=["t10"])
            V("tensor_tensor", out=xcs[R16], in0=xcs[R16], in1=cb_r, op=ALU.add, r=["t10", rpn[4]], w=["t10"])
            b = bank()
            for g in range(4):
                PE("transpose", out=pb[b][:, g * 128:(g + 1) * 128], in_=xcs[:, g * 128:(g + 1) * 128], identity=ident32[:],
                   r=["t10", "ident32"], w=[f"pb{b}"], sig=(g == 3))
            V("tensor_copy", out=xcT, in_=pb[b][:].rearrange("p (g t) -> p g t", t=128), r=[f"pb{b}"], w=["u0"])
            bA, bX = bank(), bank()
            for g in range(4):
                PE("matmul", pb[bA][:, g * 128:(g + 1) * 128], lhsT=xcT[:, g, :], rhs=bda[:, g, :], start=True, stop=True, r=["u0", "bda"], w=[f"pb{bA}"], sig=(g == 3))
            for g in range(4):
                PE("matmul", pb[bX][:, g * 128:(g + 1) * 128], lhsT=xcT[:, g, :], rhs=bdx[:, g, :], start=True, stop=True, r=["u0", "bdx"], w=[f"pb{bX}"], sig=(g == 3))
            A("activation", out=tC[R16], in_=lam_r, func=AF.Exp, scale=-1.0, r=[rpn[7]], w=["xcv1"])
            A("activation", out=tC[R16], in_=tC[R16], func=AF.Ln, bias=1.0, r=["xcv1"], w=["xcv1"])
            V("tensor_scalar", out=tC[R16], in0=tC[R16], scalar1=-4.0, scalar2=None, op0=ALU.mult, r=["xcv1"], w=["xcv1"])
            V("tensor_tensor", out=tA[R16], in0=pb[bA][R16, :], in1=bga_r, op=ALU.add, r=[f"pb{bA}", rpn[5]], w=["t11"])
            A("activation", out=tA[R16], in_=tA[R16], func=AF.Tanh, scale=0.5, r=["t11"], w=["t11"])
            V("scalar_tensor_tensor", out=tA[R16], in0=tA[R16], scalar=1.0, in1=tC[R16], op0=ALU.add, op1=ALU.mult, r=["t11", "xcv1"], w=["t11"])
            A("activation", out=tA[R16], in_=tA[R16], func=AF.Exp, r=["t11"], w=["t11"])
            V("tensor_tensor", out=tC[R16], in0=pb[bX][R16, :], in1=bgx_r, op=ALU.add, r=[f"pb{bX}", rpn[6]], w=["xcv1"])
            A("activation", out=tC[R16], in_=tC[R16], func=AF.Tanh, scale=0.5, r=["xcv1"], w=["xcv1"])
            V("scalar_tensor_tensor", out=tC[R16], in0=tC[R16], scalar=1.0, in1=xcs[R16], op0=ALU.add, op1=ALU.mult, r=["xcv1", "t10"], w=["xcv1"])
            A("activation", out=tE[R16], in_=pb[bGR][R16, :], func=AF.Tanh, scale=0.5, r=[f"pb{bGR}"], w=["sgr"])
            V("scalar_tensor_tensor", out=tE[R16], in0=tE[R16], scalar=1.0, in1=pb[bGR][R16, :], op0=ALU.add, op1=ALU.mult, r=["sgr", f"pb{bGR}"], w=["sgr"])
            V("tensor_tensor", out=tD[R16], in0=tA[R16], in1=tA[R16], op=ALU.mult, r=["t11"], w=["ua"])
            A("activation", out=tD[R16], in_=tD[R16], func=AF.Sqrt, scale=-1.0 / 16.0, bias=1.0 / 16.0, r=["ua"], w=["ua"])
            V("tensor_tensor", out=tC[R16], in0=tC[R16], in1=tD[R16], op=ALU.mult, r=["xcv1", "ua"], w=["xcv1"])
            V("tensor_tensor", out=tA[R16], in0=tA[R16], in1=hprev, op=ALU.mult, r=["t11", "th1"], w=["t11"])
            V("scalar_tensor_tensor", out=tC[R16], in0=tC[R16], scalar=2.0, in1=tA[R16], op0=ALU.mult, op1=ALU.add, r=["xcv1", "t11"], w=["xcv1"])
            out_handles.append(P.dma("sync", nrnns_d, tC[R16], "o_nrs", reads=["xcv1"], group="outs"))
            V("scalar_tensor_tensor", out=mixtm[R16], in0=tC[R16], scalar=0.5, in1=tE[R16], op0=ALU.mult, op1=ALU.mult, r=["xcv1", "sgr"], w=["xn0"])
            b = bank()
            for g in range(4):
                PE("transpose", out=pbT[b][:, g, :], in_=mixtm[:, g * 128:(g + 1) * 128], identity=ident[:], r=["xn0", "ident"], w=[f"pb{b}"], sig=(g == 3))
            V("tensor_copy", out=mixTs[:, 0:4, :], in_=pbT[b][:, 0:4, :], r=[f"pb{b}"], w=["xn1"])
            if SS < 5:
                return
            for q4 in range(4):
                b = bank()
                for i4 in range(4):
                    sq = q4 * 4 + i4
                    PE("transpose", out=pb[b][:, i4 * 128:(i4 + 1) * 128], in_=Kn[:, sq, :], identity=ident32[:], r=["xnT", "ident32"], w=[f"pb{b}"], sig=(i4 == 3))
                hh, s0 = q4 // 2, (q4 % 2) * 4
                V("tensor_copy", out=KnT[hh][:, s0:s0 + 4, :], in_=pb[b][:].rearrange("p (s k) -> p s k", k=128), r=[f"pb{b}"], w=[f"a{hh}" if hh == 0 else "a20"])
            bS = bank()
            for sq in range(SB):
                hh, s0 = sq // 8, sq % 8
                PE("matmul", pb[bS][:, sq * 8:(sq + 1) * 8], lhsT=KnT[hh][:, s0, :], rhs=Qm[:, sq, :], start=True, stop=True,
                   r=["a0" if hh == 0 else "a20", "th0"], w=[f"pb{bS}"], sig=(sq == SB - 1))
            V("tensor_tensor", out=sc.rearrange("p (s q) -> p s q", q=8), in0=pb[bS][:, 0:128].rearrange("p (s q) -> p s q", q=8),
              in1=biass[:].unsqueeze(1).broadcast_to([128, SB, 8]), op=ALU.add, r=[f"pb{bS}", "biass"], w=["Ac"])
            A("activation", out=Pt, in_=sc, func=AF.Exp, r=["Ac"], w=["hl"])
            bO = bank()
            for sq in range(SB):
                hh, s0 = sq // 8, sq % 8
                PE("matmul", pb[bO][:, sq * 8:(sq + 1) * 8], lhsT=Vn[hh][:, s0, :], rhs=Pt[:, sq * 8:(sq + 1) * 8], start=True, stop=True,
                   r=[f"PT{hh}", "hl"], w=[f"pb{bO}"], sig=(sq == SB - 1))
            bD = bank()
            PE("matmul", pb[bD][:, 0:128], lhsT=ones32[:], rhs=Pt, start=True, stop=True, r=["ones32", "hl"], w=[f"pb{bD}"], sig=True)
            V("tensor_copy", out=sm[:, 8:16].rearrange("p (c g) -> p c g", g=2), in_=sinkexp2[:].rearrange("p g c -> p c g"), r=["sinkexp2", "sm"], w=["sm"])
            V("scalar_tensor_tensor", out=rds.rearrange("p (s q) -> p s q", q=8), in0=pb[bD][:, 0:128].rearrange("p (s q) -> p s q", q=8),
              scalar=2.0, in1=sm[:, 8:16].unsqueeze(1).broadcast_to([128, SB, 8]), op0=ALU.mult, op1=ALU.add,
              r=[f"pb{bD}", "sm"], w=["Ac"])
            V("reciprocal", out=rds, in_=rds, r=["Ac"], w=["Ac"])
            V("tensor_tensor", out=attn, in0=pb[bO][:, 0:128], in1=rds, op=ALU.mult, r=[f"pb{bO}", "Ac"], w=["hl"])
            av = attn.rearrange("p (s c g) -> p c s g", c=4, g=2)
            V("tensor_tensor", out=mixTs[0:64, 4:8, 0:SB], in0=av[0:64, :, :, 0], in1=uaT[0:64, :, :], op=ALU.mult, r=["hl", "uaT"], w=["xn1"])
            V("tensor_tensor", out=mixTs[64:128, 4:8, 0:SB], in0=av[64:128, :, :, 1], in1=uaT[64:128, :, :], op=ALU.mult, r=["hl", "uaT"], w=["xn1"])
            if SS < 6:
                return
            bY = [bank(), bank()]
            for kk in range(8):
                for hf in range(2):
                    PE("matmul", pb[bY[hf]][:], lhsT=mixTs[:, kk, :], rhs=Wout[:, kk, hf * 512:(hf + 1) * 512], start=(kk == 0), stop=(kk == 7),
                       r=["xn1", f"Wout{kk}"], w=[f"pb{bY[hf]}"], sig=(kk == 7))
            for hf in range(2):
                A("activation", out=junk2[:], in_=pb[bY[hf]][:], func=AF.Square, accum_out=sm[:, 3 + hf:4 + hf], r=[f"pb{bY[hf]}", "sm"], w=["junk2", "sm"])
            V("tensor_tensor", out=sm[:, 5:6], in0=sm[:, 3:4], in1=sm[:, 4:5], op=ALU.add, r=["sm"], w=["sm"])
            V("tensor_scalar", out=sm[:, 5:6], in0=sm[:, 5:6], scalar1=1.0 / D, scalar2=EPS, op0=ALU.mult, op1=ALU.add, r=["sm"], w=["sm"])
            A("activation", out=sm[:, 5:6], in_=sm[:, 5:6], func=AF.Sqrt, r=["sm"], w=["sm"])
            V("reciprocal", out=sm[:, 6:7], in_=sm[:, 5:6], r=["sm"], w=["sm"])
            for hf in range(2):
                V("scalar_tensor_tensor", out=tt[:, hf * 512:(hf + 1) * 512], in0=pb[bY[hf]][:], scalar=sm[:, 6:7], in1=gpost[:, hf * 512:(hf + 1) * 512],
                  op0=ALU.mult, op1=ALU.mult, r=[f"pb{bY[hf]}", "sm", "gpost"], w=["tt"])
            V("tensor_tensor", out=xt[0][R16, :], in0=tt[R16, :], in1=xt[0][R16, :], op=ALU.add, r=["tt", "xt0"], w=["xt0"])
            out_handles.append(P.dma("sync", ys_d, xt[0][R16, :], "o_ys", reads=["xt0"]))


        if STAGE >= 5:
            F2 = lambda ap: ap.bitcast(F32)
            xsl = [(xt[0][:], ["xt0"]), (xt[1][:], ["xt1"]),
                   (F2(PT[0][:].rearrange("p a b n -> p (a b n)")), ["PT0"]), (F2(PT[1][:].rearrange("p a b n -> p (a b n)")), ["PT1"])]
            tsl = [(tt[:], ["tt"]), (a_b[:].rearrange("p a n -> p (a n)"), ["a0", "a1"]), (a2_b[:].rearrange("p a n -> p (a n)"), ["a20", "a21"])]
            bYs = {}

            def p2_front(i):
                s = i % 2
                xa, xnm = xsl[i % 4]
                c0 = i * 128
                blk0 = (i // 4) * 512
                P.dma("sync", xa, xc_d[PRE + (i + 1) * 128:PRE + (i + 2) * 128, :], f"x2_{i % 4}", writes=xnm)
                for g in range(4):
                    V("scalar_tensor_tensor", out=mixT[s][:, g, :], in0=P2[:, g, c0:c0 + 128], scalar=h0[:, g:g + 1], in1=P1[:, g, c0:c0 + 128],
                      op0=ALU.mult, op1=ALU.add, r=[f"P1_{g}_{blk0}", f"P2_{g}_{blk0}", "h0"], w=[f"mixT{s}"])
                bY = [bank(), bank()]
                bYs[i] = bY
                for kk in range(8):
                    lhs = mixT[s][:, kk, :] if kk < 4 else attT[:, kk - 4, c0:c0 + 128]
                    rn = [f"mixT{s}"] if kk < 4 else [f"attT{i + 1}"]
                    for hf in range(2):
                        PE("matmul", pb[bY[hf]][:], lhsT=lhs, rhs=Wout[:, kk, hf * 512:(hf + 1) * 512], start=(kk == 0), stop=(kk == 7),
                           r=rn + [f"Wout{kk}"], w=[f"pb{bY[hf]}"], sig=(kk == 7))

            def p2_back(i):
                xa, xnm = xsl[i % 4]
                ta, tnm = tsl[i % 3]
                c0 = i * 128
                bY = bYs[i]
                for hf in range(2):
                    A("activation", out=junk2[:], in_=pb[bY[hf]][:], func=AF.Square, accum_out=ss2[:, i, hf:hf + 1],
                      r=[f"pb{bY[hf]}"], w=["junk2", f"ss2_{i}_{hf}"])
                G("tensor_tensor", out=ms2[:, i:i + 1], in0=ss2[:, i, 0:1], in1=ss2[:, i, 1:2], op=ALU.add,
                  r=[f"ss2_{i}_0", f"ss2_{i}_1"], w=[f"ms2_{i}"])
                G("tensor_scalar", out=ms2[:, i:i + 1], in0=ms2[:, i:i + 1], scalar1=1.0 / D, scalar2=EPS, op0=ALU.mult, op1=ALU.add,
                  r=[f"ms2_{i}"], w=[f"ms2_{i}"])
                G("tensor_tensor", out=rstd2[:, i:i + 1], in0=ms2[:, i:i + 1], in1=cst[:, 0:1], op=ALU.pow,
                  r=[f"ms2_{i}", "cst"], w=[f"rstd2_{i}"])
                for hf in range(2):
                    V("scalar_tensor_tensor", out=ta[:, hf * 512:(hf + 1) * 512], in0=pb[bY[hf]][:], scalar=rstd2[:, i:i + 1],
                      in1=gpost[:, hf * 512:(hf + 1) * 512], op0=ALU.mult, op1=ALU.mult,
                      r=[f"pb{bY[hf]}", f"rstd2_{i}", "gpost"] + tnm, w=tnm)
                G("tensor_tensor", out=xa, in0=ta, in1=xa, op=ALU.add, r=tnm + xnm, w=xnm)
                out_handles.append(P.dma("sync", y_d[c0:c0 + 128, :], xa, f"o_y{i % 4}", reads=xnm))

            p2_front(0)
            for i in range(NT):
                if i + 1 < NT:
                    p2_front(i + 1)
                p2_back(i)

        if STAGE >= 6:
            G("memset", th[0][:, 0:128], 0.0, w=["th0"])
            G("memset", t1_b[:], 0.0, w=["t10", "t11"])
            G("memset", xn[1][:], 0.0, w=["xn1"])
            sample_path()

        P.wait_all("sync", out_handles)
        P.emit()
    return nc


def _t5_bucket(dist):
    dist = np.maximum(dist, 0)
    max_exact = 16
    d = np.maximum(dist, 1).astype(np.float32)
    large = max_exact + (np.log(d / np.float32(max_exact)) / np.float32(np.log(128 / max_exact)) * np.float32(32 - max_exact)).astype(np.int32)
    large = np.minimum(large, 31)
    return np.where(dist < max_exact, dist, large)


_NC_CACHE = {}


def kernel(x_prompt, x_sample, state_conv, state_rnn, cache_k_win, cache_v_win,
           norm_pre, norm_post, w_in, conv_w, conv_b, w_gate_a, b_gate_a, w_gate_x, b_gate_x,
           lru_lambda, attn_sinks, rel_bias, w_out):
    f32 = np.float32
    x_prompt = np.asarray(x_prompt, f32)
    w_in0 = np.asarray(w_in, f32)[0]
    w_out0 = np.asarray(w_out, f32)[0]
    qperm = np.concatenate([np.arange(h * 64, h * 64 + 64) for h in LPOS])
    cols = np.concatenate([np.arange(0, 1024), 1024 + qperm, np.arange(1536, 1792), 1792 + qperm])
    w_in_p = np.ascontiguousarray(w_in0[:, cols])
    rows = np.concatenate([np.arange(0, 512), 512 + qperm])
    w_out_p = np.ascontiguousarray(w_out0[rows, :])

    def pg(v):
        return np.ascontiguousarray(np.asarray(v, f32).reshape(4, 128).T)

    gpre = np.ascontiguousarray(np.asarray(norm_pre, f32)[0].reshape(8, 128).T)
    gpost = np.ascontiguousarray(np.broadcast_to(np.asarray(norm_post, f32)[0][None, :], (128, D)))
    cw = np.asarray(conv_w, f32)[0]
    convw = np.ascontiguousarray(cw.reshape(4, 4, 128).transpose(2, 1, 0).reshape(128, 16))
    vec4 = np.ascontiguousarray(np.stack([pg(np.asarray(conv_b)[0]), pg(np.asarray(b_gate_a)[0]), pg(np.asarray(b_gate_x)[0]),
                                          pg(np.asarray(lru_lambda)[0])], axis=1).reshape(128, 16))

    def blockdiag(w):
        w = np.asarray(w, f32)[0]
        o = np.zeros((128, 4, 128), f32)
        for g in range(4):
            for h in range(2):
                o[h * 64:(h + 1) * 64, g, h * 64:(h + 1) * 64] = w[2 * g + h]
        return o.reshape(128, 512)

    bda = blockdiag(w_gate_a)
    bdx = blockdiag(w_gate_x)
    hd = np.array([[g * 4 + cc for cc in range(4)] for g in range(2)])
    sinks = np.ascontiguousarray(np.broadcast_to(np.asarray(attn_sinks, f32)[0][hd].reshape(1, 8), (128, 8)))
    rb = np.asarray(rel_bias, f32)
    kk = np.arange(128)[:, None]
    qq = np.arange(128)[None, :]
    biasg = np.zeros((128, 2, 2, 4, 128), f32)
    maskc = np.zeros((128, 2, 128), f32)
    for blk in range(2):
        dist = qq + (128 if blk == 0 else 0) - kk
        valid = (dist >= 0) & (dist < 128)
        bkt = _t5_bucket(np.clip(dist, 0, 127))
        for g in range(2):
            for cc in range(4):
                biasg[:, blk, g, cc, :] = rb[bkt, hd[g, cc]]
        maskc[:, blk, :] = np.where(valid, 0.0, NEG)
    biasg = biasg.reshape(128, 2048)
    maskc = maskc.reshape(128, 256)
    ident = np.eye(128, dtype=f32)

    xs_all = np.asarray(x_sample, f32)[:, 0, :]
    sconv_all = np.asarray(state_conv, f32)[0].reshape(128, 1536)
    srnn_all = np.asarray(state_rnn, f32)[0]
    ck_all = np.asarray(cache_k_win, f32)[0].reshape(128, 128, 128)
    cv_all = np.asarray(cache_v_win, f32)[0].reshape(128, 128, 128)
    rowv = np.concatenate([cw.reshape(-1), np.asarray(conv_b, f32)[0], np.asarray(b_gate_a, f32)[0], np.asarray(b_gate_x, f32)[0],
                           np.asarray(lru_lambda, f32)[0]])
    rowp = np.ascontiguousarray(np.broadcast_to(rowv[None, :], (SB, 4096)))
    posh = np.array([g * 4 + cc for cc in range(4) for g in range(2)])
    biass = np.ascontiguousarray(rb[_t5_bucket(127 - np.arange(128))][:, posh])

    def padrows(a):
        return np.concatenate([a.reshape(SB * 128, 128), np.zeros((128, 128), f32)], axis=0)

    in_maps = []
    for c in range(NCORES):
        b, j = c // 4, c % 4
        xc = np.zeros((PRE + CH + 128, D), f32)
        flag = np.zeros((128, 3), f32)
        for kf in range(3):
            cj = j - 3 + kf
            if cj >= 0:
                xc[kf * CH:(kf + 1) * CH] = x_prompt[b, cj * CH:(cj + 1) * CH]
                flag[:, kf] = 1.0
        xc[PRE + 128:] = x_prompt[b, j * CH:(j + 1) * CH]
        xc[PRE:PRE + 128] = xc[PRE - 128:PRE]
        hmask = np.full((128, 1), 0.0 if j > 0 else NEG, f32)
        sel = np.zeros((128, 8), f32)
        for r in range(NCORES):
            if r // 4 == b and r < c:
                sel[:, r] = 1.0
        in_maps.append({"xc": xc, "w_in": w_in_p, "w_out": w_out_p, "gpre": gpre, "gpost": gpost, "convw": convw, "vec4": vec4,
                        "bda": bda, "bdx": bdx, "sinks": sinks, "biasg": biasg, "maskc": maskc, "hmask": hmask, "sel": sel, "flag": flag,
                        "ident": ident, "xs": np.concatenate([xs_all[c * SB:(c + 1) * SB], np.zeros((128 - SB, D), f32)], axis=0),
                        "sconv": np.ascontiguousarray(sconv_all[c * SB:(c + 1) * SB]), "srnn": np.ascontiguousarray(srnn_all[c * SB:(c + 1) * SB]),
                        "ck": padrows(ck_all[c * SB:(c + 1) * SB]), "cv": padrows(cv_all[c * SB:(c + 1) * SB]),
                        "rowp": rowp, "biass": biass})

    if "nc" not in _NC_CACHE:
        _NC_CACHE["nc"] = build_program()
    nc = _NC_CACHE["nc"]
    res = run_bass_kernel_spmd(nc, in_maps, core_ids=list(range(NCORES)))
    R = res.results

    y_prompt = np.stack([np.concatenate([R[b * 4 + j]["y"] for j in range(4)], axis=0) for b in range(2)], axis=0)
    new_conv_p = np.stack([R[3]["nconv"], R[7]["nconv"]])[None]
    new_rnn_p = np.stack([R[3]["nrnn"], R[7]["nrnn"]])[None]
    new_k_p = np.stack([R[3]["nk"].reshape(128, 2, 64), R[7]["nk"].reshape(128, 2, 64)])[None]
    new_v_p = np.stack([R[3]["nv"].reshape(128, 2, 64), R[7]["nv"].reshape(128, 2, 64)])[None]
    cat = lambda k: np.concatenate([R[c][k] for c in range(NCORES)], axis=0)
    y_sample = cat("ys").reshape(128, 1, D).astype(f32)
    new_conv_s = cat("nconvs").reshape(1, 128, 3, 512).astype(f32)
    new_rnn_s = cat("nrnns").reshape(1, 128, 512).astype(f32)
    new_k_s = cat("nks").reshape(1, 128, 128, 2, 64).astype(f32)
    new_v_s = cat("nvs").reshape(1, 128, 128, 2, 64).astype(f32)
    return (y_prompt.astype(f32), y_sample, new_conv_p.astype(f32), new_rnn_p.astype(f32), new_k_p.astype(f32), new_v_p.astype(f32),
            new_conv_s, new_rnn_s, new_k_s, new_v_s)
```

```python
import contextlib
import numpy as np
import concourse.bass as bass
import concourse.mybir as mybir
from concourse.bass_utils import run_bass_kernel_spmd

F32 = mybir.dt.float32
BF16 = mybir.dt.bfloat16
ALU = mybir.AluOpType
AF = mybir.ActivationFunctionType

NCORES = 8
D = 1024
D_IN = 2304
SEQ = 8192
CH = 2048
NT = 16
EPS = 1e-6
NEG = -1e30
LPOS = [0, 4, 1, 5, 2, 6, 3, 7]
C_XR, C_GR, C_Q, C_K, C_V, C_GA = 0, 512, 1024, 1536, 1664, 1792
SB = 16
PRE = 3 * CH
NPT = PRE // 128

ENGS = ("sync", "scalar", "vector", "gpsimd", "tensor")
CC_INC = 1
GROUPS = {"setup": 11, "W1": 9, "W2": 8, "W3": 8, "samp": 18, "samp2": 3, "outs": 14}
GROUPS_SEEN = {}
STAGE = 99
SUB = 99
NBLK = 99
NOOUT = 0
NOLAST = 0
SS = 99
DM = 255
S1 = 99


class Prog:
    def __init__(self, nc, es):
        self.nc = nc
        self.es = es
        self.q = {e: [] for e in ENGS}
        self.sig = {e: 0 for e in ENGS}
        self.pending = {e: False for e in ENGS}
        self.waited = {}
        self.bufs = {}
        self.dma_cnt = {}
        self.grp_seen = {}
        self.sems = {}

    def sem(self, key):
        if key not in self.sems:
            name = "s_" + "_".join(str(k) for k in key)
            self.sems[key] = self.es.enter_context(self.nc.semaphore(name))
        return self.sems[key]

    def _deps(self, eng, reads, writes, extra, skip_key=None):
        deps = set(extra)
        for r in reads:
            b = self.bufs.get(r)
            if b and b["w"] is not None:
                deps.add(b["w"])
            if b and r.startswith("pb"):
                deps.update(h for h in b["r"] if h[1] != eng)
        for w in writes:
            b = self.bufs.get(w)
            if b:
                if b["w"] is not None:
                    deps.add(b["w"])
                deps.update(b["r"])
        best = {}
        for d in deps:
            if d is None:
                continue
            key = d[:2]
            if eng == "tensor" and key == ("eng", "tensor"):
                continue
            if key == skip_key:
                continue
            if d[2] > best.get(key, 0):
                best[key] = d[2]
        waits = []
        for key, val in best.items():
            if self.waited.get((eng, key), 0) >= val:
                continue
            self.waited[(eng, key)] = val
            waits.append((key, val))
        return waits

    def _track(self, h, reads, writes):
        for r in reads:
            b = self.bufs.setdefault(r, {"w": None, "r": []})
            b["r"].append(h)
        for w in writes:
            self.bufs[w] = {"w": h, "r": []}

    def op(self, eng, fn, reads=(), writes=(), signal=True, deps=()):
        waits = self._deps(eng, reads, writes, deps)
        if signal:
            self.sig[eng] += 1
            h = ("eng", eng, self.sig[eng])
            self.pending[eng] = False
        else:
            h = ("eng", eng, self.sig[eng] + 1)
            self.pending[eng] = True
        self._track(h, reads, writes)
        semw = [(self.sem(k), v) for k, v in waits]
        mysem = self.sem(("eng", eng)) if signal else None

        def emit(e):
            for s, v in semw:
                e.wait_ge(s, v)
            ins = fn(e)
            if mysem is not None:
                ins.then_inc(mysem, 1)

        self.q[eng].append(emit)
        return h

    def dma(self, eng, out, in_, slot, reads=(), writes=(), deps=(), group=None, **kw):
        waits = self._deps(eng, reads, writes, deps, skip_key=(("dma", group) if group is not None else None))
        if group is not None:
            slot = group
            self.grp_seen[group] = self.grp_seen.get(group, 0) + 1
            cnt = 16 * GROUPS[group]
        else:
            cnt = self.dma_cnt.get(slot, 0) + 16
        self.dma_cnt[slot] = cnt
        h = ("dma", slot, cnt)
        self._track(h, reads, writes)
        semw = [(self.sem(k), v) for k, v in waits]
        mysem = self.sem(("dma", slot))

        def emit(e):
            for s, v in semw:
                e.wait_ge(s, v)
            e.dma_start(out=out, in_=in_, **kw).then_inc(mysem, 16)

        self.q[eng].append(emit)
        return h

    def cc(self, eng, fn, reads=(), writes=(), inc=None):
        inc = CC_INC if inc is None else inc
        waits = self._deps(eng, reads, writes, ())
        cnt = self.dma_cnt.get("cc", 0) + inc
        self.dma_cnt["cc"] = cnt
        h = ("dma", "cc", cnt)
        self._track(h, reads, writes)
        semw = [(self.sem(k), v) for k, v in waits]
        mysem = self.sem(("dma", "cc"))

        def emit(e):
            for s, v in semw:
                e.wait_ge(s, v)
            fn(e).then_inc(mysem, 1)

        self.q[eng].append(emit)
        return h

    def wait_all(self, eng, handles):
        waits = self._deps(eng, (), (), handles)
        semw = [(self.sem(k), v) for k, v in waits]

        def emit(e):
            for s, v in semw:
                e.wait_ge(s, v)

        self.q[eng].append(emit)

    def emit(self):
        assert not self.pending["tensor"], "PE has unsignalled trailing ops"
        GROUPS_SEEN.clear()
        GROUPS_SEEN.update(self.grp_seen)
        with self.nc.Block() as block:
            @block.sync
            def _(e):
                for f in self.q["sync"]:
                    f(e)

            @block.scalar
            def _(e):
                for f in self.q["scalar"]:
                    f(e)

            @block.vector
            def _(e):
                for f in self.q["vector"]:
                    f(e)

            @block.gpsimd
            def _(e):
                for f in self.q["gpsimd"]:
                    f(e)

            @block.tensor
            def _(e):
                for f in self.q["tensor"]:
                    f(e)


def build_program():
    nc = _build_program()
    if any(GROUPS.get(g) != n for g, n in GROUPS_SEEN.items()):
        GROUPS.update(GROUPS_SEEN)
        nc = _build_program()
        assert all(GROUPS.get(g) == n for g, n in GROUPS_SEEN.items())
    return nc


def _build_program():
    nc = bass.Bass("TRN2", target_bir_lowering=False)

    def din(name, shape):
        return nc.dram_tensor(name, list(shape), F32, kind="ExternalInput").ap()

    def dout(name, shape):
        return nc.dram_tensor(name, list(shape), F32, kind="ExternalOutput").ap()

    xc_d = din("xc", [PRE + CH + 128, D])
    flag_d = din("flag", [128, 3])
    w_in_d = din("w_in", [D, D_IN])
    w_out_d = din("w_out", [D, D])
    gpre_d = din("gpre", [128, 8])
    gpost_d = din("gpost", [128, D])
    convw_d = din("convw", [128, 16])
    vec4_d = din("vec4", [128, 16])
    bda_d = din("bda", [128, 512])
    bdx_d = din("bdx", [128, 512])
    sinks_d = din("sinks", [128, 8])
    biasg_d = din("biasg", [128, 2048])
    maskc_d = din("maskc", [128, 256])
    hmask_d = din("hmask", [128, 1])
    sel_d = din("sel", [128, 8])
    ident_d = din("ident", [128, 128])

    xs_d = din("xs", [128, D])
    sconv_d = din("sconv", [SB, 1536])
    srnn_d = din("srnn", [SB, 512])
    ck_d = din("ck", [SB * 128 + 128, 128])
    cv_d = din("cv", [SB * 128 + 128, 128])
    rowp_d = din("rowp", [SB, 4096])
    biass_d = din("biass", [128, 8])

    y_d = dout("y", [CH, D])
    ys_d = dout("ys", [SB, D])
    nconvs_d = dout("nconvs", [SB, 1536])
    nrnns_d = dout("nrnns", [SB, 512])
    nks_d = dout("nks", [SB, 128, 128])
    nvs_d = dout("nvs", [SB, 128, 128])
    nconv_d = dout("nconv", [3, 512])
    nrnn_d = dout("nrnn", [512])
    nk_d = dout("nk", [128, 128])
    nv_d = dout("nv", [128, 128])

    bounce = nc.dram_tensor("bounce", [128, 8], F32)
    gath = nc.dram_tensor("gath", [NCORES * 128, 8], F32)
    kvb = nc.dram_tensor("kvb", [SB, 256], F32)

    out_handles = []

    with contextlib.ExitStack() as es:
        P = Prog(nc, es)
        def sb(name, shape, dt=F32):
            return es.enter_context(nc.sbuf_tensor("sb_" + name, list(shape), dt))

        Win = sb("Win", [128, 8, D_IN], BF16)
        Wout = sb("Wout", [128, 8, D], BF16)
        P1 = sb("P1", [128, 4, CH], BF16)
        P2 = sb("P2", [128, 4, CH], BF16)
        attT = sb("attT", [128, 4, CH], BF16)
        KT = sb("KT", [128, 128 + 512], BF16)
        Vg = sb("Vg", [128, 5, 132], BF16)
        xt = [sb(f"xt{i}", [128, D]) for i in range(2)]
        xn = [sb(f"xn{i}", [128, D], BF16) for i in range(2)]
        xnT = sb("xnT", [128, 8, 512], BF16)
        xr = sb("xr", [128, 4, 515], BF16)
        xrt = sb("xrt", [128, 4, 4])
        xcv = sb("xcv", [128, 2, 512])
        u_b = sb("u_b", [128, 2, 512])
        a_b = sb("a_b", [128, 2, 512])
        a2_b = sb("a2_b", [128, 2, 512])
        t1_b = sb("t1_b", [128, 2, 512])
        th = [sb(f"th{i}", [128, 512]) for i in range(2)]
        hl = sb("hl", [128, 512])
        Ac = sb("Ac", [128, 512])
        QT = [sb(f"QT{g}", [128, 4, 512], BF16) for g in range(2)]
        BiasT = sb("BiasT", [128, 2, 2, 512], BF16)
        BiasF = sb("BiasF", [128, 2, 512], BF16)
        PT = [sb(f"PT{i}", [128, 2, 2, 512], BF16) for i in range(2)]
        ua = sb("ua", [128, 512])
        sgr = sb("sgr", [128, 512])
        att_o = sb("att_o", [128, 512], BF16)
        mixT = [sb(f"mixT{i}", [128, 4, 128], BF16) for i in range(2)]
        tt = sb("tt", [128, D])
        gpost = sb("gpost", [128, D])
        junk2 = sb("junk2", [128, 512], BF16)
        bda = sb("bda", [128, 4, 128])
        bdx = sb("bdx", [128, 4, 128])
        diagw = sb("diagw", [128, 4, 4, 128], BF16)
        ident = sb("ident", [128, 128], BF16)
        maskc = sb("maskc", [128, 2, 128])
        gpre = sb("gpre", [128, 8])
        convw = sb("convw", [128, 4, 4])
        vec4 = sb("vec4", [128, 4, 4])
        hb = sb("hb", [128, 2, 4])
        chalf = sb("chalf", [128, 4])
        sinks = sb("sinks", [128, 2, 4])
        sinkexp2 = sb("sinkexp2", [128, 2, 4])
        hmask = sb("hmask", [128, 1])
        sel = sb("sel", [128, 8])
        cst = sb("cst", [128, 4])
        ss = sb("ss", [128, 17 + NPT])
        ms = sb("ms", [128, 17 + NPT])
        rstd = sb("rstd", [128, 17 + NPT])
        flag = sb("flag", [128, 3])
        ss2 = sb("ss2", [128, NT, 2])
        ms2 = sb("ms2", [128, NT])
        rstd2 = sb("rstd2", [128, NT])
        car = sb("car", [128, 8])
        dsum = sb("dsum", [128, 4, 2])
        rden = sb("rden", [128, 4, 2])
        G_sb = sb("G_sb", [128, 8, 8])
        Ap = sb("Ap", [128, 8, 4])
        Bp = sb("Bp", [128, 8, 4])
        hscan = sb("hscan", [128, 4, 8])
        h0 = sb("h0", [128, 4])
        hfin = sb("hfin", [128, 4])
        kout = sb("kout", [128, 128])
        vout = sb("vout", [128, 128])
        sth = sb("sth", [128, 8])
        ident32 = sb("ident32", [128, 128])
        ones32 = sb("ones32", [128, 128])
        biass = sb("biass", [128, 8])
        uaT = sb("uaT", [128, 4, SB])
        sm = sb("sm", [128, 16])

        _p1f = P1[:].rearrange("p g n -> p (g n)").bitcast(F32)
        _p2f = P2[:].rearrange("p g n -> p (g n)").bitcast(F32)
        preX = _p1f[:, 0:2048].rearrange("p (g n) -> p g n", n=512)
        preA = _p1f[:, 2048:4096].rearrange("p (g n) -> p g n", n=512)
        preA2 = _p2f[:, 0:2048].rearrange("p (g n) -> p g n", n=512)
        preT1 = _p2f[:, 2048:4096].rearrange("p (g n) -> p g n", n=512)
        pb = [es.enter_context(nc.psum_tensor(f"pb{i}", [128, 512], F32)) for i in range(8)]
        pbT = [p[:].bitcast(BF16).rearrange("p (k c) -> p k c", c=128) for p in pb]
        bank_ctr = [0]

        def bank():
            i = bank_ctr[0]
            bank_ctr[0] = (i + 1) % 8
            return i

        def mk(eng):
            def f(name, *args, r=(), w=(), sig=True, **kw):
                return P.op(eng, lambda e: getattr(e, name)(*args, **kw), reads=r, writes=w, signal=sig)
            return f
        V, A, G = mk("vector"), mk("scalar"), mk("gpsimd")
        _pe = mk("tensor")

        def PE(name, *args, r=(), w=(), sig=False, **kw):
            return _pe(name, *args, r=r, w=w, sig=sig, **kw)

        def ld(dst, src, name, group="setup"):
            P.dma("sync", dst, src, name, writes=[name], group=group)

        ld(gpre[:], gpre_d, "gpre")
        ld(convw[:].rearrange("p g t -> p (g t)"), convw_d, "convw")
        ld(vec4[:].rearrange("p a g -> p (a g)"), vec4_d, "vec4")
        ld(hmask[:], hmask_d, "hmask")
        ld(sel[:], sel_d, "sel")
        ld(sinks[:].rearrange("p g c -> p (g c)"), sinks_d, "sinks")
        ld(maskc[:].rearrange("p b q -> p (b q)"), maskc_d, "maskc")
        ld(bda[:].rearrange("p g m -> p (g m)"), bda_d, "bda")
        ld(bdx[:].rearrange("p g m -> p (g m)"), bdx_d, "bdx")
        ld(gpost[:], gpost_d, "gpost")
        ld(xt[0][:], biasg_d[:, 0:1024], "xt0", group=None)
        ld(xt[1][:], biasg_d[:, 1024:2048], "xt1", group=None)
        P.dma("gpsimd", ident[:], ident_d, "ident", writes=["ident"], group="W1")
        for k in range(8):
            for h in range(2):
                P.dma("gpsimd", Win[:, k, h * 1152:(h + 1) * 1152], w_in_d[k * 128:(k + 1) * 128, h * 1152:(h + 1) * 1152],
                      f"Win{k}_{h}", writes=[f"Win{k}_{h}"], group=("W1" if h == 0 else "W2"))
        for k in range(8):
            P.dma("gpsimd", Wout[:, k, :], w_out_d[k * 128:(k + 1) * 128, :], f"Wout{k}", writes=[f"Wout{k}"], group="W3")

        G("memset", cst[:, 0:1], -0.5, w=["cst"])
        G("memset", cst[:, 1:2], 0.0, r=["cst"], w=["cst"])
        G("memset", cst[:, 2:3], 1.0 / 16.0, r=["cst"], w=["cst"])
        G("memset", Vg[:], 1.0, w=["Vg"])
        G("memset", QT[0][64:128], 0.0, w=["QT0"])
        G("memset", QT[1][0:64], 0.0, w=["QT1"])
        G("memset", ones32[:], 1.0, w=["ones32"])
        A("activation", out=sth[:, 0:4], in_=vec4[:, 3, :], func=AF.Exp, scale=-1.0, r=["vec4"], w=["sth"])
        A("activation", out=sth[:, 4:8], in_=sth[:, 0:4], func=AF.Ln, bias=1.0, r=["sth"], w=["sth"])
        V("tensor_scalar", out=chalf[:], in0=sth[:, 4:8], scalar1=-4.0, scalar2=None, op0=ALU.mult, r=["sth"], w=["chalf"])
        V("tensor_scalar", out=hb[:], in0=vec4[:, 1:3, :], scalar1=0.5, scalar2=None, op0=ALU.mult, r=["vec4"], w=["hb"])
        A("activation", out=sinkexp2[:], in_=sinks[:], func=AF.Exp, r=["sinks"], w=["sinkexp2"])
        V("tensor_scalar", out=sinkexp2[:], in0=sinkexp2[:], scalar1=2.0, scalar2=None, op0=ALU.mult, r=["sinkexp2"], w=["sinkexp2"])
        for blk in range(2):
            V("tensor_tensor", out=BiasT[:, blk].rearrange("p g (c q) -> p (g c) q", q=128),
                                                 in0=xt[blk][:].rearrange("p (h q) -> p h q", q=128),
                                                 in1=maskc[:, blk, :].unsqueeze(1).broadcast_to([128, 8, 128]), op=ALU.add,
              r=[f"xt{blk}", "maskc"], w=["BiasT"])
        V("tensor_tensor", out=xt[0][:].rearrange("p (h q) -> p h q", q=128), in0=xt[0][:].rearrange("p (h q) -> p h q", q=128),
                                    in1=maskc[:, 0, :].unsqueeze(1).broadcast_to([128, 8, 128]), op=ALU.add,
          r=["xt0", "maskc"], w=["xt0"])
        V("tensor_scalar", out=BiasF[:].rearrange("p g c -> p (g c)"), in0=xt[0][:], scalar1=hmask[:, 0:1], scalar2=None, op0=ALU.add,
          r=["xt0", "hmask"], w=["BiasF"])
        for g in range(4):
            for tap in range(4):
                V("tensor_scalar", out=diagw[:, g, tap, :], in0=ident[:], scalar1=convw[:, g, tap:tap + 1], scalar2=None, op0=ALU.mult,
                  r=["ident", "convw"], w=["diagw"])

        def win_names(c0, c1):
            hs = sorted({c0 // 1152, (c1 - 1) // 1152})
            return hs

        def tile_front(t, j):
            s = t % 2
            row = PRE + t * 128 if t < 17 else (t - 17) * 128
            P.dma("sync", xt[s][:], xc_d[row:row + 128, :], f"xt{s}", writes=[f"xt{s}"])
            A("activation", out=xn[s][:], in_=xt[s][:], func=AF.Square, accum_out=ss[:, t:t + 1],
              r=[f"xt{s}"], w=[f"xn{s}", f"ss{t}"])
            G("tensor_scalar", out=ms[:, t:t + 1], in0=ss[:, t:t + 1], scalar1=1.0 / D, scalar2=EPS, op0=ALU.mult, op1=ALU.add,
              r=[f"ss{t}"], w=[f"ms{t}"])
            G("tensor_tensor", out=rstd[:, t:t + 1], in0=ms[:, t:t + 1], in1=cst[:, 0:1], op=ALU.pow,
              r=[f"ms{t}", "cst"], w=[f"rstd{t}"])
            G("tensor_scalar", out=xn[s][:], in0=xt[s][:], scalar1=rstd[:, t:t + 1], scalar2=0.0, op0=ALU.mult, op1=ALU.add,
              r=[f"xt{s}", f"rstd{t}"], w=[f"xn{s}"])
            b = bank()
            for k in range(8):
                PE("transpose", out=pbT[b][:, k, :], in_=xn[s][:, k * 128:(k + 1) * 128], identity=ident[:],
                   r=[f"xn{s}", "ident"], w=[f"pb{b}"], sig=(k == 7))
            V("tensor_tensor", out=xnT[:, :, j * 128:(j + 1) * 128], in0=pbT[b][:, :, :],
                                        in1=gpre[:].unsqueeze(2).broadcast_to([128, 8, 128]), op=ALU.mult,
              r=[f"pb{b}", "gpre"], w=["xnT"])

        def fm_chunk(c0, N):
            b = bank()
            h = c0 // 1152
            for k in range(8):
                PE("matmul", pb[b][:, 0:N], lhsT=Win[:, k, c0:c0 + 128], rhs=xnT[:, k, 0:N], start=(k == 0), stop=(k == 7),
                   r=[f"Win{k}_{h}", "xnT"], w=[f"pb{b}"], sig=(k == 7))
            return b

        thc = [0]

        def th_slot():
            thc[0] ^= 1
            return thc[0]

        def rnn_chain(N, t0, pre=False, nxt=None):
            if pre:
                XCV, AB, A2B, T1B = preX, preA, preA2, preT1
                groups = [(0, 1, 2, 3)]
            else:
                XCV, AB, A2B, T1B = xcv, a_b, a2_b, t1_b
                groups = [(0, 1), (2, 3)]
            for gs in groups:
                for gi, g in enumerate(gs):
                    b = bank()
                    for tap in range(4):
                        PE("matmul", pb[b][:, 0:N], lhsT=diagw[:, g, tap, :], rhs=xr[:, g, tap:tap + N],
                                                                 start=(tap == 0), stop=(tap == 3),
                           r=["diagw", "xr"], w=[f"pb{b}"], sig=(tap == 3))
                    A("activation", out=XCV[:, gi, 0:N], in_=pb[b][:, 0:N], func=AF.Identity, bias=vec4[:, 0, g:g + 1],
                      r=[f"pb{b}", "vec4"], w=[f"xcv{gi}"])
                    if pre:
                        continue
                    b = fm_chunk(C_GR + g * 128, N)
                    s = th_slot()
                    A("activation", out=th[s][:, 0:N], in_=pb[b][:, 0:N], func=AF.Tanh, scale=0.5, r=[f"pb{b}"], w=[f"th{s}"])
                    V("scalar_tensor_tensor", out=u_b[:, gi, 0:N], in0=th[s][:, 0:N], scalar=1.0, in1=pb[b][:, 0:N],
                                                                        op0=ALU.add, op1=ALU.mult,
                      r=[f"pb{b}", f"th{s}"], w=[f"u{gi}"])
                for gi, g in enumerate(gs):
                    bA = bank()
                    PE("matmul", pb[bA][:, 0:N], lhsT=bda[:, g, :], rhs=XCV[:, gi, 0:N], start=True, stop=True,
                       r=["bda", f"xcv{gi}"], w=[f"pb{bA}"], sig=True)
                    bX = bank()
                    PE("matmul", pb[bX][:, 0:N], lhsT=bdx[:, g, :], rhs=XCV[:, gi, 0:N], start=True, stop=True,
                       r=["bdx", f"xcv{gi}"], w=[f"pb{bX}"], sig=True)
                    s = th_slot()
                    A("activation", out=th[s][:, 0:N], in_=pb[bA][:, 0:N], func=AF.Tanh, scale=0.5, bias=hb[:, 0, g:g + 1],
                      r=[f"pb{bA}", "hb"], w=[f"th{s}"])
                    A("activation", out=AB[:, gi, 0:N], in_=th[s][:, 0:N], func=AF.Exp, scale=chalf[:, g:g + 1], bias=chalf[:, g:g + 1],
                      r=[f"th{s}", "chalf"], w=[f"a{gi}"])
                    G("tensor_tensor", out=A2B[:, gi, 0:N], in0=AB[:, gi, 0:N], in1=AB[:, gi, 0:N], op=ALU.mult,
                      r=[f"a{gi}"], w=[f"a2{gi}"])
                    s2 = th_slot()
                    A("activation", out=th[s2][:, 0:N], in_=pb[bX][:, 0:N], func=AF.Tanh, scale=0.5, bias=hb[:, 1, g:g + 1],
                      r=[f"pb{bX}", "hb"], w=[f"th{s2}"])
                    V("scalar_tensor_tensor", out=T1B[:, gi, 0:N], in0=th[s2][:, 0:N], scalar=1.0, in1=XCV[:, gi, 0:N],
                                                                     op0=ALU.add, op1=ALU.mult,
                      r=[f"th{s2}", f"xcv{gi}"], w=[f"t1{gi}"])
                if nxt is not None:
                    front(*nxt)
                rnn_tail.append((gs, N, t0, pre, AB, A2B, T1B))
                if SUB >= 4:
                    flush_rnn_tail()
                else:
                    rnn_tail.clear()

        front_done = set()

        def front(t0, nt):
            if (t0, nt) in front_done:
                return
            front_done.add((t0, nt))
            for j in range(nt):
                tile_front(t0 + j, j)

        def block(bi, t0, nt, pre=False, nxt=None):
            N = nt * 128
            halo = (bi == 0)
            last = (bi == 4) and not NOLAST
            front(t0, nt)
            if not halo and Nprev[0] > 0:
                npv = Nprev[0]
                V("tensor_copy", out=xr[:, :, 0:3], in_=xr[:, :, npv:npv + 3], r=["xr"], w=["xr"])
            for g in range(4):
                b = fm_chunk(C_XR + g * 128, N)
                A("activation", out=xr[:, g, 3:3 + N], in_=pb[b][:, 0:N], func=AF.Copy, r=[f"pb{b}"], w=["xr"])
                if last:
                    A("activation", out=xrt[:, g, :], in_=pb[b][:, N - 4:N], func=AF.Copy, r=[f"pb{b}"], w=["xrt"])
            Nprev[0] = N
            if pre:
                rnn_chain(N, t0, pre=True, nxt=nxt)
                return
            b = fm_chunk(C_K, N)
            if not halo:
                kpv, vpv = KTprev[0], Vprev[0]
                if kpv > 0:
                    V("tensor_copy", out=KT[:, 0:128], in_=KT[:, kpv:kpv + 128], r=["KT"], w=["KT"])
                    V("tensor_copy", out=Vg[:, 0, :], in_=Vg[:, vpv, :], r=["Vg"], w=["Vg"])
                A("activation", out=KT[:, 128:128 + N], in_=pb[b][:, 0:N], func=AF.Copy, r=[f"pb{b}"], w=["KT"])
                KTprev[0] = N
                Vprev[0] = nt
            else:
                A("activation", out=KT[:, 0:128], in_=pb[b][:, 0:N], func=AF.Copy, r=[f"pb{b}"], w=["KT"])
                KTprev[0] = 0
                Vprev[0] = 0
            for j in range(nt):
                t = t0 + j
                vs = 0 if halo else j + 1
                bV = bank()
                for k in range(8):
                    PE("matmul", pb[bV][:, 0:128], lhsT=xnT[:, k, j * 128:(j + 1) * 128], rhs=Win[:, k, C_V:C_V + 128],
                                                    start=(k == 0), stop=(k == 7),
                       r=[f"Win{k}_1", "xnT"], w=[f"pb{bV}"], sig=(k == 7))
                V("tensor_copy", out=Vg[:, vs, :].rearrange("p (g e) -> p g e", e=66)[:, :, 0:64],
                                                 in_=pb[bV][:, 0:128].rearrange("p (g d) -> p g d", d=64),
                  r=[f"pb{bV}"], w=["Vg"])
                if t == NT and not NOLAST:
                    A("activation", out=vout[:], in_=pb[bV][:, 0:128], func=AF.Copy, r=[f"pb{bV}"], w=["vout"])
                    bK = bank()
                    for k in range(8):
                        PE("matmul", pb[bK][:, 0:128], lhsT=xnT[:, k, j * 128:(j + 1) * 128], rhs=Win[:, k, C_K:C_K + 128],
                                                        start=(k == 0), stop=(k == 7),
                           r=[f"Win{k}_1", "xnT"], w=[f"pb{bK}"], sig=(k == 7))
                    A("activation", out=kout[:], in_=pb[bK][:, 0:128], func=AF.Copy, r=[f"pb{bK}"], w=["kout"])
            if halo or SUB < 2:
                return
            for cc in range(4):
                b = fm_chunk(C_Q + cc * 128, N)
                V("tensor_scalar", out=QT[0][0:64, cc, 0:N], in0=pb[b][0:64, 0:N], scalar1=0.125, scalar2=None, op0=ALU.mult,
                  r=[f"pb{b}"], w=["QT0"])
                V("tensor_scalar", out=QT[1][64:128, cc, 0:N], in0=pb[b][64:128, 0:N], scalar1=0.125, scalar2=None, op0=ALU.mult,
                  r=[f"pb{b}"], w=["QT1"])
            if SUB < 3:
                return
            rnn_chain(N, t0)
            if SUB < 5:
                return
            for j in range(nt):
                t = t0 + j
                bG = bank()
                for k in range(8):
                    PE("matmul", pb[bG][:, 0:512], lhsT=xnT[:, k, j * 128:(j + 1) * 128], rhs=Win[:, k, C_GA:C_GA + 512],
                                                    start=(k == 0), stop=(k == 7),
                       r=[f"Win{k}_1", "xnT"], w=[f"pb{bG}"], sig=(k == 7))
                s = th_slot()
                A("activation", out=th[s][:], in_=pb[bG][:], func=AF.Tanh, scale=0.5, r=[f"pb{bG}"], w=[f"th{s}"])
                V("scalar_tensor_tensor", out=ua[:], in0=th[s][:], scalar=1.0, in1=pb[bG][:], op0=ALU.add, op1=ALU.mult,
                  r=[f"pb{bG}", f"th{s}"], w=["ua"])
                if SUB >= 6:
                    attention(t, j)

        def attention(t, j):
            ps = t % 2
            for blk in range(2):
                kc = (j + blk) * 128
                for g in range(2):
                    b = bank()
                    bias_ap = BiasF[:, g, :] if (t == 1 and blk == 0) else BiasT[:, blk, g, :]
                    bname = "BiasF" if (t == 1 and blk == 0) else "BiasT"
                    PE("matmul", pb[b][:].rearrange("p (c q) -> p c q", q=128), lhsT=KT[:, kc:kc + 128],
                                                           rhs=QT[g][:, :, j * 128:(j + 1) * 128], start=True, stop=False,
                       r=["KT", f"QT{g}"], w=[f"pb{b}"])
                    PE("matmul", pb[b][:], lhsT=ident[:], rhs=bias_ap, start=False, stop=True,
                       r=["ident", bname], w=[f"pb{b}"], sig=True)
                    A("activation", out=PT[ps][:, blk, g, :], in_=pb[b][:], func=AF.Exp, r=[f"pb{b}"], w=[f"PT{ps}"])
            if SUB < 7:
                return
            bO = []
            for g in range(2):
                b = bank()
                bO.append(b)
                for cc in range(4):
                    for blk in range(2):
                        PE("matmul", pb[b][:, cc * 66:cc * 66 + 66], lhsT=PT[ps][:, blk, g, cc * 128:(cc + 1) * 128],
                                                                        rhs=Vg[:, j + blk, g * 66:(g + 1) * 66], start=(blk == 0), stop=(blk == 1),
                           r=[f"PT{ps}", "Vg"], w=[f"pb{b}"], sig=(cc == 3 and blk == 1))
                V("scalar_tensor_tensor", out=dsum[:, :, g], in0=pb[b][:, 0:264].rearrange("p (c e) -> p c e", e=66)[:, :, 64],
                                                             scalar=2.0, in1=sinkexp2[:, g, :], op0=ALU.mult, op1=ALU.add,
                  r=[f"pb{b}", "sinkexp2"], w=["dsum"])
            if SUB < 8:
                return
            V("reciprocal", out=rden[:], in_=dsum[:], r=["dsum"], w=["rden"])
            G("tensor_tensor", out=sgr[:].rearrange("p (c g d) -> p c g d", g=2, d=64), in0=ua[:].rearrange("p (c g d) -> p c g d", g=2, d=64),
                                        in1=rden[:].unsqueeze(3).broadcast_to([128, 4, 2, 64]), op=ALU.mult,
              r=["ua", "rden"], w=["sgr"])
            for g in range(2):
                b = bO[g]
                V("tensor_tensor", out=att_o[:].rearrange("p (c g d) -> p c g d", g=2, d=64)[:, :, g, :],
                                                      in0=pb[b][:, 0:264].rearrange("p (c e) -> p c e", e=66)[:, :, 0:64],
                                                      in1=sgr[:].rearrange("p (c g d) -> p c g d", g=2, d=64)[:, :, g, :], op=ALU.mult,
                  r=[f"pb{b}", "sgr"], w=["att_o"])
            b = bank()
            for cc in range(4):
                PE("transpose", out=pbT[b][:, cc, :], in_=att_o[:, cc * 128:(cc + 1) * 128], identity=ident[:],
                   r=["att_o", "ident"], w=[f"pb{b}"], sig=(cc == 3))
            A("activation", out=attT[:, :, (t - 1) * 128:t * 128], in_=pbT[b][:, 0:4, :], func=AF.Copy, r=[f"pb{b}"], w=[f"attT{t}"])

        rnn_tail = []
        first_scan = [True]
        first_A = [True]

        def flush_rnn_tail():
            for gs, N, t0, pre, AB, A2B, T1B in rnn_tail:
                for gi, g in enumerate(gs):
                    A("activation", out=A2B[:, gi, 0:N], in_=A2B[:, gi, 0:N], func=AF.Sqrt, scale=-1.0 / 16.0, bias=1.0 / 16.0,
                      r=[f"a2{gi}"], w=[f"a2{gi}"])
            for gs, N, t0, pre, AB, A2B, T1B in rnn_tail:
                c0 = (t0 - 1) * 128
                for gi, g in enumerate(gs):
                    G("tensor_tensor", out=T1B[:, gi, 0:N], in0=T1B[:, gi, 0:N], in1=A2B[:, gi, 0:N], op=ALU.mult,
                      r=[f"t1{gi}", f"a2{gi}"], w=[f"t1{gi}"])
                    fs = first_scan[0]
                    fsA = first_A[0]
                    V("tensor_tensor_scan", out=hl[:, 0:N], data0=AB[:, gi, 0:N], data1=T1B[:, gi, 0:N],
                                                                              initial=(0.0 if fs else car[:, 4 + g:5 + g]), op0=ALU.mult, op1=ALU.add,
                      r=[f"a{gi}", f"t1{gi}", "car"], w=["hl"])
                    V("tensor_copy", out=car[:, 4 + g:5 + g], in_=hl[:, N - 1:N], r=["hl", "car"], w=["car"])
                    if pre:
                        continue
                    V("tensor_tensor_scan", out=Ac[:, 0:N], data0=AB[:, gi, 0:N], data1=cst[:, 1:2].broadcast_to([128, N]),
                                                                              initial=(1.0 if fsA else car[:, g:g + 1]), op0=ALU.mult, op1=ALU.add,
                      r=[f"a{gi}", "cst", "car"], w=["Ac"])
                    V("tensor_copy", out=car[:, g:g + 1], in_=Ac[:, N - 1:N], r=["Ac", "car"], w=["car"])
                    G("tensor_tensor", out=P1[:, g, c0:c0 + N], in0=hl[:, 0:N], in1=u_b[:, gi, 0:N], op=ALU.mult,
                      r=["hl", f"u{gi}"], w=[f"P1_{g}_{c0}"])
                    G("tensor_tensor", out=P2[:, g, c0:c0 + N], in0=Ac[:, 0:N], in1=u_b[:, gi, 0:N], op=ALU.mult,
                      r=["Ac", f"u{gi}"], w=[f"P2_{g}_{c0}"])
                if gs[-1] == 3:
                    first_scan[0] = False
                    if not pre:
                        first_A[0] = False
            rnn_tail.clear()

        Nprev = [0]
        KTprev = [0]
        Vprev = [0]
        G("memset", xr[:], 0.0, w=["xr"])
        ld(flag[:], flag_d, "flag")
        for pb_i in range(NPT // 4):
            nxt = (17 + 4 * (pb_i + 1), 4) if pb_i + 1 < NPT // 4 else (0, 1)
            block(100 + pb_i, 17 + 4 * pb_i, 4, pre=True, nxt=nxt)
            if pb_i % 4 == 3:
                kf = pb_i // 4
                V("tensor_scalar", out=car[:, 4:8], in0=car[:, 4:8], scalar1=flag[:, kf:kf + 1], scalar2=None, op0=ALU.mult,
                  r=["car", "flag"], w=["car"])
        Nprev[0] = 0
        blocks = [(0, 0, 1), (1, 1, 4), (2, 5, 4), (3, 9, 4), (4, 13, 4)]
        for bi, t0, nt in blocks:
            if STAGE >= (1 if bi == 0 else 2 if bi == 1 else 3) and bi <= NBLK:
                block(bi, t0, nt)

        if STAGE >= 3 and not NOOUT:
            for g in range(4):
                out_handles.append(P.dma("sync", nconv_d[:, g * 128:(g + 1) * 128].rearrange("t p -> p t"), xrt[:, g, 1:4], f"o_nconv{g}", reads=["xrt"],
                                         allow_slow_non_contiguous=True, group="outs"))
            out_handles.append(P.dma("sync", nk_d, kout[:], "o_nk", reads=["kout"], group="outs"))
            out_handles.append(P.dma("sync", nv_d, vout[:], "o_nv", reads=["vout"], group="outs"))

        if STAGE >= 4:
            G("memset", h0[:], 0.0, w=["h0"])
            V("tensor_scalar", out=hfin[:], in0=car[:, 4:8], scalar1=2.0, scalar2=None, op0=ALU.mult, r=["car"], w=["hfin"])
            out_handles.append(P.dma("sync", nrnn_d.rearrange("(g p) -> p g", p=128), hfin[:], "o_nrnn", reads=["hfin"],
                                     allow_slow_non_contiguous=True, group="outs"))

        def sample_path():
            F = lambda ap: ap.bitcast(F32)
            xnTs = QT[0][:].rearrange("p c n -> p (c n)")[:, 0:1024].rearrange("p (k t) -> p k t", t=128)
            Kn = F(xnT[:].rearrange("p k n -> p (k n)")).rearrange("p (s c) -> p s c", c=128)
            Vn = [F(PT[h][:].rearrange("p a b n -> p (a b n)")).rearrange("p (s c) -> p s c", c=128) for h in range(2)]
            KnT = [a_b[:].rearrange("p a n -> p (a n)").rearrange("p (s c) -> p s c", c=128),
                   a2_b[:].rearrange("p a n -> p (a n)").rearrange("p (s c) -> p s c", c=128)]
            xcT = u_b[:].rearrange("p a n -> p (a n)")[:, 0:512].rearrange("p (g t) -> p g t", t=128)
            Qm = th[0][:, 0:128].rearrange("p (s q) -> p s q", q=8)
            Pt = hl[:, 0:128]
            sc = Ac[:, 0:128]
            rds = Ac[:, 128:256]
            attn = hl[:, 128:256]
            xcs = t1_b[:, 0, :]
            tA = t1_b[:, 1, :]
            tB = xcv[:, 0, :]
            tC = xcv[:, 1, :]
            tD = ua[:]
            tE = sgr[:]
            mixtm = xn[0][:, 0:512]
            mixTs = xn[1][:].rearrange("p (k t) -> p k t", t=128)
            R16 = slice(0, SB)
            flat32 = lambda t, pat: F(t[:].rearrange(pat))
            hosts = [(xt[1][:], "xt1"), (flat32(QT[1], "p c n -> p (c n)"), "QT1"), (flat32(BiasT, "p a b n -> p (a b n)"), "BiasT"),
                     (flat32(diagw, "p g t m -> p (g t m)"), "diagw")]
            rp, rpn = [], []
            for hap, hname in hosts:
                for i2 in range(2):
                    rp.append(hap[R16, i2 * 512:(i2 + 1) * 512])
                    rpn.append(hname)
            for i8 in range(8):
                P.dma("sync", rp[i8], rowp_d[:, i8 * 512:(i8 + 1) * 512], f"rowp{i8}", writes=[rpn[i8]], group="samp")
            sst_t = [tt[R16, 0:512], tt[R16, 512:1024], F(BiasF[:].rearrange("p g n -> p (g n)"))[R16, 0:512]]
            sst_n = ["tt", "tt", "BiasF"]
            for t3 in range(3):
                P.dma("sync", sst_t[t3], sconv_d[:, t3 * 512:(t3 + 1) * 512], f"sst{t3}", writes=[sst_n[t3]], group="samp")
            hprev = th[1][R16, :]
            P.dma("sync", hprev, srnn_d, "hprev", writes=["th1"], group="samp")
            kv_s = F(att_o[:])[R16, :]
            ld(ident32[:], ident_d, "ident32", group="samp")
            ld(biass[:], biass_d, "biass", group="samp")
            def win(src, h):
                return src[1 + 8 * h * 128:1 + (8 * h + 8) * 128, :].rearrange("(s k) c -> k s c", k=128)
            big = [None]
            for h in range(2):
                big[0] = P.dma("sync", Kn[:, 8 * h:8 * h + 8, :], win(ck_d, h), f"Kn_a{h}", writes=["xnT", f"KnH{h}"], deps=[big[0]])
                big[0] = P.dma("sync", Vn[h][:], win(cv_d, h), f"Vn_a{h}", writes=[f"PT{h}"], deps=[big[0]])
            if SS < 1:
                return
            if S1 < 1:
                return
            pass
            if S1 < 2:
                return
            P.dma("sync", xt[0][:], xs_d, "xt0", writes=["xt0"])
            if S1 < 3:
                return
            pass
            if S1 < 4:
                return
            A("activation", out=xn[0][:], in_=xt[0][:], func=AF.Square, accum_out=sm[:, 0:1], r=["xt0", "sm"], w=["xn0", "sm"])
            if S1 < 5:
                return
            V("tensor_scalar", out=sm[:, 1:2], in0=sm[:, 0:1], scalar1=1.0 / D, scalar2=EPS, op0=ALU.mult, op1=ALU.add, r=["sm"], w=["sm"])
            if S1 < 6:
                return
            A("activation", out=sm[:, 1:2], in_=sm[:, 1:2], func=AF.Sqrt, r=["sm"], w=["sm"])
            if S1 < 7:
                return
            V("reciprocal", out=sm[:, 2:3], in_=sm[:, 1:2], r=["sm"], w=["sm"])
            if S1 < 8:
                return
            A("activation", out=xn[0][:], in_=xt[0][:], func=AF.Copy, scale=sm[:, 2:3], r=["xt0", "sm"], w=["xn0"])
            if S1 < 9:
                return
            b = bank()
            for k in range(8):
                PE("transpose", out=pbT[b][:, k, :], in_=xn[0][:, k * 128:(k + 1) * 128], identity=ident[:], r=["xn0", "ident"], w=[f"pb{b}"], sig=(k == 7))
            if S1 < 10:
                return
            V("tensor_tensor", out=xnTs, in0=pbT[b][:, :, :], in1=gpre[:].unsqueeze(2).broadcast_to([128, 8, 128]), op=ALU.mult,
              r=[f"pb{b}", "gpre"], w=["QT0"])

            def tm_cols(c0, w):
                bb = bank()
                for k in range(8):
                    PE("matmul", pb[bb][:, 0:w], lhsT=xnTs[:, k, :], rhs=Win[:, k, c0:c0 + w], start=(k == 0), stop=(k == 7),
                       r=["QT0"] + [f"Win{k}_{hh}" for hh in sorted({c0 // 1152, (c0 + w - 1) // 1152})], w=[f"pb{bb}"], sig=(k == 7))
                return bb

            def fm_cols(c0):
                bb = bank()
                hh = c0 // 1152
                for k in range(8):
                    PE("matmul", pb[bb][:, 0:128], lhsT=Win[:, k, c0:c0 + 128], rhs=xnTs[:, k, :], start=(k == 0), stop=(k == 7),
                       r=["QT0", f"Win{k}_{hh}"], w=[f"pb{bb}"], sig=(k == 7))
                return bb

            if SS < 2:
                return
            bKV = tm_cols(C_K, 256)
            A("activation", out=kv_s, in_=pb[bKV][R16, 0:256], func=AF.Copy, r=[f"pb{bKV}"], w=["att_o"])
            P.dma("sync", kvb.ap(), kv_s, "kvb", reads=["att_o"], writes=["kvb"])
            P.dma("sync", Kn[127:128], kvb.ap()[:, 0:128].unsqueeze(0), "Kn_b", reads=["kvb", "xnT", "KnH0", "KnH1"], writes=["xnT"], group="samp2")
            for h in range(2):
                P.dma("sync", Vn[h][127:128], kvb.ap()[8 * h:8 * h + 8, 128:256].unsqueeze(0), f"Vn_b{h}", reads=["kvb", f"PT{h}"], writes=[f"PT{h}"], group="samp2")
            for h in range(2):
                big[0] = P.dma("sync", nks_d[8 * h:8 * h + 8].rearrange("s k c -> k s c"), Kn[:, 8 * h:8 * h + 8, :], f"o_nks{h}", reads=["xnT", f"KnH{h}"], deps=[big[0]])
                out_handles.append(big[0])
                big[0] = P.dma("sync", nvs_d[8 * h:8 * h + 8].rearrange("s k c -> k s c"), Vn[h][:], f"o_nvs{h}", reads=[f"PT{h}"], deps=[big[0]])
                out_handles.append(big[0])
            if SS < 3:
                return
            for cc in range(4):
                bq = fm_cols(C_Q + cc * 128)
                V("tensor_scalar", out=Qm[0:64, :, 2 * cc], in0=pb[bq][0:64, 0:SB], scalar1=0.125, scalar2=None, op0=ALU.mult, r=[f"pb{bq}"], w=["th0"])
                V("tensor_scalar", out=Qm[64:128, :, 2 * cc + 1], in0=pb[bq][64:128, 0:SB], scalar1=0.125, scalar2=None, op0=ALU.mult, r=[f"pb{bq}"], w=["th0"])
            for cc in range(4):
                bg = fm_cols(C_GA + cc * 128)
                A("activation", out=tD[:, 0:SB], in_=pb[bg][:, 0:SB], func=AF.Tanh, scale=0.5, r=[f"pb{bg}"], w=["ua"])
                V("scalar_tensor_tensor", out=uaT[:, cc, :], in0=tD[:, 0:SB], scalar=1.0, in1=pb[bg][:, 0:SB], op0=ALU.add, op1=ALU.mult,
                  r=[f"pb{bg}", "ua"], w=["uaT"])
            if SS < 4:
                return
            bXR = tm_cols(C_XR, 512)
            bGR = tm_cols(C_GR, 512)
            cw = lambda tap: rp[tap]
            cb_r, bga_r, bgx_r, lam_r = rp[4], rp[5], rp[6], rp[7]
            A("activation", out=tB[R16], in_=pb[bXR][R16, :], func=AF.Copy, r=[f"pb{bXR}"], w=["xcv0"])
            out_handles.append(P.dma("sync", nconvs_d[:, 0:512], sst_t[1], "o_ncs_a", reads=["tt"]))
            out_handles.append(P.dma("sync", nconvs_d[:, 512:1024], sst_t[2], "o_ncs_c", reads=["BiasF"], group="outs"))
            out_handles.append(P.dma("sync", nconvs_d[:, 1024:1536], tB[R16], "o_ncs_b", reads=["xcv0"], group="outs"))
            V("tensor_tensor", out=xcs[R16], in0=tB[R16], in1=cw(3), op=ALU.mult, r=["xcv0", rpn[3]], w=["t10"])
            for tap in range(3):
                V("tensor_tensor", out=tA[R16], in0=sst_t[tap], in1=cw(tap), op=ALU.mult, r=[sst_n[tap], rpn[tap]], w=["t11"])
                V("tensor_tensor", out=xcs[R16], in0=xcs[R16], in1=tA[R16], op=ALU.add, r=["t10", "t11"], w=["t10"])
            V("tensor_tensor", out=xcs[R16], in0=xcs[R16], in1=cb_r, op=ALU.add, r=["t10", rpn[4]], w=["t10"])
            b = bank()
            for g in range(4):
                PE("transpose", out=pb[b][:, g * 128:(g + 1) * 128], in_=xcs[:, g * 128:(g + 1) * 128], identity=ident32[:],
                   r=["t10", "ident32"], w=[f"pb{b}"], sig=(g == 3))
            V("tensor_copy", out=xcT, in_=pb[b][:].rearrange("p (g t) -> p g t", t=128), r=[f"pb{b}"], w=["u0"])
            bA, bX = bank(), bank()
            for g in range(4):
                PE("matmul", pb[bA][:, g * 128:(g + 1) * 128], lhsT=xcT[:, g, :], rhs=bda[:, g, :], start=True, stop=True, r=["u0", "bda"], w=[f"pb{bA}"], sig=(g == 3))
            for g in range(4):
                PE("matmul", pb[bX][:, g * 128:(g + 1) * 128], lhsT=xcT[:, g, :], rhs=bdx[:, g, :], start=True, stop=True, r=["u0", "bdx"], w=[f"pb{bX}"], sig=(g == 3))
            A("activation", out=tC[R16], in_=lam_r, func=AF.Exp, scale=-1.0, r=[rpn[7]], w=["xcv1"])
            A("activation", out=tC[R16], in_=tC[R16], func=AF.Ln, bias=1.0, r=["xcv1"], w=["xcv1"])
            V("tensor_scalar", out=tC[R16], in0=tC[R16], scalar1=-4.0, scalar2=None, op0=ALU.mult, r=["xcv1"], w=["xcv1"])
            V("tensor_tensor", out=tA[R16], in0=pb[bA][R16, :], in1=bga_r, op=ALU.add, r=[f"pb{bA}", rpn[5]], w=["t11"])
            A("activation", out=tA[R16], in_=tA[R16], func=AF.Tanh, scale=0.5, r=["t11"], w=["t11"])
            V("scalar_tensor_tensor", out=tA[R16], in0=tA[R16], scalar=1.0, in1=tC[R16], op0=ALU.add, op1=ALU.mult, r=["t11", "xcv1"], w=["t11"])
            A("activation", out=tA[R16], in_=tA[R16], func=AF.Exp, r=["t11"], w=["t11"])
            V("tensor_tensor", out=tC[R16], in0=pb[bX][R16, :], in1=bgx_r, op=ALU.add, r=[f"pb{bX}", rpn[6]], w=["xcv1"])
            A("activation", out=tC[R16], in_=tC[R16], func=AF.Tanh, scale=0.5, r=["xcv1"], w=["xcv1"])
            V("scalar_tensor_tensor", out=tC[R16], in0=tC[R16], scalar=1.0, in1=xcs[R16], op0=ALU.add, op1=ALU.mult, r=["xcv1", "t10"], w=["xcv1"])
            A("activation", out=tE[R16], in_=pb[bGR][R16, :], func=AF.Tanh, scale=0.5, r=[f"pb{bGR}"], w=["sgr"])
            V("scalar_tensor_tensor", out=tE[R16], in0=tE[R16], scalar=1.0, in1=pb[bGR][R16, :], op0=ALU.add, op1=ALU.mult, r=["sgr", f"pb{bGR}"], w=["sgr"])
            V("tensor_tensor", out=tD[R16], in0=tA[R16], in1=tA[R16], op=ALU.mult, r=["t11"], w=["ua"])
            A("activation", out=tD[R16], in_=tD[R16], func=AF.Sqrt, scale=-1.0 / 16.0, bias=1.0 / 16.0, r=["ua"], w=["ua"])
            V("tensor_tensor", out=tC[R16], in0=tC[R16], in1=tD[R16], op=ALU.mult, r=["xcv1", "ua"], w=["xcv1"])
            V("tensor_tensor", out=tA[R16], in0=tA[R16], in1=hprev, op=ALU.mult, r=["t11", "th1"], w=["t11"])
            V("scalar_tensor_tensor", out=tC[R16], in0=tC[R16], scalar=2.0, in1=tA[R16], op0=ALU.mult, op1=ALU.add, r=["xcv1", "t11"], w=["xcv1"])
            out_handles.append(P.dma("sync", nrnns_d, tC[R16], "o_nrs", reads=["xcv1"], group="outs"))
            V("scalar_tensor_tensor", out=mixtm[R16], in0=tC[R16], scalar=0.5, in1=tE[R16], op0=ALU.mult, op1=ALU.mult, r=["xcv1", "sgr"], w=["xn0"])
            b = bank()
            for g in range(4):
                PE("transpose", out=pbT[b][:, g, :], in_=mixtm[:, g * 128:(g + 1) * 128], identity=ident[:], r=["xn0", "ident"], w=[f"pb{b}"], sig=(g == 3))
            V("tensor_copy", out=mixTs[:, 0:4, :], in_=pbT[b][:, 0:4, :], r=[f"pb{b}"], w=["xn1"])
            if SS < 5:
                return
            for q4 in range(4):
                b = bank()
                for i4 in range(4):
                    sq = q4 * 4 + i4
                    PE("transpose", out=pb[b][:, i4 * 128:(i4 + 1) * 128], in_=Kn[:, sq, :], identity=ident32[:], r=["xnT", "ident32"], w=[f"pb{b}"], sig=(i4 == 3))
                hh, s0 = q4 // 2, (q4 % 2) * 4
                V("tensor_copy", out=KnT[hh][:, s0:s0 + 4, :], in_=pb[b][:].rearrange("p (s k) -> p s k", k=128), r=[f"pb{b}"], w=[f"a{hh}" if hh == 0 else "a20"])
            bS = bank()
            for sq in range(SB):
                hh, s0 = sq // 8, sq % 8
                PE("matmul", pb[bS][:, sq * 8:(sq + 1) * 8], lhsT=KnT[hh][:, s0, :], rhs=Qm[:, sq, :], start=True, stop=True,
                   r=["a0" if hh == 0 else "a20", "th0"], w=[f"pb{bS}"], sig=(sq == SB - 1))
            V("tensor_tensor", out=sc.rearrange("p (s q) -> p s q", q=8), in0=pb[bS][:, 0:128].rearrange("p (s q) -> p s q", q=8),
              in1=biass[:].unsqueeze(1).broadcast_to([128, SB, 8]), op=ALU.add, r=[f"pb{bS}", "biass"], w=["Ac"])
            A("activation", out=Pt, in_=sc, func=AF.Exp, r=["Ac"], w=["hl"])
            bO = bank()
            for sq in range(SB):
                hh, s0 = sq // 8, sq % 8
                PE("matmul", pb[bO][:, sq * 8:(sq + 1) * 8], lhsT=Vn[hh][:, s0, :], rhs=Pt[:, sq * 8:(sq + 1) * 8], start=True, stop=True,
                   r=[f"PT{hh}", "hl"], w=[f"pb{bO}"], sig=(sq == SB - 1))
            bD = bank()
            PE("matmul", pb[bD][:, 0:128], lhsT=ones32[:], rhs=Pt, start=True, stop=True, r=["ones32", "hl"], w=[f"pb{bD}"], sig=True)
            V("tensor_copy", out=sm[:, 8:16].rearrange("p (c g) -> p c g", g=2), in_=sinkexp2[:].rearrange("p g c -> p c g"), r=["sinkexp2", "sm"], w=["sm"])
            V("scalar_tensor_tensor", out=rds.rearrange("p (s q) -> p s q", q=8), in0=pb[bD][:, 0:128].rearrange("p (s q) -> p s q", q=8),
              scalar=2.0, in1=sm[:, 8:16].unsqueeze(1).broadcast_to([128, SB, 8]), op0=ALU.mult, op1=ALU.add,
              r=[f"pb{bD}", "sm"], w=["Ac"])
            V("reciprocal", out=rds, in_=rds, r=["Ac"], w=["Ac"])
            V("tensor_tensor", out=attn, in0=pb[bO][:, 0:128], in1=rds, op=ALU.mult, r=[f"pb{bO}", "Ac"], w=["hl"])
            av = attn.rearrange("p (s c g) -> p c s g", c=4, g=2)
            V("tensor_tensor", out=mixTs[0:64, 4:8, 0:SB], in0=av[0:64, :, :, 0], in1=uaT[0:64, :, :], op=ALU.mult, r=["hl", "uaT"], w=["xn1"])
            V("tensor_tensor", out=mixTs[64:128, 4:8, 0:SB], in0=av[64:128, :, :, 1], in1=uaT[64:128, :, :], op=ALU.mult, r=["hl", "uaT"], w=["xn1"])
            if SS < 6:
                return
            bY = [bank(), bank()]
            for kk in range(8):
                for hf in range(2):
                    PE("matmul", pb[bY[hf]][:], lhsT=mixTs[:, kk, :], rhs=Wout[:, kk, hf * 512:(hf + 1) * 512], start=(kk == 0), stop=(kk == 7),
                       r=["xn1", f"Wout{kk}"], w=[f"pb{bY[hf]}"], sig=(kk == 7))
            for hf in range(2):
                A("activation", out=junk2[:], in_=pb[bY[hf]][:], func=AF.Square, accum_out=sm[:, 3 + hf:4 + hf], r=[f"pb{bY[hf]}", "sm"], w=["junk2", "sm"])
            V("tensor_tensor", out=sm[:, 5:6], in0=sm[:, 3:4], in1=sm[:, 4:5], op=ALU.add, r=["sm"], w=["sm"])
            V("tensor_scalar", out=sm[:, 5:6], in0=sm[:, 5:6], scalar1=1.0 / D, scalar2=EPS, op0=ALU.mult, op1=ALU.add, r=["sm"], w=["sm"])
            A("activation", out=sm[:, 5:6], in_=sm[:, 5:6], func=AF.Sqrt, r=["sm"], w=["sm"])
            V("reciprocal", out=sm[:, 6:7], in_=sm[:, 5:6], r=["sm"], w=["sm"])
            for hf in range(2):
                V("scalar_tensor_tensor", out=tt[:, hf * 512:(hf + 1) * 512], in0=pb[bY[hf]][:], scalar=sm[:, 6:7], in1=gpost[:, hf * 512:(hf + 1) * 512],
                  op0=ALU.mult, op1=ALU.mult, r=[f"pb{bY[hf]}", "sm", "gpost"], w=["tt"])
            V("tensor_tensor", out=xt[0][R16, :], in0=tt[R16, :], in1=xt[0][R16, :], op=ALU.add, r=["tt", "xt0"], w=["xt0"])
            out_handles.append(P.dma("sync", ys_d, xt[0][R16, :], "o_ys", reads=["xt0"]))


        if STAGE >= 5:
            F2 = lambda ap: ap.bitcast(F32)
            xsl = [(xt[0][:], ["xt0"]), (xt[1][:], ["xt1"]),
                   (F2(PT[0][:].rearrange("p a b n -> p (a b n)")), ["PT0"]), (F2(PT[1][:].rearrange("p a b n -> p (a b n)")), ["PT1"])]
            tsl = [(tt[:], ["tt"]), (a_b[:].rearrange("p a n -> p (a n)"), ["a0", "a1"]), (a2_b[:].rearrange("p a n -> p (a n)"), ["a20", "a21"])]
            bYs = {}

            def p2_front(i):
                s = i % 2
                xa, xnm = xsl[i % 4]
                c0 = i * 128
                blk0 = (i // 4) * 512
                P.dma("sync", xa, xc_d[PRE + (i + 1) * 128:PRE + (i + 2) * 128, :], f"x2_{i % 4}", writes=xnm)
                for g in range(4):
                    V("scalar_tensor_tensor", out=mixT[s][:, g, :], in0=P2[:, g, c0:c0 + 128], scalar=h0[:, g:g + 1], in1=P1[:, g, c0:c0 + 128],
                      op0=ALU.mult, op1=ALU.add, r=[f"P1_{g}_{blk0}", f"P2_{g}_{blk0}", "h0"], w=[f"mixT{s}"])
                bY = [bank(), bank()]
                bYs[i] = bY
                for kk in range(8):
                    lhs = mixT[s][:, kk, :] if kk < 4 else attT[:, kk - 4, c0:c0 + 128]
                    rn = [f"mixT{s}"] if kk < 4 else [f"attT{i + 1}"]
                    for hf in range(2):
                        PE("matmul", pb[bY[hf]][:], lhsT=lhs, rhs=Wout[:, kk, hf * 512:(hf + 1) * 512], start=(kk == 0), stop=(kk == 7),
                           r=rn + [f"Wout{kk}"], w=[f"pb{bY[hf]}"], sig=(kk == 7))

            def p2_back(i):
                xa, xnm = xsl[i % 4]
                ta, tnm = tsl[i % 3]
                c0 = i * 128
                bY = bYs[i]
                for hf in range(2):
                    A("activation", out=junk2[:], in_=pb[bY[hf]][:], func=AF.Square, accum_out=ss2[:, i, hf:hf + 1],
                      r=[f"pb{bY[hf]}"], w=["junk2", f"ss2_{i}_{hf}"])
                G("tensor_tensor", out=ms2[:, i:i + 1], in0=ss2[:, i, 0:1], in1=ss2[:, i, 1:2], op=ALU.add,
                  r=[f"ss2_{i}_0", f"ss2_{i}_1"], w=[f"ms2_{i}"])
                G("tensor_scalar", out=ms2[:, i:i + 1], in0=ms2[:, i:i + 1], scalar1=1.0 / D, scalar2=EPS, op0=ALU.mult, op1=ALU.add,
                  r=[f"ms2_{i}"], w=[f"ms2_{i}"])
                G("tensor_tensor", out=rstd2[:, i:i + 1], in0=ms2[:, i:i + 1], in1=cst[:, 0:1], op=ALU.pow,
                  r=[f"ms2_{i}", "cst"], w=[f"rstd2_{i}"])
                for hf in range(2):
                    V("scalar_tensor_tensor", out=ta[:, hf * 512:(hf + 1) * 512], in0=pb[bY[hf]][:], scalar=rstd2[:, i:i + 1],
                      in1=gpost[:, hf * 512:(hf + 1) * 512], op0=ALU.mult, op1=ALU.mult,
                      r=[f"pb{bY[hf]}", f"rstd2_{i}", "gpost"] + tnm, w=tnm)
                G("tensor_tensor", out=xa, in0=ta, in1=xa, op=ALU.add, r=tnm + xnm, w=xnm)
                out_handles.append(P.dma("sync", y_d[c0:c0 + 128, :], xa, f"o_y{i % 4}", reads=xnm))

            p2_front(0)
            for i in range(NT):
                if i + 1 < NT:
                    p2_front(i + 1)
                p2_back(i)

        if STAGE >= 6:
            G("memset", th[0][:, 0:128], 0.0, w=["th0"])
            G("memset", t1_b[:], 0.0, w=["t10", "t11"])
            G("memset", xn[1][:], 0.0, w=["xn1"])
            sample_path()

        P.wait_all("sync", out_handles)
        P.emit()
    return nc


def _t5_bucket(dist):
    dist = np.maximum(dist, 0)
    max_exact = 16
    d = np.maximum(dist, 1).astype(np.float32)
    large = max_exact + (np.log(d / np.float32(max_exact)) / np.float32(np.log(128 / max_exact)) * np.float32(32 - max_exact)).astype(np.int32)
    large = np.minimum(large, 31)
    return np.where(dist < max_exact, dist, large)


_NC_CACHE = {}


def kernel(x_prompt, x_sample, state_conv, state_rnn, cache_k_win, cache_v_win,
           norm_pre, norm_post, w_in, conv_w, conv_b, w_gate_a, b_gate_a, w_gate_x, b_gate_x,
           lru_lambda, attn_sinks, rel_bias, w_out):
    f32 = np.float32
    x_prompt = np.asarray(x_prompt, f32)
    w_in0 = np.asarray(w_in, f32)[0]
    w_out0 = np.asarray(w_out, f32)[0]
    qperm = np.concatenate([np.arange(h * 64, h * 64 + 64) for h in LPOS])
    cols = np.concatenate([np.arange(0, 1024), 1024 + qperm, np.arange(1536, 1792), 1792 + qperm])
    w_in_p = np.ascontiguousarray(w_in0[:, cols])
    rows = np.concatenate([np.arange(0, 512), 512 + qperm])
    w_out_p = np.ascontiguousarray(w_out0[rows, :])

    def pg(v):
        return np.ascontiguousarray(np.asarray(v, f32).reshape(4, 128).T)

    gpre = np.ascontiguousarray(np.asarray(norm_pre, f32)[0].reshape(8, 128).T)
    gpost = np.ascontiguousarray(np.broadcast_to(np.asarray(norm_post, f32)[0][None, :], (128, D)))
    cw = np.asarray(conv_w, f32)[0]
    convw = np.ascontiguousarray(cw.reshape(4, 4, 128).transpose(2, 1, 0).reshape(128, 16))
    vec4 = np.ascontiguousarray(np.stack([pg(np.asarray(conv_b)[0]), pg(np.asarray(b_gate_a)[0]), pg(np.asarray(b_gate_x)[0]),
                                          pg(np.asarray(lru_lambda)[0])], axis=1).reshape(128, 16))

    def blockdiag(w):
        w = np.asarray(w, f32)[0]
        o = np.zeros((128, 4, 128), f32)
        for g in range(4):
            for h in range(2):
                o[h * 64:(h + 1) * 64, g, h * 64:(h + 1) * 64] = w[2 * g + h]
        return o.reshape(128, 512)

    bda = blockdiag(w_gate_a)
    bdx = blockdiag(w_gate_x)
    hd = np.array([[g * 4 + cc for cc in range(4)] for g in range(2)])
    sinks = np.ascontiguousarray(np.broadcast_to(np.asarray(attn_sinks, f32)[0][hd].reshape(1, 8), (128, 8)))
    rb = np.asarray(rel_bias, f32)
    kk = np.arange(128)[:, None]
    qq = np.arange(128)[None, :]
    biasg = np.zeros((128, 2, 2, 4, 128), f32)
    maskc = np.zeros((128, 2, 128), f32)
    for blk in range(2):
        dist = qq + (128 if blk == 0 else 0) - kk
        valid = (dist >= 0) & (dist < 128)
        bkt = _t5_bucket(np.clip(dist, 0, 127))
        for g in range(2):
            for cc in range(4):
                biasg[:, blk, g, cc, :] = rb[bkt, hd[g, cc]]
        maskc[:, blk, :] = np.where(valid, 0.0, NEG)
    biasg = biasg.reshape(128, 2048)
    maskc = maskc.reshape(128, 256)
    ident = np.eye(128, dtype=f32)

    xs_all = np.asarray(x_sample, f32)[:, 0, :]
    sconv_all = np.asarray(state_conv, f32)[0].reshape(128, 1536)
    srnn_all = np.asarray(state_rnn, f32)[0]
    ck_all = np.asarray(cache_k_win, f32)[0].reshape(128, 128, 128)
    cv_all = np.asarray(cache_v_win, f32)[0].reshape(128, 128, 128)
    rowv = np.concatenate([cw.reshape(-1), np.asarray(conv_b, f32)[0], np.asarray(b_gate_a, f32)[0], np.asarray(b_gate_x, f32)[0],
                           np.asarray(lru_lambda, f32)[0]])
    rowp = np.ascontiguousarray(np.broadcast_to(rowv[None, :], (SB, 4096)))
    posh = np.array([g * 4 + cc for cc in range(4) for g in range(2)])
    biass = np.ascontiguousarray(rb[_t5_bucket(127 - np.arange(128))][:, posh])

    def padrows(a):
        return np.concatenate([a.reshape(SB * 128, 128), np.zeros((128, 128), f32)], axis=0)

    in_maps = []
    for c in range(NCORES):
        b, j = c // 4, c % 4
        xc = np.zeros((PRE + CH + 128, D), f32)
        flag = np.zeros((128, 3), f32)
        for kf in range(3):
            cj = j - 3 + kf
            if cj >= 0:
                xc[kf * CH:(kf + 1) * CH] = x_prompt[b, cj * CH:(cj + 1) * CH]
                flag[:, kf] = 1.0
        xc[PRE + 128:] = x_prompt[b, j * CH:(j + 1) * CH]
        xc[PRE:PRE + 128] = xc[PRE - 128:PRE]
        hmask = np.full((128, 1), 0.0 if j > 0 else NEG, f32)
        sel = np.zeros((128, 8), f32)
        for r in range(NCORES):
            if r // 4 == b and r < c:
                sel[:, r] = 1.0
        in_maps.append({"xc": xc, "w_in": w_in_p, "w_out": w_out_p, "gpre": gpre, "gpost": gpost, "convw": convw, "vec4": vec4,
                        "bda": bda, "bdx": bdx, "sinks": sinks, "biasg": biasg, "maskc": maskc, "hmask": hmask, "sel": sel, "flag": flag,
                        "ident": ident, "xs": np.concatenate([xs_all[c * SB:(c + 1) * SB], np.zeros((128 - SB, D), f32)], axis=0),
                        "sconv": np.ascontiguousarray(sconv_all[c * SB:(c + 1) * SB]), "srnn": np.ascontiguousarray(srnn_all[c * SB:(c + 1) * SB]),
                        "ck": padrows(ck_all[c * SB:(c + 1) * SB]), "cv": padrows(cv_all[c * SB:(c + 1) * SB]),
                        "rowp": rowp, "biass": biass})

    if "nc" not in _NC_CACHE:
        _NC_CACHE["nc"] = build_program()
    nc = _NC_CACHE["nc"]
    res = run_bass_kernel_spmd(nc, in_maps, core_ids=list(range(NCORES)))
    R = res.results

    y_prompt = np.stack([np.concatenate([R[b * 4 + j]["y"] for j in range(4)], axis=0) for b in range(2)], axis=0)
    new_conv_p = np.stack([R[3]["nconv"], R[7]["nconv"]])[None]
    new_rnn_p = np.stack([R[3]["nrnn"], R[7]["nrnn"]])[None]
    new_k_p = np.stack([R[3]["nk"].reshape(128, 2, 64), R[7]["nk"].reshape(128, 2, 64)])[None]
    new_v_p = np.stack([R[3]["nv"].reshape(128, 2, 64), R[7]["nv"].reshape(128, 2, 64)])[None]
    cat = lambda k: np.concatenate([R[c][k] for c in range(NCORES)], axis=0)
    y_sample = cat("ys").reshape(128, 1, D).astype(f32)
    new_conv_s = cat("nconvs").reshape(1, 128, 3, 512).astype(f32)
    new_rnn_s = cat("nrnns").reshape(1, 128, 512).astype(f32)
    new_k_s = cat("nks").reshape(1, 128, 128, 2, 64).astype(f32)
    new_v_s = cat("nvs").reshape(1, 128, 128, 2, 64).astype(f32)
    return (y_prompt.astype(f32), y_sample, new_conv_p.astype(f32), new_rnn_p.astype(f32), new_k_p.astype(f32), new_v_p.astype(f32),
            new_conv_s, new_rnn_s, new_k_s, new_v_s)
```
